# Optimizing a Trainium2 kernel written in Bass

```python
import math
import jax, jax.numpy as jnp
from jax import lax
import numpy as np

D_MODEL = 1024
BATCH = 8
SEQ = 8192
DEPTH = 4
DEC_BATCH = 2
DEC_SEQ = 16384
PAST_LEN = 128

HG_HEADS = 4
HG_DK = 128
HG_DV = 128
HG_WIDTH = HG_HEADS * HG_DV
HG_CHUNK = 64
AT_HEADS = 4
AT_HEAD_DIM = 64
AT_WIDTH = AT_HEADS * AT_HEAD_DIM
DIL_PATTERNS = ((128, 1), (512, 4), (2048, 16))
DIL_BLOCK = 64
ROPE_THETA = 500000.0
ROPE_DIM = AT_HEAD_DIM // 4
MEM_TOKENS = 256
MEM_HEADS = 4
MEM_HEAD_DIM = 64
MEM_WIDTH = MEM_HEADS * MEM_HEAD_DIM
MIX_WIDTH = HG_WIDTH + AT_WIDTH + MEM_WIDTH
NORM_EPS = 1e-6
MASK_VALUE = -1e30
SPLITS = (HG_WIDTH, HG_WIDTH, HG_WIDTH, HG_WIDTH, AT_WIDTH, AT_WIDTH, AT_WIDTH, MEM_WIDTH, HG_WIDTH, AT_WIDTH, MEM_WIDTH)
IN_WIDTH = 4 * HG_WIDTH + 3 * AT_WIDTH + MEM_WIDTH + MIX_WIDTH

kernel_name = 'hybrid_hgrn2_dilated_memory_encoder'


def _rms_norm(x, w):
    xf = x.astype(jnp.float32)
    y = xf * lax.rsqrt(jnp.mean(xf * xf, axis=-1, keepdims=True) + NORM_EPS)
    return (y * w.astype(jnp.float32)).astype(x.dtype)


def _head_norm(t, w):
    return t * lax.rsqrt(jnp.mean(t * t, axis=-1, keepdims=True) + NORM_EPS) * w.astype(jnp.float32)


def _partial_rope(t, pos):
    half = ROPE_DIM // 2
    inv_freq = ROPE_THETA ** (-jnp.arange(half, dtype=jnp.float32) * 2.0 / ROPE_DIM)
    ang = pos[:, None] * inv_freq[None, :]
    cos = jnp.cos(ang)[None, :, None, :]
    sin = jnp.sin(ang)[None, :, None, :]
    t1 = t[..., :half]
    t2 = t[..., half:ROPE_DIM]
    return jnp.concatenate([t1 * cos - t2 * sin, t2 * cos + t1 * sin, t[..., ROPE_DIM:]], axis=-1)


def _layer_lower_bounds(p):
    sm = jax.nn.softmax(p.astype(jnp.float32), axis=0)
    return jnp.cumsum(sm, axis=0) - sm[0:1]


def _hgrn2_chunk_scan(q, k, log_f, v):
    B, H, S, dk = q.shape
    dv = v.shape[-1]
    nc = S // HG_CHUNK

    def to_chunks(t):
        return t.reshape(B, H, nc, HG_CHUNK, t.shape[-1]).transpose(2, 0, 1, 3, 4)

    causal = jnp.tril(jnp.ones((HG_CHUNK, HG_CHUNK), dtype=bool))[None, None, :, :, None]

    def step(state, inp):
        qc, kc, lfc, vc = inp
        b = jnp.cumsum(lfc, axis=-2)
        b_last = b[:, :, -1:, :]
        o_inter = jnp.einsum('bhtk,bhkv->bhtv', qc * jnp.exp(b), state)
        diff = b[:, :, :, None, :] - b[:, :, None, :, :]
        decay = jnp.where(causal, jnp.exp(jnp.where(causal, diff, 0.0)), 0.0)
        scores = jnp.einsum('bhtk,bhsk,bhtsk->bhts', qc, kc, decay)
        o_intra = jnp.einsum('bhts,bhsv->bhtv', scores, vc)
        new_state = (jnp.exp(b_last[:, :, 0, :])[..., None] * state
                     + jnp.einsum('bhsk,bhsv->bhkv', kc * jnp.exp(b_last - b), vc))
        return new_state, o_inter + o_intra

    state0 = jnp.zeros((B, H, dk, dv), jnp.float32)
    _, o = lax.scan(step, state0, (to_chunks(q), to_chunks(k), to_chunks(log_f), to_chunks(v)))
    return o.transpose(1, 2, 0, 3, 4).reshape(B, H, S, dv)


def _dilated_branch(q, k, v, window, dilation):
    B, S, H, Dh = q.shape
    half = window // (2 * dilation)
    L = S // dilation
    nb = -(-L // DIL_BLOCK)
    Lp = nb * DIL_BLOCK

    def to_sub(t):
        return t.reshape(B, L, dilation, H, Dh).transpose(0, 2, 3, 1, 4)

    qs = jnp.pad(to_sub(q), ((0, 0), (0, 0), (0, 0), (0, Lp - L), (0, 0))).reshape(B, dilation, H, nb, DIL_BLOCK, Dh)

    def neighbour_blocks(t):
        tp = jnp.pad(to_sub(t), ((0, 0), (0, 0), (0, 0), (DIL_BLOCK, DIL_BLOCK + Lp - L), (0, 0)))
        tp = tp.reshape(B, dilation, H, nb + 2, DIL_BLOCK, Dh)
        return jnp.concatenate([tp[:, :, :, :-2], tp[:, :, :, 1:-1], tp[:, :, :, 2:]], axis=-2)

    kb = neighbour_blocks(k)
    vb = neighbour_blocks(v)
    qi = (jnp.arange(nb)[:, None] * DIL_BLOCK + jnp.arange(DIL_BLOCK)[None, :])[:, :, None]
    kj = (jnp.arange(nb)[:, None] * DIL_BLOCK - DIL_BLOCK + jnp.arange(3 * DIL_BLOCK)[None, :])[:, None, :]
    valid = (jnp.abs(kj - qi) <= half) & (kj >= 0) & (kj < L)
    s = jnp.einsum('brhnqd,brhnkd->brhnqk', qs, kb) * (Dh ** -0.5)
    s = jnp.where(valid, s, MASK_VALUE)
    lse = jax.nn.logsumexp(s, axis=-1)
    p = jnp.exp(s - lse[..., None])
    o = jnp.einsum('brhnqk,brhnkd->brhnqd', p, vb)

    def from_sub(t):
        t = t.reshape((B, dilation, H, Lp) + t.shape[5:])[:, :, :, :L]
        t = jnp.moveaxis(t, 3, 1)
        return t.reshape((B, S, H) + t.shape[4:])

    return from_sub(o), from_sub(lse)


def _layer(x, mem, pos, lb_f, lb_b, norm_w, w_in, hg_onorm_w, aq_w, ak_w, mem_norm_w, mem_wkv, mq_w, mk_w, w_out):
    B, S, _ = x.shape
    f32 = jnp.float32
    h = _rms_norm(x, norm_w)
    z = jnp.matmul(h, w_in).astype(f32)
    points = [int(p) for p in np.cumsum(SPLITS)[:-1]]
    hq, hf_fwd, hf_bwd, hi, aq, ak, av, mq, g_hg, g_at, g_mem = jnp.split(z, points, axis=-1)

    def hg_heads(t, d):
        return t.reshape(B, S, HG_HEADS, d).transpose(0, 2, 1, 3)

    q = jax.nn.silu(hg_heads(hq, HG_DK))
    v = hg_heads(hi, HG_DV)

    def gate_terms(zf, lb):
        lb = lb.astype(f32).reshape(1, HG_HEADS, 1, HG_DK)
        f = lb + (1.0 - lb) * jax.nn.sigmoid(zf)
        log_f = jnp.log(f)
        kk = (1.0 - lb) * jax.nn.sigmoid(-zf)
        return log_f, kk

    logf_f, k_f = gate_terms(hg_heads(hf_fwd, HG_DK), lb_f)
    logf_b, k_b = gate_terms(hg_heads(hf_bwd, HG_DK), lb_b)
    o_fwd = _hgrn2_chunk_scan(q, k_f, logf_f, v)
    flip = lambda t: jnp.flip(t, axis=2)
    o_bwd = flip(_hgrn2_chunk_scan(flip(q), flip(k_b), flip(logf_b), flip(v)))
    o_hg = _head_norm((o_fwd + o_bwd).transpose(0, 2, 1, 3), hg_onorm_w).reshape(B, S, HG_WIDTH)
    o_hg = o_hg * jax.nn.silu(g_hg)

    qa = _partial_rope(_head_norm(aq.reshape(B, S, AT_HEADS, AT_HEAD_DIM), aq_w), pos)
    ka = _partial_rope(_head_norm(ak.reshape(B, S, AT_HEADS, AT_HEAD_DIM), ak_w), pos)
    va = av.reshape(B, S, AT_HEADS, AT_HEAD_DIM)
    outs = []
    lses = []
    for window, dilation in DIL_PATTERNS:
        o_i, lse_i = _dilated_branch(qa, ka, va, window, dilation)
        outs.append(o_i)
        lses.append(lse_i)
    wts = jax.nn.softmax(jnp.stack(lses, axis=0), axis=0)
    o_at = jnp.sum(wts[..., None] * jnp.stack(outs, axis=0), axis=0).reshape(B, S, AT_WIDTH)
    o_at = o_at * jax.nn.silu(g_at)

    M = mem.shape[1]
    mh = _rms_norm(mem, mem_norm_w)
    mkv = jnp.matmul(mh, mem_wkv).astype(f32)
    mk = _head_norm(mkv[..., :MEM_WIDTH].reshape(B, M, MEM_HEADS, MEM_HEAD_DIM), mk_w)
    mv = mkv[..., MEM_WIDTH:].reshape(B, M, MEM_HEADS, MEM_HEAD_DIM)
    mqh = _head_norm(mq.reshape(B, S, MEM_HEADS, MEM_HEAD_DIM), mq_w)
    sm = jnp.einsum('bshd,bmhd->bhsm', mqh, mk) * (MEM_HEAD_DIM ** -0.5)
    pm = jax.nn.softmax(sm, axis=-1)
    o_mem = jnp.einsum('bhsm,bmhd->bshd', pm, mv).reshape(B, S, MEM_WIDTH)
    o_mem = o_mem * jax.nn.silu(g_mem)

    mixed = jnp.concatenate([o_hg, o_at, o_mem], axis=-1).astype(x.dtype)
    return x + jnp.matmul(mixed, w_out)


def setup_inputs(seed: int = 0) -> dict:
    key = jax.random.key(seed)
    ks = jax.random.split(key, 16)
    f32 = jnp.float32
    nrm = lambda k, shape, scale: scale * jax.random.normal(k, shape, f32)
    return {
        'x_prompt': nrm(ks[0], (BATCH, SEQ, D_MODEL), 1.0),
        'x_sample': nrm(ks[1], (DEC_BATCH, DEC_SEQ, D_MODEL), 1.0),
        'mem_prompt': nrm(ks[2], (BATCH, MEM_TOKENS, D_MODEL), 1.0),
        'mem_sample': nrm(ks[3], (DEC_BATCH, MEM_TOKENS, D_MODEL), 1.0),
        'norm_w': 1.0 + nrm(ks[4], (DEPTH, D_MODEL), 0.02),
        'w_in': nrm(ks[5], (DEPTH, D_MODEL, IN_WIDTH), D_MODEL ** -0.5),
        'hgrn_lb_fwd': nrm(ks[6], (DEPTH, HG_WIDTH), 0.1),
        'hgrn_lb_bwd': nrm(ks[7], (DEPTH, HG_WIDTH), 0.1),
        'hgrn_onorm_w': 1.0 + nrm(ks[8], (DEPTH, HG_DV), 0.02),
        'attn_qnorm_w': 1.0 + nrm(ks[9], (DEPTH, AT_HEAD_DIM), 0.02),
        'attn_knorm_w': 1.0 + nrm(ks[10], (DEPTH, AT_HEAD_DIM), 0.02),
        'mem_norm_w': 1.0 + nrm(ks[11], (DEPTH, D_MODEL), 0.02),
        'mem_wkv': nrm(ks[12], (DEPTH, D_MODEL, 2 * MEM_WIDTH), D_MODEL ** -0.5),
        'mem_qnorm_w': 1.0 + nrm(ks[13], (DEPTH, MEM_HEAD_DIM), 0.02),
        'mem_knorm_w': 1.0 + nrm(ks[14], (DEPTH, MEM_HEAD_DIM), 0.02),
        'w_out': nrm(ks[15], (DEPTH, MIX_WIDTH, D_MODEL), MIX_WIDTH ** -0.5),
    }


def reference(x_prompt, x_sample, mem_prompt, mem_sample, norm_w, w_in, hgrn_lb_fwd, hgrn_lb_bwd, hgrn_onorm_w, attn_qnorm_w, attn_knorm_w, mem_norm_w, mem_wkv, mem_qnorm_w, mem_knorm_w, w_out):
    lb_fwd = _layer_lower_bounds(hgrn_lb_fwd)
    lb_bwd = _layer_lower_bounds(hgrn_lb_bwd)

    def trunk(x, mem):
        pos = jnp.arange(x.shape[1], dtype=jnp.float32)
        for l in range(DEPTH):
            x = _layer(x, mem, pos, lb_fwd[l], lb_bwd[l], norm_w[l], w_in[l], hgrn_onorm_w[l],
                       attn_qnorm_w[l], attn_knorm_w[l], mem_norm_w[l], mem_wkv[l],
                       mem_qnorm_w[l], mem_knorm_w[l], w_out[l])
        return x

    y_prompt = trunk(x_prompt, mem_prompt)
    y_sample = trunk(x_sample, mem_sample)
    return (y_prompt, y_sample)
```

```python
import math
import os as _os
from contextlib import ExitStack
import numpy as np
import concourse.bass as bass
import concourse.mybir as mybir
from concourse.bass_utils import run_bass_kernel_spmd

F32 = mybir.dt.float32
BF16 = mybir.dt.bfloat16
AF = mybir.ActivationFunctionType
ALU = mybir.AluOpType
EPS = 1e-6
NSLOT = 8


class Buf:
    __slots__ = ("w", "r")

    def __init__(self):
        self.w = None
        self.r = {}


class Tl:
    def __init__(self, t):
        self.t = t
        self.b = Buf()


class FW:
    LIM = 30000

    def __init__(self, nc):
        self.nc = nc
        self.engs = {"pe": nc.tensor, "act": nc.scalar, "dve": nc.vector, "pool": nc.gpsimd, "sp": nc.sync}
        self.cur = {}
        self.seen = {k: {} for k in self.engs}
        self.last = {}
        self.nsem = 0
        self.dslots = {"sp": [], "pool": []}
        self.dnext = {"sp": 0, "pool": 0}
        self.nins = 0

    def newsem(self):
        self.nsem += 1
        return [self.nsem, self.nc.alloc_semaphore(name=f"fs{self.nsem}")]

    def _wait(self, e, ev):
        if self.seen[e].get(ev[0], 0) >= ev[2]:
            return
        self.engs[e].wait_ge(ev[1], ev[2])
        self.seen[e][ev[0]] = ev[2]

    def _deps(self, e, reads, writes):
        for b in reads:
            if b.w is not None and not (e == "pe" and b.w[3] == "pe"):
                self._wait(e, b.w)
        for b in writes:
            if b.w is not None and not (e == "pe" and b.w[3] == "pe"):
                self._wait(e, b.w)
            for ev in b.r.values():
                if not (e == "pe" and ev[3] == "pe"):
                    self._wait(e, ev)

    def _mark(self, ev, key, reads, writes):
        for b in reads:
            b.r[key] = ev
        for b in writes:
            b.w = ev
            b.r = {}

    def op(self, e, fn, reads=(), writes=()):
        self._deps(e, reads, writes)
        ins = fn(self.engs[e])
        c = self.cur.get(e)
        if c is None or c[2] >= self.LIM:
            c = self.newsem() + [0]
            self.cur[e] = c
        c[2] += 1
        ins.then_inc(c[1], 1)
        ev = (c[0], c[1], c[2], e)
        self._mark(ev, e, reads, writes)
        self.last[e] = ev
        self.nins += 1

    def dma(self, q, out, in_, reads=(), writes=(), **kw):
        self._deps(q, reads, writes)
        slots = self.dslots[q]
        if len(slots) < NSLOT:
            slots.append(self.newsem() + [0])
            s = slots[-1]
        else:
            s = slots[self.dnext[q] % NSLOT]
        self.dnext[q] += 1
        if s[2] > 0:
            self._wait(q, (s[0], s[1], s[2], "dma"))
        if s[2] + 16 > self.LIM:
            s[:] = self.newsem() + [0]
        ins = self.engs[q].dma_start(out=out, in_=in_, **kw)
        s[2] += 16
        ins.then_inc(s[1], 16)
        ev = (s[0], s[1], s[2], "dma")
        self._mark(ev, ("d", s[0], s[2]), reads, writes)
        self.nins += 1

    def barrier(self):
        evs = [self.last[e] for e in ("pe", "act", "dve", "pool") if e in self.last]
        for q in self.dslots:
            for s in self.dslots[q]:
                if s[2] > 0:
                    evs.append((s[0], s[1], s[2], "dma"))
        for e in self.engs:
            for ev in evs:
                if e == "pe" and ev[3] == "pe":
                    continue
                self._wait(e, ev)


class DT:
    def __init__(self, ap):
        self.ap = ap
        self.bufs = {}

    def b(self, i):
        if i not in self.bufs:
            self.bufs[i] = Buf()
        return self.bufs[i]


def host_consts(T):
    c = {}
    s = np.arange(128)[:, None]
    t = np.arange(128)[None, :]
    same = (s // 64) == (t // 64)
    c0 = (t // 64) * 64
    A_f = (same & (s <= t)).astype(np.float32) - (same & (s <= c0 + 31)).astype(np.float32)
    A_b = (same & (s >= t)).astype(np.float32) - (same & (s >= c0 + 32)).astype(np.float32)
    B_f = (same & (s > t)).astype(np.float32)
    B_b = (same & (s < t)).astype(np.float32)
    M_f = np.zeros((128, 4), np.float32)
    M_b = np.zeros((128, 4), np.float32)
    sv = np.arange(128)
    for ch in range(2):
        inc = (sv // 64) == ch
        M_f[:, 2 * ch] = inc & (sv <= ch * 64 + 31)
        M_f[:, 2 * ch + 1] = inc
        M_b[:, 2 * ch] = inc & (sv >= ch * 64 + 32)
        M_b[:, 2 * ch + 1] = inc
    K_f = (same & (s <= t)).astype(np.float32)
    K_b = (same & (s >= t)).astype(np.float32)
    ident = np.eye(128, dtype=np.float32)
    ones64 = ((s // 64) == (t // 64)).astype(np.float32)
    ones128 = np.ones((128, 128), np.float32)
    Rm = np.zeros((128, 128), np.float32)
    for m in range(128):
        j = m % 64
        if j < 8:
            Rm[m + 8, m] = -1.0
        elif j < 16:
            Rm[m - 8, m] = 1.0
    cst = np.concatenate([A_f, A_b, B_f, B_b, K_f, K_b, ident, ones64, ones128, Rm, M_f, M_b], axis=1)
    c["cst"] = np.ascontiguousarray(cst.astype(np.float32))
    am = np.zeros((20, 128, 512), np.float32)
    j = np.arange(128)[:, None]
    i = np.arange(512)[None, :]
    for ri, r in enumerate(range(-8, 12)):
        d = r * 128 + j - i
        am[ri] = ((np.abs(d) <= 64).astype(np.float32)
                  + ((d % 4 == 0) & (np.abs(d) <= 256)).astype(np.float32)
                  + ((d % 16 == 0) & (np.abs(d) <= 1024)).astype(np.float32))
    c["amask"] = np.where(am > 0, 8.0 * np.log(np.maximum(am, 1.0)), -80000.0).astype(np.float32)
    return c


def rope_tables(pos):
    half = 8
    inv = (500000.0 ** (-np.arange(half, dtype=np.float32) * 2.0 / 16.0)).astype(np.float32)
    ang = pos.astype(np.float32)[None, :] * inv[:, None]
    C = np.ones((128, pos.shape[0]), np.float32)
    S = np.zeros((128, pos.shape[0]), np.float32)
    for hb in (0, 64):
        C[hb:hb + 8] = np.cos(ang)
        C[hb + 8:hb + 16] = np.cos(ang)
        S[hb:hb + 8] = np.sin(ang)
        S[hb + 8:hb + 16] = np.sin(ang)
    return C, S


CO = {"A_f": 0, "A_b": 128, "B_f": 256, "B_b": 384, "K_f": 512, "K_b": 640, "ident": 768, "ones64": 896,
      "ones128": 1024, "Rm": 1152, "M_f": 1280, "M_b": 1284}
NCST = 1288


def build(T, depth, debug=False):
    NT = T // 512
    HALF = T // 2
    nc = bass.Bass("TRN2", target_bir_lowering=False)
    fw = FW(nc)

    def din(name, shape, dt=F32):
        return nc.dram_tensor(name, list(shape), dt, kind="ExternalInput").ap()

    x_in = din("x", [T, 1024])
    mem_in = din("mem", [2, 256, 1024])
    flag_in = din("flag", [128, 1])
    ropeC_in = din("ropeC", [128, T])
    ropeS_in = din("ropeS", [128, T])
    cst_in = din("cst", [128, NCST])
    amask_in = din("amask", [20, 128, 512])
    norm_w = din("norm_w", [depth, 1024])
    w_in = din("w_in", [depth, 1024, 4096])
    lbp = {"f": din("hgrn_lb_fwd", [depth, 512]), "b": din("hgrn_lb_bwd", [depth, 512])}
    onorm_w = din("hgrn_onorm_w", [depth, 128])
    aq_w = din("attn_qnorm_w", [depth, 64])
    ak_w = din("attn_knorm_w", [depth, 64])
    mem_norm_w = din("mem_norm_w", [depth, 1024])
    mem_wkv = din("mem_wkv", [depth, 1024, 512])
    mq_w = din("mem_qnorm_w", [depth, 64])
    mk_w = din("mem_knorm_w", [depth, 64])
    w_out = din("w_out", [depth, 1024, 1024])
    y_out = DT(nc.dram_tensor("y", [T, 1024], F32, kind="ExternalOutput").ap())
    x_dt = DT(x_in)

    skind = "ExternalOutput" if debug else "Internal"

    def scr(name, shape, dt):
        return DT(nc.dram_tensor(name, list(shape), dt, kind=skind).ap())

    qT = scr("s_qT", [512, T], BF16)
    kT = {"f": scr("s_kTf", [512, T], BF16), "b": scr("s_kTb", [512, T], BF16)}
    ktok = {"f": scr("s_kf", [T, 512], BF16), "b": scr("s_kb", [T, 512], BF16)}
    lfh = {"f": scr("s_lfhf", [T, 512], BF16), "b": scr("s_lfhb", [T, 512], BF16)}
    lfl = {"f": scr("s_lflf", [T, 512], BF16), "b": scr("s_lflb", [T, 512], BF16)}
    vtok = scr("s_v", [T, 512], BF16)
    gT = scr("s_gT", [1024, T], BF16)
    aqT = scr("s_aqT", [256, T], BF16)
    akT = scr("s_akT", [256, T], BF16)
    va = scr("s_va", [T, 260], BF16)
    mqT = scr("s_mqT", [256, T], BF16)
    oT = {"f": scr("s_ofT", [512, T], F32), "b": scr("s_obT", [512, T], F32)}
    mixT = scr("s_mixT", [512, T], BF16)

    es0 = ExitStack()
    with es0:
        def sb0(name, shape, dt):
            return Tl(es0.enter_context(nc.sbuf_tensor(name, list(shape), dt)))

        ps_t = [Tl(es0.enter_context(nc.psum_tensor(f"ps{i}", [128, 512], F32))) for i in range(8)]
        ps_i = [0]

        def psum():
            p = ps_t[ps_i[0] % 6]
            ps_i[0] += 1
            return p
        pa_i = [0]

        def psum_acc():
            p = ps_t[6 + pa_i[0] % 2]
            pa_i[0] += 1
            return p

        def mm(out, lhsT, rhs, reads, writes, start=True, stop=True):
            fw.op("pe", lambda e: e.matmul(out, lhsT=lhsT, rhs=rhs, start=start, stop=stop), reads, writes)

        def act(out, in_, func, reads, writes, **kw):
            fw.op("act", lambda e: e.activation(out=out, in_=in_, func=func, **kw), reads, writes)

        def tt(eng, out, in0, in1, op, reads, writes):
            fw.op(eng, lambda e: e.tensor_tensor(out=out, in0=in0, in1=in1, op=op), reads, writes)

        def ts(eng, out, in0, s1, op0, reads, writes, s2=None, op1=None):
            if op1 is None:
                fw.op(eng, lambda e: e.tensor_scalar(out=out, in0=in0, scalar1=s1, scalar2=None, op0=op0), reads, writes)
            else:
                fw.op(eng, lambda e: e.tensor_scalar(out=out, in0=in0, scalar1=s1, scalar2=s2, op0=op0, op1=op1),
                      reads, writes)

        def stt(out, in0, scalar, in1, op0, op1, reads, writes):
            fw.op("dve", lambda e: e.scalar_tensor_tensor(out=out, in0=in0, scalar=scalar, in1=in1, op0=op0, op1=op1),
                  reads, writes)

        def cp(eng, out, in_, reads, writes):
            if eng == "act":
                fw.op("act", lambda e: e.copy(out=out, in_=in_), reads, writes)
            else:
                fw.op(eng, lambda e: e.tensor_copy(out=out, in_=in_), reads, writes)

        cst = sb0("cst_sb", [128, NCST], F32)
        fw.dma("sp", cst.t[:], cst_in[:, :], writes=[cst.b])
        cstb = sb0("cstb_sb", [128, NCST], BF16)
        cp("dve", cstb.t[:], cst.t[:], [cst.b], [cstb.b])
        flag = sb0("flag_sb", [128, 1], F32)
        fw.dma("sp", flag.t[:], flag_in[:, :], writes=[flag.b])

        def C32(name, w=128):
            return cst.t[:, CO[name]:CO[name] + w]

        def C16(name, w=128):
            return cstb.t[:, CO[name]:CO[name] + w]

        def col_load(dst, src1d, n):
            fw.dma("sp", dst.t[:, 0:n], src1d.rearrange("(c p) -> p c", p=128), writes=[dst.b],
                   allow_slow_non_contiguous=True)

        def col_load64(dst, src1d):
            for hb in (0, 64):
                fw.dma("sp", dst.t[hb:hb + 64, 0:1], src1d.rearrange("(p o) -> p o", o=1), writes=[dst.b],
                       allow_slow_non_contiguous=True)

        def rstd_inplace(tl_ap, n, reads_b):
            act(tl_ap, tl_ap, AF.Ln, [reads_b], [reads_b], scale=1.0 / n, bias=EPS)
            act(tl_ap, tl_ap, AF.Exp, [reads_b], [reads_b], scale=-0.5)

        def pass_A(l):
            src = x_dt if l == 0 else y_out
            with ExitStack() as es:
                def sb(name, shape, dt):
                    return Tl(es.enter_context(nc.sbuf_tensor(f"{name}_L{l}", list(shape), dt)))
                w = sb("A_w", [128, 8, 4096], BF16)
                for kc in range(8):
                    for q4 in range(4):
                        fw.dma("pool", w.t[:, kc, q4 * 1024:(q4 + 1) * 1024],
                               w_in[l, kc * 128:(kc + 1) * 128, q4 * 1024:(q4 + 1) * 1024], writes=[w.b])
                normw = sb("A_normw", [128, 8], F32)
                col_load(normw, norm_w[l], 8)
                gq = sb("A_gq", [128, 1], F32)
                gk = sb("A_gk", [128, 1], F32)
                gm = sb("A_gm", [128, 1], F32)
                col_load64(gq, aq_w[l])
                col_load64(gk, ak_w[l])
                col_load64(gm, mq_w[l])
                oml = {d: sb(f"A_oml{d}", [128, 512], F32) for d in "fb"}
                with ExitStack() as es2:
                    def sb2(name, shape, dt):
                        return Tl(es2.enter_context(nc.sbuf_tensor(f"{name}_L{l}", list(shape), dt)))
                    row = sb2("A_lbrow", [1, depth, 512], F32)
                    mx = sb2("A_lbmx", [1, 512], F32)
                    sm = sb2("A_lbsm", [1, 512], F32)
                    acc = sb2("A_lbacc", [1, 512], F32)
                    tmp = sb2("A_lbtmp", [1, 512], F32)
                    for d in ("f", "b"):
                        fw.dma("sp", row.t[0:1, :, :], lbp[d].rearrange("(o l) f -> o l f", o=1), writes=[row.b])
                        cp("dve", mx.t[:], row.t[0:1, 0, :], [row.b], [mx.b])
                        for j in range(1, depth):
                            tt("dve", mx.t[:], mx.t[:], row.t[0:1, j, :], ALU.max, [mx.b, row.b], [mx.b])
                        for j in range(depth):
                            tt("dve", row.t[0:1, j, :], row.t[0:1, j, :], mx.t[:], ALU.subtract, [row.b, mx.b], [row.b])
                        act(row.t[:], row.t[:], AF.Exp, [row.b], [row.b])
                        cp("dve", sm.t[:], row.t[0:1, 0, :], [row.b], [sm.b])
                        for j in range(1, depth):
                            tt("dve", sm.t[:], sm.t[:], row.t[0:1, j, :], ALU.add, [sm.b, row.b], [sm.b])
                        fw.op("dve", lambda e: e.reciprocal(out=sm.t[:], in_=sm.t[:]), [sm.b], [sm.b])
                        fw.op("dve", lambda e: e.memset(acc.t[:], 0.0), [], [acc.b])
                        for j in range(1, l + 1):
                            tt("dve", tmp.t[:], row.t[0:1, j, :], sm.t[:], ALU.mult, [row.b, sm.b], [tmp.b])
                            tt("dve", acc.t[:], acc.t[:], tmp.t[:], ALU.add, [acc.b, tmp.b], [acc.b])
                        ts("dve", acc.t[:], acc.t[:], -1.0, ALU.mult, [acc.b], [acc.b], s2=1.0, op1=ALU.add)
                        p = psum()
                        mm(p.t[:, :], C32("ones128")[0:1, :], acc.t[0:1, :], [cst.b, acc.b], [p.b])
                        cp("dve", oml[d].t[:], p.t[:, :], [p.b], [oml[d].b])
                    fw.barrier()

                xt = [sb(f"A_x{i}", [128, 1024], F32) for i in range(4)]
                ss = sb("A_ss", [128, 4], F32)
                hb = sb("A_hb", [128, 4, 1024], BF16)
                hTr = [sb(f"A_hT{j}", [128, 8, 512], BF16) for j in range(2)]
                qTs = sb("A_qTs", [128, 4, 512], BF16)
                gTs = sb("A_gTs", [128, 8, 512], BF16)
                aqTs = sb("A_aqTs", [128, 2, 512], BF16)
                akTs = sb("A_akTs", [128, 2, 512], BF16)
                mqTs = sb("A_mqTs", [128, 2, 512], BF16)
                lfhs = {d: sb(f"A_lfhs{d}", [128, 4, 512], BF16) for d in "fb"}
                lfls = {d: sb(f"A_lfls{d}", [128, 4, 512], BF16) for d in "fb"}
                ks = {d: sb(f"A_ks{d}", [128, 4, 512], BF16) for d in "fb"}
                kTs = {d: sb(f"A_kTs{d}", [128, 4, 512], BF16) for d in "fb"}
                vs = sb("A_vs", [128, 4, 512], BF16)
                vas = sb("A_vas", [128, 4, 260], BF16)
                fw.op("dve", lambda e: e.memset(vas.t[:], 1.0), [], [vas.b])
                NR = 2
                sq = [sb(f"A_sq{j}", [128, 512], BF16) for j in range(NR)]
                rsf = [sb(f"A_rsf{j}", [128, 512], F32) for j in range(NR)]
                qn = [sb(f"A_qn{j}", [128, 512], F32) for j in range(NR)]
                qnb = [sb(f"A_qnb{j}", [128, 512], BF16) for j in range(NR)]
                t1 = rsf
                t2 = [sb(f"A_t2{j}", [128, 512], F32) for j in range(NR)]
                rC = [sb(f"A_rC{j}", [128, 512], F32) for j in range(2)]
                rS = [sb(f"A_rS{j}", [128, 512], F32) for j in range(2)]
                NG = 2
                sg = [sb(f"A_sg{j}", [128, 512], F32) for j in range(NG)]
                k32 = [sb(f"A_k32{j}", [128, 512], F32) for j in range(NG)]
                sub_b = {}

                def SB(tl, idx):
                    key = (id(tl), idx)
                    if key not in sub_b:
                        sub_b[key] = Buf()
                    return sub_b[key]

                def SBall(tl, n):
                    return [SB(tl, j) for j in range(n)]
                pfree = list(ps_t)

                def palloc():
                    return pfree.pop(0)

                def prel(p):
                    pfree.append(p)

                def run_chains(gens, K):
                    active = []
                    it = iter(gens)
                    done = False
                    while True:
                        while len(active) < K and not done:
                            g = next(it, None)
                            if g is None:
                                done = True
                            else:
                                active.append(g)
                        if not active:
                            break
                        for g in list(active):
                            try:
                                next(g)
                            except StopIteration:
                                active.remove(g)

                def prologue(i):
                    t0 = i * 512
                    hT = hTr[i % 2]
                    fw.dma("sp", rC[i % 2].t[:], ropeC_in[:, t0:t0 + 512], writes=[rC[i % 2].b])
                    fw.dma("sp", rS[i % 2].t[:], ropeS_in[:, t0:t0 + 512], writes=[rS[i % 2].b])
                    for s in range(4):
                        fw.dma("sp", xt[s].t[:], src.ap[t0 + s * 128:t0 + (s + 1) * 128, :], reads=[src.b(i)], writes=[xt[s].b])
                        act(hb.t[:, s, :], xt[s].t[:], AF.Square, [xt[s].b], [SB(hb, s), ss.b], accum_out=ss.t[:, s:s + 1])
                        yield
                    rstd_inplace(ss.t[:], 1024.0, ss.b)
                    yield
                    for s in range(4):
                        ts("dve", hb.t[:, s, :], xt[s].t[:], ss.t[:, s:s + 1], ALU.mult, [xt[s].b, ss.b], [SB(hb, s)])
                        yield
                    for kc in range(8):
                        p = palloc()
                        for s in range(4):
                            mm(p.t[:, s * 128:(s + 1) * 128], hb.t[:, s, kc * 128:(kc + 1) * 128], C16("ident"),
                               [SB(hb, s), cstb.b], [p.b])
                        yield
                        ts("dve", hT.t[:, kc, :], p.t[:, :], normw.t[:, kc:kc + 1], ALU.mult, [p.b, normw.b], [SB(hT, kc)])
                        prel(p)
                        yield

                rfree = list(range(NR))
                gfree = list(range(NG))

                def tile_chains(i):
                    hT = hTr[i % 2]
                    hTb = SBall(hT, 8)
                    RC, RS = rC[i % 2], rS[i % 2]

                    def fm(p, col0):
                        for kc in range(8):
                            mm(p.t[:, :], w.t[:, kc, col0:col0 + 128], hT.t[:, kc, :], [w.b, hTb[kc]], [p.b],
                               start=(kc == 0), stop=(kc == 7))

                    def tm(p, s, col0, n):
                        for kc in range(8):
                            mm(p.t[:, 0:n], hT.t[:, kc, s * 128:(s + 1) * 128], w.t[:, kc, col0:col0 + n],
                               [hTb[kc], w.b], [p.b], start=(kc == 0), stop=(kc == 7))

                    def silu_chain(col0, dst, c):
                        p = palloc()
                        fm(p, col0)
                        yield
                        act(dst.t[:, c, :], p.t[:, :], AF.Silu, [p.b], [SB(dst, c)])
                        prel(p)

                    def v_chain(s):
                        p = palloc()
                        tm(p, s, 1536, 512)
                        yield
                        cp("act", vs.t[:, s, :], p.t[:, :], [p.b], [SB(vs, s)])
                        prel(p)

                    def av_chain(s):
                        p = palloc()
                        tm(p, s, 2560, 256)
                        yield
                        cp("act", vas.t[:, s, :].rearrange("p (h e) -> p h e", e=65)[:, :, 0:64],
                           p.t[:, 0:256].rearrange("p (h e) -> p h e", e=64), [p.b], [vas.b, SB(vas, s)])
                        prel(p)

                    def norm_chain(col0, gain, dst, c, rope):
                        while not rfree:
                            yield
                        r = rfree.pop(0)
                        p = palloc()
                        fm(p, col0 + c * 128)
                        yield
                        act(sq[r].t[:], p.t[:, :], AF.Square, [p.b], [sq[r].b])
                        yield
                        p2 = palloc()
                        mm(p2.t[:, :], C16("ones64"), sq[r].t[:], [cstb.b, sq[r].b], [p2.b])
                        yield
                        act(rsf[r].t[:], p2.t[:, :], AF.Ln, [p2.b], [rsf[r].b], scale=1.0 / 64, bias=EPS)
                        prel(p2)
                        act(rsf[r].t[:], rsf[r].t[:], AF.Exp, [rsf[r].b], [rsf[r].b], scale=-0.5)
                        yield
                        if not rope:
                            stt(dst.t[:, c, :], p.t[:, :], gain.t[:, 0:1], rsf[r].t[:], ALU.mult, ALU.mult,
                                [p.b, gain.b, rsf[r].b], [SB(dst, c)])
                            prel(p)
                            rfree.append(r)
                            return
                        stt(qn[r].t[:], p.t[:, :], gain.t[:, 0:1], rsf[r].t[:], ALU.mult, ALU.mult,
                            [p.b, gain.b, rsf[r].b], [qn[r].b])
                        prel(p)
                        yield
                        cp("dve", qnb[r].t[:], qn[r].t[:], [qn[r].b], [qnb[r].b])
                        tt("pool", t2[r].t[:], qn[r].t[:], RC.t[:], ALU.mult, [qn[r].b, RC.b], [t2[r].b])
                        yield
                        p3 = palloc()
                        mm(p3.t[:, :], C16("Rm"), qnb[r].t[:], [cstb.b, qnb[r].b], [p3.b])
                        yield
                        tt("dve", t1[r].t[:], p3.t[:, :], RS.t[:], ALU.mult, [p3.b, RS.b], [t1[r].b])
                        prel(p3)
                        yield
                        tt("dve", dst.t[:, c, :], t1[r].t[:], t2[r].t[:], ALU.add, [t1[r].b, t2[r].b], [SB(dst, c)])
                        rfree.append(r)

                    def fgate_chain(s, d, col0):
                        while not gfree:
                            yield
                        g = gfree.pop(0)
                        p = palloc()
                        tm(p, s, col0, 512)
                        yield
                        act(sg[g].t[:], p.t[:, :], AF.Exp, [p.b], [sg[g].b])
                        prel(p)
                        yield
                        act(sg[g].t[:], sg[g].t[:], AF.Ln, [sg[g].b], [sg[g].b], bias=1.0)
                        yield
                        act(sg[g].t[:], sg[g].t[:], AF.Exp, [sg[g].b], [sg[g].b], scale=-1.0)
                        yield
                        tt("dve", k32[g].t[:], sg[g].t[:], oml[d].t[:], ALU.mult, [sg[g].b, oml[d].b], [k32[g].b])
                        yield
                        act(sg[g].t[:], k32[g].t[:], AF.Ln, [k32[g].b], [sg[g].b], scale=-1.0, bias=1.0)
                        cp("pool", ks[d].t[:, s, :], k32[g].t[:], [k32[g].b], [SB(ks[d], s)])
                        yield
                        cp("dve", lfhs[d].t[:, s, :], sg[g].t[:], [sg[g].b], [SB(lfhs[d], s)])
                        yield
                        tt("pool", lfls[d].t[:, s, :], sg[g].t[:], lfhs[d].t[:, s, :], ALU.subtract,
                           [sg[g].b, SB(lfhs[d], s)], [SB(lfls[d], s)])
                        gfree.append(g)

                    def kT_chain(d, h):
                        p = palloc()
                        for s in range(4):
                            mm(p.t[:, s * 128:(s + 1) * 128], ks[d].t[:, s, h * 128:(h + 1) * 128], C16("ident"),
                               [SB(ks[d], s), cstb.b], [p.b])
                        yield
                        cp("dve", kTs[d].t[:, h, :], p.t[:, :], [p.b], [SB(kTs[d], h)])
                        prel(p)

                    ph1 = []
                    sil = [silu_chain(c * 128, qTs, c) for c in range(4)] + [silu_chain(3072 + c * 128, gTs, c) for c in range(8)]
                    cps = []
                    for s in range(4):
                        cps += [v_chain(s), av_chain(s)]
                    for j in range(12):
                        ph1.append(sil[j])
                        if j < 8:
                            ph1.append(cps[j])
                    nrm = []
                    for (col0, gain, dst, rope) in ((2048, gq, aqTs, True), (2304, gk, akTs, True), (2816, gm, mqTs, False)):
                        for c in range(2):
                            nrm.append(norm_chain(col0, gain, dst, c, rope))
                    fg = []
                    for s in range(4):
                        fg += [fgate_chain(s, "f", 512), fgate_chain(s, "b", 1024)]
                    ph2 = []
                    for j in range(8):
                        ph2.append(fg[j])
                        if j < 6:
                            ph2.append(nrm[j])
                    ph3 = [kT_chain(d, h) for d in "fb" for h in range(4)]
                    return ph1, ph2, ph3

                def stores(i):
                    t0 = i * 512
                    fmv = lambda dd: dd.ap[:, t0:t0 + 512].rearrange("(h p) t -> p h t", p=128)
                    tmv = lambda dd: dd.ap[t0:t0 + 512, :].rearrange("(s p) f -> p s f", p=128)
                    fw.dma("pool", fmv(qT), qTs.t[:], reads=SBall(qTs, 4), writes=[qT.b(i)])
                    fw.dma("pool", fmv(gT), gTs.t[:], reads=SBall(gTs, 8), writes=[gT.b(i)])
                    fw.dma("pool", tmv(vtok), vs.t[:], reads=SBall(vs, 4), writes=[vtok.b(i)])
                    fw.dma("pool", tmv(va), vas.t[:], reads=[vas.b] + SBall(vas, 4), writes=[va.b(i)])
                    for (dst, dd) in ((aqTs, aqT), (akTs, akT), (mqTs, mqT)):
                        fw.dma("pool", fmv(dd), dst.t[:], reads=SBall(dst, 2), writes=[dd.b(i)])
                    for d in "fb":
                        fw.dma("pool", tmv(lfh[d]), lfhs[d].t[:], reads=SBall(lfhs[d], 4), writes=[lfh[d].b(i)])
                        fw.dma("pool", tmv(lfl[d]), lfls[d].t[:], reads=SBall(lfls[d], 4), writes=[lfl[d].b(i)])
                        fw.dma("pool", tmv(ktok[d]), ks[d].t[:], reads=SBall(ks[d], 4), writes=[ktok[d].b(i)])
                        fw.dma("pool", fmv(kT[d]), kTs[d].t[:], reads=SBall(kTs[d], 4), writes=[kT[d].b(i)])

                def mark_store_reads(i):
                    pass

                run_chains([prologue(0)], 1)
                for i in range(NT):
                    ph1, ph2, ph3 = tile_chains(i)
                    run_chains(ph1, 3)
                    nxt = [prologue(i + 1)] if i + 1 < NT else []
                    run_chains(nxt + ph2, 4)
                    run_chains(ph3, 3)
                    stores(i)
                fw.barrier()

        def pass_H(l, d):
            fwd = d == "f"
            with ExitStack() as es:
                def sb(name, shape, dt):
                    return Tl(es.enter_context(nc.sbuf_tensor(f"{name}{d}_L{l}", list(shape), dt)))
                nb = 3
                lft = [sb(f"H_lfh{j}", [128, 4, 512], BF16) for j in range(nb)]
                llt = [sb(f"H_lfl{j}", [128, 4, 512], BF16) for j in range(nb)]
                kt_ = [sb(f"H_k{j}", [128, 4, 512], BF16) for j in range(nb)]
                kTt = [sb(f"H_kT{j}", [128, 4, 512], BF16) for j in range(nb)]
                qTt = [sb(f"H_qT{j}", [128, 4, 512], BF16) for j in range(nb)]
                vt = [sb(f"H_v{j}", [128, 4, 512], BF16) for j in range(nb)]
                ex3 = [sb(f"H_ex3{j}", [128, 512], F32) for j in range(2)]
                kd2 = [sb(f"H_kd{j}", [128, 4, 512], BF16) for j in range(2)]
                vm = [[sb(f"H_vm{j}_{c}", [128, 4, 512], BF16) for c in range(2)] for j in range(nb)]
                for j in range(nb):
                    for c in range(2):
                        fw.op("pool", lambda e: e.memset(vm[j][c].t[:], 0.0), [], [vm[j][c].b])
                qx = [sb(f"H_qx{j}", [128, 512], F32) for j in range(2)]
                kx = [sb(f"H_kx{j}", [128, 512], F32) for j in range(2)]
                qe2 = [sb(f"H_qe{j}", [128, 4, 512], BF16) for j in range(2)]
                ke2 = [sb(f"H_ke{j}", [128, 4, 512], BF16) for j in range(2)]
                ext2 = [sb(f"H_ext{j}", [128, 64], F32) for j in range(2)]
                PT = [sb(f"H_PT{j}", [128, 4, 128], BF16) for j in range(3)]
                S = [sb(f"H_S{h}", [128, 128], F32) for h in range(4)]
                Sm = [sb(f"H_Sm{j}", [128, 128], BF16) for j in range(16)]
                oTs = [sb(f"H_oT{j}", [128, 4, 512], F32) for j in range(2)]
                An, Bn, Kn, Mn = ("A_f", "B_f", "K_f", "M_f") if fwd else ("A_b", "B_b", "K_b", "M_b")
                K4 = sb("H_K4", [128, 4, 128], F32)
                for h in range(4):
                    cp("dve", K4.t[:, h, :], C32(Kn), [cst.b], [K4.b])
                    fw.op("dve", lambda e: e.memset(S[h].t[:], 0.0), [], [S[h].b])
                order = list(range(NT)) if fwd else list(range(NT - 1, -1, -1))
                pfree = list(ps_t)

                def palloc():
                    return pfree.pop(0)

                def prel(p):
                    pfree.append(p)

                def load(n):
                    i = order[n]
                    j = n % nb
                    t0 = i * 512
                    fw.dma("sp", lft[j].t[:], lfh[d].ap[t0:t0 + 512, :].rearrange("(s p) f -> p s f", p=128),
                           reads=[lfh[d].b(i)], writes=[lft[j].b])
                    fw.dma("sp", llt[j].t[:], lfl[d].ap[t0:t0 + 512, :].rearrange("(s p) f -> p s f", p=128),
                           reads=[lfl[d].b(i)], writes=[llt[j].b])
                    fw.dma("sp", kt_[j].t[:], ktok[d].ap[t0:t0 + 512, :].rearrange("(s p) f -> p s f", p=128),
                           reads=[ktok[d].b(i)], writes=[kt_[j].b])
                    vsrc = vtok.ap[t0:t0 + 512, :].rearrange("(s p) f -> p s f", p=128)
                    fw.dma("sp", vt[j].t[:], vsrc, reads=[vtok.b(i)], writes=[vt[j].b])
                    for c in range(2):
                        fw.dma("sp", vm[j][c].t[c * 64:(c + 1) * 64, :, :], vsrc[c * 64:(c + 1) * 64],
                               reads=[vtok.b(i)], writes=[vm[j][c].b])
                    fw.dma("sp", kTt[j].t[:], kT[d].ap[:, t0:t0 + 512].rearrange("(h p) t -> p h t", p=128),
                           reads=[kT[d].b(i)], writes=[kTt[j].b])
                    fw.dma("sp", qTt[j].t[:], qT.ap[:, t0:t0 + 512].rearrange("(h p) t -> p h t", p=128),
                           reads=[qT.b(i)], writes=[qTt[j].b])

                def ephase(n):
                    j = n % nb
                    L, L2, K_, KT_, QT_ = lft[j], llt[j], kt_[j], kTt[j], qTt[j]
                    kd, qe, ke, ext = kd2[n % 2], qe2[n % 2], ke2[n % 2], ext2[n % 2]
                    for s in range(4):
                        p = palloc()
                        mm(p.t[:, :], C16(Bn), L.t[:, s, :], [cstb.b, L.b], [p.b], start=True, stop=False)
                        mm(p.t[:, :], C16(Bn), L2.t[:, s, :], [cstb.b, L2.b], [p.b], start=False, stop=True)
                        e3 = ex3[s % 2]
                        act(e3.t[:], p.t[:, :], AF.Exp, [p.b], [e3.b])
                        prel(p)
                        tt("pool", kd.t[:, s, :], K_.t[:, s, :], e3.t[:], ALU.mult, [K_.b, e3.b], [kd.b])
                    pe_ = palloc()
                    for h in range(4):
                        for s in range(4):
                            c0 = (h * 4 + s) * 4
                            mm(pe_.t[:, c0:c0 + 4], L.t[:, s, h * 128:(h + 1) * 128], C16(Mn, 4), [L.b, cstb.b], [pe_.b],
                               start=True, stop=False)
                            mm(pe_.t[:, c0:c0 + 4], L2.t[:, s, h * 128:(h + 1) * 128], C16(Mn, 4), [L2.b, cstb.b], [pe_.b],
                               start=False, stop=True)
                    act(ext.t[:], pe_.t[:, 0:64], AF.Exp, [pe_.b], [ext.b])
                    prel(pe_)
                    for h in range(4):
                        p = palloc()
                        for s in range(4):
                            mm(p.t[:, s * 128:(s + 1) * 128], L.t[:, s, h * 128:(h + 1) * 128], C16(An), [L.b, cstb.b], [p.b],
                               start=True, stop=False)
                            mm(p.t[:, s * 128:(s + 1) * 128], L2.t[:, s, h * 128:(h + 1) * 128], C16(An), [L2.b, cstb.b], [p.b],
                               start=False, stop=True)
                        a, b_ = qx[h % 2], kx[h % 2]
                        act(a.t[:], p.t[:, :], AF.Exp, [p.b], [a.b])
                        act(b_.t[:], p.t[:, :], AF.Exp, [p.b], [b_.b], scale=-1.0)
                        prel(p)
                        tt("dve", qe.t[:, h, :], QT_.t[:, h, :], a.t[:], ALU.mult, [QT_.b, a.b], [qe.b])
                        tt("pool", ke.t[:, h, :], KT_.t[:, h, :], b_.t[:], ALU.mult, [KT_.b, b_.b], [ke.b])

                load(0)
                if NT > 1:
                    load(1)
                ephase(0)
                smi = [0]
                pti = [0]
                for n in range(NT):
                    i = order[n]
                    j = n % nb
                    t0 = i * 512
                    V_, VM = vt[j], vm[j]
                    kd, qe, ke, ext = kd2[n % 2], qe2[n % 2], ke2[n % 2], ext2[n % 2]
                    o_ = oTs[n % 2]
                    subs = list(range(4)) if fwd else list(range(3, -1, -1))

                    def stageA(s):
                        ssl = slice(s * 128, (s + 1) * 128)
                        p = palloc()
                        for h in range(4):
                            mm(p.t[:, h * 128:(h + 1) * 128], ke.t[:, h, ssl], qe.t[:, h, ssl], [ke.b, qe.b], [p.b])
                        pt = PT[pti[0] % 3]
                        pti[0] += 1
                        tt("dve", pt.t[:], p.t[:, :].rearrange("p (h t) -> p h t", h=4), K4.t[:], ALU.mult, [p.b, K4.b], [pt.b])
                        prel(p)
                        pd = [palloc(), palloc()]
                        for c in range(2):
                            for h in range(4):
                                hs = slice(h * 128, (h + 1) * 128)
                                mm(pd[c].t[:, hs], kd.t[:, s, hs], VM[c].t[:, s, hs], [kd.b, VM[c].b], [pd[c].b])
                        return pt, pd

                    def stageC(s, pt, pd):
                        ssl = slice(s * 128, (s + 1) * 128)
                        tg = t0 + s * 128
                        po = palloc()
                        for h in range(4):
                            hs = slice(h * 128, (h + 1) * 128)
                            mm(po.t[:, hs], V_.t[:, s, hs], pt.t[:, h, :], [V_.b, pt.b], [po.b], start=(h == 0), stop=False)
                        chs = (0, 1) if fwd else (1, 0)
                        for ci, c in enumerate(chs):
                            tok = tg + c * 64
                            if (fwd and tok == HALF) or ((not fwd) and tok + 64 == HALF):
                                for h in range(4):
                                    ts("dve", S[h].t[:], S[h].t[:], flag.t[:, 0:1], ALU.mult, [S[h].b, flag.b], [S[h].b])
                            sms = []
                            for h in range(4):
                                ec = (h * 4 + s) * 4 + 2 * c
                                sm_ = Sm[smi[0] % 16]
                                smi[0] += 1
                                act(sm_.t[:], S[h].t[:], AF.Copy, [S[h].b, ext.b], [sm_.b], scale=ext.t[:, ec:ec + 1])
                                sms.append(sm_)
                            for h in range(4):
                                ec = (h * 4 + s) * 4 + 2 * c
                                stt(S[h].t[:], S[h].t[:], ext.t[:, ec + 1:ec + 2], pd[c].t[:, h * 128:(h + 1) * 128],
                                    ALU.mult, ALU.add, [S[h].b, ext.b, pd[c].b], [S[h].b])
                            for h in range(4):
                                mm(po.t[:, h * 128 + c * 64:h * 128 + (c + 1) * 64], sms[h].t[:],
                                   qe.t[:, h, s * 128 + c * 64:s * 128 + (c + 1) * 64],
                                   [sms[h].b, qe.b], [po.b], start=False, stop=(ci == 1))
                        prel(pd[0])
                        prel(pd[1])
                        cp("act", o_.t[:, :, ssl], po.t[:, :].rearrange("p (h t) -> p h t", h=4), [po.b], [o_.b])
                        prel(po)

                    if n + 2 < NT:
                        load(n + 2)
                    prev = None
                    for bi, s in enumerate(subs):
                        cur = (s,) + stageA(s)
                        if bi == 1 and n + 1 < NT:
                            ephase(n + 1)
                        if prev is not None:
                            stageC(*prev)
                        prev = cur
                    stageC(*prev)
                    fw.dma("pool", oT[d].ap[:, t0:t0 + 512].rearrange("(h p) t -> p h t", p=128), o_.t[:], reads=[o_.b],
                           writes=[oT[d].b(i)])
                fw.barrier()

        def attn_core(sb, NTq, qsrc, kv_for_tile, masks, out_row0, tag):
            LA = 5
            qm = [[sb(f"{tag}_qm{j}_{par}", [128, 2, 512], BF16) for par in range(2)] for j in range(2)]
            for j in range(2):
                for par in range(2):
                    fw.op("pool", lambda e: e.memset(qm[j][par].t[:], 0.0), [], [qm[j][par].b])
            gt = [sb(f"{tag}_g{j}", [128, 2, 512], BF16) for j in range(2)]
            pT = [sb(f"{tag}_pT{j}", [128, 512], BF16) for j in range(8)]
            sc = [sb(f"{tag}_sc{j}", [128, 512], F32) for j in range(6)] if masks is not None else None
            otok = sb(f"{tag}_otok", [128, 4, 256], BF16)
            rden = sb(f"{tag}_rden", [128, 4], F32)
            mo = [sb(f"{tag}_mo{j}", [128, 2, 512], BF16) for j in range(2)]
            pi = [0]
            for i in range(NTq):
                t0 = i * 512
                QM, G = qm[i % 2], gt[i % 2]
                qv = qsrc.ap[:, t0:t0 + 512].rearrange("(h p) t -> p h t", p=128)
                for par in range(2):
                    fw.dma("sp", QM[par].t[par * 64:(par + 1) * 64, :, :], qv[par * 64:(par + 1) * 64], reads=[qsrc.b(i)],
                           writes=[QM[par].b])
                fw.dma("sp", G.t[:], gT.ap[512 + out_row0:512 + out_row0 + 256, t0:t0 + 512].rearrange("(h p) t -> p h t", p=128),
                       reads=[gT.b(i)], writes=[G.b])
                ktl = kv_for_tile(i)
                nk = len(ktl)
                po_h = {}

                def stage1(h, ki):
                    ch, pr = h // 2, (h % 2) * 64
                    kfn, vfn, mi, kbufs = ktl[ki]
                    p = psum()
                    mm(p.t[:, :], kfn(ch, pr), QM[h % 2].t[:, ch, :], kbufs + [QM[h % 2].b], [p.b])
                    e_ = pT[pi[0] % 8]
                    if masks is not None:
                        s_ = sc[pi[0] % 6]
                        tt("dve", s_.t[:], p.t[:, :], masks.t[:, mi, :], ALU.add, [p.b, masks.b], [s_.b])
                        act(e_.t[:], s_.t[:], AF.Exp, [s_.b], [e_.b], scale=0.125)
                    else:
                        act(e_.t[:], p.t[:, :], AF.Exp, [p.b], [e_.b], scale=0.125)
                    pi[0] += 1
                    return e_

                def stage2(h, ki, e_):
                    kfn, vfn, mi, kbufs = ktl[ki]
                    if ki == 0:
                        po_h[h] = psum_acc()
                    po = po_h[h]
                    for s in range(4):
                        mm(po.t[:, s * 128:s * 128 + 65], e_.t[:, s * 128:(s + 1) * 128], vfn(h), [e_.b] + kbufs, [po.b],
                           start=(ki == 0 and s == 0), stop=(ki == nk - 1))
                    if ki == nk - 1:
                        pov = po.t[:, :].rearrange("p (s e) -> p s e", e=128)
                        fw.op("dve", lambda e: e.reciprocal(out=rden.t[:, :], in_=pov[:, :, 64]), [po.b], [rden.b])
                        for s in range(4):
                            ts("dve", otok.t[:, s, h * 64:(h + 1) * 64], po.t[:, s * 128:s * 128 + 64], rden.t[:, s:s + 1],
                               ALU.mult, [po.b, rden.b], [otok.b])

                pend = []
                for pair in range(2):
                    for ki in range(nk):
                        for h in (2 * pair, 2 * pair + 1):
                            pend.append((h, ki, stage1(h, ki)))
                            if len(pend) > LA:
                                stage2(*pend.pop(0))
                while pend:
                    stage2(*pend.pop(0))
                M_ = mo[i % 2]
                for ch in range(2):
                    p = psum()
                    for s in range(4):
                        mm(p.t[:, s * 128:(s + 1) * 128], otok.t[:, s, ch * 128:(ch + 1) * 128], C16("ident"),
                           [otok.b, cstb.b], [p.b])
                    tt("dve", M_.t[:, ch, :], p.t[:, :], G.t[:, ch, :], ALU.mult, [p.b, G.b], [M_.b])
                fw.dma("pool", mixT.ap[out_row0:out_row0 + 256, t0:t0 + 512].rearrange("(h p) t -> p h t", p=128), M_.t[:],
                       reads=[M_.b], writes=[mixT.b((out_row0, i))])

        def pass_AT(l):
            with ExitStack() as es:
                def sb(name, shape, dt):
                    return Tl(es.enter_context(nc.sbuf_tensor(f"{name}_L{l}", list(shape), dt)))
                masks = sb("AT_masks", [128, 20, 512], F32)
                fw.dma("sp", masks.t[:], amask_in.rearrange("r p q -> p r q"), writes=[masks.b])
                NW = 20
                kw_ = [sb(f"AT_kw{j}", [128, 2, NW * 128], BF16) for j in range(2)]
                vw_ = [sb(f"AT_vw{j}", [128, NW, 260], BF16) for j in range(2)]

                def kv_for_tile(i):
                    KW, VW = kw_[i % 2], vw_[i % 2]
                    q0 = i * 512
                    half_lo = (q0 // HALF) * HALF
                    lo = max(q0 - 1024, 0)
                    hi = min(q0 + 512 + 1024, T)
                    n = (hi - lo) // 128
                    tiles = sorted(set(range(lo // 512, (hi + 511) // 512)))
                    fw.dma("sp", KW.t[:, :, 0:n * 128], akT.ap[:, lo:hi].rearrange("(h p) t -> p h t", p=128),
                           reads=[akT.b(x) for x in tiles], writes=[KW.b])
                    fw.dma("sp", VW.t[:, 0:n, :], va.ap[lo:hi, :].rearrange("(j p) f -> p j f", p=128),
                           reads=[va.b(x) for x in tiles], writes=[VW.b])
                    out = []
                    for jt in range(n):
                        k0 = lo + jt * 128
                        r = (k0 - q0) // 128
                        assert -8 <= r <= 11
                        if not (half_lo <= k0 < half_lo + HALF):
                            ts("pool", VW.t[:, jt, :], VW.t[:, jt, :], flag.t[:, 0:1], ALU.mult, [VW.b, flag.b], [VW.b])
                        out.append(((lambda ch, pr, jt=jt, KW=KW: KW.t[:, ch, jt * 128:(jt + 1) * 128]),
                                    (lambda h, jt=jt, VW=VW: VW.t[:, jt, h * 65:(h + 1) * 65]),
                                    r + 8, [KW.b, VW.b]))
                    return out
                attn_core(sb, NT, aqT, kv_for_tile, masks, 0, "AT")
                fw.barrier()

        def pass_ME(l):
            with ExitStack() as es:
                def sb(name, shape, dt):
                    return Tl(es.enter_context(nc.sbuf_tensor(f"{name}_L{l}", list(shape), dt)))
                wkv = sb("ME_w", [128, 8, 512], BF16)
                for kc in range(8):
                    fw.dma("pool", wkv.t[:, kc, :], mem_wkv[l, kc * 128:(kc + 1) * 128, :], writes=[wkv.b])
                mnw = sb("ME_mnw", [128, 8], F32)
                col_load(mnw, mem_norm_w[l], 8)
                gk = sb("ME_gk", [128, 1], F32)
                col_load64(gk, mk_w[l])
                mkT = [sb(f"ME_mkT{g}", [128, 2, 256], BF16) for g in range(2)]
                mv = [sb(f"ME_mv{g}", [128, 2, 260], BF16) for g in range(2)]
                xm = sb("ME_x", [128, 1024], F32)
                junk = sb("ME_junk", [128, 1024], BF16)
                ssm = sb("ME_ss", [128, 1], F32)
                hbm = sb("ME_hb", [128, 2, 1024], BF16)
                hTm = sb("ME_hT", [128, 8, 256], BF16)
                sqm = sb("ME_sq", [128, 256], BF16)
                rsm = sb("ME_rs", [128, 256], F32)
                for g in range(2):
                    fw.op("dve", lambda e: e.memset(mv[g].t[:], 1.0), [], [mv[g].b])
                    for s in range(2):
                        fw.dma("sp", xm.t[:], mem_in[g, s * 128:(s + 1) * 128, :], writes=[xm.b])
                        act(junk.t[:], xm.t[:], AF.Square, [xm.b], [junk.b, ssm.b], accum_out=ssm.t[:, 0:1])
                        rstd_inplace(ssm.t[:], 1024.0, ssm.b)
                        ts("dve", hbm.t[:, s, :], xm.t[:], ssm.t[:, 0:1], ALU.mult, [xm.b, ssm.b], [hbm.b])
                    for kc in range(8):
                        p = psum()
                        for s in range(2):
                            mm(p.t[:, s * 128:(s + 1) * 128], hbm.t[:, s, kc * 128:(kc + 1) * 128], C16("ident"),
                               [hbm.b, cstb.b], [p.b])
                        ts("dve", hTm.t[:, kc, :], p.t[:, 0:256], mnw.t[:, kc:kc + 1], ALU.mult, [p.b, mnw.b], [hTm.b])
                    for c in range(2):
                        p = psum()
                        for kc in range(8):
                            mm(p.t[:, 0:256], wkv.t[:, kc, c * 128:(c + 1) * 128], hTm.t[:, kc, :], [wkv.b, hTm.b], [p.b],
                               start=(kc == 0), stop=(kc == 7))
                        act(sqm.t[:], p.t[:, 0:256], AF.Square, [p.b], [sqm.b])
                        p2 = psum()
                        mm(p2.t[:, 0:256], C16("ones64"), sqm.t[:], [cstb.b, sqm.b], [p2.b])
                        act(rsm.t[:], p2.t[:, 0:256], AF.Ln, [p2.b], [rsm.b], scale=1.0 / 64, bias=EPS)
                        act(rsm.t[:], rsm.t[:], AF.Exp, [rsm.b], [rsm.b], scale=-0.5)
                        stt(mkT[g].t[:, c, :], p.t[:, 0:256], gk.t[:, 0:1], rsm.t[:], ALU.mult, ALU.mult,
                            [p.b, gk.b, rsm.b], [mkT[g].b])
                    for s in range(2):
                        p = psum()
                        for kc in range(8):
                            mm(p.t[:, 0:256], hTm.t[:, kc, s * 128:(s + 1) * 128], wkv.t[:, kc, 256:512], [hTm.b, wkv.b], [p.b],
                               start=(kc == 0), stop=(kc == 7))
                        cp("act", mv[g].t[:, s, :].rearrange("p (h e) -> p h e", e=65)[:, :, 0:64],
                           p.t[:, 0:256].rearrange("p (h e) -> p h e", e=64), [p.b], [mv[g].b])

                def kv_for_tile(i):
                    g = (i * 512) // HALF
                    out = []
                    for jt in range(2):
                        out.append(((lambda ch, pr, jt=jt, g=g: mkT[g].t[:, ch, jt * 128:(jt + 1) * 128]),
                                    (lambda h, jt=jt, g=g: mv[g].t[:, jt, h * 65:(h + 1) * 65]),
                                    0, [mkT[g].b, mv[g].b]))
                    return out
                attn_core(sb, NT, mqT, kv_for_tile, None, 256, "ME")
                fw.barrier()

        def pass_C(l):
            src = x_dt if l == 0 else y_out
            with ExitStack() as es:
                def sb(name, shape, dt):
                    return Tl(es.enter_context(nc.sbuf_tensor(f"{name}_L{l}", list(shape), dt)))
                wo = sb("C_w", [128, 8, 1024], BF16)
                for kc in range(8):
                    fw.dma("pool", wo.t[:, kc, :], w_out[l, kc * 128:(kc + 1) * 128, :], writes=[wo.b])
                gon = sb("C_gon", [128, 1], F32)
                fw.dma("sp", gon.t[:, 0:1], onorm_w[l].rearrange("(p o) -> p o", o=1), writes=[gon.b],
                       allow_slow_non_contiguous=True)
                of_ = [sb(f"C_of{j}", [128, 4, 512], F32) for j in range(2)]
                ob_ = [sb(f"C_ob{j}", [128, 4, 512], F32) for j in range(2)]
                gh = [sb(f"C_g{j}", [128, 4, 512], BF16) for j in range(2)]
                mx_ = [sb(f"C_mx{j}", [128, 8, 512], BF16) for j in range(2)]
                xt = [sb(f"C_x{j}", [128, 4, 1024], F32) for j in range(2)]
                osum = [sb(f"C_osum{h}", [128, 512], F32) for h in range(4)]
                sq = [sb(f"C_sq{h}", [128, 512], BF16) for h in range(4)]
                rsf = [sb(f"C_rsf{h}", [128, 512], F32) for h in range(4)]
                m1 = [sb(f"C_m1{h}", [128, 512], F32) for h in range(4)]
                mxb = [[Buf() for _ in range(8)] for _ in range(2)]
                yt = [sb(f"C_y{j}", [128, 1024], F32) for j in range(2)]

                def load(i):
                    j = i % 2
                    t0 = i * 512
                    fw.dma("sp", of_[j].t[:], oT["f"].ap[:, t0:t0 + 512].rearrange("(h p) t -> p h t", p=128),
                           reads=[oT["f"].b(i)], writes=[of_[j].b])
                    fw.dma("sp", ob_[j].t[:], oT["b"].ap[:, t0:t0 + 512].rearrange("(h p) t -> p h t", p=128),
                           reads=[oT["b"].b(i)], writes=[ob_[j].b])
                    fw.dma("sp", gh[j].t[:], gT.ap[0:512, t0:t0 + 512].rearrange("(h p) t -> p h t", p=128),
                           reads=[gT.b(i)], writes=[gh[j].b])
                    fw.dma("sp", mx_[j].t[:, 4:8, :], mixT.ap[:, t0:t0 + 512].rearrange("(h p) t -> p h t", p=128),
                           reads=[mixT.b((0, i)), mixT.b((256, i))], writes=mxb[j][4:8])
                    fw.dma("sp", xt[j].t[:], src.ap[t0:t0 + 512, :].rearrange("(s p) f -> p s f", p=128),
                           reads=[src.b(i)], writes=[xt[j].b])
                yi = [0]
                H4 = range(4)

                def normphase(i):
                    j = i % 2
                    for h in H4:
                        tt("pool", osum[h].t[:], of_[j].t[:, h, :], ob_[j].t[:, h, :], ALU.add, [of_[j].b, ob_[j].b], [osum[h].b])
                    for h in H4:
                        act(sq[h].t[:], osum[h].t[:], AF.Square, [osum[h].b], [sq[h].b])
                    pp = []
                    for h in H4:
                        p = psum()
                        mm(p.t[:, :], C16("ones128"), sq[h].t[:], [cstb.b, sq[h].b], [p.b])
                        pp.append(p)
                    for h in H4:
                        act(rsf[h].t[:], pp[h].t[:, :], AF.Ln, [pp[h].b], [rsf[h].b], scale=1.0 / 128, bias=EPS)
                    for h in H4:
                        act(rsf[h].t[:], rsf[h].t[:], AF.Exp, [rsf[h].b], [rsf[h].b], scale=-0.5)
                    for h in H4:
                        stt(m1[h].t[:], osum[h].t[:], gon.t[:, 0:1], rsf[h].t[:], ALU.mult, ALU.mult,
                            [osum[h].b, gon.b, rsf[h].b], [m1[h].b])
                    for h in H4:
                        tt("dve", mx_[j].t[:, h, :], m1[h].t[:], gh[j].t[:, h, :], ALU.mult, [m1[h].b, gh[j].b], [mxb[j][h]])

                def outproj(i):
                    j = i % 2
                    t0 = i * 512
                    for s in range(4):
                        Y = yt[yi[0] % 2]
                        yi[0] += 1
                        for nh in range(2):
                            p = psum()
                            for mc in range(8):
                                mm(p.t[:, :], mx_[j].t[:, mc, s * 128:(s + 1) * 128], wo.t[:, mc, nh * 512:(nh + 1) * 512],
                                   [mxb[j][mc], wo.b], [p.b], start=(mc == 0), stop=(mc == 7))
                            tt("dve", Y.t[:, nh * 512:(nh + 1) * 512], p.t[:, :], xt[j].t[:, s, nh * 512:(nh + 1) * 512], ALU.add,
                               [p.b, xt[j].b], [Y.b])
                        fw.dma("pool", y_out.ap[t0 + s * 128:t0 + (s + 1) * 128, :], Y.t[:], reads=[Y.b], writes=[y_out.b(i)])

                load(0)
                if NT > 1:
                    load(1)
                normphase(0)
                for i in range(NT):
                    if i + 1 < NT:
                        normphase(i + 1)
                    outproj(i)
                    if i + 2 < NT:
                        load(i + 2)
                fw.barrier()

        _P = _os.environ.get("KPASSES", "A,HF,HB,AT,ME,C").split(",")
        for l in range(depth):
            if "A" in _P:
                pass_A(l)
            if "HF" in _P:
                pass_H(l, "f")
            if "HB" in _P:
                pass_H(l, "b")
            if "AT" in _P:
                pass_AT(l)
            if "ME" in _P:
                pass_ME(l)
            if "C" in _P:
                pass_C(l)
        fw.barrier()
    return nc, fw


T_CORE = 16384
DEPTH = 4
_CACHE = {}


def kernel(x_prompt, x_sample, mem_prompt, mem_sample, norm_w, w_in, hgrn_lb_fwd, hgrn_lb_bwd, hgrn_onorm_w,
           attn_qnorm_w, attn_knorm_w, mem_norm_w, mem_wkv, mem_qnorm_w, mem_knorm_w, w_out):
    f = lambda a: np.ascontiguousarray(np.asarray(a, dtype=np.float32))
    x_prompt, x_sample, mem_prompt, mem_sample = f(x_prompt), f(x_sample), f(mem_prompt), f(mem_sample)
    T = T_CORE
    if "nc" not in _CACHE:
        _CACHE["nc"] = build(T, DEPTH)[0]
        _CACHE["hc"] = host_consts(T)
    nc = _CACHE["nc"]
    hc = _CACHE["hc"]
    pos_p = np.concatenate([np.arange(8192), np.arange(8192)])
    pos_s = np.arange(16384)
    Cp, Sp = rope_tables(pos_p)
    Cs, Ss = rope_tables(pos_s)
    shared = {"cst": hc["cst"], "amask": hc["amask"], "norm_w": f(norm_w), "w_in": f(w_in),
              "hgrn_lb_fwd": f(hgrn_lb_fwd), "hgrn_lb_bwd": f(hgrn_lb_bwd), "hgrn_onorm_w": f(hgrn_onorm_w),
              "attn_qnorm_w": f(attn_qnorm_w), "attn_knorm_w": f(attn_knorm_w), "mem_norm_w": f(mem_norm_w),
              "mem_wkv": f(mem_wkv), "mem_qnorm_w": f(mem_qnorm_w), "mem_knorm_w": f(mem_knorm_w), "w_out": f(w_out)}
    in_maps = []
    for c in range(8):
        m = dict(shared)
        if c < 4:
            m["x"] = x_prompt[2 * c:2 * c + 2].reshape(T, 1024)
            m["mem"] = mem_prompt[2 * c:2 * c + 2]
            m["flag"] = np.zeros((128, 1), np.float32)
            m["ropeC"], m["ropeS"] = Cp, Sp
        else:
            s = (c - 4) % 2
            m["x"] = x_sample[s]
            m["mem"] = np.stack([mem_sample[s], mem_sample[s]])
            m["flag"] = np.ones((128, 1), np.float32)
            m["ropeC"], m["ropeS"] = Cs, Ss
        in_maps.append(m)
    res = run_bass_kernel_spmd(nc, in_maps, core_ids=list(range(8)))
    ys = [np.asarray(r["y"], dtype=np.float32) for r in res.results]
    y_prompt = np.stack([ys[c].reshape(2, 8192, 1024) for c in range(4)]).reshape(8, 8192, 1024)
    y_sample = np.stack([ys[4], ys[5]])
    return (y_prompt, y_sample)
```

```python
import math
import os as _os
from contextlib import ExitStack
import numpy as np
import concourse.bass as bass
import concourse.mybir as mybir
from concourse.bass_utils import run_bass_kernel_spmd

F32 = mybir.dt.float32
BF16 = mybir.dt.bfloat16
AF = mybir.ActivationFunctionType
ALU = mybir.AluOpType
EPS = 1e-6
NSLOT = 8


class Buf:
    __slots__ = ("w", "r")

    def __init__(self):
        self.w = None
        self.r = {}


class Tl:
    def __init__(self, t):
        self.t = t
        self.b = Buf()


class FW:
    LIM = 30000

    def __init__(self, nc):
        self.nc = nc
        self.engs = {"pe": nc.tensor, "act": nc.scalar, "dve": nc.vector, "pool": nc.gpsimd, "sp": nc.sync}
        self.cur = {}
        self.seen = {k: {} for k in self.engs}
        self.last = {}
        self.nsem = 0
        self.dslots = {"sp": [], "pool": []}
        self.dnext = {"sp": 0, "pool": 0}
        self.nins = 0

    def newsem(self):
        self.nsem += 1
        return [self.nsem, self.nc.alloc_semaphore(name=f"fs{self.nsem}")]

    def _wait(self, e, ev):
        if self.seen[e].get(ev[0], 0) >= ev[2]:
            return
        self.engs[e].wait_ge(ev[1], ev[2])
        self.seen[e][ev[0]] = ev[2]

    def _deps(self, e, reads, writes):
        for b in reads:
            if b.w is not None and not (e == "pe" and b.w[3] == "pe"):
                self._wait(e, b.w)
        for b in writes:
            if b.w is not None and not (e == "pe" and b.w[3] == "pe"):
                self._wait(e, b.w)
            for ev in b.r.values():
                if not (e == "pe" and ev[3] == "pe"):
                    self._wait(e, ev)

    def _mark(self, ev, key, reads, writes):
        for b in reads:
            b.r[key] = ev
        for b in writes:
            b.w = ev
            b.r = {}

    def op(self, e, fn, reads=(), writes=()):
        self._deps(e, reads, writes)
        ins = fn(self.engs[e])
        c = self.cur.get(e)
        if c is None or c[2] >= self.LIM:
            c = self.newsem() + [0]
            self.cur[e] = c
        c[2] += 1
        ins.then_inc(c[1], 1)
        ev = (c[0], c[1], c[2], e)
        self._mark(ev, e, reads, writes)
        self.last[e] = ev
        self.nins += 1

    def dma(self, q, out, in_, reads=(), writes=(), **kw):
        self._deps(q, reads, writes)
        slots = self.dslots[q]
        if len(slots) < NSLOT:
            slots.append(self.newsem() + [0])
            s = slots[-1]
        else:
            s = slots[self.dnext[q] % NSLOT]
        self.dnext[q] += 1
        if s[2] > 0:
            self._wait(q, (s[0], s[1], s[2], "dma"))
        if s[2] + 16 > self.LIM:
            s[:] = self.newsem() + [0]
        ins = self.engs[q].dma_start(out=out, in_=in_, **kw)
        s[2] += 16
        ins.then_inc(s[1], 16)
        ev = (s[0], s[1], s[2], "dma")
        self._mark(ev, ("d", s[0], s[2]), reads, writes)
        self.nins += 1

    def barrier(self):
        evs = [self.last[e] for e in ("pe", "act", "dve", "pool") if e in self.last]
        for q in self.dslots:
            for s in self.dslots[q]:
                if s[2] > 0:
                    evs.append((s[0], s[1], s[2], "dma"))
        for e in self.engs:
            for ev in evs:
                if e == "pe" and ev[3] == "pe":
                    continue
                self._wait(e, ev)


class DT:
    def __init__(self, ap):
        self.ap = ap
        self.bufs = {}

    def b(self, i):
        if i not in self.bufs:
            self.bufs[i] = Buf()
        return self.bufs[i]


def host_consts(T):
    c = {}
    s = np.arange(128)[:, None]
    t = np.arange(128)[None, :]
    same = (s // 64) == (t // 64)
    c0 = (t // 64) * 64
    A_f = (same & (s <= t)).astype(np.float32) - (same & (s <= c0 + 31)).astype(np.float32)
    A_b = (same & (s >= t)).astype(np.float32) - (same & (s >= c0 + 32)).astype(np.float32)
    B_f = (same & (s > t)).astype(np.float32)
    B_b = (same & (s < t)).astype(np.float32)
    M_f = np.zeros((128, 4), np.float32)
    M_b = np.zeros((128, 4), np.float32)
    sv = np.arange(128)
    for ch in range(2):
        inc = (sv // 64) == ch
        M_f[:, 2 * ch] = inc & (sv <= ch * 64 + 31)
        M_f[:, 2 * ch + 1] = inc
        M_b[:, 2 * ch] = inc & (sv >= ch * 64 + 32)
        M_b[:, 2 * ch + 1] = inc
    K_f = (same & (s <= t)).astype(np.float32)
    K_b = (same & (s >= t)).astype(np.float32)
    ident = np.eye(128, dtype=np.float32)
    ones64 = ((s // 64) == (t // 64)).astype(np.float32)
    ones128 = np.ones((128, 128), np.float32)
    Rm = np.zeros((128, 128), np.float32)
    for m in range(128):
        j = m % 64
        if j < 8:
            Rm[m + 8, m] = -1.0
        elif j < 16:
            Rm[m - 8, m] = 1.0
    cst = np.concatenate([A_f, A_b, B_f, B_b, K_f, K_b, ident, ones64, ones128, Rm, M_f, M_b], axis=1)
    c["cst"] = np.ascontiguousarray(cst.astype(np.float32))
    am = np.zeros((20, 128, 512), np.float32)
    j = np.arange(128)[:, None]
    i = np.arange(512)[None, :]
    for ri, r in enumerate(range(-8, 12)):
        d = r * 128 + j - i
        am[ri] = ((np.abs(d) <= 64).astype(np.float32)
                  + ((d % 4 == 0) & (np.abs(d) <= 256)).astype(np.float32)
                  + ((d % 16 == 0) & (np.abs(d) <= 1024)).astype(np.float32))
    c["amask"] = np.where(am > 0, 8.0 * np.log(np.maximum(am, 1.0)), -80000.0).astype(np.float32)
    return c


def rope_tables(pos):
    half = 8
    inv = (500000.0 ** (-np.arange(half, dtype=np.float32) * 2.0 / 16.0)).astype(np.float32)
    ang = pos.astype(np.float32)[None, :] * inv[:, None]
    C = np.ones((128, pos.shape[0]), np.float32)
    S = np.zeros((128, pos.shape[0]), np.float32)
    for hb in (0, 64):
        C[hb:hb + 8] = np.cos(ang)
        C[hb + 8:hb + 16] = np.cos(ang)
        S[hb:hb + 8] = np.sin(ang)
        S[hb + 8:hb + 16] = np.sin(ang)
    return C, S


CO = {"A_f": 0, "A_b": 128, "B_f": 256, "B_b": 384, "K_f": 512, "K_b": 640, "ident": 768, "ones64": 896,
      "ones128": 1024, "Rm": 1152, "M_f": 1280, "M_b": 1284}
NCST = 1288


def build(T, depth, debug=False):
    NT = T // 512
    HALF = T // 2
    nc = bass.Bass("TRN2", target_bir_lowering=False)
    fw = FW(nc)

    def din(name, shape, dt=F32):
        return nc.dram_tensor(name, list(shape), dt, kind="ExternalInput").ap()

    x_in = din("x", [T, 1024])
    mem_in = din("mem", [2, 256, 1024])
    flag_in = din("flag", [128, 1])
    ropeC_in = din("ropeC", [128, T])
    ropeS_in = din("ropeS", [128, T])
    cst_in = din("cst", [128, NCST])
    amask_in = din("amask", [20, 128, 512])
    norm_w = din("norm_w", [depth, 1024])
    w_in = din("w_in", [depth, 1024, 4096])
    lbp = {"f": din("hgrn_lb_fwd", [depth, 512]), "b": din("hgrn_lb_bwd", [depth, 512])}
    onorm_w = din("hgrn_onorm_w", [depth, 128])
    aq_w = din("attn_qnorm_w", [depth, 64])
    ak_w = din("attn_knorm_w", [depth, 64])
    mem_norm_w = din("mem_norm_w", [depth, 1024])
    mem_wkv = din("mem_wkv", [depth, 1024, 512])
    mq_w = din("mem_qnorm_w", [depth, 64])
    mk_w = din("mem_knorm_w", [depth, 64])
    w_out = din("w_out", [depth, 1024, 1024])
    y_out = DT(nc.dram_tensor("y", [T, 1024], F32, kind="ExternalOutput").ap())
    x_dt = DT(x_in)

    skind = "ExternalOutput" if debug else "Internal"

    def scr(name, shape, dt):
        return DT(nc.dram_tensor(name, list(shape), dt, kind=skind).ap())

    qT = scr("s_qT", [512, T], BF16)
    kT = {"f": scr("s_kTf", [512, T], BF16), "b": scr("s_kTb", [512, T], BF16)}
    ktok = {"f": scr("s_kf", [T, 512], BF16), "b": scr("s_kb", [T, 512], BF16)}
    lfh = {"f": scr("s_lfhf", [T, 512], BF16), "b": scr("s_lfhb", [T, 512], BF16)}
    lfl = {"f": scr("s_lflf", [T, 512], BF16), "b": scr("s_lflb", [T, 512], BF16)}
    vtok = scr("s_v", [T, 512], BF16)
    gT = scr("s_gT", [1024, T], BF16)
    aqT = scr("s_aqT", [256, T], BF16)
    akT = scr("s_akT", [256, T], BF16)
    va = scr("s_va", [T, 260], BF16)
    mqT = scr("s_mqT", [256, T], BF16)
    oT = {"f": scr("s_ofT", [512, T], F32), "b": scr("s_obT", [512, T], F32)}
    mixT = scr("s_mixT", [512, T], BF16)

    es0 = ExitStack()
    with es0:
        def sb0(name, shape, dt):
            return Tl(es0.enter_context(nc.sbuf_tensor(name, list(shape), dt)))

        ps_t = [Tl(es0.enter_context(nc.psum_tensor(f"ps{i}", [128, 512], F32))) for i in range(8)]
        ps_i = [0]

        def psum():
            p = ps_t[ps_i[0] % 6]
            ps_i[0] += 1
            return p
        pa_i = [0]

        def psum_acc():
            p = ps_t[6 + pa_i[0] % 2]
            pa_i[0] += 1
            return p

        def mm(out, lhsT, rhs, reads, writes, start=True, stop=True):
            fw.op("pe", lambda e: e.matmul(out, lhsT=lhsT, rhs=rhs, start=start, stop=stop), reads, writes)

        def act(out, in_, func, reads, writes, **kw):
            fw.op("act", lambda e: e.activation(out=out, in_=in_, func=func, **kw), reads, writes)

        def tt(eng, out, in0, in1, op, reads, writes):
            fw.op(eng, lambda e: e.tensor_tensor(out=out, in0=in0, in1=in1, op=op), reads, writes)

        def ts(eng, out, in0, s1, op0, reads, writes, s2=None, op1=None):
            if op1 is None:
                fw.op(eng, lambda e: e.tensor_scalar(out=out, in0=in0, scalar1=s1, scalar2=None, op0=op0), reads, writes)
            else:
                fw.op(eng, lambda e: e.tensor_scalar(out=out, in0=in0, scalar1=s1, scalar2=s2, op0=op0, op1=op1),
                      reads, writes)

        def stt(out, in0, scalar, in1, op0, op1, reads, writes):
            fw.op("dve", lambda e: e.scalar_tensor_tensor(out=out, in0=in0, scalar=scalar, in1=in1, op0=op0, op1=op1),
                  reads, writes)

        def cp(eng, out, in_, reads, writes):
            if eng == "act":
                fw.op("act", lambda e: e.copy(out=out, in_=in_), reads, writes)
            else:
                fw.op(eng, lambda e: e.tensor_copy(out=out, in_=in_), reads, writes)

        cst = sb0("cst_sb", [128, NCST], F32)
        fw.dma("sp", cst.t[:], cst_in[:, :], writes=[cst.b])
        cstb = sb0("cstb_sb", [128, NCST], BF16)
        cp("dve", cstb.t[:], cst.t[:], [cst.b], [cstb.b])
        flag = sb0("flag_sb", [128, 1], F32)
        fw.dma("sp", flag.t[:], flag_in[:, :], writes=[flag.b])

        def C32(name, w=128):
            return cst.t[:, CO[name]:CO[name] + w]

        def C16(name, w=128):
            return cstb.t[:, CO[name]:CO[name] + w]

        def col_load(dst, src1d, n):
            fw.dma("sp", dst.t[:, 0:n], src1d.rearrange("(c p) -> p c", p=128), writes=[dst.b],
                   allow_slow_non_contiguous=True)

        def col_load64(dst, src1d):
            for hb in (0, 64):
                fw.dma("sp", dst.t[hb:hb + 64, 0:1], src1d.rearrange("(p o) -> p o", o=1), writes=[dst.b],
                       allow_slow_non_contiguous=True)

        def rstd_inplace(tl_ap, n, reads_b):
            act(tl_ap, tl_ap, AF.Ln, [reads_b], [reads_b], scale=1.0 / n, bias=EPS)
            act(tl_ap, tl_ap, AF.Exp, [reads_b], [reads_b], scale=-0.5)

        def load_w_in(l, es):
            w = Tl(es.enter_context(nc.sbuf_tensor(f"A_w_L{l}", [128, 8, 4096], BF16)))
            for kc in range(8):
                for q4 in range(4):
                    fw.dma("pool", w.t[:, kc, q4 * 1024:(q4 + 1) * 1024],
                           w_in[l, kc * 128:(kc + 1) * 128, q4 * 1024:(q4 + 1) * 1024], writes=[w.b])
            return w

        def pass_A(l, w):
            src = x_dt if l == 0 else y_out
            with ExitStack() as es:
                def sb(name, shape, dt):
                    return Tl(es.enter_context(nc.sbuf_tensor(f"{name}_L{l}", list(shape), dt)))
                normw = sb("A_normw", [128, 8], F32)
                col_load(normw, norm_w[l], 8)
                gq = sb("A_gq", [128, 1], F32)
                gk = sb("A_gk", [128, 1], F32)
                gm = sb("A_gm", [128, 1], F32)
                col_load64(gq, aq_w[l])
                col_load64(gk, ak_w[l])
                col_load64(gm, mq_w[l])
                oml = {d: sb(f"A_oml{d}", [128, 512], F32) for d in "fb"}
                with ExitStack() as es2:
                    def sb2(name, shape, dt):
                        return Tl(es2.enter_context(nc.sbuf_tensor(f"{name}_L{l}", list(shape), dt)))
                    row = sb2("A_lbrow", [1, depth, 512], F32)
                    mx = sb2("A_lbmx", [1, 512], F32)
                    sm = sb2("A_lbsm", [1, 512], F32)
                    acc = sb2("A_lbacc", [1, 512], F32)
                    tmp = sb2("A_lbtmp", [1, 512], F32)
                    for d in ("f", "b"):
                        fw.dma("sp", row.t[0:1, :, :], lbp[d].rearrange("(o l) f -> o l f", o=1), writes=[row.b])
                        cp("dve", mx.t[:], row.t[0:1, 0, :], [row.b], [mx.b])
                        for j in range(1, depth):
                            tt("dve", mx.t[:], mx.t[:], row.t[0:1, j, :], ALU.max, [mx.b, row.b], [mx.b])
                        for j in range(depth):
                            tt("dve", row.t[0:1, j, :], row.t[0:1, j, :], mx.t[:], ALU.subtract, [row.b, mx.b], [row.b])
                        act(row.t[:], row.t[:], AF.Exp, [row.b], [row.b])
                        cp("dve", sm.t[:], row.t[0:1, 0, :], [row.b], [sm.b])
                        for j in range(1, depth):
                            tt("dve", sm.t[:], sm.t[:], row.t[0:1, j, :], ALU.add, [sm.b, row.b], [sm.b])
                        fw.op("dve", lambda e: e.reciprocal(out=sm.t[:], in_=sm.t[:]), [sm.b], [sm.b])
                        fw.op("dve", lambda e: e.memset(acc.t[:], 0.0), [], [acc.b])
                        for j in range(1, l + 1):
                            tt("dve", tmp.t[:], row.t[0:1, j, :], sm.t[:], ALU.mult, [row.b, sm.b], [tmp.b])
                            tt("dve", acc.t[:], acc.t[:], tmp.t[:], ALU.add, [acc.b, tmp.b], [acc.b])
                        ts("dve", acc.t[:], acc.t[:], -1.0, ALU.mult, [acc.b], [acc.b], s2=1.0, op1=ALU.add)
                        p = psum()
                        mm(p.t[:, :], C32("ones128")[0:1, :], acc.t[0:1, :], [cst.b, acc.b], [p.b])
                        cp("dve", oml[d].t[:], p.t[:, :], [p.b], [oml[d].b])
                    fw.barrier()

                xt = [sb(f"A_x{i}", [128, 1024], F32) for i in range(4)]
                ss = sb("A_ss", [128, 4], F32)
                hb = sb("A_hb", [128, 4, 1024], BF16)
                hTr = [sb(f"A_hT{j}", [128, 8, 512], BF16) for j in range(2)]
                qTs = sb("A_qTs", [128, 4, 512], BF16)
                gTs = sb("A_gTs", [128, 8, 512], BF16)
                aqTs = sb("A_aqTs", [128, 2, 512], BF16)
                akTs = sb("A_akTs", [128, 2, 512], BF16)
                mqTs = sb("A_mqTs", [128, 2, 512], BF16)
                lfhs = {d: sb(f"A_lfhs{d}", [128, 4, 512], BF16) for d in "fb"}
                lfls = {d: sb(f"A_lfls{d}", [128, 4, 512], BF16) for d in "fb"}
                ks = {d: sb(f"A_ks{d}", [128, 4, 512], BF16) for d in "fb"}
                kTs = {d: sb(f"A_kTs{d}", [128, 4, 512], BF16) for d in "fb"}
                vs = sb("A_vs", [128, 4, 512], BF16)
                vas = sb("A_vas", [128, 4, 260], BF16)
                fw.op("dve", lambda e: e.memset(vas.t[:], 1.0), [], [vas.b])
                NR = 2
                sq = [sb(f"A_sq{j}", [128, 512], BF16) for j in range(NR)]
                rsf = [sb(f"A_rsf{j}", [128, 512], F32) for j in range(NR)]
                qn = [sb(f"A_qn{j}", [128, 512], F32) for j in range(NR)]
                qnb = [sb(f"A_qnb{j}", [128, 512], BF16) for j in range(NR)]
                t1 = rsf
                t2 = [sb(f"A_t2{j}", [128, 512], F32) for j in range(NR)]
                rC = [sb(f"A_rC{j}", [128, 512], F32) for j in range(2)]
                rS = [sb(f"A_rS{j}", [128, 512], F32) for j in range(2)]
                NG = 2
                sg = [sb(f"A_sg{j}", [128, 512], F32) for j in range(NG)]
                k32 = [sb(f"A_k32{j}", [128, 512], F32) for j in range(NG)]
                sub_b = {}

                def SB(tl, idx):
                    key = (id(tl), idx)
                    if key not in sub_b:
                        sub_b[key] = Buf()
                    return sub_b[key]

                def SBall(tl, n):
                    return [SB(tl, j) for j in range(n)]
                pfree = list(ps_t)

                def palloc():
                    return pfree.pop(0)

                def prel(p):
                    pfree.append(p)

                def run_chains(gens, K):
                    active = []
                    it = iter(gens)
                    done = False
                    while True:
                        while len(active) < K and not done:
                            g = next(it, None)
                            if g is None:
                                done = True
                            else:
                                active.append(g)
                        if not active:
                            break
                        for g in list(active):
                            try:
                                next(g)
                            except StopIteration:
                                active.remove(g)

                def prologue(i):
                    t0 = i * 512
                    hT = hTr[i % 2]
                    fw.dma("sp", rC[i % 2].t[:], ropeC_in[:, t0:t0 + 512], writes=[rC[i % 2].b])
                    fw.dma("sp", rS[i % 2].t[:], ropeS_in[:, t0:t0 + 512], writes=[rS[i % 2].b])
                    for s in range(4):
                        fw.dma("sp", xt[s].t[:], src.ap[t0 + s * 128:t0 + (s + 1) * 128, :], reads=[src.b(i)], writes=[xt[s].b])
                        act(hb.t[:, s, :], xt[s].t[:], AF.Square, [xt[s].b], [SB(hb, s), ss.b], accum_out=ss.t[:, s:s + 1])
                        yield
                    rstd_inplace(ss.t[:], 1024.0, ss.b)
                    yield
                    for s in range(4):
                        ts("dve", hb.t[:, s, :], xt[s].t[:], ss.t[:, s:s + 1], ALU.mult, [xt[s].b, ss.b], [SB(hb, s)])
                        yield
                    for kc in range(8):
                        p = palloc()
                        for s in range(4):
                            mm(p.t[:, s * 128:(s + 1) * 128], hb.t[:, s, kc * 128:(kc + 1) * 128], C16("ident"),
                               [SB(hb, s), cstb.b], [p.b])
                        yield
                        ts("dve", hT.t[:, kc, :], p.t[:, :], normw.t[:, kc:kc + 1], ALU.mult, [p.b, normw.b], [SB(hT, kc)])
                        prel(p)
                        yield

                rfree = list(range(NR))
                gfree = list(range(NG))

                def tile_chains(i):
                    hT = hTr[i % 2]
                    hTb = SBall(hT, 8)
                    RC, RS = rC[i % 2], rS[i % 2]

                    def fm(p, col0):
                        for kc in range(8):
                            mm(p.t[:, :], w.t[:, kc, col0:col0 + 128], hT.t[:, kc, :], [w.b, hTb[kc]], [p.b],
                               start=(kc == 0), stop=(kc == 7))

                    def tm(p, s, col0, n):
                        for kc in range(8):
                            mm(p.t[:, 0:n], hT.t[:, kc, s * 128:(s + 1) * 128], w.t[:, kc, col0:col0 + n],
                               [hTb[kc], w.b], [p.b], start=(kc == 0), stop=(kc == 7))

                    def silu_chain(col0, dst, c):
                        p = palloc()
                        fm(p, col0)
                        yield
                        act(dst.t[:, c, :], p.t[:, :], AF.Silu, [p.b], [SB(dst, c)])
                        prel(p)

                    def v_chain(s):
                        p = palloc()
                        tm(p, s, 1536, 512)
                        yield
                        cp("act", vs.t[:, s, :], p.t[:, :], [p.b], [SB(vs, s)])
                        prel(p)

                    def av_chain(s):
                        p = palloc()
                        tm(p, s, 2560, 256)
                        yield
                        cp("act", vas.t[:, s, :].rearrange("p (h e) -> p h e", e=65)[:, :, 0:64],
                           p.t[:, 0:256].rearrange("p (h e) -> p h e", e=64), [p.b], [vas.b, SB(vas, s)])
                        prel(p)

                    def norm_chain(col0, gain, dst, c, rope):
                        while not rfree:
                            yield
                        r = rfree.pop(0)
                        p = palloc()
                        fm(p, col0 + c * 128)
                        yield
                        act(sq[r].t[:], p.t[:, :], AF.Square, [p.b], [sq[r].b])
                        yield
                        p2 = palloc()
                        mm(p2.t[:, :], C16("ones64"), sq[r].t[:], [cstb.b, sq[r].b], [p2.b])
                        yield
                        act(rsf[r].t[:], p2.t[:, :], AF.Ln, [p2.b], [rsf[r].b], scale=1.0 / 64, bias=EPS)
                        prel(p2)
                        act(rsf[r].t[:], rsf[r].t[:], AF.Exp, [rsf[r].b], [rsf[r].b], scale=-0.5)
                        yield
                        if not rope:
                            stt(dst.t[:, c, :], p.t[:, :], gain.t[:, 0:1], rsf[r].t[:], ALU.mult, ALU.mult,
                                [p.b, gain.b, rsf[r].b], [SB(dst, c)])
                            prel(p)
                            rfree.append(r)
                            return
                        stt(qn[r].t[:], p.t[:, :], gain.t[:, 0:1], rsf[r].t[:], ALU.mult, ALU.mult,
                            [p.b, gain.b, rsf[r].b], [qn[r].b])
                        prel(p)
                        yield
                        cp("act", qnb[r].t[:], qn[r].t[:], [qn[r].b], [qnb[r].b])
                        tt("pool", t2[r].t[:], qn[r].t[:], RC.t[:], ALU.mult, [qn[r].b, RC.b], [t2[r].b])
                        yield
                        p3 = palloc()
                        mm(p3.t[:, :], C16("Rm"), qnb[r].t[:], [cstb.b, qnb[r].b], [p3.b])
                        yield
                        tt("dve", t1[r].t[:], p3.t[:, :], RS.t[:], ALU.mult, [p3.b, RS.b], [t1[r].b])
                        prel(p3)
                        yield
                        tt("dve", dst.t[:, c, :], t1[r].t[:], t2[r].t[:], ALU.add, [t1[r].b, t2[r].b], [SB(dst, c)])
                        rfree.append(r)

                    def fgate_chain(s, d, col0):
                        while not gfree:
                            yield
                        g = gfree.pop(0)
                        p = palloc()
                        tm(p, s, col0, 512)
                        yield
                        act(sg[g].t[:], p.t[:, :], AF.Exp, [p.b], [sg[g].b])
                        prel(p)
                        yield
                        act(sg[g].t[:], sg[g].t[:], AF.Ln, [sg[g].b], [sg[g].b], bias=1.0)
                        yield
                        act(sg[g].t[:], sg[g].t[:], AF.Exp, [sg[g].b], [sg[g].b], scale=-1.0)
                        yield
                        tt("dve", k32[g].t[:], sg[g].t[:], oml[d].t[:], ALU.mult, [sg[g].b, oml[d].b], [k32[g].b])
                        yield
                        act(sg[g].t[:], k32[g].t[:], AF.Ln, [k32[g].b], [sg[g].b], scale=-1.0, bias=1.0)
                        cp("pool", ks[d].t[:, s, :], k32[g].t[:], [k32[g].b], [SB(ks[d], s)])
                        yield
                        cp("act", lfhs[d].t[:, s, :], sg[g].t[:], [sg[g].b], [SB(lfhs[d], s)])
                        yield
                        tt("pool", lfls[d].t[:, s, :], sg[g].t[:], lfhs[d].t[:, s, :], ALU.subtract,
                           [sg[g].b, SB(lfhs[d], s)], [SB(lfls[d], s)])
                        gfree.append(g)

                    def kT_chain(d, h):
                        p = palloc()
                        for s in range(4):
                            mm(p.t[:, s * 128:(s + 1) * 128], ks[d].t[:, s, h * 128:(h + 1) * 128], C16("ident"),
                               [SB(ks[d], s), cstb.b], [p.b])
                        yield
                        cp("dve", kTs[d].t[:, h, :], p.t[:, :], [p.b], [SB(kTs[d], h)])
                        prel(p)

                    ph1 = []
                    sil = [silu_chain(c * 128, qTs, c) for c in range(4)] + [silu_chain(3072 + c * 128, gTs, c) for c in range(8)]
                    cps = []
                    for s in range(4):
                        cps += [v_chain(s), av_chain(s)]
                    for j in range(12):
                        ph1.append(sil[j])
                        if j < 8:
                            ph1.append(cps[j])
                    nrm = []
                    for (col0, gain, dst, rope) in ((2048, gq, aqTs, True), (2304, gk, akTs, True), (2816, gm, mqTs, False)):
                        for c in range(2):
                            nrm.append(norm_chain(col0, gain, dst, c, rope))
                    fg = []
                    for s in range(4):
                        fg += [fgate_chain(s, "f", 512), fgate_chain(s, "b", 1024)]
                    ph2 = []
                    for j in range(8):
                        ph2.append(fg[j])
                        if j < 6:
                            ph2.append(nrm[j])
                    ph3 = [kT_chain(d, h) for d in "fb" for h in range(4)]
                    return ph1, ph2, ph3

                def stores(i):
                    t0 = i * 512
                    fmv = lambda dd: dd.ap[:, t0:t0 + 512].rearrange("(h p) t -> p h t", p=128)
                    tmv = lambda dd: dd.ap[t0:t0 + 512, :].rearrange("(s p) f -> p s f", p=128)
                    fw.dma("pool", fmv(qT), qTs.t[:], reads=SBall(qTs, 4), writes=[qT.b(i)])
                    fw.dma("pool", fmv(gT), gTs.t[:], reads=SBall(gTs, 8), writes=[gT.b(i)])
                    fw.dma("pool", tmv(vtok), vs.t[:], reads=SBall(vs, 4), writes=[vtok.b(i)])
                    fw.dma("pool", tmv(va), vas.t[:], reads=[vas.b] + SBall(vas, 4), writes=[va.b(i)])
                    for (dst, dd) in ((aqTs, aqT), (akTs, akT), (mqTs, mqT)):
                        fw.dma("pool", fmv(dd), dst.t[:], reads=SBall(dst, 2), writes=[dd.b(i)])
                    for d in "fb":
                        fw.dma("pool", tmv(lfh[d]), lfhs[d].t[:], reads=SBall(lfhs[d], 4), writes=[lfh[d].b(i)])
                        fw.dma("pool", tmv(lfl[d]), lfls[d].t[:], reads=SBall(lfls[d], 4), writes=[lfl[d].b(i)])
                        fw.dma("pool", tmv(ktok[d]), ks[d].t[:], reads=SBall(ks[d], 4), writes=[ktok[d].b(i)])
                        fw.dma("pool", fmv(kT[d]), kTs[d].t[:], reads=SBall(kTs[d], 4), writes=[kT[d].b(i)])

                def mark_store_reads(i):
                    pass

                run_chains([prologue(0)], 1)
                for i in range(NT):
                    ph1, ph2, ph3 = tile_chains(i)
                    run_chains(ph1, 3)
                    nxt = [prologue(i + 1)] if i + 1 < NT else []
                    run_chains(nxt + ph2, 4)
                    run_chains(ph3, 3)
                    stores(i)
                fw.barrier()

        def pass_H(l, d):
            fwd = d == "f"
            with ExitStack() as es:
                def sb(name, shape, dt):
                    return Tl(es.enter_context(nc.sbuf_tensor(f"{name}{d}_L{l}", list(shape), dt)))
                nb = 3
                lft = [sb(f"H_lfh{j}", [128, 4, 512], BF16) for j in range(nb)]
                llt = [sb(f"H_lfl{j}", [128, 4, 512], BF16) for j in range(nb)]
                kt_ = [sb(f"H_k{j}", [128, 4, 512], BF16) for j in range(nb)]
                kTt = [sb(f"H_kT{j}", [128, 4, 512], BF16) for j in range(nb)]
                qTt = [sb(f"H_qT{j}", [128, 4, 512], BF16) for j in range(nb)]
                vt = [sb(f"H_v{j}", [128, 4, 512], BF16) for j in range(nb)]
                ex3 = [sb(f"H_ex3{j}", [128, 512], F32) for j in range(2)]
                kd2 = [sb(f"H_kd{j}", [128, 4, 512], BF16) for j in range(2)]
                vm = [[sb(f"H_vm{j}_{c}", [128, 4, 512], BF16) for c in range(2)] for j in range(nb)]
                for j in range(nb):
                    for c in range(2):
                        fw.op("pool", lambda e: e.memset(vm[j][c].t[:], 0.0), [], [vm[j][c].b])
                qx = [sb(f"H_qx{j}", [128, 512], F32) for j in range(2)]
                kx = [sb(f"H_kx{j}", [128, 512], F32) for j in range(2)]
                qe2 = [sb(f"H_qe{j}", [128, 4, 512], BF16) for j in range(2)]
                ke2 = [sb(f"H_ke{j}", [128, 4, 512], BF16) for j in range(2)]
                ext2 = [sb(f"H_ext{j}", [128, 64], F32) for j in range(2)]
                PT = [sb(f"H_PT{j}", [128, 4, 128], BF16) for j in range(3)]
                S = [sb(f"H_S{h}", [128, 128], F32) for h in range(4)]
                Sm = [sb(f"H_Sm{j}", [128, 128], BF16) for j in range(16)]
                oTs = [sb(f"H_oT{j}", [128, 4, 512], F32) for j in range(2)]
                An, Bn, Kn, Mn = ("A_f", "B_f", "K_f", "M_f") if fwd else ("A_b", "B_b", "K_b", "M_b")
                K4 = sb("H_K4", [128, 4, 128], F32)
                for h in range(4):
                    cp("dve", K4.t[:, h, :], C32(Kn), [cst.b], [K4.b])
                    fw.op("dve", lambda e: e.memset(S[h].t[:], 0.0), [], [S[h].b])
                order = list(range(NT)) if fwd else list(range(NT - 1, -1, -1))
                pfree = list(ps_t)

                def palloc():
                    return pfree.pop(0)

                def prel(p):
                    pfree.append(p)

                def load(n):
                    i = order[n]
                    j = n % nb
                    t0 = i * 512
                    fw.dma("sp", lft[j].t[:], lfh[d].ap[t0:t0 + 512, :].rearrange("(s p) f -> p s f", p=128),
                           reads=[lfh[d].b(i)], writes=[lft[j].b])
                    fw.dma("sp", llt[j].t[:], lfl[d].ap[t0:t0 + 512, :].rearrange("(s p) f -> p s f", p=128),
                           reads=[lfl[d].b(i)], writes=[llt[j].b])
                    fw.dma("sp", kt_[j].t[:], ktok[d].ap[t0:t0 + 512, :].rearrange("(s p) f -> p s f", p=128),
                           reads=[ktok[d].b(i)], writes=[kt_[j].b])
                    vsrc = vtok.ap[t0:t0 + 512, :].rearrange("(s p) f -> p s f", p=128)
                    fw.dma("sp", vt[j].t[:], vsrc, reads=[vtok.b(i)], writes=[vt[j].b])
                    for c in range(2):
                        fw.dma("sp", vm[j][c].t[c * 64:(c + 1) * 64, :, :], vsrc[c * 64:(c + 1) * 64],
                               reads=[vtok.b(i)], writes=[vm[j][c].b])
                    fw.dma("sp", kTt[j].t[:], kT[d].ap[:, t0:t0 + 512].rearrange("(h p) t -> p h t", p=128),
                           reads=[kT[d].b(i)], writes=[kTt[j].b])
                    fw.dma("sp", qTt[j].t[:], qT.ap[:, t0:t0 + 512].rearrange("(h p) t -> p h t", p=128),
                           reads=[qT.b(i)], writes=[qTt[j].b])

                def ephase(n):
                    j = n % nb
                    L, L2, K_, KT_, QT_ = lft[j], llt[j], kt_[j], kTt[j], qTt[j]
                    kd, qe, ke, ext = kd2[n % 2], qe2[n % 2], ke2[n % 2], ext2[n % 2]
                    for s in range(4):
                        p = palloc()
                        mm(p.t[:, :], C16(Bn), L.t[:, s, :], [cstb.b, L.b], [p.b], start=True, stop=False)
                        mm(p.t[:, :], C16(Bn), L2.t[:, s, :], [cstb.b, L2.b], [p.b], start=False, stop=True)
                        e3 = ex3[s % 2]
                        act(e3.t[:], p.t[:, :], AF.Exp, [p.b], [e3.b])
                        prel(p)
                        tt("pool", kd.t[:, s, :], K_.t[:, s, :], e3.t[:], ALU.mult, [K_.b, e3.b], [kd.b])
                    pe_ = palloc()
                    for h in range(4):
                        for s in range(4):
                            c0 = (h * 4 + s) * 4
                            mm(pe_.t[:, c0:c0 + 4], L.t[:, s, h * 128:(h + 1) * 128], C16(Mn, 4), [L.b, cstb.b], [pe_.b],
                               start=True, stop=False)
                            mm(pe_.t[:, c0:c0 + 4], L2.t[:, s, h * 128:(h + 1) * 128], C16(Mn, 4), [L2.b, cstb.b], [pe_.b],
                               start=False, stop=True)
                    act(ext.t[:], pe_.t[:, 0:64], AF.Exp, [pe_.b], [ext.b])
                    prel(pe_)
                    for h in range(4):
                        p = palloc()
                        for s in range(4):
                            mm(p.t[:, s * 128:(s + 1) * 128], L.t[:, s, h * 128:(h + 1) * 128], C16(An), [L.b, cstb.b], [p.b],
                               start=True, stop=False)
                            mm(p.t[:, s * 128:(s + 1) * 128], L2.t[:, s, h * 128:(h + 1) * 128], C16(An), [L2.b, cstb.b], [p.b],
                               start=False, stop=True)
                        a, b_ = qx[h % 2], kx[h % 2]
                        act(a.t[:], p.t[:, :], AF.Exp, [p.b], [a.b])
                        act(b_.t[:], p.t[:, :], AF.Exp, [p.b], [b_.b], scale=-1.0)
                        prel(p)
                        tt("dve", qe.t[:, h, :], QT_.t[:, h, :], a.t[:], ALU.mult, [QT_.b, a.b], [qe.b])
                        tt("pool", ke.t[:, h, :], KT_.t[:, h, :], b_.t[:], ALU.mult, [KT_.b, b_.b], [ke.b])

                load(0)
                if NT > 1:
                    load(1)
                ephase(0)
                smi = [0]
                pti = [0]
                for n in range(NT):
                    i = order[n]
                    j = n % nb
                    t0 = i * 512
                    V_, VM = vt[j], vm[j]
                    kd, qe, ke, ext = kd2[n % 2], qe2[n % 2], ke2[n % 2], ext2[n % 2]
                    o_ = oTs[n % 2]
                    subs = list(range(4)) if fwd else list(range(3, -1, -1))

                    def stageA(s):
                        ssl = slice(s * 128, (s + 1) * 128)
                        p = palloc()
                        for h in range(4):
                            mm(p.t[:, h * 128:(h + 1) * 128], ke.t[:, h, ssl], qe.t[:, h, ssl], [ke.b, qe.b], [p.b])
                        pt = PT[pti[0] % 3]
                        pti[0] += 1
                        tt("dve", pt.t[:], p.t[:, :].rearrange("p (h t) -> p h t", h=4), K4.t[:], ALU.mult, [p.b, K4.b], [pt.b])
                        prel(p)
                        pd = [palloc(), palloc()]
                        for c in range(2):
                            for h in range(4):
                                hs = slice(h * 128, (h + 1) * 128)
                                mm(pd[c].t[:, hs], kd.t[:, s, hs], VM[c].t[:, s, hs], [kd.b, VM[c].b], [pd[c].b])
                        return pt, pd

                    def stageC(s, pt, pd):
                        ssl = slice(s * 128, (s + 1) * 128)
                        tg = t0 + s * 128
                        po = palloc()
                        for h in range(4):
                            hs = slice(h * 128, (h + 1) * 128)
                            mm(po.t[:, hs], V_.t[:, s, hs], pt.t[:, h, :], [V_.b, pt.b], [po.b], start=(h == 0), stop=False)
                        chs = (0, 1) if fwd else (1, 0)
                        for ci, c in enumerate(chs):
                            tok = tg + c * 64
                            if (fwd and tok == HALF) or ((not fwd) and tok + 64 == HALF):
                                for h in range(4):
                                    ts("dve", S[h].t[:], S[h].t[:], flag.t[:, 0:1], ALU.mult, [S[h].b, flag.b], [S[h].b])
                            sms = []
                            for h in range(4):
                                ec = (h * 4 + s) * 4 + 2 * c
                                sm_ = Sm[smi[0] % 16]
                                smi[0] += 1
                                act(sm_.t[:], S[h].t[:], AF.Copy, [S[h].b, ext.b], [sm_.b], scale=ext.t[:, ec:ec + 1])
                                sms.append(sm_)
                            for h in range(4):
                                ec = (h * 4 + s) * 4 + 2 * c
                                stt(S[h].t[:], S[h].t[:], ext.t[:, ec + 1:ec + 2], pd[c].t[:, h * 128:(h + 1) * 128],
                                    ALU.mult, ALU.add, [S[h].b, ext.b, pd[c].b], [S[h].b])
                            for h in range(4):
                                mm(po.t[:, h * 128 + c * 64:h * 128 + (c + 1) * 64], sms[h].t[:],
                                   qe.t[:, h, s * 128 + c * 64:s * 128 + (c + 1) * 64],
                                   [sms[h].b, qe.b], [po.b], start=False, stop=(ci == 1))
                        prel(pd[0])
                        prel(pd[1])
                        cp("act", o_.t[:, :, ssl], po.t[:, :].rearrange("p (h t) -> p h t", h=4), [po.b], [o_.b])
                        prel(po)

                    if n + 2 < NT:
                        load(n + 2)
                    prev = None
                    for bi, s in enumerate(subs):
                        cur = (s,) + stageA(s)
                        if bi == 1 and n + 1 < NT:
                            ephase(n + 1)
                        if prev is not None:
                            stageC(*prev)
                        prev = cur
                    stageC(*prev)
                    fw.dma("pool", oT[d].ap[:, t0:t0 + 512].rearrange("(h p) t -> p h t", p=128), o_.t[:], reads=[o_.b],
                           writes=[oT[d].b(i)])
                fw.barrier()

        def attn_core(sb, NTq, qsrc, kv_for_tile, masks, out_row0, tag):
            LA = 5
            qm = [[sb(f"{tag}_qm{j}_{par}", [128, 2, 512], BF16) for par in range(2)] for j in range(2)]
            for j in range(2):
                for par in range(2):
                    fw.op("pool", lambda e: e.memset(qm[j][par].t[:], 0.0), [], [qm[j][par].b])
            gt = [sb(f"{tag}_g{j}", [128, 2, 512], BF16) for j in range(2)]
            pT = [sb(f"{tag}_pT{j}", [128, 512], BF16) for j in range(8)]
            sc = [sb(f"{tag}_sc{j}", [128, 512], F32) for j in range(6)] if masks is not None else None
            otok = sb(f"{tag}_otok", [128, 4, 256], BF16)
            rden = sb(f"{tag}_rden", [128, 4], F32)
            mo = [sb(f"{tag}_mo{j}", [128, 2, 512], BF16) for j in range(2)]
            pi = [0]
            for i in range(NTq):
                t0 = i * 512
                QM, G = qm[i % 2], gt[i % 2]
                qv = qsrc.ap[:, t0:t0 + 512].rearrange("(h p) t -> p h t", p=128)
                for par in range(2):
                    fw.dma("sp", QM[par].t[par * 64:(par + 1) * 64, :, :], qv[par * 64:(par + 1) * 64], reads=[qsrc.b(i)],
                           writes=[QM[par].b])
                fw.dma("sp", G.t[:], gT.ap[512 + out_row0:512 + out_row0 + 256, t0:t0 + 512].rearrange("(h p) t -> p h t", p=128),
                       reads=[gT.b(i)], writes=[G.b])
                ktl = kv_for_tile(i)
                nk = len(ktl)
                po_h = {}

                def stage1(h, ki):
                    ch, pr = h // 2, (h % 2) * 64
                    kfn, vfn, mi, kbufs = ktl[ki]
                    p = psum()
                    mm(p.t[:, :], kfn(ch, pr), QM[h % 2].t[:, ch, :], kbufs + [QM[h % 2].b], [p.b])
                    e_ = pT[pi[0] % 8]
                    if masks is not None:
                        s_ = sc[pi[0] % 6]
                        tt("dve", s_.t[:], p.t[:, :], masks.t[:, mi, :], ALU.add, [p.b, masks.b], [s_.b])
                        act(e_.t[:], s_.t[:], AF.Exp, [s_.b], [e_.b], scale=0.125)
                    else:
                        act(e_.t[:], p.t[:, :], AF.Exp, [p.b], [e_.b], scale=0.125)
                    pi[0] += 1
                    return e_

                def stage2(h, ki, e_):
                    kfn, vfn, mi, kbufs = ktl[ki]
                    if ki == 0:
                        po_h[h] = psum_acc()
                    po = po_h[h]
                    for s in range(4):
                        mm(po.t[:, s * 128:s * 128 + 65], e_.t[:, s * 128:(s + 1) * 128], vfn(h), [e_.b] + kbufs, [po.b],
                           start=(ki == 0 and s == 0), stop=(ki == nk - 1))
                    if ki == nk - 1:
                        pov = po.t[:, :].rearrange("p (s e) -> p s e", e=128)
                        fw.op("dve", lambda e: e.reciprocal(out=rden.t[:, :], in_=pov[:, :, 64]), [po.b], [rden.b])
                        for s in range(4):
                            ts("dve", otok.t[:, s, h * 64:(h + 1) * 64], po.t[:, s * 128:s * 128 + 64], rden.t[:, s:s + 1],
                               ALU.mult, [po.b, rden.b], [otok.b])

                pend = []
                for pair in range(2):
                    for ki in range(nk):
                        for h in (2 * pair, 2 * pair + 1):
                            pend.append((h, ki, stage1(h, ki)))
                            if len(pend) > LA:
                                stage2(*pend.pop(0))
                while pend:
                    stage2(*pend.pop(0))
                M_ = mo[i % 2]
                for ch in range(2):
                    p = psum()
                    for s in range(4):
                        mm(p.t[:, s * 128:(s + 1) * 128], otok.t[:, s, ch * 128:(ch + 1) * 128], C16("ident"),
                           [otok.b, cstb.b], [p.b])
                    tt("dve", M_.t[:, ch, :], p.t[:, :], G.t[:, ch, :], ALU.mult, [p.b, G.b], [M_.b])
                fw.dma("pool", mixT.ap[out_row0:out_row0 + 256, t0:t0 + 512].rearrange("(h p) t -> p h t", p=128), M_.t[:],
                       reads=[M_.b], writes=[mixT.b((out_row0, i))])

        def pass_AT(l):
            with ExitStack() as es:
                def sb(name, shape, dt):
                    return Tl(es.enter_context(nc.sbuf_tensor(f"{name}_L{l}", list(shape), dt)))
                masks = sb("AT_masks", [128, 20, 512], F32)
                fw.dma("sp", masks.t[:], amask_in.rearrange("r p q -> p r q"), writes=[masks.b])
                NW = 20
                kw_ = [sb(f"AT_kw{j}", [128, 2, NW * 128], BF16) for j in range(2)]
                vw_ = [sb(f"AT_vw{j}", [128, NW, 260], BF16) for j in range(2)]

                def kv_for_tile(i):
                    KW, VW = kw_[i % 2], vw_[i % 2]
                    q0 = i * 512
                    half_lo = (q0 // HALF) * HALF
                    lo = max(q0 - 1024, 0)
                    hi = min(q0 + 512 + 1024, T)
                    n = (hi - lo) // 128
                    tiles = sorted(set(range(lo // 512, (hi + 511) // 512)))
                    fw.dma("sp", KW.t[:, :, 0:n * 128], akT.ap[:, lo:hi].rearrange("(h p) t -> p h t", p=128),
                           reads=[akT.b(x) for x in tiles], writes=[KW.b])
                    fw.dma("sp", VW.t[:, 0:n, :], va.ap[lo:hi, :].rearrange("(j p) f -> p j f", p=128),
                           reads=[va.b(x) for x in tiles], writes=[VW.b])
                    out = []
                    for jt in range(n):
                        k0 = lo + jt * 128
                        r = (k0 - q0) // 128
                        assert -8 <= r <= 11
                        if not (half_lo <= k0 < half_lo + HALF):
                            ts("dve", VW.t[:, jt, :], VW.t[:, jt, :], flag.t[:, 0:1], ALU.mult, [VW.b, flag.b], [VW.b])
                        out.append(((lambda ch, pr, jt=jt, KW=KW: KW.t[:, ch, jt * 128:(jt + 1) * 128]),
                                    (lambda h, jt=jt, VW=VW: VW.t[:, jt, h * 65:(h + 1) * 65]),
                                    r + 8, [KW.b, VW.b]))
                    return out
                attn_core(sb, NT, aqT, kv_for_tile, masks, 0, "AT")
                fw.barrier()

        def pass_ME(l):
            with ExitStack() as es:
                def sb(name, shape, dt):
                    return Tl(es.enter_context(nc.sbuf_tensor(f"{name}_L{l}", list(shape), dt)))
                wkv = sb("ME_w", [128, 8, 512], BF16)
                for kc in range(8):
                    fw.dma("pool", wkv.t[:, kc, :], mem_wkv[l, kc * 128:(kc + 1) * 128, :], writes=[wkv.b])
                mnw = sb("ME_mnw", [128, 8], F32)
                col_load(mnw, mem_norm_w[l], 8)
                gk = sb("ME_gk", [128, 1], F32)
                col_load64(gk, mk_w[l])
                mkT = [sb(f"ME_mkT{g}", [128, 2, 256], BF16) for g in range(2)]
                mv = [sb(f"ME_mv{g}", [128, 2, 260], BF16) for g in range(2)]
                xm = sb("ME_x", [128, 1024], F32)
                junk = sb("ME_junk", [128, 1024], BF16)
                ssm = sb("ME_ss", [128, 1], F32)
                hbm = sb("ME_hb", [128, 2, 1024], BF16)
                hTm = sb("ME_hT", [128, 8, 256], BF16)
                sqm = sb("ME_sq", [128, 256], BF16)
                rsm = sb("ME_rs", [128, 256], F32)
                for g in range(2):
                    fw.op("dve", lambda e: e.memset(mv[g].t[:], 1.0), [], [mv[g].b])
                    for s in range(2):
                        fw.dma("sp", xm.t[:], mem_in[g, s * 128:(s + 1) * 128, :], writes=[xm.b])
                        act(junk.t[:], xm.t[:], AF.Square, [xm.b], [junk.b, ssm.b], accum_out=ssm.t[:, 0:1])
                        rstd_inplace(ssm.t[:], 1024.0, ssm.b)
                        ts("dve", hbm.t[:, s, :], xm.t[:], ssm.t[:, 0:1], ALU.mult, [xm.b, ssm.b], [hbm.b])
                    for kc in range(8):
                        p = psum()
                        for s in range(2):
                            mm(p.t[:, s * 128:(s + 1) * 128], hbm.t[:, s, kc * 128:(kc + 1) * 128], C16("ident"),
                               [hbm.b, cstb.b], [p.b])
                        ts("dve", hTm.t[:, kc, :], p.t[:, 0:256], mnw.t[:, kc:kc + 1], ALU.mult, [p.b, mnw.b], [hTm.b])
                    for c in range(2):
                        p = psum()
                        for kc in range(8):
                            mm(p.t[:, 0:256], wkv.t[:, kc, c * 128:(c + 1) * 128], hTm.t[:, kc, :], [wkv.b, hTm.b], [p.b],
                               start=(kc == 0), stop=(kc == 7))
                        act(sqm.t[:], p.t[:, 0:256], AF.Square, [p.b], [sqm.b])
                        p2 = psum()
                        mm(p2.t[:, 0:256], C16("ones64"), sqm.t[:], [cstb.b, sqm.b], [p2.b])
                        act(rsm.t[:], p2.t[:, 0:256], AF.Ln, [p2.b], [rsm.b], scale=1.0 / 64, bias=EPS)
                        act(rsm.t[:], rsm.t[:], AF.Exp, [rsm.b], [rsm.b], scale=-0.5)
                        stt(mkT[g].t[:, c, :], p.t[:, 0:256], gk.t[:, 0:1], rsm.t[:], ALU.mult, ALU.mult,
                            [p.b, gk.b, rsm.b], [mkT[g].b])
                    for s in range(2):
                        p = psum()
                        for kc in range(8):
                            mm(p.t[:, 0:256], hTm.t[:, kc, s * 128:(s + 1) * 128], wkv.t[:, kc, 256:512], [hTm.b, wkv.b], [p.b],
                               start=(kc == 0), stop=(kc == 7))
                        cp("act", mv[g].t[:, s, :].rearrange("p (h e) -> p h e", e=65)[:, :, 0:64],
                           p.t[:, 0:256].rearrange("p (h e) -> p h e", e=64), [p.b], [mv[g].b])

                def kv_for_tile(i):
                    g = (i * 512) // HALF
                    out = []
                    for jt in range(2):
                        out.append(((lambda ch, pr, jt=jt, g=g: mkT[g].t[:, ch, jt * 128:(jt + 1) * 128]),
                                    (lambda h, jt=jt, g=g: mv[g].t[:, jt, h * 65:(h + 1) * 65]),
                                    0, [mkT[g].b, mv[g].b]))
                    return out
                attn_core(sb, NT, mqT, kv_for_tile, None, 256, "ME")
                fw.barrier()

        def pass_C(l):
            src = x_dt if l == 0 else y_out
            with ExitStack() as es:
                def sb(name, shape, dt):
                    return Tl(es.enter_context(nc.sbuf_tensor(f"{name}_L{l}", list(shape), dt)))
                wo = sb("C_w", [128, 8, 1024], BF16)
                for kc in range(8):
                    fw.dma("pool", wo.t[:, kc, :], w_out[l, kc * 128:(kc + 1) * 128, :], writes=[wo.b])
                gon = sb("C_gon", [128, 1], F32)
                fw.dma("sp", gon.t[:, 0:1], onorm_w[l].rearrange("(p o) -> p o", o=1), writes=[gon.b],
                       allow_slow_non_contiguous=True)
                of_ = [sb(f"C_of{j}", [128, 4, 512], F32) for j in range(2)]
                ob_ = [sb(f"C_ob{j}", [128, 4, 512], F32) for j in range(2)]
                gh = [sb(f"C_g{j}", [128, 4, 512], BF16) for j in range(2)]
                mx_ = [sb(f"C_mx{j}", [128, 8, 512], BF16) for j in range(2)]
                xt = [sb(f"C_x{j}", [128, 4, 1024], F32) for j in range(2)]
                osum = [sb(f"C_osum{h}", [128, 512], F32) for h in range(4)]
                sq = [sb(f"C_sq{h}", [128, 512], BF16) for h in range(4)]
                rsf = [sb(f"C_rsf{h}", [128, 512], F32) for h in range(4)]
                m1 = osum
                mxb = [[Buf() for _ in range(8)] for _ in range(2)]
                yt = [sb(f"C_y{j}", [128, 1024], F32) for j in range(2)]

                def load(i):
                    j = i % 2
                    t0 = i * 512
                    fw.dma("sp", of_[j].t[:], oT["f"].ap[:, t0:t0 + 512].rearrange("(h p) t -> p h t", p=128),
                           reads=[oT["f"].b(i)], writes=[of_[j].b])
                    fw.dma("sp", ob_[j].t[:], oT["b"].ap[:, t0:t0 + 512].rearrange("(h p) t -> p h t", p=128),
                           reads=[oT["b"].b(i)], writes=[ob_[j].b])
                    fw.dma("sp", gh[j].t[:], gT.ap[0:512, t0:t0 + 512].rearrange("(h p) t -> p h t", p=128),
                           reads=[gT.b(i)], writes=[gh[j].b])
                    fw.dma("sp", mx_[j].t[:, 4:8, :], mixT.ap[:, t0:t0 + 512].rearrange("(h p) t -> p h t", p=128),
                           reads=[mixT.b((0, i)), mixT.b((256, i))], writes=mxb[j][4:8])
                    fw.dma("sp", xt[j].t[:], src.ap[t0:t0 + 512, :].rearrange("(s p) f -> p s f", p=128),
                           reads=[src.b(i)], writes=[xt[j].b])
                yi = [0]
                H4 = range(4)

                def normphase(i):
                    j = i % 2
                    for h in H4:
                        tt("pool", osum[h].t[:], of_[j].t[:, h, :], ob_[j].t[:, h, :], ALU.add, [of_[j].b, ob_[j].b], [osum[h].b])
                    for h in H4:
                        act(sq[h].t[:], osum[h].t[:], AF.Square, [osum[h].b], [sq[h].b])
                    pp = []
                    for h in H4:
                        p = psum()
                        mm(p.t[:, :], C16("ones128"), sq[h].t[:], [cstb.b, sq[h].b], [p.b])
                        pp.append(p)
                    for h in H4:
                        act(rsf[h].t[:], pp[h].t[:, :], AF.Ln, [pp[h].b], [rsf[h].b], scale=1.0 / 128, bias=EPS)
                    for h in H4:
                        act(rsf[h].t[:], rsf[h].t[:], AF.Exp, [rsf[h].b], [rsf[h].b], scale=-0.5)
                    for h in H4:
                        stt(m1[h].t[:], osum[h].t[:], gon.t[:, 0:1], rsf[h].t[:], ALU.mult, ALU.mult,
                            [osum[h].b, gon.b, rsf[h].b], [m1[h].b])
                    for h in H4:
                        tt("dve", mx_[j].t[:, h, :], m1[h].t[:], gh[j].t[:, h, :], ALU.mult, [m1[h].b, gh[j].b], [mxb[j][h]])

                def outproj(i):
                    j = i % 2
                    t0 = i * 512
                    for s in range(4):
                        Y = yt[yi[0] % 2]
                        yi[0] += 1
                        for nh in range(2):
                            p = psum()
                            for mc in range(8):
                                mm(p.t[:, :], mx_[j].t[:, mc, s * 128:(s + 1) * 128], wo.t[:, mc, nh * 512:(nh + 1) * 512],
                                   [mxb[j][mc], wo.b], [p.b], start=(mc == 0), stop=(mc == 7))
                            tt("dve", Y.t[:, nh * 512:(nh + 1) * 512], p.t[:, :], xt[j].t[:, s, nh * 512:(nh + 1) * 512], ALU.add,
                               [p.b, xt[j].b], [Y.b])
                        fw.dma("pool", y_out.ap[t0 + s * 128:t0 + (s + 1) * 128, :], Y.t[:], reads=[Y.b], writes=[y_out.b(i)])

                load(0)
                if NT > 1:
                    load(1)
                normphase(0)
                for i in range(NT):
                    if i + 1 < NT:
                        normphase(i + 1)
                    outproj(i)
                    if i + 2 < NT:
                        load(i + 2)
                fw.barrier()

        _P = _os.environ.get("KPASSES", "A,HF,HB,AT,ME,C").split(",")
        wes = ExitStack()
        w_cur = load_w_in(0, wes)
        for l in range(depth):
            if "A" in _P:
                pass_A(l, w_cur)
            fw.barrier()
            wes.close()
            if "HF" in _P:
                pass_H(l, "f")
            if "HB" in _P:
                pass_H(l, "b")
            if "AT" in _P:
                pass_AT(l)
            if l + 1 < depth:
                wes = ExitStack()
                w_cur = load_w_in(l + 1, wes)
            if "ME" in _P:
                pass_ME(l)
            if "C" in _P:
                pass_C(l)
        fw.barrier()
    return nc, fw


T_CORE = 16384
DEPTH = 4
_CACHE = {}


def kernel(x_prompt, x_sample, mem_prompt, mem_sample, norm_w, w_in, hgrn_lb_fwd, hgrn_lb_bwd, hgrn_onorm_w,
           attn_qnorm_w, attn_knorm_w, mem_norm_w, mem_wkv, mem_qnorm_w, mem_knorm_w, w_out):
    f = lambda a: np.ascontiguousarray(np.asarray(a, dtype=np.float32))
    x_prompt, x_sample, mem_prompt, mem_sample = f(x_prompt), f(x_sample), f(mem_prompt), f(mem_sample)
    T = T_CORE
    if "nc" not in _CACHE:
        _CACHE["nc"] = build(T, DEPTH)[0]
        _CACHE["hc"] = host_consts(T)
    nc = _CACHE["nc"]
    hc = _CACHE["hc"]
    pos_p = np.concatenate([np.arange(8192), np.arange(8192)])
    pos_s = np.arange(16384)
    Cp, Sp = rope_tables(pos_p)
    Cs, Ss = rope_tables(pos_s)
    shared = {"cst": hc["cst"], "amask": hc["amask"], "norm_w": f(norm_w), "w_in": f(w_in),
              "hgrn_lb_fwd": f(hgrn_lb_fwd), "hgrn_lb_bwd": f(hgrn_lb_bwd), "hgrn_onorm_w": f(hgrn_onorm_w),
              "attn_qnorm_w": f(attn_qnorm_w), "attn_knorm_w": f(attn_knorm_w), "mem_norm_w": f(mem_norm_w),
              "mem_wkv": f(mem_wkv), "mem_qnorm_w": f(mem_qnorm_w), "mem_knorm_w": f(mem_knorm_w), "w_out": f(w_out)}
    in_maps = []
    for c in range(8):
        m = dict(shared)
        if c < 4:
            m["x"] = x_prompt[2 * c:2 * c + 2].reshape(T, 1024)
            m["mem"] = mem_prompt[2 * c:2 * c + 2]
            m["flag"] = np.zeros((128, 1), np.float32)
            m["ropeC"], m["ropeS"] = Cp, Sp
        else:
            s = (c - 4) % 2
            m["x"] = x_sample[s]
            m["mem"] = np.stack([mem_sample[s], mem_sample[s]])
            m["flag"] = np.ones((128, 1), np.float32)
            m["ropeC"], m["ropeS"] = Cs, Ss
        in_maps.append(m)
    res = run_bass_kernel_spmd(nc, in_maps, core_ids=list(range(8)))
    ys = [np.asarray(r["y"], dtype=np.float32) for r in res.results]
    y_prompt = np.stack([ys[c].reshape(2, 8192, 1024) for c in range(4)]).reshape(8, 8192, 1024)
    y_sample = np.stack([ys[4], ys[5]])
    return (y_prompt, y_sample)
```

```python
import math
import os as _os
from contextlib import ExitStack
import numpy as np
import concourse.bass as bass
import concourse.mybir as mybir
from concourse.bass_utils import run_bass_kernel_spmd

F32 = mybir.dt.float32
BF16 = mybir.dt.bfloat16
AF = mybir.ActivationFunctionType
ALU = mybir.AluOpType
EPS = 1e-6
NSLOT = 8


class Buf:
    __slots__ = ("w", "r")

    def __init__(self):
        self.w = None
        self.r = {}


class Tl:
    def __init__(self, t):
        self.t = t
        self.b = Buf()


class FW:
    LIM = 30000

    def __init__(self, nc):
        self.nc = nc
        self.engs = {"pe": nc.tensor, "act": nc.scalar, "dve": nc.vector, "pool": nc.gpsimd, "sp": nc.sync}
        self.cur = {}
        self.seen = {k: {} for k in self.engs}
        self.last = {}
        self.nsem = 0
        self.dslots = {"sp": [], "pool": []}
        self.dnext = {"sp": 0, "pool": 0}
        self.nins = 0

    def newsem(self):
        self.nsem += 1
        return [self.nsem, self.nc.alloc_semaphore(name=f"fs{self.nsem}")]

    def _wait(self, e, ev):
        if self.seen[e].get(ev[0], 0) >= ev[2]:
            return
        self.engs[e].wait_ge(ev[1], ev[2])
        self.seen[e][ev[0]] = ev[2]

    def _deps(self, e, reads, writes):
        for b in reads:
            if b.w is not None and not (e == "pe" and b.w[3] == "pe"):
                self._wait(e, b.w)
        for b in writes:
            if b.w is not None and not (e == "pe" and b.w[3] == "pe"):
                self._wait(e, b.w)
            for ev in b.r.values():
                if not (e == "pe" and ev[3] == "pe"):
                    self._wait(e, ev)

    def _mark(self, ev, key, reads, writes):
        for b in reads:
            b.r[key] = ev
        for b in writes:
            b.w = ev
            b.r = {}

    def op(self, e, fn, reads=(), writes=()):
        self._deps(e, reads, writes)
        ins = fn(self.engs[e])
        c = self.cur.get(e)
        if c is None or c[2] >= self.LIM:
            c = self.newsem() + [0]
            self.cur[e] = c
        c[2] += 1
        ins.then_inc(c[1], 1)
        ev = (c[0], c[1], c[2], e)
        self._mark(ev, e, reads, writes)
        self.last[e] = ev
        self.nins += 1

    def dma(self, q, out, in_, reads=(), writes=(), **kw):
        self._deps(q, reads, writes)
        slots = self.dslots[q]
        if len(slots) < NSLOT:
            slots.append(self.newsem() + [0])
            s = slots[-1]
        else:
            s = slots[self.dnext[q] % NSLOT]
        self.dnext[q] += 1
        if s[2] > 0:
            self._wait(q, (s[0], s[1], s[2], "dma"))
        if s[2] + 16 > self.LIM:
            s[:] = self.newsem() + [0]
        ins = self.engs[q].dma_start(out=out, in_=in_, **kw)
        s[2] += 16
        ins.then_inc(s[1], 16)
        ev = (s[0], s[1], s[2], "dma")
        self._mark(ev, ("d", s[0], s[2]), reads, writes)
        self.nins += 1

    def barrier(self):
        evs = [self.last[e] for e in ("pe", "act", "dve", "pool") if e in self.last]
        for q in self.dslots:
            for s in self.dslots[q]:
                if s[2] > 0:
                    evs.append((s[0], s[1], s[2], "dma"))
        for e in self.engs:
            for ev in evs:
                if e == "pe" and ev[3] == "pe":
                    continue
                self._wait(e, ev)


class DT:
    def __init__(self, ap):
        self.ap = ap
        self.bufs = {}

    def b(self, i):
        if i not in self.bufs:
            self.bufs[i] = Buf()
        return self.bufs[i]


def host_consts(T):
    c = {}
    s = np.arange(128)[:, None]
    t = np.arange(128)[None, :]
    same = (s // 64) == (t // 64)
    c0 = (t // 64) * 64
    A_f = (same & (s <= t)).astype(np.float32) - (same & (s <= c0 + 31)).astype(np.float32)
    A_b = (same & (s >= t)).astype(np.float32) - (same & (s >= c0 + 32)).astype(np.float32)
    B_f = (same & (s > t)).astype(np.float32)
    B_b = (same & (s < t)).astype(np.float32)
    M_f = np.zeros((128, 4), np.float32)
    M_b = np.zeros((128, 4), np.float32)
    sv = np.arange(128)
    for ch in range(2):
        inc = (sv // 64) == ch
        M_f[:, 2 * ch] = inc & (sv <= ch * 64 + 31)
        M_f[:, 2 * ch + 1] = inc
        M_b[:, 2 * ch] = inc & (sv >= ch * 64 + 32)
        M_b[:, 2 * ch + 1] = inc
    K_f = (same & (s <= t)).astype(np.float32)
    K_b = (same & (s >= t)).astype(np.float32)
    ident = np.eye(128, dtype=np.float32)
    ones64 = ((s // 64) == (t // 64)).astype(np.float32)
    ones128 = np.ones((128, 128), np.float32)
    Rm = np.zeros((128, 128), np.float32)
    for m in range(128):
        j = m % 64
        if j < 8:
            Rm[m + 8, m] = -1.0
        elif j < 16:
            Rm[m - 8, m] = 1.0
    cst = np.concatenate([A_f, A_b, B_f, B_b, K_f, K_b, ident, ones64, ones128, Rm, M_f, M_b], axis=1)
    c["cst"] = np.ascontiguousarray(cst.astype(np.float32))
    am = np.zeros((20, 128, 512), np.float32)
    j = np.arange(128)[:, None]
    i = np.arange(512)[None, :]
    for ri, r in enumerate(range(-8, 12)):
        d = r * 128 + j - i
        am[ri] = ((np.abs(d) <= 64).astype(np.float32)
                  + ((d % 4 == 0) & (np.abs(d) <= 256)).astype(np.float32)
                  + ((d % 16 == 0) & (np.abs(d) <= 1024)).astype(np.float32))
    c["amask"] = np.where(am > 0, 8.0 * np.log(np.maximum(am, 1.0)), -80000.0).astype(np.float32)
    return c


def rope_tables(pos):
    half = 8
    inv = (500000.0 ** (-np.arange(half, dtype=np.float32) * 2.0 / 16.0)).astype(np.float32)
    ang = pos.astype(np.float32)[None, :] * inv[:, None]
    C = np.ones((128, pos.shape[0]), np.float32)
    S = np.zeros((128, pos.shape[0]), np.float32)
    for hb in (0, 64):
        C[hb:hb + 8] = np.cos(ang)
        C[hb + 8:hb + 16] = np.cos(ang)
        S[hb:hb + 8] = np.sin(ang)
        S[hb + 8:hb + 16] = np.sin(ang)
    return C, S


def _sub_ranges():
    out = []
    j = np.arange(128)[:, None]
    i = np.arange(512)[None, :]
    for r in range(-8, 12):
        d = r * 128 + j - i
        ok = (np.abs(d) <= 64) | ((d % 4 == 0) & (np.abs(d) <= 256)) | ((d % 16 == 0) & (np.abs(d) <= 1024))
        subs = [s for s in range(4) if ok[:, s * 128:(s + 1) * 128].any()]
        out.append((min(subs), max(subs) + 1))
    return out


SUBR = _sub_ranges()
CO = {"A_f": 0, "A_b": 128, "B_f": 256, "B_b": 384, "K_f": 512, "K_b": 640, "ident": 768, "ones64": 896,
      "ones128": 1024, "Rm": 1152, "M_f": 1280, "M_b": 1284}
NCST = 1288


def build(T, depth, debug=False):
    NT = T // 512
    HALF = T // 2
    nc = bass.Bass("TRN2", target_bir_lowering=False)
    fw = FW(nc)

    def din(name, shape, dt=F32):
        return nc.dram_tensor(name, list(shape), dt, kind="ExternalInput").ap()

    x_in = din("x", [T, 1024])
    mem_in = din("mem", [2, 256, 1024])
    flag_in = din("flag", [128, 1])
    ropeC_in = din("ropeC", [128, T])
    ropeS_in = din("ropeS", [128, T])
    cst_in = din("cst", [128, NCST])
    amask_in = din("amask", [20, 128, 512])
    norm_w = din("norm_w", [depth, 1024])
    w_in = din("w_in", [depth, 1024, 4096])
    lbp = {"f": din("hgrn_lb_fwd", [depth, 512]), "b": din("hgrn_lb_bwd", [depth, 512])}
    onorm_w = din("hgrn_onorm_w", [depth, 128])
    aq_w = din("attn_qnorm_w", [depth, 64])
    ak_w = din("attn_knorm_w", [depth, 64])
    mem_norm_w = din("mem_norm_w", [depth, 1024])
    mem_wkv = din("mem_wkv", [depth, 1024, 512])
    mq_w = din("mem_qnorm_w", [depth, 64])
    mk_w = din("mem_knorm_w", [depth, 64])
    w_out = din("w_out", [depth, 1024, 1024])
    y_out = DT(nc.dram_tensor("y", [T, 1024], F32, kind="ExternalOutput").ap())
    x_dt = DT(x_in)

    skind = "ExternalOutput" if debug else "Internal"

    def scr(name, shape, dt):
        return DT(nc.dram_tensor(name, list(shape), dt, kind=skind).ap())

    qT = scr("s_qT", [512, T], BF16)
    kT = {"f": scr("s_kTf", [512, T], BF16), "b": scr("s_kTb", [512, T], BF16)}
    ktok = {"f": scr("s_kf", [T, 512], BF16), "b": scr("s_kb", [T, 512], BF16)}
    lfh = {"f": scr("s_lfhf", [T, 512], BF16), "b": scr("s_lfhb", [T, 512], BF16)}
    lfl = {"f": scr("s_lflf", [T, 512], BF16), "b": scr("s_lflb", [T, 512], BF16)}
    vtok = scr("s_v", [T, 512], BF16)
    gT = scr("s_gT", [1024, T], BF16)
    aqT = scr("s_aqT", [256, T], BF16)
    akT = scr("s_akT", [256, T], BF16)
    va = scr("s_va", [T, 260], BF16)
    mqT = scr("s_mqT", [256, T], BF16)
    oT = {"f": scr("s_ofT", [512, T], F32), "b": scr("s_obT", [512, T], F32)}
    mixT = scr("s_mixT", [512, T], BF16)

    es0 = ExitStack()
    with es0:
        def sb0(name, shape, dt):
            return Tl(es0.enter_context(nc.sbuf_tensor(name, list(shape), dt)))

        ps_t = [Tl(es0.enter_context(nc.psum_tensor(f"ps{i}", [128, 512], F32))) for i in range(8)]
        ps_i = [0]

        def psum():
            p = ps_t[ps_i[0] % 6]
            ps_i[0] += 1
            return p
        pa_i = [0]

        def psum_acc():
            p = ps_t[6 + pa_i[0] % 2]
            pa_i[0] += 1
            return p

        def mm(out, lhsT, rhs, reads, writes, start=True, stop=True):
            fw.op("pe", lambda e: e.matmul(out, lhsT=lhsT, rhs=rhs, start=start, stop=stop), reads, writes)

        def act(out, in_, func, reads, writes, **kw):
            fw.op("act", lambda e: e.activation(out=out, in_=in_, func=func, **kw), reads, writes)

        def tt(eng, out, in0, in1, op, reads, writes):
            fw.op(eng, lambda e: e.tensor_tensor(out=out, in0=in0, in1=in1, op=op), reads, writes)

        def ts(eng, out, in0, s1, op0, reads, writes, s2=None, op1=None):
            if op1 is None:
                fw.op(eng, lambda e: e.tensor_scalar(out=out, in0=in0, scalar1=s1, scalar2=None, op0=op0), reads, writes)
            else:
                fw.op(eng, lambda e: e.tensor_scalar(out=out, in0=in0, scalar1=s1, scalar2=s2, op0=op0, op1=op1),
                      reads, writes)

        def stt(out, in0, scalar, in1, op0, op1, reads, writes):
            fw.op("dve", lambda e: e.scalar_tensor_tensor(out=out, in0=in0, scalar=scalar, in1=in1, op0=op0, op1=op1),
                  reads, writes)

        def cp(eng, out, in_, reads, writes):
            if eng == "act":
                fw.op("act", lambda e: e.copy(out=out, in_=in_), reads, writes)
            else:
                fw.op(eng, lambda e: e.tensor_copy(out=out, in_=in_), reads, writes)

        cst = sb0("cst_sb", [128, NCST], F32)
        fw.dma("sp", cst.t[:], cst_in[:, :], writes=[cst.b])
        cstb = sb0("cstb_sb", [128, NCST], BF16)
        cp("dve", cstb.t[:], cst.t[:], [cst.b], [cstb.b])
        flag = sb0("flag_sb", [128, 1], F32)
        fw.dma("sp", flag.t[:], flag_in[:, :], writes=[flag.b])

        def C32(name, w=128):
            return cst.t[:, CO[name]:CO[name] + w]

        def C16(name, w=128):
            return cstb.t[:, CO[name]:CO[name] + w]

        def col_load(dst, src1d, n):
            fw.dma("sp", dst.t[:, 0:n], src1d.rearrange("(c p) -> p c", p=128), writes=[dst.b],
                   allow_slow_non_contiguous=True)

        def col_load64(dst, src1d):
            for hb in (0, 64):
                fw.dma("sp", dst.t[hb:hb + 64, 0:1], src1d.rearrange("(p o) -> p o", o=1), writes=[dst.b],
                       allow_slow_non_contiguous=True)

        def rstd_inplace(tl_ap, n, reads_b):
            act(tl_ap, tl_ap, AF.Ln, [reads_b], [reads_b], scale=1.0 / n, bias=EPS)
            act(tl_ap, tl_ap, AF.Exp, [reads_b], [reads_b], scale=-0.5)

        def pass_A(l):
            src = x_dt if l == 0 else y_out
            with ExitStack() as es:
                def sb(name, shape, dt):
                    return Tl(es.enter_context(nc.sbuf_tensor(f"{name}_L{l}", list(shape), dt)))
                w = sb("A_w", [128, 8, 4096], BF16)
                for kc in range(8):
                    for q4 in range(4):
                        fw.dma("pool", w.t[:, kc, q4 * 1024:(q4 + 1) * 1024],
                               w_in[l, kc * 128:(kc + 1) * 128, q4 * 1024:(q4 + 1) * 1024], writes=[w.b])
                normw = sb("A_normw", [128, 8], F32)
                col_load(normw, norm_w[l], 8)
                gq = sb("A_gq", [128, 1], F32)
                gk = sb("A_gk", [128, 1], F32)
                gm = sb("A_gm", [128, 1], F32)
                col_load64(gq, aq_w[l])
                col_load64(gk, ak_w[l])
                col_load64(gm, mq_w[l])
                oml = {d: sb(f"A_oml{d}", [128, 512], F32) for d in "fb"}
                with ExitStack() as es2:
                    def sb2(name, shape, dt):
                        return Tl(es2.enter_context(nc.sbuf_tensor(f"{name}_L{l}", list(shape), dt)))
                    row = sb2("A_lbrow", [1, depth, 512], F32)
                    mx = sb2("A_lbmx", [1, 512], F32)
                    sm = sb2("A_lbsm", [1, 512], F32)
                    acc = sb2("A_lbacc", [1, 512], F32)
                    tmp = sb2("A_lbtmp", [1, 512], F32)
                    for d in ("f", "b"):
                        fw.dma("sp", row.t[0:1, :, :], lbp[d].rearrange("(o l) f -> o l f", o=1), writes=[row.b])
                        cp("dve", mx.t[:], row.t[0:1, 0, :], [row.b], [mx.b])
                        for j in range(1, depth):
                            tt("dve", mx.t[:], mx.t[:], row.t[0:1, j, :], ALU.max, [mx.b, row.b], [mx.b])
                        for j in range(depth):
                            tt("dve", row.t[0:1, j, :], row.t[0:1, j, :], mx.t[:], ALU.subtract, [row.b, mx.b], [row.b])
                        act(row.t[:], row.t[:], AF.Exp, [row.b], [row.b])
                        cp("dve", sm.t[:], row.t[0:1, 0, :], [row.b], [sm.b])
                        for j in range(1, depth):
                            tt("dve", sm.t[:], sm.t[:], row.t[0:1, j, :], ALU.add, [sm.b, row.b], [sm.b])
                        fw.op("dve", lambda e: e.reciprocal(out=sm.t[:], in_=sm.t[:]), [sm.b], [sm.b])
                        fw.op("dve", lambda e: e.memset(acc.t[:], 0.0), [], [acc.b])
                        for j in range(1, l + 1):
                            tt("dve", tmp.t[:], row.t[0:1, j, :], sm.t[:], ALU.mult, [row.b, sm.b], [tmp.b])
                            tt("dve", acc.t[:], acc.t[:], tmp.t[:], ALU.add, [acc.b, tmp.b], [acc.b])
                        ts("dve", acc.t[:], acc.t[:], -1.0, ALU.mult, [acc.b], [acc.b], s2=1.0, op1=ALU.add)
                        p = psum()
                        mm(p.t[:, :], C32("ones128")[0:1, :], acc.t[0:1, :], [cst.b, acc.b], [p.b])
                        cp("dve", oml[d].t[:], p.t[:, :], [p.b], [oml[d].b])
                    fw.barrier()

                xt = [sb(f"A_x{i}", [128, 1024], F32) for i in range(4)]
                ss = sb("A_ss", [128, 4], F32)
                hb = sb("A_hb", [128, 4, 1024], BF16)
                hTr = [sb(f"A_hT{j}", [128, 8, 512], BF16) for j in range(2)]
                qTs = sb("A_qTs", [128, 4, 512], BF16)
                gTs = sb("A_gTs", [128, 8, 512], BF16)
                aqTs = sb("A_aqTs", [128, 2, 512], BF16)
                akTs = sb("A_akTs", [128, 2, 512], BF16)
                mqTs = sb("A_mqTs", [128, 2, 512], BF16)
                lfhs = {d: sb(f"A_lfhs{d}", [128, 4, 512], BF16) for d in "fb"}
                lfls = {d: sb(f"A_lfls{d}", [128, 4, 512], BF16) for d in "fb"}
                ks = {d: sb(f"A_ks{d}", [128, 4, 512], BF16) for d in "fb"}
                kTs = {d: sb(f"A_kTs{d}", [128, 4, 512], BF16) for d in "fb"}
                vs = sb("A_vs", [128, 4, 512], BF16)
                vas = sb("A_vas", [128, 4, 260], BF16)
                fw.op("dve", lambda e: e.memset(vas.t[:], 1.0), [], [vas.b])
                NR = 2
                sq = [sb(f"A_sq{j}", [128, 512], BF16) for j in range(NR)]
                rsf = [sb(f"A_rsf{j}", [128, 512], F32) for j in range(NR)]
                qn = [sb(f"A_qn{j}", [128, 512], F32) for j in range(NR)]
                qnb = [sb(f"A_qnb{j}", [128, 512], BF16) for j in range(NR)]
                t1 = rsf
                t2 = [sb(f"A_t2{j}", [128, 512], F32) for j in range(NR)]
                rC = [sb(f"A_rC{j}", [128, 512], F32) for j in range(2)]
                rS = [sb(f"A_rS{j}", [128, 512], F32) for j in range(2)]
                NG = 2
                sg = [sb(f"A_sg{j}", [128, 512], F32) for j in range(NG)]
                k32 = [sb(f"A_k32{j}", [128, 512], F32) for j in range(NG)]
                sub_b = {}

                def SB(tl, idx):
                    key = (id(tl), idx)
                    if key not in sub_b:
                        sub_b[key] = Buf()
                    return sub_b[key]

                def SBall(tl, n):
                    return [SB(tl, j) for j in range(n)]
                pfree = list(ps_t)

                def palloc():
                    return pfree.pop(0)

                def prel(p):
                    pfree.append(p)

                def run_chains(gens, K):
                    active = []
                    it = iter(gens)
                    done = False
                    while True:
                        while len(active) < K and not done:
                            g = next(it, None)
                            if g is None:
                                done = True
                            else:
                                active.append(g)
                        if not active:
                            break
                        for g in list(active):
                            try:
                                next(g)
                            except StopIteration:
                                active.remove(g)

                def prologue(i):
                    t0 = i * 512
                    hT = hTr[i % 2]
                    fw.dma("sp", rC[i % 2].t[:], ropeC_in[:, t0:t0 + 512], writes=[rC[i % 2].b])
                    fw.dma("sp", rS[i % 2].t[:], ropeS_in[:, t0:t0 + 512], writes=[rS[i % 2].b])
                    for s in range(4):
                        fw.dma("sp", xt[s].t[:], src.ap[t0 + s * 128:t0 + (s + 1) * 128, :], reads=[src.b(i)], writes=[xt[s].b])
                        act(hb.t[:, s, :], xt[s].t[:], AF.Square, [xt[s].b], [SB(hb, s), ss.b], accum_out=ss.t[:, s:s + 1])
                        yield
                    rstd_inplace(ss.t[:], 1024.0, ss.b)
                    yield
                    for s in range(4):
                        ts("dve", hb.t[:, s, :], xt[s].t[:], ss.t[:, s:s + 1], ALU.mult, [xt[s].b, ss.b], [SB(hb, s)])
                        yield
                    for kc in range(8):
                        p = palloc()
                        for s in range(4):
                            mm(p.t[:, s * 128:(s + 1) * 128], hb.t[:, s, kc * 128:(kc + 1) * 128], C16("ident"),
                               [SB(hb, s), cstb.b], [p.b])
                        yield
                        ts("dve", hT.t[:, kc, :], p.t[:, :], normw.t[:, kc:kc + 1], ALU.mult, [p.b, normw.b], [SB(hT, kc)])
                        prel(p)
                        yield

                rfree = list(range(NR))
                gfree = list(range(NG))

                def tile_chains(i):
                    hT = hTr[i % 2]
                    hTb = SBall(hT, 8)
                    RC, RS = rC[i % 2], rS[i % 2]

                    def fm(p, col0):
                        for kc in range(8):
                            mm(p.t[:, :], w.t[:, kc, col0:col0 + 128], hT.t[:, kc, :], [w.b, hTb[kc]], [p.b],
                               start=(kc == 0), stop=(kc == 7))

                    def tm(p, s, col0, n):
                        for kc in range(8):
                            mm(p.t[:, 0:n], hT.t[:, kc, s * 128:(s + 1) * 128], w.t[:, kc, col0:col0 + n],
                               [hTb[kc], w.b], [p.b], start=(kc == 0), stop=(kc == 7))

                    def silu_chain(col0, dst, c):
                        p = palloc()
                        fm(p, col0)
                        yield
                        act(dst.t[:, c, :], p.t[:, :], AF.Silu, [p.b], [SB(dst, c)])
                        prel(p)

                    def v_chain(s):
                        p = palloc()
                        tm(p, s, 1536, 512)
                        yield
                        cp("act", vs.t[:, s, :], p.t[:, :], [p.b], [SB(vs, s)])
                        prel(p)

                    def av_chain(s):
                        p = palloc()
                        tm(p, s, 2560, 256)
                        yield
                        cp("act", vas.t[:, s, :].rearrange("p (h e) -> p h e", e=65)[:, :, 0:64],
                           p.t[:, 0:256].rearrange("p (h e) -> p h e", e=64), [p.b], [vas.b, SB(vas, s)])
                        prel(p)

                    def norm_chain(col0, gain, dst, c, rope):
                        while not rfree:
                            yield
                        r = rfree.pop(0)
                        p = palloc()
                        fm(p, col0 + c * 128)
                        yield
                        act(sq[r].t[:], p.t[:, :], AF.Square, [p.b], [sq[r].b])
                        yield
                        p2 = palloc()
                        mm(p2.t[:, :], C16("ones64"), sq[r].t[:], [cstb.b, sq[r].b], [p2.b])
                        yield
                        act(rsf[r].t[:], p2.t[:, :], AF.Ln, [p2.b], [rsf[r].b], scale=1.0 / 64, bias=EPS)
                        prel(p2)
                        act(rsf[r].t[:], rsf[r].t[:], AF.Exp, [rsf[r].b], [rsf[r].b], scale=-0.5)
                        yield
                        if not rope:
                            stt(dst.t[:, c, :], p.t[:, :], gain.t[:, 0:1], rsf[r].t[:], ALU.mult, ALU.mult,
                                [p.b, gain.b, rsf[r].b], [SB(dst, c)])
                            prel(p)
                            rfree.append(r)
                            return
                        stt(qn[r].t[:], p.t[:, :], gain.t[:, 0:1], rsf[r].t[:], ALU.mult, ALU.mult,
                            [p.b, gain.b, rsf[r].b], [qn[r].b])
                        prel(p)
                        yield
                        cp("act", qnb[r].t[:], qn[r].t[:], [qn[r].b], [qnb[r].b])
                        tt("pool", t2[r].t[:], qn[r].t[:], RC.t[:], ALU.mult, [qn[r].b, RC.b], [t2[r].b])
                        yield
                        p3 = palloc()
                        mm(p3.t[:, :], C16("Rm"), qnb[r].t[:], [cstb.b, qnb[r].b], [p3.b])
                        yield
                        tt("dve", t1[r].t[:], p3.t[:, :], RS.t[:], ALU.mult, [p3.b, RS.b], [t1[r].b])
                        prel(p3)
                        yield
                        tt("dve", dst.t[:, c, :], t1[r].t[:], t2[r].t[:], ALU.add, [t1[r].b, t2[r].b], [SB(dst, c)])
                        rfree.append(r)

                    def fgate_chain(s, d, col0):
                        while not gfree:
                            yield
                        g = gfree.pop(0)
                        p = palloc()
                        tm(p, s, col0, 512)
                        yield
                        act(sg[g].t[:], p.t[:, :], AF.Exp, [p.b], [sg[g].b])
                        prel(p)
                        yield
                        act(sg[g].t[:], sg[g].t[:], AF.Ln, [sg[g].b], [sg[g].b], bias=1.0)
                        yield
                        act(sg[g].t[:], sg[g].t[:], AF.Exp, [sg[g].b], [sg[g].b], scale=-1.0)
                        yield
                        tt("dve", k32[g].t[:], sg[g].t[:], oml[d].t[:], ALU.mult, [sg[g].b, oml[d].b], [k32[g].b])
                        yield
                        act(sg[g].t[:], k32[g].t[:], AF.Ln, [k32[g].b], [sg[g].b], scale=-1.0, bias=1.0)
                        cp("pool", ks[d].t[:, s, :], k32[g].t[:], [k32[g].b], [SB(ks[d], s)])
                        yield
                        cp("act", lfhs[d].t[:, s, :], sg[g].t[:], [sg[g].b], [SB(lfhs[d], s)])
                        yield
                        tt("pool", lfls[d].t[:, s, :], sg[g].t[:], lfhs[d].t[:, s, :], ALU.subtract,
                           [sg[g].b, SB(lfhs[d], s)], [SB(lfls[d], s)])
                        gfree.append(g)

                    def kT_chain(d, h):
                        p = palloc()
                        for s in range(4):
                            mm(p.t[:, s * 128:(s + 1) * 128], ks[d].t[:, s, h * 128:(h + 1) * 128], C16("ident"),
                               [SB(ks[d], s), cstb.b], [p.b])
                        yield
                        cp("dve", kTs[d].t[:, h, :], p.t[:, :], [p.b], [SB(kTs[d], h)])
                        prel(p)

                    ph1 = []
                    sil = [silu_chain(c * 128, qTs, c) for c in range(4)] + [silu_chain(3072 + c * 128, gTs, c) for c in range(8)]
                    cps = []
                    for s in range(4):
                        cps += [v_chain(s), av_chain(s)]
                    for j in range(12):
                        ph1.append(sil[j])
                        if j < 8:
                            ph1.append(cps[j])
                    nrm = []
                    for (col0, gain, dst, rope) in ((2048, gq, aqTs, True), (2304, gk, akTs, True), (2816, gm, mqTs, False)):
                        for c in range(2):
                            nrm.append(norm_chain(col0, gain, dst, c, rope))
                    fg = []
                    for s in range(4):
                        fg += [fgate_chain(s, "f", 512), fgate_chain(s, "b", 1024)]
                    ph2 = []
                    for j in range(8):
                        ph2.append(fg[j])
                        if j < 6:
                            ph2.append(nrm[j])
                    ph3 = [kT_chain(d, h) for d in "fb" for h in range(4)]
                    return ph1, ph2, ph3

                def stores(i):
                    t0 = i * 512
                    fmv = lambda dd: dd.ap[:, t0:t0 + 512].rearrange("(h p) t -> p h t", p=128)
                    tmv = lambda dd: dd.ap[t0:t0 + 512, :].rearrange("(s p) f -> p s f", p=128)
                    fw.dma("pool", fmv(qT), qTs.t[:], reads=SBall(qTs, 4), writes=[qT.b(i)])
                    fw.dma("pool", fmv(gT), gTs.t[:], reads=SBall(gTs, 8), writes=[gT.b(i)])
                    fw.dma("pool", tmv(vtok), vs.t[:], reads=SBall(vs, 4), writes=[vtok.b(i)])
                    fw.dma("pool", tmv(va), vas.t[:], reads=[vas.b] + SBall(vas, 4), writes=[va.b(i)])
                    for (dst, dd) in ((aqTs, aqT), (akTs, akT), (mqTs, mqT)):
                        fw.dma("pool", fmv(dd), dst.t[:], reads=SBall(dst, 2), writes=[dd.b(i)])
                    for d in "fb":
                        fw.dma("pool", tmv(lfh[d]), lfhs[d].t[:], reads=SBall(lfhs[d], 4), writes=[lfh[d].b(i)])
                        fw.dma("pool", tmv(lfl[d]), lfls[d].t[:], reads=SBall(lfls[d], 4), writes=[lfl[d].b(i)])
                        fw.dma("pool", tmv(ktok[d]), ks[d].t[:], reads=SBall(ks[d], 4), writes=[ktok[d].b(i)])
                        fw.dma("pool", fmv(kT[d]), kTs[d].t[:], reads=SBall(kTs[d], 4), writes=[kT[d].b(i)])

                def mark_store_reads(i):
                    pass

                run_chains([prologue(0)], 1)
                for i in range(NT):
                    ph1, ph2, ph3 = tile_chains(i)
                    run_chains(ph1, 3)
                    nxt = [prologue(i + 1)] if i + 1 < NT else []
                    run_chains(nxt + ph2, 4)
                    run_chains(ph3, 3)
                    stores(i)
                fw.barrier()

        def pass_H(l, d):
            fwd = d == "f"
            with ExitStack() as es:
                def sb(name, shape, dt):
                    return Tl(es.enter_context(nc.sbuf_tensor(f"{name}{d}_L{l}", list(shape), dt)))
                nb = 3
                lft = [sb(f"H_lfh{j}", [128, 4, 512], BF16) for j in range(nb)]
                llt = [sb(f"H_lfl{j}", [128, 4, 512], BF16) for j in range(nb)]
                kt_ = [sb(f"H_k{j}", [128, 4, 512], BF16) for j in range(nb)]
                kTt = [sb(f"H_kT{j}", [128, 4, 512], BF16) for j in range(nb)]
                qTt = [sb(f"H_qT{j}", [128, 4, 512], BF16) for j in range(nb)]
                vt = [sb(f"H_v{j}", [128, 4, 512], BF16) for j in range(nb)]
                ex3 = [sb(f"H_ex3{j}", [128, 512], F32) for j in range(2)]
                kd2 = [sb(f"H_kd{j}", [128, 4, 512], BF16) for j in range(2)]
                vm = [[sb(f"H_vm{j}_{c}", [128, 4, 512], BF16) for c in range(2)] for j in range(nb)]
                for j in range(nb):
                    for c in range(2):
                        fw.op("pool", lambda e: e.memset(vm[j][c].t[:], 0.0), [], [vm[j][c].b])
                qx = [sb(f"H_qx{j}", [128, 512], F32) for j in range(2)]
                kx = [sb(f"H_kx{j}", [128, 512], F32) for j in range(2)]
                qe2 = [sb(f"H_qe{j}", [128, 4, 512], BF16) for j in range(2)]
                ke2 = [sb(f"H_ke{j}", [128, 4, 512], BF16) for j in range(2)]
                ext2 = [sb(f"H_ext{j}", [128, 64], F32) for j in range(2)]
                PT = [sb(f"H_PT{j}", [128, 4, 128], BF16) for j in range(3)]
                S = [sb(f"H_S{h}", [128, 128], F32) for h in range(4)]
                Sm = [sb(f"H_Sm{j}", [128, 128], BF16) for j in range(16)]
                oTs = [sb(f"H_oT{j}", [128, 4, 512], F32) for j in range(2)]
                An, Bn, Kn, Mn = ("A_f", "B_f", "K_f", "M_f") if fwd else ("A_b", "B_b", "K_b", "M_b")
                K4 = sb("H_K4", [128, 4, 128], F32)
                for h in range(4):
                    cp("dve", K4.t[:, h, :], C32(Kn), [cst.b], [K4.b])
                    fw.op("dve", lambda e: e.memset(S[h].t[:], 0.0), [], [S[h].b])
                order = list(range(NT)) if fwd else list(range(NT - 1, -1, -1))
                pfree = list(ps_t)

                def palloc():
                    return pfree.pop(0)

                def prel(p):
                    pfree.append(p)

                def load(n):
                    i = order[n]
                    j = n % nb
                    t0 = i * 512
                    fw.dma("sp", lft[j].t[:], lfh[d].ap[t0:t0 + 512, :].rearrange("(s p) f -> p s f", p=128),
                           reads=[lfh[d].b(i)], writes=[lft[j].b])
                    fw.dma("sp", llt[j].t[:], lfl[d].ap[t0:t0 + 512, :].rearrange("(s p) f -> p s f", p=128),
                           reads=[lfl[d].b(i)], writes=[llt[j].b])
                    fw.dma("sp", kt_[j].t[:], ktok[d].ap[t0:t0 + 512, :].rearrange("(s p) f -> p s f", p=128),
                           reads=[ktok[d].b(i)], writes=[kt_[j].b])
                    vsrc = vtok.ap[t0:t0 + 512, :].rearrange("(s p) f -> p s f", p=128)
                    fw.dma("sp", vt[j].t[:], vsrc, reads=[vtok.b(i)], writes=[vt[j].b])
                    for c in range(2):
                        fw.dma("sp", vm[j][c].t[c * 64:(c + 1) * 64, :, :], vsrc[c * 64:(c + 1) * 64],
                               reads=[vtok.b(i)], writes=[vm[j][c].b])
                    fw.dma("sp", kTt[j].t[:], kT[d].ap[:, t0:t0 + 512].rearrange("(h p) t -> p h t", p=128),
                           reads=[kT[d].b(i)], writes=[kTt[j].b])
                    fw.dma("sp", qTt[j].t[:], qT.ap[:, t0:t0 + 512].rearrange("(h p) t -> p h t", p=128),
                           reads=[qT.b(i)], writes=[qTt[j].b])

                def ephase(n):
                    j = n % nb
                    L, L2, K_, KT_, QT_ = lft[j], llt[j], kt_[j], kTt[j], qTt[j]
                    kd, qe, ke, ext = kd2[n % 2], qe2[n % 2], ke2[n % 2], ext2[n % 2]
                    for s in range(4):
                        p = palloc()
                        mm(p.t[:, :], C16(Bn), L.t[:, s, :], [cstb.b, L.b], [p.b], start=True, stop=False)
                        mm(p.t[:, :], C16(Bn), L2.t[:, s, :], [cstb.b, L2.b], [p.b], start=False, stop=True)
                        e3 = ex3[s % 2]
                        act(e3.t[:], p.t[:, :], AF.Exp, [p.b], [e3.b])
                        prel(p)
                        tt("pool", kd.t[:, s, :], K_.t[:, s, :], e3.t[:], ALU.mult, [K_.b, e3.b], [kd.b])
                    pe_ = palloc()
                    for h in range(4):
                        for s in range(4):
                            c0 = (h * 4 + s) * 4
                            mm(pe_.t[:, c0:c0 + 4], L.t[:, s, h * 128:(h + 1) * 128], C16(Mn, 4), [L.b, cstb.b], [pe_.b],
                               start=True, stop=False)
                            mm(pe_.t[:, c0:c0 + 4], L2.t[:, s, h * 128:(h + 1) * 128], C16(Mn, 4), [L2.b, cstb.b], [pe_.b],
                               start=False, stop=True)
                    act(ext.t[:], pe_.t[:, 0:64], AF.Exp, [pe_.b], [ext.b])
                    prel(pe_)
                    for h in range(4):
                        p = palloc()
                        for s in range(4):
                            mm(p.t[:, s * 128:(s + 1) * 128], L.t[:, s, h * 128:(h + 1) * 128], C16(An), [L.b, cstb.b], [p.b],
                               start=True, stop=False)
                            mm(p.t[:, s * 128:(s + 1) * 128], L2.t[:, s, h * 128:(h + 1) * 128], C16(An), [L2.b, cstb.b], [p.b],
                               start=False, stop=True)
                        a, b_ = qx[h % 2], kx[h % 2]
                        act(a.t[:], p.t[:, :], AF.Exp, [p.b], [a.b])
                        act(b_.t[:], p.t[:, :], AF.Exp, [p.b], [b_.b], scale=-1.0)
                        prel(p)
                        tt("dve", qe.t[:, h, :], QT_.t[:, h, :], a.t[:], ALU.mult, [QT_.b, a.b], [qe.b])
                        tt("pool", ke.t[:, h, :], KT_.t[:, h, :], b_.t[:], ALU.mult, [KT_.b, b_.b], [ke.b])

                load(0)
                if NT > 1:
                    load(1)
                ephase(0)
                smi = [0]
                pti = [0]
                for n in range(NT):
                    i = order[n]
                    j = n % nb
                    t0 = i * 512
                    V_, VM = vt[j], vm[j]
                    kd, qe, ke, ext = kd2[n % 2], qe2[n % 2], ke2[n % 2], ext2[n % 2]
                    o_ = oTs[n % 2]
                    subs = list(range(4)) if fwd else list(range(3, -1, -1))

                    def stageA(s):
                        ssl = slice(s * 128, (s + 1) * 128)
                        p = palloc()
                        for h in range(4):
                            mm(p.t[:, h * 128:(h + 1) * 128], ke.t[:, h, ssl], qe.t[:, h, ssl], [ke.b, qe.b], [p.b])
                        pt = PT[pti[0] % 3]
                        pti[0] += 1
                        tt("dve", pt.t[:], p.t[:, :].rearrange("p (h t) -> p h t", h=4), K4.t[:], ALU.mult, [p.b, K4.b], [pt.b])
                        prel(p)
                        pd = [palloc(), palloc()]
                        for c in range(2):
                            for h in range(4):
                                hs = slice(h * 128, (h + 1) * 128)
                                mm(pd[c].t[:, hs], kd.t[:, s, hs], VM[c].t[:, s, hs], [kd.b, VM[c].b], [pd[c].b])
                        return pt, pd

                    def stageC(s, pt, pd):
                        ssl = slice(s * 128, (s + 1) * 128)
                        tg = t0 + s * 128
                        po = palloc()
                        for h in range(4):
                            hs = slice(h * 128, (h + 1) * 128)
                            mm(po.t[:, hs], V_.t[:, s, hs], pt.t[:, h, :], [V_.b, pt.b], [po.b], start=(h == 0), stop=False)
                        chs = (0, 1) if fwd else (1, 0)
                        for ci, c in enumerate(chs):
                            tok = tg + c * 64
                            if (fwd and tok == HALF) or ((not fwd) and tok + 64 == HALF):
                                for h in range(4):
                                    ts("dve", S[h].t[:], S[h].t[:], flag.t[:, 0:1], ALU.mult, [S[h].b, flag.b], [S[h].b])
                            sms = []
                            for h in range(4):
                                ec = (h * 4 + s) * 4 + 2 * c
                                sm_ = Sm[smi[0] % 16]
                                smi[0] += 1
                                act(sm_.t[:], S[h].t[:], AF.Copy, [S[h].b, ext.b], [sm_.b], scale=ext.t[:, ec:ec + 1])
                                sms.append(sm_)
                            for h in range(4):
                                ec = (h * 4 + s) * 4 + 2 * c
                                stt(S[h].t[:], S[h].t[:], ext.t[:, ec + 1:ec + 2], pd[c].t[:, h * 128:(h + 1) * 128],
                                    ALU.mult, ALU.add, [S[h].b, ext.b, pd[c].b], [S[h].b])
                            for h in range(4):
                                mm(po.t[:, h * 128 + c * 64:h * 128 + (c + 1) * 64], sms[h].t[:],
                                   qe.t[:, h, s * 128 + c * 64:s * 128 + (c + 1) * 64],
                                   [sms[h].b, qe.b], [po.b], start=False, stop=(ci == 1))
                        prel(pd[0])
                        prel(pd[1])
                        cp("act", o_.t[:, :, ssl], po.t[:, :].rearrange("p (h t) -> p h t", h=4), [po.b], [o_.b])
                        prel(po)

                    if n + 2 < NT:
                        load(n + 2)
                    prev = None
                    for bi, s in enumerate(subs):
                        cur = (s,) + stageA(s)
                        if bi == 1 and n + 1 < NT:
                            ephase(n + 1)
                        if prev is not None:
                            stageC(*prev)
                        prev = cur
                    stageC(*prev)
                    fw.dma("pool", oT[d].ap[:, t0:t0 + 512].rearrange("(h p) t -> p h t", p=128), o_.t[:], reads=[o_.b],
                           writes=[oT[d].b(i)])
                fw.barrier()

        def attn_core(sb, NTq, qsrc, kv_for_tile, masks, out_row0, tag):
            LA = 5
            qm = [[sb(f"{tag}_qm{j}_{par}", [128, 2, 512], BF16) for par in range(2)] for j in range(2)]
            for j in range(2):
                for par in range(2):
                    fw.op("pool", lambda e: e.memset(qm[j][par].t[:], 0.0), [], [qm[j][par].b])
            gt = [sb(f"{tag}_g{j}", [128, 2, 512], BF16) for j in range(2)]
            pT = [sb(f"{tag}_pT{j}", [128, 512], BF16) for j in range(8)]
            sc = [sb(f"{tag}_sc{j}", [128, 512], F32) for j in range(6)] if masks is not None else None
            otok = sb(f"{tag}_otok", [128, 4, 256], BF16)
            rden = sb(f"{tag}_rden", [128, 4], F32)
            mo = [sb(f"{tag}_mo{j}", [128, 2, 512], BF16) for j in range(2)]
            pi = [0]
            for i in range(NTq):
                t0 = i * 512
                QM, G = qm[i % 2], gt[i % 2]
                qv = qsrc.ap[:, t0:t0 + 512].rearrange("(h p) t -> p h t", p=128)
                for par in range(2):
                    fw.dma("sp", QM[par].t[par * 64:(par + 1) * 64, :, :], qv[par * 64:(par + 1) * 64], reads=[qsrc.b(i)],
                           writes=[QM[par].b])
                fw.dma("sp", G.t[:], gT.ap[512 + out_row0:512 + out_row0 + 256, t0:t0 + 512].rearrange("(h p) t -> p h t", p=128),
                       reads=[gT.b(i)], writes=[G.b])
                ktl = kv_for_tile(i)
                nk = len(ktl)
                po_h = {}

                started = {}

                def stage1(h, ki):
                    ch, pr = h // 2, (h % 2) * 64
                    kfn, vfn, mi, kbufs = ktl[ki]
                    s0, s1 = SUBR[mi] if masks is not None else (0, 4)
                    cs = slice(s0 * 128, s1 * 128)
                    p = psum()
                    mm(p.t[:, cs], kfn(ch, pr), QM[h % 2].t[:, ch, cs], kbufs + [QM[h % 2].b], [p.b])
                    e_ = pT[pi[0] % 8]
                    if masks is not None:
                        s_ = sc[pi[0] % 6]
                        tt("dve", s_.t[:, cs], p.t[:, cs], masks.t[:, mi, cs], ALU.add, [p.b, masks.b], [s_.b])
                        act(e_.t[:, cs], s_.t[:, cs], AF.Exp, [s_.b], [e_.b], scale=0.125)
                    else:
                        act(e_.t[:, cs], p.t[:, cs], AF.Exp, [p.b], [e_.b], scale=0.125)
                    pi[0] += 1
                    return e_

                def stage2(h, ki, e_):
                    kfn, vfn, mi, kbufs = ktl[ki]
                    if ki == 0:
                        po_h[h] = psum_acc()
                    po = po_h[h]
                    s0, s1 = SUBR[mi] if masks is not None else (0, 4)
                    for s in range(s0, s1):
                        mm(po.t[:, s * 128:s * 128 + 65], e_.t[:, s * 128:(s + 1) * 128], vfn(h), [e_.b] + kbufs, [po.b],
                           start=(h not in started), stop=(ki == nk - 1))
                        started[h] = True
                    if ki == nk - 1:
                        pov = po.t[:, :].rearrange("p (s e) -> p s e", e=128)
                        fw.op("dve", lambda e: e.reciprocal(out=rden.t[:, :], in_=pov[:, :, 64]), [po.b], [rden.b])
                        for s in range(4):
                            ts("dve", otok.t[:, s, h * 64:(h + 1) * 64], po.t[:, s * 128:s * 128 + 64], rden.t[:, s:s + 1],
                               ALU.mult, [po.b, rden.b], [otok.b])

                pend = []
                for pair in range(2):
                    for ki in range(nk):
                        for h in (2 * pair, 2 * pair + 1):
                            pend.append((h, ki, stage1(h, ki)))
                            if len(pend) > LA:
                                stage2(*pend.pop(0))
                while pend:
                    stage2(*pend.pop(0))
                M_ = mo[i % 2]
                for ch in range(2):
                    p = psum()
                    for s in range(4):
                        mm(p.t[:, s * 128:(s + 1) * 128], otok.t[:, s, ch * 128:(ch + 1) * 128], C16("ident"),
                           [otok.b, cstb.b], [p.b])
                    tt("dve", M_.t[:, ch, :], p.t[:, :], G.t[:, ch, :], ALU.mult, [p.b, G.b], [M_.b])
                fw.dma("pool", mixT.ap[out_row0:out_row0 + 256, t0:t0 + 512].rearrange("(h p) t -> p h t", p=128), M_.t[:],
                       reads=[M_.b], writes=[mixT.b((out_row0, i))])

        def pass_AT(l):
            with ExitStack() as es:
                def sb(name, shape, dt):
                    return Tl(es.enter_context(nc.sbuf_tensor(f"{name}_L{l}", list(shape), dt)))
                masks = sb("AT_masks", [128, 20, 512], F32)
                fw.dma("sp", masks.t[:], amask_in.rearrange("r p q -> p r q"), writes=[masks.b])
                NW = 20
                kw_ = [sb(f"AT_kw{j}", [128, 2, NW * 128], BF16) for j in range(2)]
                vw_ = [sb(f"AT_vw{j}", [128, NW, 260], BF16) for j in range(2)]

                def kv_for_tile(i):
                    KW, VW = kw_[i % 2], vw_[i % 2]
                    q0 = i * 512
                    half_lo = (q0 // HALF) * HALF
                    lo = max(q0 - 1024, 0)
                    hi = min(q0 + 512 + 1024, T)
                    n = (hi - lo) // 128
                    tiles = sorted(set(range(lo // 512, (hi + 511) // 512)))
                    fw.dma("sp", KW.t[:, :, 0:n * 128], akT.ap[:, lo:hi].rearrange("(h p) t -> p h t", p=128),
                           reads=[akT.b(x) for x in tiles], writes=[KW.b])
                    fw.dma("sp", VW.t[:, 0:n, :], va.ap[lo:hi, :].rearrange("(j p) f -> p j f", p=128),
                           reads=[va.b(x) for x in tiles], writes=[VW.b])
                    out = []
                    for jt in range(n):
                        k0 = lo + jt * 128
                        r = (k0 - q0) // 128
                        assert -8 <= r <= 11
                        if not (half_lo <= k0 < half_lo + HALF):
                            ts("dve", VW.t[:, jt, :], VW.t[:, jt, :], flag.t[:, 0:1], ALU.mult, [VW.b, flag.b], [VW.b])
                        out.append(((lambda ch, pr, jt=jt, KW=KW: KW.t[:, ch, jt * 128:(jt + 1) * 128]),
                                    (lambda h, jt=jt, VW=VW: VW.t[:, jt, h * 65:(h + 1) * 65]),
                                    r + 8, [KW.b, VW.b]))
                    return out
                attn_core(sb, NT, aqT, kv_for_tile, masks, 0, "AT")
                fw.barrier()

        def pass_ME(l):
            with ExitStack() as es:
                def sb(name, shape, dt):
                    return Tl(es.enter_context(nc.sbuf_tensor(f"{name}_L{l}", list(shape), dt)))
                wkv = sb("ME_w", [128, 8, 512], BF16)
                for kc in range(8):
                    fw.dma("pool", wkv.t[:, kc, :], mem_wkv[l, kc * 128:(kc + 1) * 128, :], writes=[wkv.b])
                mnw = sb("ME_mnw", [128, 8], F32)
                col_load(mnw, mem_norm_w[l], 8)
                gk = sb("ME_gk", [128, 1], F32)
                col_load64(gk, mk_w[l])
                mkT = [sb(f"ME_mkT{g}", [128, 2, 256], BF16) for g in range(2)]
                mv = [sb(f"ME_mv{g}", [128, 2, 260], BF16) for g in range(2)]
                xm = sb("ME_x", [128, 1024], F32)
                junk = sb("ME_junk", [128, 1024], BF16)
                ssm = sb("ME_ss", [128, 1], F32)
                hbm = sb("ME_hb", [128, 2, 1024], BF16)
                hTm = sb("ME_hT", [128, 8, 256], BF16)
                sqm = sb("ME_sq", [128, 256], BF16)
                rsm = sb("ME_rs", [128, 256], F32)
                for g in range(2):
                    fw.op("dve", lambda e: e.memset(mv[g].t[:], 1.0), [], [mv[g].b])
                    for s in range(2):
                        fw.dma("sp", xm.t[:], mem_in[g, s * 128:(s + 1) * 128, :], writes=[xm.b])
                        act(junk.t[:], xm.t[:], AF.Square, [xm.b], [junk.b, ssm.b], accum_out=ssm.t[:, 0:1])
                        rstd_inplace(ssm.t[:], 1024.0, ssm.b)
                        ts("dve", hbm.t[:, s, :], xm.t[:], ssm.t[:, 0:1], ALU.mult, [xm.b, ssm.b], [hbm.b])
                    for kc in range(8):
                        p = psum()
                        for s in range(2):
                            mm(p.t[:, s * 128:(s + 1) * 128], hbm.t[:, s, kc * 128:(kc + 1) * 128], C16("ident"),
                               [hbm.b, cstb.b], [p.b])
                        ts("dve", hTm.t[:, kc, :], p.t[:, 0:256], mnw.t[:, kc:kc + 1], ALU.mult, [p.b, mnw.b], [hTm.b])
                    for c in range(2):
                        p = psum()
                        for kc in range(8):
                            mm(p.t[:, 0:256], wkv.t[:, kc, c * 128:(c + 1) * 128], hTm.t[:, kc, :], [wkv.b, hTm.b], [p.b],
                               start=(kc == 0), stop=(kc == 7))
                        act(sqm.t[:], p.t[:, 0:256], AF.Square, [p.b], [sqm.b])
                        p2 = psum()
                        mm(p2.t[:, 0:256], C16("ones64"), sqm.t[:], [cstb.b, sqm.b], [p2.b])
                        act(rsm.t[:], p2.t[:, 0:256], AF.Ln, [p2.b], [rsm.b], scale=1.0 / 64, bias=EPS)
                        act(rsm.t[:], rsm.t[:], AF.Exp, [rsm.b], [rsm.b], scale=-0.5)
                        stt(mkT[g].t[:, c, :], p.t[:, 0:256], gk.t[:, 0:1], rsm.t[:], ALU.mult, ALU.mult,
                            [p.b, gk.b, rsm.b], [mkT[g].b])
                    for s in range(2):
                        p = psum()
                        for kc in range(8):
                            mm(p.t[:, 0:256], hTm.t[:, kc, s * 128:(s + 1) * 128], wkv.t[:, kc, 256:512], [hTm.b, wkv.b], [p.b],
                               start=(kc == 0), stop=(kc == 7))
                        cp("act", mv[g].t[:, s, :].rearrange("p (h e) -> p h e", e=65)[:, :, 0:64],
                           p.t[:, 0:256].rearrange("p (h e) -> p h e", e=64), [p.b], [mv[g].b])

                def kv_for_tile(i):
                    g = (i * 512) // HALF
                    out = []
                    for jt in range(2):
                        out.append(((lambda ch, pr, jt=jt, g=g: mkT[g].t[:, ch, jt * 128:(jt + 1) * 128]),
                                    (lambda h, jt=jt, g=g: mv[g].t[:, jt, h * 65:(h + 1) * 65]),
                                    0, [mkT[g].b, mv[g].b]))
                    return out
                attn_core(sb, NT, mqT, kv_for_tile, None, 256, "ME")
                fw.barrier()

        def pass_C(l):
            src = x_dt if l == 0 else y_out
            with ExitStack() as es:
                def sb(name, shape, dt):
                    return Tl(es.enter_context(nc.sbuf_tensor(f"{name}_L{l}", list(shape), dt)))
                wo = sb("C_w", [128, 8, 1024], BF16)
                for kc in range(8):
                    fw.dma("pool", wo.t[:, kc, :], w_out[l, kc * 128:(kc + 1) * 128, :], writes=[wo.b])
                gon = sb("C_gon", [128, 1], F32)
                fw.dma("sp", gon.t[:, 0:1], onorm_w[l].rearrange("(p o) -> p o", o=1), writes=[gon.b],
                       allow_slow_non_contiguous=True)
                of_ = [sb(f"C_of{j}", [128, 4, 512], F32) for j in range(2)]
                ob_ = [sb(f"C_ob{j}", [128, 4, 512], F32) for j in range(2)]
                gh = [sb(f"C_g{j}", [128, 4, 512], BF16) for j in range(2)]
                mx_ = [sb(f"C_mx{j}", [128, 8, 512], BF16) for j in range(2)]
                xt = [sb(f"C_x{j}", [128, 4, 1024], F32) for j in range(2)]
                osum = [sb(f"C_osum{h}", [128, 512], F32) for h in range(4)]
                sq = [sb(f"C_sq{h}", [128, 512], BF16) for h in range(4)]
                rsf = [sb(f"C_rsf{h}", [128, 512], F32) for h in range(4)]
                m1 = [sb(f"C_m1{h}", [128, 512], F32) for h in range(4)]
                mxb = [[Buf() for _ in range(8)] for _ in range(2)]
                yt = [sb(f"C_y{j}", [128, 1024], F32) for j in range(2)]

                def load(i):
                    j = i % 2
                    t0 = i * 512
                    fw.dma("sp", of_[j].t[:], oT["f"].ap[:, t0:t0 + 512].rearrange("(h p) t -> p h t", p=128),
                           reads=[oT["f"].b(i)], writes=[of_[j].b])
                    fw.dma("sp", ob_[j].t[:], oT["b"].ap[:, t0:t0 + 512].rearrange("(h p) t -> p h t", p=128),
                           reads=[oT["b"].b(i)], writes=[ob_[j].b])
                    fw.dma("sp", gh[j].t[:], gT.ap[0:512, t0:t0 + 512].rearrange("(h p) t -> p h t", p=128),
                           reads=[gT.b(i)], writes=[gh[j].b])
                    fw.dma("sp", mx_[j].t[:, 4:8, :], mixT.ap[:, t0:t0 + 512].rearrange("(h p) t -> p h t", p=128),
                           reads=[mixT.b((0, i)), mixT.b((256, i))], writes=mxb[j][4:8])
                    fw.dma("sp", xt[j].t[:], src.ap[t0:t0 + 512, :].rearrange("(s p) f -> p s f", p=128),
                           reads=[src.b(i)], writes=[xt[j].b])
                yi = [0]
                H4 = range(4)

                def normphase(i):
                    j = i % 2
                    for h in H4:
                        tt("pool", osum[h].t[:], of_[j].t[:, h, :], ob_[j].t[:, h, :], ALU.add, [of_[j].b, ob_[j].b], [osum[h].b])
                    for h in H4:
                        act(sq[h].t[:], osum[h].t[:], AF.Square, [osum[h].b], [sq[h].b])
                    pp = []
                    for h in H4:
                        p = psum()
                        mm(p.t[:, :], C16("ones128"), sq[h].t[:], [cstb.b, sq[h].b], [p.b])
                        pp.append(p)
                    for h in H4:
                        act(rsf[h].t[:], pp[h].t[:, :], AF.Ln, [pp[h].b], [rsf[h].b], scale=1.0 / 128, bias=EPS)
                    for h in H4:
                        act(rsf[h].t[:], rsf[h].t[:], AF.Exp, [rsf[h].b], [rsf[h].b], scale=-0.5)
                    for h in H4:
                        stt(m1[h].t[:], osum[h].t[:], gon.t[:, 0:1], rsf[h].t[:], ALU.mult, ALU.mult,
                            [osum[h].b, gon.b, rsf[h].b], [m1[h].b])
                    for h in H4:
                        tt("dve", mx_[j].t[:, h, :], m1[h].t[:], gh[j].t[:, h, :], ALU.mult, [m1[h].b, gh[j].b], [mxb[j][h]])

                def outproj(i):
                    j = i % 2
                    t0 = i * 512
                    for s in range(4):
                        Y = yt[yi[0] % 2]
                        yi[0] += 1
                        for nh in range(2):
                            p = psum()
                            for mc in range(8):
                                mm(p.t[:, :], mx_[j].t[:, mc, s * 128:(s + 1) * 128], wo.t[:, mc, nh * 512:(nh + 1) * 512],
                                   [mxb[j][mc], wo.b], [p.b], start=(mc == 0), stop=(mc == 7))
                            tt("dve", Y.t[:, nh * 512:(nh + 1) * 512], p.t[:, :], xt[j].t[:, s, nh * 512:(nh + 1) * 512], ALU.add,
                               [p.b, xt[j].b], [Y.b])
                        fw.dma("pool", y_out.ap[t0 + s * 128:t0 + (s + 1) * 128, :], Y.t[:], reads=[Y.b], writes=[y_out.b(i)])

                load(0)
                if NT > 1:
                    load(1)
                normphase(0)
                for i in range(NT):
                    if i + 1 < NT:
                        normphase(i + 1)
                    outproj(i)
                    if i + 2 < NT:
                        load(i + 2)
                fw.barrier()

        _P = _os.environ.get("KPASSES", "A,HF,HB,AT,ME,C").split(",")
        for l in range(depth):
            if "A" in _P:
                pass_A(l)
            if "HF" in _P:
                pass_H(l, "f")
            if "HB" in _P:
                pass_H(l, "b")
            if "AT" in _P:
                pass_AT(l)
            if "ME" in _P:
                pass_ME(l)
            if "C" in _P:
                pass_C(l)
        fw.barrier()
    return nc, fw


T_CORE = 16384
DEPTH = 4
_CACHE = {}


def kernel(x_prompt, x_sample, mem_prompt, mem_sample, norm_w, w_in, hgrn_lb_fwd, hgrn_lb_bwd, hgrn_onorm_w,
           attn_qnorm_w, attn_knorm_w, mem_norm_w, mem_wkv, mem_qnorm_w, mem_knorm_w, w_out):
    f = lambda a: np.ascontiguousarray(np.asarray(a, dtype=np.float32))
    x_prompt, x_sample, mem_prompt, mem_sample = f(x_prompt), f(x_sample), f(mem_prompt), f(mem_sample)
    T = T_CORE
    if "nc" not in _CACHE:
        _CACHE["nc"] = build(T, DEPTH)[0]
        _CACHE["hc"] = host_consts(T)
    nc = _CACHE["nc"]
    hc = _CACHE["hc"]
    pos_p = np.concatenate([np.arange(8192), np.arange(8192)])
    pos_s = np.arange(16384)
    Cp, Sp = rope_tables(pos_p)
    Cs, Ss = rope_tables(pos_s)
    shared = {"cst": hc["cst"], "amask": hc["amask"], "norm_w": f(norm_w), "w_in": f(w_in),
              "hgrn_lb_fwd": f(hgrn_lb_fwd), "hgrn_lb_bwd": f(hgrn_lb_bwd), "hgrn_onorm_w": f(hgrn_onorm_w),
              "attn_qnorm_w": f(attn_qnorm_w), "attn_knorm_w": f(attn_knorm_w), "mem_norm_w": f(mem_norm_w),
              "mem_wkv": f(mem_wkv), "mem_qnorm_w": f(mem_qnorm_w), "mem_knorm_w": f(mem_knorm_w), "w_out": f(w_out)}
    in_maps = []
    for c in range(8):
        m = dict(shared)
        if c < 4:
            m["x"] = x_prompt[2 * c:2 * c + 2].reshape(T, 1024)
            m["mem"] = mem_prompt[2 * c:2 * c + 2]
            m["flag"] = np.zeros((128, 1), np.float32)
            m["ropeC"], m["ropeS"] = Cp, Sp
        else:
            s = (c - 4) % 2
            m["x"] = x_sample[s]
            m["mem"] = np.stack([mem_sample[s], mem_sample[s]])
            m["flag"] = np.ones((128, 1), np.float32)
            m["ropeC"], m["ropeS"] = Cs, Ss
        in_maps.append(m)
    res = run_bass_kernel_spmd(nc, in_maps, core_ids=list(range(8)))
    ys = [np.asarray(r["y"], dtype=np.float32) for r in res.results]
    y_prompt = np.stack([ys[c].reshape(2, 8192, 1024) for c in range(4)]).reshape(8, 8192, 1024)
    y_sample = np.stack([ys[4], ys[5]])
    return (y_prompt, y_sample)
```

```python
import math
import os as _os
from contextlib import ExitStack
import numpy as np
import concourse.bass as bass
import concourse.mybir as mybir
from concourse.bass_utils import run_bass_kernel_spmd

F32 = mybir.dt.float32
BF16 = mybir.dt.bfloat16
AF = mybir.ActivationFunctionType
ALU = mybir.AluOpType
EPS = 1e-6
NSLOT = 8


class Buf:
    __slots__ = ("w", "r")

    def __init__(self):
        self.w = None
        self.r = {}


class Tl:
    def __init__(self, t):
        self.t = t
        self.b = Buf()


class FW:
    LIM = 30000

    def __init__(self, nc):
        self.nc = nc
        self.engs = {"pe": nc.tensor, "act": nc.scalar, "dve": nc.vector, "pool": nc.gpsimd, "sp": nc.sync}
        self.cur = {}
        self.seen = {k: {} for k in self.engs}
        self.last = {}
        self.nsem = 0
        self.dslots = {"sp": [], "pool": []}
        self.dnext = {"sp": 0, "pool": 0}
        self.nins = 0

    def newsem(self):
        self.nsem += 1
        return [self.nsem, self.nc.alloc_semaphore(name=f"fs{self.nsem}")]

    def _wait(self, e, ev):
        if self.seen[e].get(ev[0], 0) >= ev[2]:
            return
        self.engs[e].wait_ge(ev[1], ev[2])
        self.seen[e][ev[0]] = ev[2]

    def _deps(self, e, reads, writes):
        for b in reads:
            if b.w is not None and not (e == "pe" and b.w[3] == "pe"):
                self._wait(e, b.w)
        for b in writes:
            if b.w is not None and not (e == "pe" and b.w[3] == "pe"):
                self._wait(e, b.w)
            for ev in b.r.values():
                if not (e == "pe" and ev[3] == "pe"):
                    self._wait(e, ev)

    def _mark(self, ev, key, reads, writes):
        for b in reads:
            b.r[key] = ev
        for b in writes:
            b.w = ev
            b.r = {}

    def op(self, e, fn, reads=(), writes=()):
        self._deps(e, reads, writes)
        ins = fn(self.engs[e])
        c = self.cur.get(e)
        if c is None or c[2] >= self.LIM:
            c = self.newsem() + [0]
            self.cur[e] = c
        c[2] += 1
        ins.then_inc(c[1], 1)
        ev = (c[0], c[1], c[2], e)
        self._mark(ev, e, reads, writes)
        self.last[e] = ev
        self.nins += 1

    def dma(self, q, out, in_, reads=(), writes=(), **kw):
        self._deps(q, reads, writes)
        slots = self.dslots[q]
        if len(slots) < NSLOT:
            slots.append(self.newsem() + [0])
            s = slots[-1]
        else:
            s = slots[self.dnext[q] % NSLOT]
        self.dnext[q] += 1
        if s[2] > 0:
            self._wait(q, (s[0], s[1], s[2], "dma"))
        if s[2] + 16 > self.LIM:
            s[:] = self.newsem() + [0]
        ins = self.engs[q].dma_start(out=out, in_=in_, **kw)
        s[2] += 16
        ins.then_inc(s[1], 16)
        ev = (s[0], s[1], s[2], "dma")
        self._mark(ev, ("d", s[0], s[2]), reads, writes)
        self.nins += 1

    def barrier(self):
        evs = [self.last[e] for e in ("pe", "act", "dve", "pool") if e in self.last]
        for q in self.dslots:
            for s in self.dslots[q]:
                if s[2] > 0:
                    evs.append((s[0], s[1], s[2], "dma"))
        for e in self.engs:
            for ev in evs:
                if e == "pe" and ev[3] == "pe":
                    continue
                self._wait(e, ev)


class DT:
    def __init__(self, ap):
        self.ap = ap
        self.bufs = {}

    def b(self, i):
        if i not in self.bufs:
            self.bufs[i] = Buf()
        return self.bufs[i]


def host_consts(T):
    c = {}
    s = np.arange(128)[:, None]
    t = np.arange(128)[None, :]
    same = (s // 64) == (t // 64)
    c0 = (t // 64) * 64
    A_f = (same & (s <= t)).astype(np.float32) - (same & (s <= c0 + 31)).astype(np.float32)
    A_b = (same & (s >= t)).astype(np.float32) - (same & (s >= c0 + 32)).astype(np.float32)
    B_f = (same & (s > t)).astype(np.float32)
    B_b = (same & (s < t)).astype(np.float32)
    M_f = np.zeros((128, 4), np.float32)
    M_b = np.zeros((128, 4), np.float32)
    sv = np.arange(128)
    for ch in range(2):
        inc = (sv // 64) == ch
        M_f[:, 2 * ch] = inc & (sv <= ch * 64 + 31)
        M_f[:, 2 * ch + 1] = inc
        M_b[:, 2 * ch] = inc & (sv >= ch * 64 + 32)
        M_b[:, 2 * ch + 1] = inc
    K_f = (same & (s <= t)).astype(np.float32)
    K_b = (same & (s >= t)).astype(np.float32)
    ident = np.eye(128, dtype=np.float32)
    ones64 = ((s // 64) == (t // 64)).astype(np.float32)
    ones128 = np.ones((128, 128), np.float32)
    Rm = np.zeros((128, 128), np.float32)
    for m in range(128):
        j = m % 64
        if j < 8:
            Rm[m + 8, m] = -1.0
        elif j < 16:
            Rm[m - 8, m] = 1.0
    cst = np.concatenate([A_f, A_b, B_f, B_b, K_f, K_b, ident, ones64, ones128, Rm, M_f, M_b], axis=1)
    c["cst"] = np.ascontiguousarray(cst.astype(np.float32))
    am = np.zeros((20, 128, 512), np.float32)
    j = np.arange(128)[:, None]
    i = np.arange(512)[None, :]
    for ri, r in enumerate(range(-8, 12)):
        d = r * 128 + j - i
        am[ri] = ((np.abs(d) <= 64).astype(np.float32)
                  + ((d % 4 == 0) & (np.abs(d) <= 256)).astype(np.float32)
                  + ((d % 16 == 0) & (np.abs(d) <= 1024)).astype(np.float32))
    c["amask"] = np.where(am > 0, 8.0 * np.log(np.maximum(am, 1.0)), -80000.0).astype(np.float32)
    return c


def rope_tables(pos):
    half = 8
    inv = (500000.0 ** (-np.arange(half, dtype=np.float32) * 2.0 / 16.0)).astype(np.float32)
    ang = pos.astype(np.float32)[None, :] * inv[:, None]
    C = np.ones((128, pos.shape[0]), np.float32)
    S = np.zeros((128, pos.shape[0]), np.float32)
    for hb in (0, 64):
        C[hb:hb + 8] = np.cos(ang)
        C[hb + 8:hb + 16] = np.cos(ang)
        S[hb:hb + 8] = np.sin(ang)
        S[hb + 8:hb + 16] = np.sin(ang)
    return C, S


def _sub_ranges():
    out = []
    j = np.arange(128)[:, None]
    i = np.arange(512)[None, :]
    for r in range(-8, 12):
        d = r * 128 + j - i
        ok = (np.abs(d) <= 64) | ((d % 4 == 0) & (np.abs(d) <= 256)) | ((d % 16 == 0) & (np.abs(d) <= 1024))
        subs = [s for s in range(4) if ok[:, s * 128:(s + 1) * 128].any()]
        out.append((min(subs), max(subs) + 1))
    return out


SUBR = _sub_ranges()
CO = {"A_f": 0, "A_b": 128, "B_f": 256, "B_b": 384, "K_f": 512, "K_b": 640, "ident": 768, "ones64": 896,
      "ones128": 1024, "Rm": 1152, "M_f": 1280, "M_b": 1284}
NCST = 1288


def build(T, depth, debug=False):
    NT = T // 512
    HALF = T // 2
    nc = bass.Bass("TRN2", target_bir_lowering=False)
    fw = FW(nc)

    def din(name, shape, dt=F32):
        return nc.dram_tensor(name, list(shape), dt, kind="ExternalInput").ap()

    x_in = din("x", [T, 1024])
    mem_in = din("mem", [2, 256, 1024])
    flag_in = din("flag", [128, 1])
    ropeC_in = din("ropeC", [128, T])
    ropeS_in = din("ropeS", [128, T])
    cst_in = din("cst", [128, NCST])
    amask_in = din("amask", [20, 128, 512])
    norm_w = din("norm_w", [depth, 1024])
    w_in = din("w_in", [depth, 1024, 4096])
    lbp = {"f": din("hgrn_lb_fwd", [depth, 512]), "b": din("hgrn_lb_bwd", [depth, 512])}
    onorm_w = din("hgrn_onorm_w", [depth, 128])
    aq_w = din("attn_qnorm_w", [depth, 64])
    ak_w = din("attn_knorm_w", [depth, 64])
    mem_norm_w = din("mem_norm_w", [depth, 1024])
    mem_wkv = din("mem_wkv", [depth, 1024, 512])
    mq_w = din("mem_qnorm_w", [depth, 64])
    mk_w = din("mem_knorm_w", [depth, 64])
    w_out = din("w_out", [depth, 1024, 1024])
    y_out = DT(nc.dram_tensor("y", [T, 1024], F32, kind="ExternalOutput").ap())
    x_dt = DT(x_in)

    skind = "ExternalOutput" if debug else "Internal"

    def scr(name, shape, dt):
        return DT(nc.dram_tensor(name, list(shape), dt, kind=skind).ap())

    qT = scr("s_qT", [512, T], BF16)
    kT = {"f": scr("s_kTf", [512, T], BF16), "b": scr("s_kTb", [512, T], BF16)}
    ktok = {"f": scr("s_kf", [T, 512], BF16), "b": scr("s_kb", [T, 512], BF16)}
    lfh = {"f": scr("s_lfhf", [T, 512], BF16), "b": scr("s_lfhb", [T, 512], BF16)}
    lfl = {"f": scr("s_lflf", [T, 512], BF16), "b": scr("s_lflb", [T, 512], BF16)}
    vtok = scr("s_v", [T, 512], BF16)
    gT = scr("s_gT", [1024, T], BF16)
    aqT = scr("s_aqT", [256, T], BF16)
    akT = scr("s_akT", [256, T], BF16)
    va = scr("s_va", [T, 260], BF16)
    mqT = scr("s_mqT", [256, T], BF16)
    oT = {"f": scr("s_ofT", [512, T], F32), "b": scr("s_obT", [512, T], F32)}
    mixT = scr("s_mixT", [512, T], BF16)

    es0 = ExitStack()
    with es0:
        def sb0(name, shape, dt):
            return Tl(es0.enter_context(nc.sbuf_tensor(name, list(shape), dt)))

        ps_t = [Tl(es0.enter_context(nc.psum_tensor(f"ps{i}", [128, 512], F32))) for i in range(8)]
        ps_i = [0]

        def psum():
            p = ps_t[ps_i[0] % 6]
            ps_i[0] += 1
            return p
        pa_i = [0]

        def psum_acc():
            p = ps_t[6 + pa_i[0] % 2]
            pa_i[0] += 1
            return p

        def mm(out, lhsT, rhs, reads, writes, start=True, stop=True):
            fw.op("pe", lambda e: e.matmul(out, lhsT=lhsT, rhs=rhs, start=start, stop=stop), reads, writes)

        def act(out, in_, func, reads, writes, **kw):
            fw.op("act", lambda e: e.activation(out=out, in_=in_, func=func, **kw), reads, writes)

        def tt(eng, out, in0, in1, op, reads, writes):
            fw.op(eng, lambda e: e.tensor_tensor(out=out, in0=in0, in1=in1, op=op), reads, writes)

        def ts(eng, out, in0, s1, op0, reads, writes, s2=None, op1=None):
            if op1 is None:
                fw.op(eng, lambda e: e.tensor_scalar(out=out, in0=in0, scalar1=s1, scalar2=None, op0=op0), reads, writes)
            else:
                fw.op(eng, lambda e: e.tensor_scalar(out=out, in0=in0, scalar1=s1, scalar2=s2, op0=op0, op1=op1),
                      reads, writes)

        def stt(out, in0, scalar, in1, op0, op1, reads, writes):
            fw.op("dve", lambda e: e.scalar_tensor_tensor(out=out, in0=in0, scalar=scalar, in1=in1, op0=op0, op1=op1),
                  reads, writes)

        def cp(eng, out, in_, reads, writes):
            if eng == "act":
                fw.op("act", lambda e: e.copy(out=out, in_=in_), reads, writes)
            else:
                fw.op(eng, lambda e: e.tensor_copy(out=out, in_=in_), reads, writes)

        cst = sb0("cst_sb", [128, NCST], F32)
        fw.dma("sp", cst.t[:], cst_in[:, :], writes=[cst.b])
        cstb = sb0("cstb_sb", [128, NCST], BF16)
        cp("dve", cstb.t[:], cst.t[:], [cst.b], [cstb.b])
        flag = sb0("flag_sb", [128, 1], F32)
        fw.dma("sp", flag.t[:], flag_in[:, :], writes=[flag.b])

        def C32(name, w=128):
            return cst.t[:, CO[name]:CO[name] + w]

        def C16(name, w=128):
            return cstb.t[:, CO[name]:CO[name] + w]

        def col_load(dst, src1d, n):
            fw.dma("sp", dst.t[:, 0:n], src1d.rearrange("(c p) -> p c", p=128), writes=[dst.b],
                   allow_slow_non_contiguous=True)

        def col_load64(dst, src1d):
            for hb in (0, 64):
                fw.dma("sp", dst.t[hb:hb + 64, 0:1], src1d.rearrange("(p o) -> p o", o=1), writes=[dst.b],
                       allow_slow_non_contiguous=True)

        def rstd_inplace(tl_ap, n, reads_b):
            act(tl_ap, tl_ap, AF.Ln, [reads_b], [reads_b], scale=1.0 / n, bias=EPS)
            act(tl_ap, tl_ap, AF.Exp, [reads_b], [reads_b], scale=-0.5)

        def pass_A(l):
            src = x_dt if l == 0 else y_out
            with ExitStack() as es:
                def sb(name, shape, dt):
                    return Tl(es.enter_context(nc.sbuf_tensor(f"{name}_L{l}", list(shape), dt)))
                w = sb("A_w", [128, 8, 4096], BF16)
                for kc in range(8):
                    for q4 in range(4):
                        fw.dma("pool", w.t[:, kc, q4 * 1024:(q4 + 1) * 1024],
                               w_in[l, kc * 128:(kc + 1) * 128, q4 * 1024:(q4 + 1) * 1024], writes=[w.b])
                normw = sb("A_normw", [128, 8], F32)
                col_load(normw, norm_w[l], 8)
                gq = sb("A_gq", [128, 1], F32)
                gk = sb("A_gk", [128, 1], F32)
                gm = sb("A_gm", [128, 1], F32)
                col_load64(gq, aq_w[l])
                col_load64(gk, ak_w[l])
                col_load64(gm, mq_w[l])
                oml = {d: sb(f"A_oml{d}", [128, 512], F32) for d in "fb"}
                with ExitStack() as es2:
                    def sb2(name, shape, dt):
                        return Tl(es2.enter_context(nc.sbuf_tensor(f"{name}_L{l}", list(shape), dt)))
                    row = sb2("A_lbrow", [1, depth, 512], F32)
                    mx = sb2("A_lbmx", [1, 512], F32)
                    sm = sb2("A_lbsm", [1, 512], F32)
                    acc = sb2("A_lbacc", [1, 512], F32)
                    tmp = sb2("A_lbtmp", [1, 512], F32)
                    for d in ("f", "b"):
                        fw.dma("sp", row.t[0:1, :, :], lbp[d].rearrange("(o l) f -> o l f", o=1), writes=[row.b])
                        cp("dve", mx.t[:], row.t[0:1, 0, :], [row.b], [mx.b])
                        for j in range(1, depth):
                            tt("dve", mx.t[:], mx.t[:], row.t[0:1, j, :], ALU.max, [mx.b, row.b], [mx.b])
                        for j in range(depth):
                            tt("dve", row.t[0:1, j, :], row.t[0:1, j, :], mx.t[:], ALU.subtract, [row.b, mx.b], [row.b])
                        act(row.t[:], row.t[:], AF.Exp, [row.b], [row.b])
                        cp("dve", sm.t[:], row.t[0:1, 0, :], [row.b], [sm.b])
                        for j in range(1, depth):
                            tt("dve", sm.t[:], sm.t[:], row.t[0:1, j, :], ALU.add, [sm.b, row.b], [sm.b])
                        fw.op("dve", lambda e: e.reciprocal(out=sm.t[:], in_=sm.t[:]), [sm.b], [sm.b])
                        fw.op("dve", lambda e: e.memset(acc.t[:], 0.0), [], [acc.b])
                        for j in range(1, l + 1):
                            tt("dve", tmp.t[:], row.t[0:1, j, :], sm.t[:], ALU.mult, [row.b, sm.b], [tmp.b])
                            tt("dve", acc.t[:], acc.t[:], tmp.t[:], ALU.add, [acc.b, tmp.b], [acc.b])
                        ts("dve", acc.t[:], acc.t[:], -1.0, ALU.mult, [acc.b], [acc.b], s2=1.0, op1=ALU.add)
                        p = psum()
                        mm(p.t[:, :], C32("ones128")[0:1, :], acc.t[0:1, :], [cst.b, acc.b], [p.b])
                        cp("dve", oml[d].t[:], p.t[:, :], [p.b], [oml[d].b])
                    fw.barrier()

                xt = [sb(f"A_x{i}", [128, 1024], F32) for i in range(4)]
                ss = sb("A_ss", [128, 4], F32)
                hb = sb("A_hb", [128, 4, 1024], BF16)
                hTr = [sb(f"A_hT{j}", [128, 8, 512], BF16) for j in range(2)]
                qTs = sb("A_qTs", [128, 4, 512], BF16)
                gTs = sb("A_gTs", [128, 8, 512], BF16)
                aqTs = sb("A_aqTs", [128, 2, 512], BF16)
                akTs = sb("A_akTs", [128, 2, 512], BF16)
                mqTs = sb("A_mqTs", [128, 2, 512], BF16)
                lfhs = {d: sb(f"A_lfhs{d}", [128, 4, 512], BF16) for d in "fb"}
                lfls = {d: sb(f"A_lfls{d}", [128, 4, 512], BF16) for d in "fb"}
                ks = {d: sb(f"A_ks{d}", [128, 4, 512], BF16) for d in "fb"}
                kTs = {d: sb(f"A_kTs{d}", [128, 4, 512], BF16) for d in "fb"}
                vs = sb("A_vs", [128, 4, 512], BF16)
                vas = sb("A_vas", [128, 4, 260], BF16)
                fw.op("dve", lambda e: e.memset(vas.t[:], 1.0), [], [vas.b])
                NR = 2
                sq = [sb(f"A_sq{j}", [128, 512], BF16) for j in range(NR)]
                rsf = [sb(f"A_rsf{j}", [128, 512], F32) for j in range(NR)]
                qn = [sb(f"A_qn{j}", [128, 512], F32) for j in range(NR)]
                qnb = [sb(f"A_qnb{j}", [128, 512], BF16) for j in range(NR)]
                t1 = rsf
                t2 = [sb(f"A_t2{j}", [128, 512], F32) for j in range(NR)]
                rC = [sb(f"A_rC{j}", [128, 512], F32) for j in range(2)]
                rS = [sb(f"A_rS{j}", [128, 512], F32) for j in range(2)]
                NG = 2
                sg = [sb(f"A_sg{j}", [128, 512], F32) for j in range(NG)]
                k32 = [sb(f"A_k32{j}", [128, 512], F32) for j in range(NG)]
                sub_b = {}

                def SB(tl, idx):
                    key = (id(tl), idx)
                    if key not in sub_b:
                        sub_b[key] = Buf()
                    return sub_b[key]

                def SBall(tl, n):
                    return [SB(tl, j) for j in range(n)]
                pfree = list(ps_t)

                def palloc():
                    return pfree.pop(0)

                def prel(p):
                    pfree.append(p)

                def run_chains(gens, K):
                    active = []
                    it = iter(gens)
                    done = False
                    while True:
                        while len(active) < K and not done:
                            g = next(it, None)
                            if g is None:
                                done = True
                            else:
                                active.append(g)
                        if not active:
                            break
                        for g in list(active):
                            try:
                                next(g)
                            except StopIteration:
                                active.remove(g)

                def prologue(i):
                    t0 = i * 512
                    hT = hTr[i % 2]
                    fw.dma("sp", rC[i % 2].t[:], ropeC_in[:, t0:t0 + 512], writes=[rC[i % 2].b])
                    fw.dma("sp", rS[i % 2].t[:], ropeS_in[:, t0:t0 + 512], writes=[rS[i % 2].b])
                    for s in range(4):
                        fw.dma("sp", xt[s].t[:], src.ap[t0 + s * 128:t0 + (s + 1) * 128, :], reads=[src.b(i)], writes=[xt[s].b])
                        act(hb.t[:, s, :], xt[s].t[:], AF.Square, [xt[s].b], [SB(hb, s), ss.b], accum_out=ss.t[:, s:s + 1])
                        yield
                    rstd_inplace(ss.t[:], 1024.0, ss.b)
                    yield
                    for s in range(4):
                        ts("dve", hb.t[:, s, :], xt[s].t[:], ss.t[:, s:s + 1], ALU.mult, [xt[s].b, ss.b], [SB(hb, s)])
                        yield
                    for kc in range(8):
                        p = palloc()
                        for s in range(4):
                            mm(p.t[:, s * 128:(s + 1) * 128], hb.t[:, s, kc * 128:(kc + 1) * 128], C16("ident"),
                               [SB(hb, s), cstb.b], [p.b])
                        yield
                        ts("dve", hT.t[:, kc, :], p.t[:, :], normw.t[:, kc:kc + 1], ALU.mult, [p.b, normw.b], [SB(hT, kc)])
                        prel(p)
                        yield

                rfree = list(range(NR))
                gfree = list(range(NG))

                def tile_chains(i):
                    hT = hTr[i % 2]
                    hTb = SBall(hT, 8)
                    RC, RS = rC[i % 2], rS[i % 2]

                    def fm(p, col0):
                        for kc in range(8):
                            mm(p.t[:, :], w.t[:, kc, col0:col0 + 128], hT.t[:, kc, :], [w.b, hTb[kc]], [p.b],
                               start=(kc == 0), stop=(kc == 7))

                    def tm(p, s, col0, n):
                        for kc in range(8):
                            mm(p.t[:, 0:n], hT.t[:, kc, s * 128:(s + 1) * 128], w.t[:, kc, col0:col0 + n],
                               [hTb[kc], w.b], [p.b], start=(kc == 0), stop=(kc == 7))

                    def silu_chain(col0, dst, c):
                        p = palloc()
                        fm(p, col0)
                        yield
                        act(dst.t[:, c, :], p.t[:, :], AF.Silu, [p.b], [SB(dst, c)])
                        prel(p)

                    def v_chain(s):
                        p = palloc()
                        tm(p, s, 1536, 512)
                        yield
                        cp("act", vs.t[:, s, :], p.t[:, :], [p.b], [SB(vs, s)])
                        prel(p)

                    def av_chain(s):
                        p = palloc()
                        tm(p, s, 2560, 256)
                        yield
                        cp("act", vas.t[:, s, :].rearrange("p (h e) -> p h e", e=65)[:, :, 0:64],
                           p.t[:, 0:256].rearrange("p (h e) -> p h e", e=64), [p.b], [vas.b, SB(vas, s)])
                        prel(p)

                    def norm_chain(col0, gain, dst, c, rope):
                        while not rfree:
                            yield
                        r = rfree.pop(0)
                        p = palloc()
                        fm(p, col0 + c * 128)
                        yield
                        act(sq[r].t[:], p.t[:, :], AF.Square, [p.b], [sq[r].b])
                        yield
                        p2 = palloc()
                        mm(p2.t[:, :], C16("ones64"), sq[r].t[:], [cstb.b, sq[r].b], [p2.b])
                        yield
                        act(rsf[r].t[:], p2.t[:, :], AF.Ln, [p2.b], [rsf[r].b], scale=1.0 / 64, bias=EPS)
                        prel(p2)
                        act(rsf[r].t[:], rsf[r].t[:], AF.Exp, [rsf[r].b], [rsf[r].b], scale=-0.5)
                        yield
                        if not rope:
                            stt(dst.t[:, c, :], p.t[:, :], gain.t[:, 0:1], rsf[r].t[:], ALU.mult, ALU.mult,
                                [p.b, gain.b, rsf[r].b], [SB(dst, c)])
                            prel(p)
                            rfree.append(r)
                            return
                        stt(qn[r].t[:], p.t[:, :], gain.t[:, 0:1], rsf[r].t[:], ALU.mult, ALU.mult,
                            [p.b, gain.b, rsf[r].b], [qn[r].b])
                        prel(p)
                        yield
                        cp("act", qnb[r].t[:], qn[r].t[:], [qn[r].b], [qnb[r].b])
                        tt("pool", t2[r].t[:], qn[r].t[:], RC.t[:], ALU.mult, [qn[r].b, RC.b], [t2[r].b])
                        yield
                        p3 = palloc()
                        mm(p3.t[:, :], C16("Rm"), qnb[r].t[:], [cstb.b, qnb[r].b], [p3.b])
                        yield
                        tt("dve", t1[r].t[:], p3.t[:, :], RS.t[:], ALU.mult, [p3.b, RS.b], [t1[r].b])
                        prel(p3)
                        yield
                        tt("dve", dst.t[:, c, :], t1[r].t[:], t2[r].t[:], ALU.add, [t1[r].b, t2[r].b], [SB(dst, c)])
                        rfree.append(r)

                    def fgate_chain(s, d, col0):
                        while not gfree:
                            yield
                        g = gfree.pop(0)
                        p = palloc()
                        tm(p, s, col0, 512)
                        yield
                        act(sg[g].t[:], p.t[:, :], AF.Exp, [p.b], [sg[g].b])
                        prel(p)
                        yield
                        act(sg[g].t[:], sg[g].t[:], AF.Ln, [sg[g].b], [sg[g].b], bias=1.0)
                        yield
                        act(sg[g].t[:], sg[g].t[:], AF.Exp, [sg[g].b], [sg[g].b], scale=-1.0)
                        yield
                        tt("dve", k32[g].t[:], sg[g].t[:], oml[d].t[:], ALU.mult, [sg[g].b, oml[d].b], [k32[g].b])
                        yield
                        act(sg[g].t[:], k32[g].t[:], AF.Ln, [k32[g].b], [sg[g].b], scale=-1.0, bias=1.0)
                        cp("pool", ks[d].t[:, s, :], k32[g].t[:], [k32[g].b], [SB(ks[d], s)])
                        yield
                        cp("act", lfhs[d].t[:, s, :], sg[g].t[:], [sg[g].b], [SB(lfhs[d], s)])
                        yield
                        tt("pool", lfls[d].t[:, s, :], sg[g].t[:], lfhs[d].t[:, s, :], ALU.subtract,
                           [sg[g].b, SB(lfhs[d], s)], [SB(lfls[d], s)])
                        gfree.append(g)

                    def kT_chain(d, h):
                        p = palloc()
                        for s in range(4):
                            mm(p.t[:, s * 128:(s + 1) * 128], ks[d].t[:, s, h * 128:(h + 1) * 128], C16("ident"),
                               [SB(ks[d], s), cstb.b], [p.b])
                        yield
                        cp("dve", kTs[d].t[:, h, :], p.t[:, :], [p.b], [SB(kTs[d], h)])
                        prel(p)

                    ph1 = []
                    sil = [silu_chain(c * 128, qTs, c) for c in range(4)] + [silu_chain(3072 + c * 128, gTs, c) for c in range(8)]
                    cps = []
                    for s in range(4):
                        cps += [v_chain(s), av_chain(s)]
                    for j in range(12):
                        ph1.append(sil[j])
                        if j < 8:
                            ph1.append(cps[j])
                    nrm = []
                    for (col0, gain, dst, rope) in ((2048, gq, aqTs, True), (2304, gk, akTs, True), (2816, gm, mqTs, False)):
                        for c in range(2):
                            nrm.append(norm_chain(col0, gain, dst, c, rope))
                    fg = []
                    for s in range(4):
                        fg += [fgate_chain(s, "f", 512), fgate_chain(s, "b", 1024)]
                    ph2 = []
                    for j in range(8):
                        ph2.append(fg[j])
                        if j < 6:
                            ph2.append(nrm[j])
                    ph3 = [kT_chain(d, h) for d in "fb" for h in range(4)]
                    return ph1, ph2, ph3

                def stores(i):
                    t0 = i * 512
                    fmv = lambda dd: dd.ap[:, t0:t0 + 512].rearrange("(h p) t -> p h t", p=128)
                    tmv = lambda dd: dd.ap[t0:t0 + 512, :].rearrange("(s p) f -> p s f", p=128)
                    fw.dma("pool", fmv(qT), qTs.t[:], reads=SBall(qTs, 4), writes=[qT.b(i)])
                    fw.dma("pool", fmv(gT), gTs.t[:], reads=SBall(gTs, 8), writes=[gT.b(i)])
                    fw.dma("pool", tmv(vtok), vs.t[:], reads=SBall(vs, 4), writes=[vtok.b(i)])
                    fw.dma("pool", tmv(va), vas.t[:], reads=[vas.b] + SBall(vas, 4), writes=[va.b(i)])
                    for (dst, dd) in ((aqTs, aqT), (akTs, akT), (mqTs, mqT)):
                        fw.dma("pool", fmv(dd), dst.t[:], reads=SBall(dst, 2), writes=[dd.b(i)])
                    for d in "fb":
                        fw.dma("pool", tmv(lfh[d]), lfhs[d].t[:], reads=SBall(lfhs[d], 4), writes=[lfh[d].b(i)])
                        fw.dma("pool", tmv(lfl[d]), lfls[d].t[:], reads=SBall(lfls[d], 4), writes=[lfl[d].b(i)])
                        fw.dma("pool", tmv(ktok[d]), ks[d].t[:], reads=SBall(ks[d], 4), writes=[ktok[d].b(i)])
                        fw.dma("pool", fmv(kT[d]), kTs[d].t[:], reads=SBall(kTs[d], 4), writes=[kT[d].b(i)])

                def mark_store_reads(i):
                    pass

                run_chains([prologue(0)], 1)
                for i in range(NT):
                    ph1, ph2, ph3 = tile_chains(i)
                    run_chains(ph1, 3)
                    nxt = [prologue(i + 1)] if i + 1 < NT else []
                    run_chains(nxt + ph2, 4)
                    run_chains(ph3, 3)
                    stores(i)
                fw.barrier()

        def pass_H(l, d):
            fwd = d == "f"
            with ExitStack() as es:
                def sb(name, shape, dt):
                    return Tl(es.enter_context(nc.sbuf_tensor(f"{name}{d}_L{l}", list(shape), dt)))
                nb = 3
                lft = [sb(f"H_lfh{j}", [128, 4, 512], BF16) for j in range(nb)]
                llt = [sb(f"H_lfl{j}", [128, 4, 512], BF16) for j in range(nb)]
                kt_ = [sb(f"H_k{j}", [128, 4, 512], BF16) for j in range(nb)]
                kTt = [sb(f"H_kT{j}", [128, 4, 512], BF16) for j in range(nb)]
                qTt = [sb(f"H_qT{j}", [128, 4, 512], BF16) for j in range(nb)]
                vt = [sb(f"H_v{j}", [128, 4, 512], BF16) for j in range(nb)]
                ex3 = [sb(f"H_ex3{j}", [128, 512], F32) for j in range(2)]
                kd2 = [sb(f"H_kd{j}", [128, 4, 512], BF16) for j in range(2)]
                vm = [[sb(f"H_vm{j}_{c}", [128, 4, 512], BF16) for c in range(2)] for j in range(nb)]
                for j in range(nb):
                    for c in range(2):
                        fw.op("pool", lambda e: e.memset(vm[j][c].t[:], 0.0), [], [vm[j][c].b])
                qx = [sb(f"H_qx{j}", [128, 512], F32) for j in range(2)]
                kx = [sb(f"H_kx{j}", [128, 512], F32) for j in range(2)]
                qe2 = [sb(f"H_qe{j}", [128, 4, 512], BF16) for j in range(2)]
                ke2 = [sb(f"H_ke{j}", [128, 4, 512], BF16) for j in range(2)]
                ext2 = [sb(f"H_ext{j}", [128, 64], F32) for j in range(2)]
                PT = [sb(f"H_PT{j}", [128, 4, 128], BF16) for j in range(3)]
                S = [sb(f"H_S{h}", [128, 128], F32) for h in range(4)]
                Sm = [sb(f"H_Sm{j}", [128, 128], BF16) for j in range(16)]
                oTs = [sb(f"H_oT{j}", [128, 4, 512], F32) for j in range(2)]
                An, Bn, Kn, Mn = ("A_f", "B_f", "K_f", "M_f") if fwd else ("A_b", "B_b", "K_b", "M_b")
                K4 = sb("H_K4", [128, 4, 128], F32)
                for h in range(4):
                    cp("dve", K4.t[:, h, :], C32(Kn), [cst.b], [K4.b])
                    fw.op("dve", lambda e: e.memset(S[h].t[:], 0.0), [], [S[h].b])
                order = list(range(NT)) if fwd else list(range(NT - 1, -1, -1))
                pfree = list(ps_t)

                def palloc():
                    return pfree.pop(0)

                def prel(p):
                    pfree.append(p)

                def load(n):
                    i = order[n]
                    j = n % nb
                    t0 = i * 512
                    fw.dma("sp", lft[j].t[:], lfh[d].ap[t0:t0 + 512, :].rearrange("(s p) f -> p s f", p=128),
                           reads=[lfh[d].b(i)], writes=[lft[j].b])
                    fw.dma("sp", llt[j].t[:], lfl[d].ap[t0:t0 + 512, :].rearrange("(s p) f -> p s f", p=128),
                           reads=[lfl[d].b(i)], writes=[llt[j].b])
                    fw.dma("sp", kt_[j].t[:], ktok[d].ap[t0:t0 + 512, :].rearrange("(s p) f -> p s f", p=128),
                           reads=[ktok[d].b(i)], writes=[kt_[j].b])
                    vsrc = vtok.ap[t0:t0 + 512, :].rearrange("(s p) f -> p s f", p=128)
                    fw.dma("sp", vt[j].t[:], vsrc, reads=[vtok.b(i)], writes=[vt[j].b])
                    for c in range(2):
                        fw.dma("sp", vm[j][c].t[c * 64:(c + 1) * 64, :, :], vsrc[c * 64:(c + 1) * 64],
                               reads=[vtok.b(i)], writes=[vm[j][c].b])
                    fw.dma("sp", kTt[j].t[:], kT[d].ap[:, t0:t0 + 512].rearrange("(h p) t -> p h t", p=128),
                           reads=[kT[d].b(i)], writes=[kTt[j].b])
                    fw.dma("sp", qTt[j].t[:], qT.ap[:, t0:t0 + 512].rearrange("(h p) t -> p h t", p=128),
                           reads=[qT.b(i)], writes=[qTt[j].b])

                def ephase(n):
                    j = n % nb
                    L, L2, K_, KT_, QT_ = lft[j], llt[j], kt_[j], kTt[j], qTt[j]
                    kd, qe, ke, ext = kd2[n % 2], qe2[n % 2], ke2[n % 2], ext2[n % 2]
                    for s in range(4):
                        p = palloc()
                        mm(p.t[:, :], C16(Bn), L.t[:, s, :], [cstb.b, L.b], [p.b], start=True, stop=False)
                        mm(p.t[:, :], C16(Bn), L2.t[:, s, :], [cstb.b, L2.b], [p.b], start=False, stop=True)
                        e3 = ex3[s % 2]
                        act(e3.t[:], p.t[:, :], AF.Exp, [p.b], [e3.b])
                        prel(p)
                        tt("pool", kd.t[:, s, :], K_.t[:, s, :], e3.t[:], ALU.mult, [K_.b, e3.b], [kd.b])
                    pe_ = palloc()
                    for h in range(4):
                        for s in range(4):
                            c0 = (h * 4 + s) * 4
                            mm(pe_.t[:, c0:c0 + 4], L.t[:, s, h * 128:(h + 1) * 128], C16(Mn, 4), [L.b, cstb.b], [pe_.b],
                               start=True, stop=False)
                            mm(pe_.t[:, c0:c0 + 4], L2.t[:, s, h * 128:(h + 1) * 128], C16(Mn, 4), [L2.b, cstb.b], [pe_.b],
                               start=False, stop=True)
                    act(ext.t[:], pe_.t[:, 0:64], AF.Exp, [pe_.b], [ext.b])
                    prel(pe_)
                    for h in range(4):
                        p = palloc()
                        for s in range(4):
                            mm(p.t[:, s * 128:(s + 1) * 128], L.t[:, s, h * 128:(h + 1) * 128], C16(An), [L.b, cstb.b], [p.b],
                               start=True, stop=False)
                            mm(p.t[:, s * 128:(s + 1) * 128], L2.t[:, s, h * 128:(h + 1) * 128], C16(An), [L2.b, cstb.b], [p.b],
                               start=False, stop=True)
                        a, b_ = qx[h % 2], kx[h % 2]
                        act(a.t[:], p.t[:, :], AF.Exp, [p.b], [a.b])
                        act(b_.t[:], p.t[:, :], AF.Exp, [p.b], [b_.b], scale=-1.0)
                        prel(p)
                        tt("dve", qe.t[:, h, :], QT_.t[:, h, :], a.t[:], ALU.mult, [QT_.b, a.b], [qe.b])
                        tt("pool", ke.t[:, h, :], KT_.t[:, h, :], b_.t[:], ALU.mult, [KT_.b, b_.b], [ke.b])

                load(0)
                if NT > 1:
                    load(1)
                ephase(0)
                smi = [0]
                pti = [0]
                for n in range(NT):
                    i = order[n]
                    j = n % nb
                    t0 = i * 512
                    V_, VM = vt[j], vm[j]
                    kd, qe, ke, ext = kd2[n % 2], qe2[n % 2], ke2[n % 2], ext2[n % 2]
                    o_ = oTs[n % 2]
                    subs = list(range(4)) if fwd else list(range(3, -1, -1))

                    def stageA(s):
                        ssl = slice(s * 128, (s + 1) * 128)
                        p = palloc()
                        for h in range(4):
                            mm(p.t[:, h * 128:(h + 1) * 128], ke.t[:, h, ssl], qe.t[:, h, ssl], [ke.b, qe.b], [p.b])
                        pt = PT[pti[0] % 3]
                        pti[0] += 1
                        tt("dve", pt.t[:], p.t[:, :].rearrange("p (h t) -> p h t", h=4), K4.t[:], ALU.mult, [p.b, K4.b], [pt.b])
                        prel(p)
                        pd = [palloc(), palloc()]
                        for c in range(2):
                            for h in range(4):
                                hs = slice(h * 128, (h + 1) * 128)
                                mm(pd[c].t[:, hs], kd.t[:, s, hs], VM[c].t[:, s, hs], [kd.b, VM[c].b], [pd[c].b])
                        return pt, pd

                    def stageC(s, pt, pd):
                        ssl = slice(s * 128, (s + 1) * 128)
                        tg = t0 + s * 128
                        po = palloc()
                        for h in range(4):
                            hs = slice(h * 128, (h + 1) * 128)
                            mm(po.t[:, hs], V_.t[:, s, hs], pt.t[:, h, :], [V_.b, pt.b], [po.b], start=(h == 0), stop=False)
                        chs = (0, 1) if fwd else (1, 0)
                        for ci, c in enumerate(chs):
                            tok = tg + c * 64
                            if (fwd and tok == HALF) or ((not fwd) and tok + 64 == HALF):
                                for h in range(4):
                                    ts("dve", S[h].t[:], S[h].t[:], flag.t[:, 0:1], ALU.mult, [S[h].b, flag.b], [S[h].b])
                            sms = []
                            for h in range(4):
                                ec = (h * 4 + s) * 4 + 2 * c
                                sm_ = Sm[smi[0] % 16]
                                smi[0] += 1
                                act(sm_.t[:], S[h].t[:], AF.Copy, [S[h].b, ext.b], [sm_.b], scale=ext.t[:, ec:ec + 1])
                                sms.append(sm_)
                            for h in range(4):
                                ec = (h * 4 + s) * 4 + 2 * c
                                stt(S[h].t[:], S[h].t[:], ext.t[:, ec + 1:ec + 2], pd[c].t[:, h * 128:(h + 1) * 128],
                                    ALU.mult, ALU.add, [S[h].b, ext.b, pd[c].b], [S[h].b])
                            for h in range(4):
                                mm(po.t[:, h * 128 + c * 64:h * 128 + (c + 1) * 64], sms[h].t[:],
                                   qe.t[:, h, s * 128 + c * 64:s * 128 + (c + 1) * 64],
                                   [sms[h].b, qe.b], [po.b], start=False, stop=(ci == 1))
                        prel(pd[0])
                        prel(pd[1])
                        cp("dve", o_.t[:, :, ssl], po.t[:, :].rearrange("p (h t) -> p h t", h=4), [po.b], [o_.b])
                        prel(po)

                    if n + 2 < NT:
                        load(n + 2)
                    prev = None
                    for bi, s in enumerate(subs):
                        cur = (s,) + stageA(s)
                        if bi == 1 and n + 1 < NT:
                            ephase(n + 1)
                        if prev is not None:
                            stageC(*prev)
                        prev = cur
                    stageC(*prev)
                    fw.dma("pool", oT[d].ap[:, t0:t0 + 512].rearrange("(h p) t -> p h t", p=128), o_.t[:], reads=[o_.b],
                           writes=[oT[d].b(i)])
                fw.barrier()

        def attn_core(sb, NTq, qsrc, kv_for_tile, masks, out_row0, tag):
            LA = 5
            qm = [[sb(f"{tag}_qm{j}_{par}", [128, 2, 512], BF16) for par in range(2)] for j in range(2)]
            for j in range(2):
                for par in range(2):
                    fw.op("pool", lambda e: e.memset(qm[j][par].t[:], 0.0), [], [qm[j][par].b])
            gt = [sb(f"{tag}_g{j}", [128, 2, 512], BF16) for j in range(2)]
            pT = [sb(f"{tag}_pT{j}", [128, 512], BF16) for j in range(8)]
            sc = [sb(f"{tag}_sc{j}", [128, 512], F32) for j in range(6)] if masks is not None else None
            otok = sb(f"{tag}_otok", [128, 4, 256], BF16)
            rden = sb(f"{tag}_rden", [128, 4], F32)
            mo = [sb(f"{tag}_mo{j}", [128, 2, 512], BF16) for j in range(2)]
            pi = [0]
            for i in range(NTq):
                t0 = i * 512
                QM, G = qm[i % 2], gt[i % 2]
                qv = qsrc.ap[:, t0:t0 + 512].rearrange("(h p) t -> p h t", p=128)
                for par in range(2):
                    fw.dma("sp", QM[par].t[par * 64:(par + 1) * 64, :, :], qv[par * 64:(par + 1) * 64], reads=[qsrc.b(i)],
                           writes=[QM[par].b])
                fw.dma("sp", G.t[:], gT.ap[512 + out_row0:512 + out_row0 + 256, t0:t0 + 512].rearrange("(h p) t -> p h t", p=128),
                       reads=[gT.b(i)], writes=[G.b])
                ktl = kv_for_tile(i)
                nk = len(ktl)
                po_h = {}

                started = {}

                def stage1(h, ki):
                    ch, pr = h // 2, (h % 2) * 64
                    kfn, vfn, mi, kbufs = ktl[ki]
                    s0, s1 = SUBR[mi] if masks is not None else (0, 4)
                    cs = slice(s0 * 128, s1 * 128)
                    p = psum()
                    mm(p.t[:, cs], kfn(ch, pr), QM[h % 2].t[:, ch, cs], kbufs + [QM[h % 2].b], [p.b])
                    e_ = pT[pi[0] % 8]
                    if masks is not None:
                        s_ = sc[pi[0] % 6]
                        tt("dve", s_.t[:, cs], p.t[:, cs], masks.t[:, mi, cs], ALU.add, [p.b, masks.b], [s_.b])
                        act(e_.t[:, cs], s_.t[:, cs], AF.Exp, [s_.b], [e_.b], scale=0.125)
                    else:
                        act(e_.t[:, cs], p.t[:, cs], AF.Exp, [p.b], [e_.b], scale=0.125)
                    pi[0] += 1
                    return e_

                def stage2(h, ki, e_):
                    kfn, vfn, mi, kbufs = ktl[ki]
                    if ki == 0:
                        po_h[h] = psum_acc()
                    po = po_h[h]
                    s0, s1 = SUBR[mi] if masks is not None else (0, 4)
                    for s in range(s0, s1):
                        mm(po.t[:, s * 128:s * 128 + 65], e_.t[:, s * 128:(s + 1) * 128], vfn(h), [e_.b] + kbufs, [po.b],
                           start=(h not in started), stop=(ki == nk - 1))
                        started[h] = True
                    if ki == nk - 1:
                        pov = po.t[:, :].rearrange("p (s e) -> p s e", e=128)
                        fw.op("dve", lambda e: e.reciprocal(out=rden.t[:, :], in_=pov[:, :, 64]), [po.b], [rden.b])
                        for s in range(4):
                            ts("dve", otok.t[:, s, h * 64:(h + 1) * 64], po.t[:, s * 128:s * 128 + 64], rden.t[:, s:s + 1],
                               ALU.mult, [po.b, rden.b], [otok.b])

                pend = []
                for pair in range(2):
                    for ki in range(nk):
                        for h in (2 * pair, 2 * pair + 1):
                            pend.append((h, ki, stage1(h, ki)))
                            if len(pend) > LA:
                                stage2(*pend.pop(0))
                while pend:
                    stage2(*pend.pop(0))
                M_ = mo[i % 2]
                for ch in range(2):
                    p = psum()
                    for s in range(4):
                        mm(p.t[:, s * 128:(s + 1) * 128], otok.t[:, s, ch * 128:(ch + 1) * 128], C16("ident"),
                           [otok.b, cstb.b], [p.b])
                    tt("dve", M_.t[:, ch, :], p.t[:, :], G.t[:, ch, :], ALU.mult, [p.b, G.b], [M_.b])
                fw.dma("pool", mixT.ap[out_row0:out_row0 + 256, t0:t0 + 512].rearrange("(h p) t -> p h t", p=128), M_.t[:],
                       reads=[M_.b], writes=[mixT.b((out_row0, i))])

        def pass_AT(l):
            with ExitStack() as es:
                def sb(name, shape, dt):
                    return Tl(es.enter_context(nc.sbuf_tensor(f"{name}_L{l}", list(shape), dt)))
                masks = sb("AT_masks", [128, 20, 512], F32)
                fw.dma("sp", masks.t[:], amask_in.rearrange("r p q -> p r q"), writes=[masks.b])
                NW = 20
                kw_ = [sb(f"AT_kw{j}", [128, 2, NW * 128], BF16) for j in range(2)]
                vw_ = [sb(f"AT_vw{j}", [128, NW, 260], BF16) for j in range(2)]

                def kv_for_tile(i):
                    KW, VW = kw_[i % 2], vw_[i % 2]
                    q0 = i * 512
                    half_lo = (q0 // HALF) * HALF
                    lo = max(q0 - 1024, 0)
                    hi = min(q0 + 512 + 1024, T)
                    n = (hi - lo) // 128
                    tiles = sorted(set(range(lo // 512, (hi + 511) // 512)))
                    fw.dma("sp", KW.t[:, :, 0:n * 128], akT.ap[:, lo:hi].rearrange("(h p) t -> p h t", p=128),
                           reads=[akT.b(x) for x in tiles], writes=[KW.b])
                    fw.dma("sp", VW.t[:, 0:n, :], va.ap[lo:hi, :].rearrange("(j p) f -> p j f", p=128),
                           reads=[va.b(x) for x in tiles], writes=[VW.b])
                    out = []
                    for jt in range(n):
                        k0 = lo + jt * 128
                        r = (k0 - q0) // 128
                        assert -8 <= r <= 11
                        if not (half_lo <= k0 < half_lo + HALF):
                            ts("dve", VW.t[:, jt, :], VW.t[:, jt, :], flag.t[:, 0:1], ALU.mult, [VW.b, flag.b], [VW.b])
                        out.append(((lambda ch, pr, jt=jt, KW=KW: KW.t[:, ch, jt * 128:(jt + 1) * 128]),
                                    (lambda h, jt=jt, VW=VW: VW.t[:, jt, h * 65:(h + 1) * 65]),
                                    r + 8, [KW.b, VW.b]))
                    return out
                attn_core(sb, NT, aqT, kv_for_tile, masks, 0, "AT")
                fw.barrier()

        def pass_ME(l):
            with ExitStack() as es:
                def sb(name, shape, dt):
                    return Tl(es.enter_context(nc.sbuf_tensor(f"{name}_L{l}", list(shape), dt)))
                wkv = sb("ME_w", [128, 8, 512], BF16)
                for kc in range(8):
                    fw.dma("pool", wkv.t[:, kc, :], mem_wkv[l, kc * 128:(kc + 1) * 128, :], writes=[wkv.b])
                mnw = sb("ME_mnw", [128, 8], F32)
                col_load(mnw, mem_norm_w[l], 8)
                gk = sb("ME_gk", [128, 1], F32)
                col_load64(gk, mk_w[l])
                mkT = [sb(f"ME_mkT{g}", [128, 2, 256], BF16) for g in range(2)]
                mv = [sb(f"ME_mv{g}", [128, 2, 260], BF16) for g in range(2)]
                xm = sb("ME_x", [128, 1024], F32)
                junk = sb("ME_junk", [128, 1024], BF16)
                ssm = sb("ME_ss", [128, 1], F32)
                hbm = sb("ME_hb", [128, 2, 1024], BF16)
                hTm = sb("ME_hT", [128, 8, 256], BF16)
                sqm = sb("ME_sq", [128, 256], BF16)
                rsm = sb("ME_rs", [128, 256], F32)
                for g in range(2):
                    fw.op("dve", lambda e: e.memset(mv[g].t[:], 1.0), [], [mv[g].b])
                    for s in range(2):
                        fw.dma("sp", xm.t[:], mem_in[g, s * 128:(s + 1) * 128, :], writes=[xm.b])
                        act(junk.t[:], xm.t[:], AF.Square, [xm.b], [junk.b, ssm.b], accum_out=ssm.t[:, 0:1])
                        rstd_inplace(ssm.t[:], 1024.0, ssm.b)
                        ts("dve", hbm.t[:, s, :], xm.t[:], ssm.t[:, 0:1], ALU.mult, [xm.b, ssm.b], [hbm.b])
                    for kc in range(8):
                        p = psum()
                        for s in range(2):
                            mm(p.t[:, s * 128:(s + 1) * 128], hbm.t[:, s, kc * 128:(kc + 1) * 128], C16("ident"),
                               [hbm.b, cstb.b], [p.b])
                        ts("dve", hTm.t[:, kc, :], p.t[:, 0:256], mnw.t[:, kc:kc + 1], ALU.mult, [p.b, mnw.b], [hTm.b])
                    for c in range(2):
                        p = psum()
                        for kc in range(8):
                            mm(p.t[:, 0:256], wkv.t[:, kc, c * 128:(c + 1) * 128], hTm.t[:, kc, :], [wkv.b, hTm.b], [p.b],
                               start=(kc == 0), stop=(kc == 7))
                        act(sqm.t[:], p.t[:, 0:256], AF.Square, [p.b], [sqm.b])
                        p2 = psum()
                        mm(p2.t[:, 0:256], C16("ones64"), sqm.t[:], [cstb.b, sqm.b], [p2.b])
                        act(rsm.t[:], p2.t[:, 0:256], AF.Ln, [p2.b], [rsm.b], scale=1.0 / 64, bias=EPS)
                        act(rsm.t[:], rsm.t[:], AF.Exp, [rsm.b], [rsm.b], scale=-0.5)
                        stt(mkT[g].t[:, c, :], p.t[:, 0:256], gk.t[:, 0:1], rsm.t[:], ALU.mult, ALU.mult,
                            [p.b, gk.b, rsm.b], [mkT[g].b])
                    for s in range(2):
                        p = psum()
                        for kc in range(8):
                            mm(p.t[:, 0:256], hTm.t[:, kc, s * 128:(s + 1) * 128], wkv.t[:, kc, 256:512], [hTm.b, wkv.b], [p.b],
                               start=(kc == 0), stop=(kc == 7))
                        cp("act", mv[g].t[:, s, :].rearrange("p (h e) -> p h e", e=65)[:, :, 0:64],
                           p.t[:, 0:256].rearrange("p (h e) -> p h e", e=64), [p.b], [mv[g].b])

                def kv_for_tile(i):
                    g = (i * 512) // HALF
                    out = []
                    for jt in range(2):
                        out.append(((lambda ch, pr, jt=jt, g=g: mkT[g].t[:, ch, jt * 128:(jt + 1) * 128]),
                                    (lambda h, jt=jt, g=g: mv[g].t[:, jt, h * 65:(h + 1) * 65]),
                                    0, [mkT[g].b, mv[g].b]))
                    return out
                attn_core(sb, NT, mqT, kv_for_tile, None, 256, "ME")
                fw.barrier()

        def pass_C(l):
            src = x_dt if l == 0 else y_out
            with ExitStack() as es:
                def sb(name, shape, dt):
                    return Tl(es.enter_context(nc.sbuf_tensor(f"{name}_L{l}", list(shape), dt)))
                wo = sb("C_w", [128, 8, 1024], BF16)
                for kc in range(8):
                    fw.dma("pool", wo.t[:, kc, :], w_out[l, kc * 128:(kc + 1) * 128, :], writes=[wo.b])
                gon = sb("C_gon", [128, 1], F32)
                fw.dma("sp", gon.t[:, 0:1], onorm_w[l].rearrange("(p o) -> p o", o=1), writes=[gon.b],
                       allow_slow_non_contiguous=True)
                of_ = [sb(f"C_of{j}", [128, 4, 512], F32) for j in range(2)]
                ob_ = [sb(f"C_ob{j}", [128, 4, 512], F32) for j in range(2)]
                gh = [sb(f"C_g{j}", [128, 4, 512], BF16) for j in range(2)]
                mx_ = [sb(f"C_mx{j}", [128, 8, 512], BF16) for j in range(2)]
                xt = [sb(f"C_x{j}", [128, 4, 1024], F32) for j in range(2)]
                osum = [sb(f"C_osum{h}", [128, 512], F32) for h in range(4)]
                sq = [sb(f"C_sq{h}", [128, 512], BF16) for h in range(4)]
                rsf = [sb(f"C_rsf{h}", [128, 512], F32) for h in range(4)]
                m1 = [sb(f"C_m1{h}", [128, 512], F32) for h in range(4)]
                mxb = [[Buf() for _ in range(8)] for _ in range(2)]
                yt = [sb(f"C_y{j}", [128, 1024], F32) for j in range(2)]

                def load(i):
                    j = i % 2
                    t0 = i * 512
                    fw.dma("sp", of_[j].t[:], oT["f"].ap[:, t0:t0 + 512].rearrange("(h p) t -> p h t", p=128),
                           reads=[oT["f"].b(i)], writes=[of_[j].b])
                    fw.dma("sp", ob_[j].t[:], oT["b"].ap[:, t0:t0 + 512].rearrange("(h p) t -> p h t", p=128),
                           reads=[oT["b"].b(i)], writes=[ob_[j].b])
                    fw.dma("sp", gh[j].t[:], gT.ap[0:512, t0:t0 + 512].rearrange("(h p) t -> p h t", p=128),
                           reads=[gT.b(i)], writes=[gh[j].b])
                    fw.dma("sp", mx_[j].t[:, 4:8, :], mixT.ap[:, t0:t0 + 512].rearrange("(h p) t -> p h t", p=128),
                           reads=[mixT.b((0, i)), mixT.b((256, i))], writes=mxb[j][4:8])
                    fw.dma("sp", xt[j].t[:], src.ap[t0:t0 + 512, :].rearrange("(s p) f -> p s f", p=128),
                           reads=[src.b(i)], writes=[xt[j].b])
                yi = [0]
                H4 = range(4)

                def normphase(i):
                    j = i % 2
                    for h in H4:
                        tt("pool", osum[h].t[:], of_[j].t[:, h, :], ob_[j].t[:, h, :], ALU.add, [of_[j].b, ob_[j].b], [osum[h].b])
                    for h in H4:
                        act(sq[h].t[:], osum[h].t[:], AF.Square, [osum[h].b], [sq[h].b])
                    pp = []
                    for h in H4:
                        p = psum()
                        mm(p.t[:, :], C16("ones128"), sq[h].t[:], [cstb.b, sq[h].b], [p.b])
                        pp.append(p)
                    for h in H4:
                        act(rsf[h].t[:], pp[h].t[:, :], AF.Ln, [pp[h].b], [rsf[h].b], scale=1.0 / 128, bias=EPS)
                    for h in H4:
                        act(rsf[h].t[:], rsf[h].t[:], AF.Exp, [rsf[h].b], [rsf[h].b], scale=-0.5)
                    for h in H4:
                        stt(m1[h].t[:], osum[h].t[:], gon.t[:, 0:1], rsf[h].t[:], ALU.mult, ALU.mult,
                            [osum[h].b, gon.b, rsf[h].b], [m1[h].b])
                    for h in H4:
                        tt("dve", mx_[j].t[:, h, :], m1[h].t[:], gh[j].t[:, h, :], ALU.mult, [m1[h].b, gh[j].b], [mxb[j][h]])

                def outproj(i):
                    j = i % 2
                    t0 = i * 512
                    for s in range(4):
                        Y = yt[yi[0] % 2]
                        yi[0] += 1
                        for nh in range(2):
                            p = psum()
                            for mc in range(8):
                                mm(p.t[:, :], mx_[j].t[:, mc, s * 128:(s + 1) * 128], wo.t[:, mc, nh * 512:(nh + 1) * 512],
                                   [mxb[j][mc], wo.b], [p.b], start=(mc == 0), stop=(mc == 7))
                            tt("dve", Y.t[:, nh * 512:(nh + 1) * 512], p.t[:, :], xt[j].t[:, s, nh * 512:(nh + 1) * 512], ALU.add,
                               [p.b, xt[j].b], [Y.b])
                        fw.dma("pool", y_out.ap[t0 + s * 128:t0 + (s + 1) * 128, :], Y.t[:], reads=[Y.b], writes=[y_out.b(i)])

                load(0)
                if NT > 1:
                    load(1)
                normphase(0)
                for i in range(NT):
                    if i + 1 < NT:
                        normphase(i + 1)
                    outproj(i)
                    if i + 2 < NT:
                        load(i + 2)
                fw.barrier()

        _P = _os.environ.get("KPASSES", "A,HF,HB,AT,ME,C").split(",")
        for l in range(depth):
            if "A" in _P:
                pass_A(l)
            if "HF" in _P:
                pass_H(l, "f")
            if "HB" in _P:
                pass_H(l, "b")
            if "AT" in _P:
                pass_AT(l)
            if "ME" in _P:
                pass_ME(l)
            if "C" in _P:
                pass_C(l)
        fw.barrier()
    return nc, fw


T_CORE = 16384
DEPTH = 4
_CACHE = {}


def kernel(x_prompt, x_sample, mem_prompt, mem_sample, norm_w, w_in, hgrn_lb_fwd, hgrn_lb_bwd, hgrn_onorm_w,
           attn_qnorm_w, attn_knorm_w, mem_norm_w, mem_wkv, mem_qnorm_w, mem_knorm_w, w_out):
    f = lambda a: np.ascontiguousarray(np.asarray(a, dtype=np.float32))
    x_prompt, x_sample, mem_prompt, mem_sample = f(x_prompt), f(x_sample), f(mem_prompt), f(mem_sample)
    T = T_CORE
    if "nc" not in _CACHE:
        _CACHE["nc"] = build(T, DEPTH)[0]
        _CACHE["hc"] = host_consts(T)
    nc = _CACHE["nc"]
    hc = _CACHE["hc"]
    pos_p = np.concatenate([np.arange(8192), np.arange(8192)])
    pos_s = np.arange(16384)
    Cp, Sp = rope_tables(pos_p)
    Cs, Ss = rope_tables(pos_s)
    shared = {"cst": hc["cst"], "amask": hc["amask"], "norm_w": f(norm_w), "w_in": f(w_in),
              "hgrn_lb_fwd": f(hgrn_lb_fwd), "hgrn_lb_bwd": f(hgrn_lb_bwd), "hgrn_onorm_w": f(hgrn_onorm_w),
              "attn_qnorm_w": f(attn_qnorm_w), "attn_knorm_w": f(attn_knorm_w), "mem_norm_w": f(mem_norm_w),
              "mem_wkv": f(mem_wkv), "mem_qnorm_w": f(mem_qnorm_w), "mem_knorm_w": f(mem_knorm_w), "w_out": f(w_out)}
    in_maps = []
    for c in range(8):
        m = dict(shared)
        if c < 4:
            m["x"] = x_prompt[2 * c:2 * c + 2].reshape(T, 1024)
            m["mem"] = mem_prompt[2 * c:2 * c + 2]
            m["flag"] = np.zeros((128, 1), np.float32)
            m["ropeC"], m["ropeS"] = Cp, Sp
        else:
            s = (c - 4) % 2
            m["x"] = x_sample[s]
            m["mem"] = np.stack([mem_sample[s], mem_sample[s]])
            m["flag"] = np.ones((128, 1), np.float32)
            m["ropeC"], m["ropeS"] = Cs, Ss
        in_maps.append(m)
    res = run_bass_kernel_spmd(nc, in_maps, core_ids=list(range(8)))
    ys = [np.asarray(r["y"], dtype=np.float32) for r in res.results]
    y_prompt = np.stack([ys[c].reshape(2, 8192, 1024) for c in range(4)]).reshape(8, 8192, 1024)
    y_sample = np.stack([ys[4], ys[5]])
    return (y_prompt, y_sample)
```

```python
import math
import os as _os
from contextlib import ExitStack
import numpy as np
import concourse.bass as bass
import concourse.mybir as mybir
from concourse.bass_utils import run_bass_kernel_spmd

F32 = mybir.dt.float32
BF16 = mybir.dt.bfloat16
AF = mybir.ActivationFunctionType
ALU = mybir.AluOpType
EPS = 1e-6
NSLOT = 8


class Buf:
    __slots__ = ("w", "r")

    def __init__(self):
        self.w = None
        self.r = {}


class Tl:
    def __init__(self, t):
        self.t = t
        self.b = Buf()


class FW:
    LIM = 30000

    def __init__(self, nc):
        self.nc = nc
        self.engs = {"pe": nc.tensor, "act": nc.scalar, "dve": nc.vector, "pool": nc.gpsimd, "sp": nc.sync}
        self.cur = {}
        self.seen = {k: {} for k in self.engs}
        self.last = {}
        self.nsem = 0
        self.dslots = {"sp": [], "pool": []}
        self.dnext = {"sp": 0, "pool": 0}
        self.nins = 0

    def newsem(self):
        self.nsem += 1
        return [self.nsem, self.nc.alloc_semaphore(name=f"fs{self.nsem}")]

    def _wait(self, e, ev):
        if self.seen[e].get(ev[0], 0) >= ev[2]:
            return
        self.engs[e].wait_ge(ev[1], ev[2])
        self.seen[e][ev[0]] = ev[2]

    def _deps(self, e, reads, writes):
        for b in reads:
            if b.w is not None and not (e == "pe" and b.w[3] == "pe"):
                self._wait(e, b.w)
        for b in writes:
            if b.w is not None and not (e == "pe" and b.w[3] == "pe"):
                self._wait(e, b.w)
            for ev in b.r.values():
                if not (e == "pe" and ev[3] == "pe"):
                    self._wait(e, ev)

    def _mark(self, ev, key, reads, writes):
        for b in reads:
            b.r[key] = ev
        for b in writes:
            b.w = ev
            b.r = {}

    def op(self, e, fn, reads=(), writes=()):
        self._deps(e, reads, writes)
        ins = fn(self.engs[e])
        c = self.cur.get(e)
        if c is None or c[2] >= self.LIM:
            c = self.newsem() + [0]
            self.cur[e] = c
        c[2] += 1
        ins.then_inc(c[1], 1)
        ev = (c[0], c[1], c[2], e)
        self._mark(ev, e, reads, writes)
        self.last[e] = ev
        self.nins += 1

    def dma(self, q, out, in_, reads=(), writes=(), **kw):
        self._deps(q, reads, writes)
        slots = self.dslots[q]
        if len(slots) < NSLOT:
            slots.append(self.newsem() + [0])
            s = slots[-1]
        else:
            s = slots[self.dnext[q] % NSLOT]
        self.dnext[q] += 1
        if s[2] > 0:
            self._wait(q, (s[0], s[1], s[2], "dma"))
        if s[2] + 16 > self.LIM:
            s[:] = self.newsem() + [0]
        ins = self.engs[q].dma_start(out=out, in_=in_, **kw)
        s[2] += 16
        ins.then_inc(s[1], 16)
        ev = (s[0], s[1], s[2], "dma")
        self._mark(ev, ("d", s[0], s[2]), reads, writes)
        self.nins += 1

    def barrier(self):
        evs = [self.last[e] for e in ("pe", "act", "dve", "pool") if e in self.last]
        for q in self.dslots:
            for s in self.dslots[q]:
                if s[2] > 0:
                    evs.append((s[0], s[1], s[2], "dma"))
        for e in self.engs:
            for ev in evs:
                if e == "pe" and ev[3] == "pe":
                    continue
                self._wait(e, ev)


class DT:
    def __init__(self, ap):
        self.ap = ap
        self.bufs = {}

    def b(self, i):
        if i not in self.bufs:
            self.bufs[i] = Buf()
        return self.bufs[i]


def host_consts(T):
    c = {}
    s = np.arange(128)[:, None]
    t = np.arange(128)[None, :]
    same = (s // 64) == (t // 64)
    c0 = (t // 64) * 64
    A_f = (same & (s <= t)).astype(np.float32) - (same & (s <= c0 + 31)).astype(np.float32)
    A_b = (same & (s >= t)).astype(np.float32) - (same & (s >= c0 + 32)).astype(np.float32)
    B_f = (same & (s > t)).astype(np.float32)
    B_b = (same & (s < t)).astype(np.float32)
    M_f = np.zeros((128, 4), np.float32)
    M_b = np.zeros((128, 4), np.float32)
    sv = np.arange(128)
    for ch in range(2):
        inc = (sv // 64) == ch
        M_f[:, 2 * ch] = inc & (sv <= ch * 64 + 31)
        M_f[:, 2 * ch + 1] = inc
        M_b[:, 2 * ch] = inc & (sv >= ch * 64 + 32)
        M_b[:, 2 * ch + 1] = inc
    K_f = (same & (s <= t)).astype(np.float32)
    K_b = (same & (s >= t)).astype(np.float32)
    ident = np.eye(128, dtype=np.float32)
    ones64 = ((s // 64) == (t // 64)).astype(np.float32)
    ones128 = np.ones((128, 128), np.float32)
    Rm = np.zeros((128, 128), np.float32)
    for m in range(128):
        j = m % 64
        if j < 8:
            Rm[m + 8, m] = -1.0
        elif j < 16:
            Rm[m - 8, m] = 1.0
    cst = np.concatenate([A_f, A_b, B_f, B_b, K_f, K_b, ident, ones64, ones128, Rm, M_f, M_b], axis=1)
    c["cst"] = np.ascontiguousarray(cst.astype(np.float32))
    am = np.zeros((20, 128, 512), np.float32)
    j = np.arange(128)[:, None]
    i = np.arange(512)[None, :]
    for ri, r in enumerate(range(-8, 12)):
        d = r * 128 + j - i
        am[ri] = ((np.abs(d) <= 64).astype(np.float32)
                  + ((d % 4 == 0) & (np.abs(d) <= 256)).astype(np.float32)
                  + ((d % 16 == 0) & (np.abs(d) <= 1024)).astype(np.float32))
    c["amask"] = np.where(am > 0, 8.0 * np.log(np.maximum(am, 1.0)), -80000.0).astype(np.float32)
    return c


def rope_tables(pos):
    half = 8
    inv = (500000.0 ** (-np.arange(half, dtype=np.float32) * 2.0 / 16.0)).astype(np.float32)
    ang = pos.astype(np.float32)[None, :] * inv[:, None]
    C = np.ones((128, pos.shape[0]), np.float32)
    S = np.zeros((128, pos.shape[0]), np.float32)
    for hb in (0, 64):
        C[hb:hb + 8] = np.cos(ang)
        C[hb + 8:hb + 16] = np.cos(ang)
        S[hb:hb + 8] = np.sin(ang)
        S[hb + 8:hb + 16] = np.sin(ang)
    return C, S


def _sub_ranges():
    out = []
    j = np.arange(128)[:, None]
    i = np.arange(512)[None, :]
    for r in range(-8, 12):
        d = r * 128 + j - i
        ok = (np.abs(d) <= 64) | ((d % 4 == 0) & (np.abs(d) <= 256)) | ((d % 16 == 0) & (np.abs(d) <= 1024))
        subs = [s for s in range(4) if ok[:, s * 128:(s + 1) * 128].any()]
        out.append((min(subs), max(subs) + 1))
    return out


SUBR = _sub_ranges()
CO = {"A_f": 0, "A_b": 128, "B_f": 256, "B_b": 384, "K_f": 512, "K_b": 640, "ident": 768, "ones64": 896,
      "ones128": 1024, "Rm": 1152, "M_f": 1280, "M_b": 1284}
NCST = 1288


def build(T, depth, debug=False):
    NT = T // 512
    HALF = T // 2
    nc = bass.Bass("TRN2", target_bir_lowering=False)
    fw = FW(nc)

    def din(name, shape, dt=F32):
        return nc.dram_tensor(name, list(shape), dt, kind="ExternalInput").ap()

    x_in = din("x", [T, 1024])
    mem_in = din("mem", [2, 256, 1024])
    flag_in = din("flag", [128, 1])
    ropeC_in = din("ropeC", [128, T])
    ropeS_in = din("ropeS", [128, T])
    cst_in = din("cst", [128, NCST])
    amask_in = din("amask", [20, 128, 512])
    norm_w = din("norm_w", [depth, 1024])
    w_in = din("w_in", [depth, 1024, 4096])
    lbp = {"f": din("hgrn_lb_fwd", [depth, 512]), "b": din("hgrn_lb_bwd", [depth, 512])}
    onorm_w = din("hgrn_onorm_w", [depth, 128])
    aq_w = din("attn_qnorm_w", [depth, 64])
    ak_w = din("attn_knorm_w", [depth, 64])
    mem_norm_w = din("mem_norm_w", [depth, 1024])
    mem_wkv = din("mem_wkv", [depth, 1024, 512])
    mq_w = din("mem_qnorm_w", [depth, 64])
    mk_w = din("mem_knorm_w", [depth, 64])
    w_out = din("w_out", [depth, 1024, 1024])
    y_out = DT(nc.dram_tensor("y", [T, 1024], F32, kind="ExternalOutput").ap())
    x_dt = DT(x_in)

    skind = "ExternalOutput" if debug else "Internal"

    def scr(name, shape, dt):
        return DT(nc.dram_tensor(name, list(shape), dt, kind=skind).ap())

    qT = scr("s_qT", [512, T], BF16)
    kT = {"f": scr("s_kTf", [512, T], BF16), "b": scr("s_kTb", [512, T], BF16)}
    ktok = {"f": scr("s_kf", [T, 512], BF16), "b": scr("s_kb", [T, 512], BF16)}
    lfh = {"f": scr("s_lfhf", [T, 512], BF16), "b": scr("s_lfhb", [T, 512], BF16)}
    lfl = {"f": scr("s_lflf", [T, 512], BF16), "b": scr("s_lflb", [T, 512], BF16)}
    vtok = scr("s_v", [T, 512], BF16)
    gT = scr("s_gT", [1024, T], BF16)
    aqT = scr("s_aqT", [256, T], BF16)
    akT = scr("s_akT", [256, T], BF16)
    va = scr("s_va", [T, 260], BF16)
    mqT = scr("s_mqT", [256, T], BF16)
    oT = {"f": scr("s_ofT", [512, T], F32), "b": scr("s_obT", [512, T], F32)}
    mixT = scr("s_mixT", [512, T], BF16)

    es0 = ExitStack()
    with es0:
        def sb0(name, shape, dt):
            return Tl(es0.enter_context(nc.sbuf_tensor(name, list(shape), dt)))

        ps_t = [Tl(es0.enter_context(nc.psum_tensor(f"ps{i}", [128, 512], F32))) for i in range(8)]
        ps_i = [0]

        def psum():
            p = ps_t[ps_i[0] % 6]
            ps_i[0] += 1
            return p
        pa_i = [0]

        def psum_acc():
            p = ps_t[6 + pa_i[0] % 2]
            pa_i[0] += 1
            return p

        def mm(out, lhsT, rhs, reads, writes, start=True, stop=True):
            fw.op("pe", lambda e: e.matmul(out, lhsT=lhsT, rhs=rhs, start=start, stop=stop), reads, writes)

        def act(out, in_, func, reads, writes, **kw):
            fw.op("act", lambda e: e.activation(out=out, in_=in_, func=func, **kw), reads, writes)

        def tt(eng, out, in0, in1, op, reads, writes):
            fw.op(eng, lambda e: e.tensor_tensor(out=out, in0=in0, in1=in1, op=op), reads, writes)

        def ts(eng, out, in0, s1, op0, reads, writes, s2=None, op1=None):
            if op1 is None:
                fw.op(eng, lambda e: e.tensor_scalar(out=out, in0=in0, scalar1=s1, scalar2=None, op0=op0), reads, writes)
            else:
                fw.op(eng, lambda e: e.tensor_scalar(out=out, in0=in0, scalar1=s1, scalar2=s2, op0=op0, op1=op1),
                      reads, writes)

        def stt(out, in0, scalar, in1, op0, op1, reads, writes):
            fw.op("dve", lambda e: e.scalar_tensor_tensor(out=out, in0=in0, scalar=scalar, in1=in1, op0=op0, op1=op1),
                  reads, writes)

        def cp(eng, out, in_, reads, writes):
            if eng == "act":
                fw.op("act", lambda e: e.copy(out=out, in_=in_), reads, writes)
            else:
                fw.op(eng, lambda e: e.tensor_copy(out=out, in_=in_), reads, writes)

        cst = sb0("cst_sb", [128, NCST], F32)
        fw.dma("sp", cst.t[:], cst_in[:, :], writes=[cst.b])
        cstb = sb0("cstb_sb", [128, NCST], BF16)
        cp("dve", cstb.t[:], cst.t[:], [cst.b], [cstb.b])
        flag = sb0("flag_sb", [128, 1], F32)
        fw.dma("sp", flag.t[:], flag_in[:, :], writes=[flag.b])

        def C32(name, w=128):
            return cst.t[:, CO[name]:CO[name] + w]

        def C16(name, w=128):
            return cstb.t[:, CO[name]:CO[name] + w]

        def col_load(dst, src1d, n):
            fw.dma("sp", dst.t[:, 0:n], src1d.rearrange("(c p) -> p c", p=128), writes=[dst.b],
                   allow_slow_non_contiguous=True)

        def col_load64(dst, src1d):
            for hb in (0, 64):
                fw.dma("sp", dst.t[hb:hb + 64, 0:1], src1d.rearrange("(p o) -> p o", o=1), writes=[dst.b],
                       allow_slow_non_contiguous=True)

        def rstd_inplace(tl_ap, n, reads_b):
            act(tl_ap, tl_ap, AF.Ln, [reads_b], [reads_b], scale=1.0 / n, bias=EPS)
            act(tl_ap, tl_ap, AF.Exp, [reads_b], [reads_b], scale=-0.5)

        def pass_A(l):
            src = x_dt if l == 0 else y_out
            with ExitStack() as es:
                def sb(name, shape, dt):
                    return Tl(es.enter_context(nc.sbuf_tensor(f"{name}_L{l}", list(shape), dt)))
                w = sb("A_w", [128, 8, 4096], BF16)
                for kc in range(8):
                    for q4 in range(4):
                        fw.dma("pool", w.t[:, kc, q4 * 1024:(q4 + 1) * 1024],
                               w_in[l, kc * 128:(kc + 1) * 128, q4 * 1024:(q4 + 1) * 1024], writes=[w.b])
                normw = sb("A_normw", [128, 8], F32)
                col_load(normw, norm_w[l], 8)
                gq = sb("A_gq", [128, 1], F32)
                gk = sb("A_gk", [128, 1], F32)
                gm = sb("A_gm", [128, 1], F32)
                col_load64(gq, aq_w[l])
                col_load64(gk, ak_w[l])
                col_load64(gm, mq_w[l])
                oml = {d: sb(f"A_oml{d}", [128, 512], F32) for d in "fb"}
                with ExitStack() as es2:
                    def sb2(name, shape, dt):
                        return Tl(es2.enter_context(nc.sbuf_tensor(f"{name}_L{l}", list(shape), dt)))
                    row = sb2("A_lbrow", [1, depth, 512], F32)
                    mx = sb2("A_lbmx", [1, 512], F32)
                    sm = sb2("A_lbsm", [1, 512], F32)
                    acc = sb2("A_lbacc", [1, 512], F32)
                    tmp = sb2("A_lbtmp", [1, 512], F32)
                    for d in ("f", "b"):
                        fw.dma("sp", row.t[0:1, :, :], lbp[d].rearrange("(o l) f -> o l f", o=1), writes=[row.b])
                        cp("dve", mx.t[:], row.t[0:1, 0, :], [row.b], [mx.b])
                        for j in range(1, depth):
                            tt("dve", mx.t[:], mx.t[:], row.t[0:1, j, :], ALU.max, [mx.b, row.b], [mx.b])
                        for j in range(depth):
                            tt("dve", row.t[0:1, j, :], row.t[0:1, j, :], mx.t[:], ALU.subtract, [row.b, mx.b], [row.b])
                        act(row.t[:], row.t[:], AF.Exp, [row.b], [row.b])
                        cp("dve", sm.t[:], row.t[0:1, 0, :], [row.b], [sm.b])
                        for j in range(1, depth):
                            tt("dve", sm.t[:], sm.t[:], row.t[0:1, j, :], ALU.add, [sm.b, row.b], [sm.b])
                        fw.op("dve", lambda e: e.reciprocal(out=sm.t[:], in_=sm.t[:]), [sm.b], [sm.b])
                        fw.op("dve", lambda e: e.memset(acc.t[:], 0.0), [], [acc.b])
                        for j in range(1, l + 1):
                            tt("dve", tmp.t[:], row.t[0:1, j, :], sm.t[:], ALU.mult, [row.b, sm.b], [tmp.b])
                            tt("dve", acc.t[:], acc.t[:], tmp.t[:], ALU.add, [acc.b, tmp.b], [acc.b])
                        ts("dve", acc.t[:], acc.t[:], -1.0, ALU.mult, [acc.b], [acc.b], s2=1.0, op1=ALU.add)
                        p = psum()
                        mm(p.t[:, :], C32("ones128")[0:1, :], acc.t[0:1, :], [cst.b, acc.b], [p.b])
                        cp("dve", oml[d].t[:], p.t[:, :], [p.b], [oml[d].b])
                    fw.barrier()

                xt = [sb(f"A_x{i}", [128, 1024], F32) for i in range(4)]
                ss = sb("A_ss", [128, 4], F32)
                hb = sb("A_hb", [128, 4, 1024], BF16)
                hTr = [sb(f"A_hT{j}", [128, 8, 512], BF16) for j in range(2)]
                qTs = sb("A_qTs", [128, 4, 512], BF16)
                gTs = sb("A_gTs", [128, 8, 512], BF16)
                aqTs = sb("A_aqTs", [128, 2, 512], BF16)
                akTs = sb("A_akTs", [128, 2, 512], BF16)
                mqTs = sb("A_mqTs", [128, 2, 512], BF16)
                lfhs = {d: sb(f"A_lfhs{d}", [128, 4, 512], BF16) for d in "fb"}
                lfls = {d: sb(f"A_lfls{d}", [128, 4, 512], BF16) for d in "fb"}
                ks = {d: sb(f"A_ks{d}", [128, 4, 512], BF16) for d in "fb"}
                kTs = {d: sb(f"A_kTs{d}", [128, 4, 512], BF16) for d in "fb"}
                vs = sb("A_vs", [128, 4, 512], BF16)
                vas = sb("A_vas", [128, 4, 260], BF16)
                fw.op("dve", lambda e: e.memset(vas.t[:], 1.0), [], [vas.b])
                NR = 2
                sq = [sb(f"A_sq{j}", [128, 512], BF16) for j in range(NR)]
                rsf = [sb(f"A_rsf{j}", [128, 512], F32) for j in range(NR)]
                qn = [sb(f"A_qn{j}", [128, 512], F32) for j in range(NR)]
                qnb = [sb(f"A_qnb{j}", [128, 512], BF16) for j in range(NR)]
                t1 = rsf
                t2 = [sb(f"A_t2{j}", [128, 512], F32) for j in range(NR)]
                rC = [sb(f"A_rC{j}", [128, 512], F32) for j in range(2)]
                rS = [sb(f"A_rS{j}", [128, 512], F32) for j in range(2)]
                NG = 2
                sg = [sb(f"A_sg{j}", [128, 512], F32) for j in range(NG)]
                k32 = [sb(f"A_k32{j}", [128, 512], F32) for j in range(NG)]
                sub_b = {}

                def SB(tl, idx):
                    key = (id(tl), idx)
                    if key not in sub_b:
                        sub_b[key] = Buf()
                    return sub_b[key]

                def SBall(tl, n):
                    return [SB(tl, j) for j in range(n)]
                pfree = list(ps_t)

                def palloc():
                    return pfree.pop(0)

                def prel(p):
                    pfree.append(p)

                def run_chains(gens, K):
                    active = []
                    it = iter(gens)
                    done = False
                    while True:
                        while len(active) < K and not done:
                            g = next(it, None)
                            if g is None:
                                done = True
                            else:
                                active.append(g)
                        if not active:
                            break
                        for g in list(active):
                            try:
                                next(g)
                            except StopIteration:
                                active.remove(g)

                def prologue(i):
                    t0 = i * 512
                    hT = hTr[i % 2]
                    fw.dma("sp", rC[i % 2].t[:], ropeC_in[:, t0:t0 + 512], writes=[rC[i % 2].b])
                    fw.dma("sp", rS[i % 2].t[:], ropeS_in[:, t0:t0 + 512], writes=[rS[i % 2].b])
                    for s in range(4):
                        fw.dma("sp", xt[s].t[:], src.ap[t0 + s * 128:t0 + (s + 1) * 128, :], reads=[src.b(i)], writes=[xt[s].b])
                        act(hb.t[:, s, :], xt[s].t[:], AF.Square, [xt[s].b], [SB(hb, s), ss.b], accum_out=ss.t[:, s:s + 1])
                        yield
                    rstd_inplace(ss.t[:], 1024.0, ss.b)
                    yield
                    for s in range(4):
                        ts("dve", hb.t[:, s, :], xt[s].t[:], ss.t[:, s:s + 1], ALU.mult, [xt[s].b, ss.b], [SB(hb, s)])
                        yield
                    for kc in range(8):
                        p = palloc()
                        for s in range(4):
                            mm(p.t[:, s * 128:(s + 1) * 128], hb.t[:, s, kc * 128:(kc + 1) * 128], C16("ident"),
                               [SB(hb, s), cstb.b], [p.b])
                        yield
                        ts("dve", hT.t[:, kc, :], p.t[:, :], normw.t[:, kc:kc + 1], ALU.mult, [p.b, normw.b], [SB(hT, kc)])
                        prel(p)
                        yield

                rfree = list(range(NR))
                gfree = list(range(NG))

                def tile_chains(i):
                    hT = hTr[i % 2]
                    hTb = SBall(hT, 8)
                    RC, RS = rC[i % 2], rS[i % 2]

                    def fm(p, col0):
                        for kc in range(8):
                            mm(p.t[:, :], w.t[:, kc, col0:col0 + 128], hT.t[:, kc, :], [w.b, hTb[kc]], [p.b],
                               start=(kc == 0), stop=(kc == 7))

                    def tm(p, s, col0, n):
                        for kc in range(8):
                            mm(p.t[:, 0:n], hT.t[:, kc, s * 128:(s + 1) * 128], w.t[:, kc, col0:col0 + n],
                               [hTb[kc], w.b], [p.b], start=(kc == 0), stop=(kc == 7))

                    def silu_chain(col0, dst, c):
                        p = palloc()
                        fm(p, col0)
                        yield
                        act(dst.t[:, c, :], p.t[:, :], AF.Silu, [p.b], [SB(dst, c)])
                        prel(p)

                    def v_chain(s):
                        p = palloc()
                        tm(p, s, 1536, 512)
                        yield
                        cp("act", vs.t[:, s, :], p.t[:, :], [p.b], [SB(vs, s)])
                        prel(p)

                    def av_chain(s):
                        p = palloc()
                        tm(p, s, 2560, 256)
                        yield
                        cp("act", vas.t[:, s, :].rearrange("p (h e) -> p h e", e=65)[:, :, 0:64],
                           p.t[:, 0:256].rearrange("p (h e) -> p h e", e=64), [p.b], [vas.b, SB(vas, s)])
                        prel(p)

                    def norm_chain(col0, gain, dst, c, rope):
                        while not rfree:
                            yield
                        r = rfree.pop(0)
                        p = palloc()
                        fm(p, col0 + c * 128)
                        yield
                        act(sq[r].t[:], p.t[:, :], AF.Square, [p.b], [sq[r].b])
                        yield
                        p2 = palloc()
                        mm(p2.t[:, :], C16("ones64"), sq[r].t[:], [cstb.b, sq[r].b], [p2.b])
                        yield
                        act(rsf[r].t[:], p2.t[:, :], AF.Ln, [p2.b], [rsf[r].b], scale=1.0 / 64, bias=EPS)
                        prel(p2)
                        act(rsf[r].t[:], rsf[r].t[:], AF.Exp, [rsf[r].b], [rsf[r].b], scale=-0.5)
                        yield
                        if not rope:
                            stt(dst.t[:, c, :], p.t[:, :], gain.t[:, 0:1], rsf[r].t[:], ALU.mult, ALU.mult,
                                [p.b, gain.b, rsf[r].b], [SB(dst, c)])
                            prel(p)
                            rfree.append(r)
                            return
                        stt(qn[r].t[:], p.t[:, :], gain.t[:, 0:1], rsf[r].t[:], ALU.mult, ALU.mult,
                            [p.b, gain.b, rsf[r].b], [qn[r].b])
                        prel(p)
                        yield
                        cp("act", qnb[r].t[:], qn[r].t[:], [qn[r].b], [qnb[r].b])
                        tt("pool", t2[r].t[:], qn[r].t[:], RC.t[:], ALU.mult, [qn[r].b, RC.b], [t2[r].b])
                        yield
                        p3 = palloc()
                        mm(p3.t[:, :], C16("Rm"), qnb[r].t[:], [cstb.b, qnb[r].b], [p3.b])
                        yield
                        tt("dve", t1[r].t[:], p3.t[:, :], RS.t[:], ALU.mult, [p3.b, RS.b], [t1[r].b])
                        prel(p3)
                        yield
                        tt("dve", dst.t[:, c, :], t1[r].t[:], t2[r].t[:], ALU.add, [t1[r].b, t2[r].b], [SB(dst, c)])
                        rfree.append(r)

                    def fgate_chain(s, d, col0):
                        while not gfree:
                            yield
                        g = gfree.pop(0)
                        p = palloc()
                        tm(p, s, col0, 512)
                        yield
                        act(sg[g].t[:], p.t[:, :], AF.Exp, [p.b], [sg[g].b])
                        prel(p)
                        yield
                        act(sg[g].t[:], sg[g].t[:], AF.Ln, [sg[g].b], [sg[g].b], bias=1.0)
                        yield
                        act(sg[g].t[:], sg[g].t[:], AF.Exp, [sg[g].b], [sg[g].b], scale=-1.0)
                        yield
                        tt("dve", k32[g].t[:], sg[g].t[:], oml[d].t[:], ALU.mult, [sg[g].b, oml[d].b], [k32[g].b])
                        yield
                        act(sg[g].t[:], k32[g].t[:], AF.Ln, [k32[g].b], [sg[g].b], scale=-1.0, bias=1.0)
                        cp("pool", ks[d].t[:, s, :], k32[g].t[:], [k32[g].b], [SB(ks[d], s)])
                        yield
                        cp("act", lfhs[d].t[:, s, :], sg[g].t[:], [sg[g].b], [SB(lfhs[d], s)])
                        yield
                        tt("pool", lfls[d].t[:, s, :], sg[g].t[:], lfhs[d].t[:, s, :], ALU.subtract,
                           [sg[g].b, SB(lfhs[d], s)], [SB(lfls[d], s)])
                        gfree.append(g)

                    def kT_chain(d, h):
                        p = palloc()
                        for s in range(4):
                            mm(p.t[:, s * 128:(s + 1) * 128], ks[d].t[:, s, h * 128:(h + 1) * 128], C16("ident"),
                               [SB(ks[d], s), cstb.b], [p.b])
                        yield
                        cp("dve", kTs[d].t[:, h, :], p.t[:, :], [p.b], [SB(kTs[d], h)])
                        prel(p)

                    ph1 = []
                    sil = [silu_chain(c * 128, qTs, c) for c in range(4)] + [silu_chain(3072 + c * 128, gTs, c) for c in range(8)]
                    cps = []
                    for s in range(4):
                        cps += [v_chain(s), av_chain(s)]
                    for j in range(12):
                        ph1.append(sil[j])
                        if j < 8:
                            ph1.append(cps[j])
                    nrm = []
                    for (col0, gain, dst, rope) in ((2048, gq, aqTs, True), (2304, gk, akTs, True), (2816, gm, mqTs, False)):
                        for c in range(2):
                            nrm.append(norm_chain(col0, gain, dst, c, rope))
                    fg = []
                    for s in range(4):
                        fg += [fgate_chain(s, "f", 512), fgate_chain(s, "b", 1024)]
                    ph2 = []
                    for j in range(8):
                        ph2.append(fg[j])
                        if j < 6:
                            ph2.append(nrm[j])
                    ph3 = [kT_chain(d, h) for d in "fb" for h in range(4)]
                    return ph1, ph2, ph3

                def stores(i):
                    t0 = i * 512
                    fmv = lambda dd: dd.ap[:, t0:t0 + 512].rearrange("(h p) t -> p h t", p=128)
                    tmv = lambda dd: dd.ap[t0:t0 + 512, :].rearrange("(s p) f -> p s f", p=128)
                    fw.dma("pool", fmv(qT), qTs.t[:], reads=SBall(qTs, 4), writes=[qT.b(i)])
                    fw.dma("pool", fmv(gT), gTs.t[:], reads=SBall(gTs, 8), writes=[gT.b(i)])
                    fw.dma("pool", tmv(vtok), vs.t[:], reads=SBall(vs, 4), writes=[vtok.b(i)])
                    fw.dma("pool", tmv(va), vas.t[:], reads=[vas.b] + SBall(vas, 4), writes=[va.b(i)])
                    for (dst, dd) in ((aqTs, aqT), (akTs, akT), (mqTs, mqT)):
                        fw.dma("pool", fmv(dd), dst.t[:], reads=SBall(dst, 2), writes=[dd.b(i)])
                    for d in "fb":
                        fw.dma("pool", tmv(lfh[d]), lfhs[d].t[:], reads=SBall(lfhs[d], 4), writes=[lfh[d].b(i)])
                        fw.dma("pool", tmv(lfl[d]), lfls[d].t[:], reads=SBall(lfls[d], 4), writes=[lfl[d].b(i)])
                        fw.dma("pool", tmv(ktok[d]), ks[d].t[:], reads=SBall(ks[d], 4), writes=[ktok[d].b(i)])

                def stores_kT(i):
                    t0 = i * 512
                    for d in "fb":
                        fw.dma("pool", kT[d].ap[:, t0:t0 + 512].rearrange("(h p) t -> p h t", p=128), kTs[d].t[:],
                               reads=SBall(kTs[d], 4), writes=[kT[d].b(i)])

                def mark_store_reads(i):
                    pass

                run_chains([prologue(0)], 1)
                prev_ph3 = []
                for i in range(NT):
                    ph1, ph2, ph3 = tile_chains(i)
                    run_chains(prev_ph3 + ph1, 3)
                    if i > 0:
                        stores_kT(i - 1)
                    nxt = [prologue(i + 1)] if i + 1 < NT else []
                    run_chains(nxt + ph2, 4)
                    stores(i)
                    prev_ph3 = ph3
                run_chains(prev_ph3, 3)
                stores_kT(NT - 1)
                fw.barrier()

        def pass_H(l, d):
            fwd = d == "f"
            with ExitStack() as es:
                def sb(name, shape, dt):
                    return Tl(es.enter_context(nc.sbuf_tensor(f"{name}{d}_L{l}", list(shape), dt)))
                nb = 3
                lft = [sb(f"H_lfh{j}", [128, 4, 512], BF16) for j in range(nb)]
                llt = [sb(f"H_lfl{j}", [128, 4, 512], BF16) for j in range(nb)]
                kt_ = [sb(f"H_k{j}", [128, 4, 512], BF16) for j in range(nb)]
                kTt = [sb(f"H_kT{j}", [128, 4, 512], BF16) for j in range(nb)]
                qTt = [sb(f"H_qT{j}", [128, 4, 512], BF16) for j in range(nb)]
                vt = [sb(f"H_v{j}", [128, 4, 512], BF16) for j in range(nb)]
                ex3 = [sb(f"H_ex3{j}", [128, 512], F32) for j in range(2)]
                kd2 = [sb(f"H_kd{j}", [128, 4, 512], BF16) for j in range(2)]
                vm = [[sb(f"H_vm{j}_{c}", [128, 4, 512], BF16) for c in range(2)] for j in range(nb)]
                for j in range(nb):
                    for c in range(2):
                        fw.op("pool", lambda e: e.memset(vm[j][c].t[:], 0.0), [], [vm[j][c].b])
                qx = [sb(f"H_qx{j}", [128, 512], F32) for j in range(2)]
                kx = [sb(f"H_kx{j}", [128, 512], F32) for j in range(2)]
                qe2 = [sb(f"H_qe{j}", [128, 4, 512], BF16) for j in range(2)]
                ke2 = [sb(f"H_ke{j}", [128, 4, 512], BF16) for j in range(2)]
                ext2 = [sb(f"H_ext{j}", [128, 64], F32) for j in range(2)]
                PT = [sb(f"H_PT{j}", [128, 4, 128], BF16) for j in range(3)]
                S = [sb(f"H_S{h}", [128, 128], F32) for h in range(4)]
                Sm = [sb(f"H_Sm{j}", [128, 128], BF16) for j in range(16)]
                oTs = [sb(f"H_oT{j}", [128, 4, 512], F32) for j in range(2)]
                An, Bn, Kn, Mn = ("A_f", "B_f", "K_f", "M_f") if fwd else ("A_b", "B_b", "K_b", "M_b")
                K4 = sb("H_K4", [128, 4, 128], F32)
                for h in range(4):
                    cp("dve", K4.t[:, h, :], C32(Kn), [cst.b], [K4.b])
                    fw.op("dve", lambda e: e.memset(S[h].t[:], 0.0), [], [S[h].b])
                order = list(range(NT)) if fwd else list(range(NT - 1, -1, -1))
                pfree = list(ps_t)

                def palloc():
                    return pfree.pop(0)

                def prel(p):
                    pfree.append(p)

                def load(n):
                    i = order[n]
                    j = n % nb
                    t0 = i * 512
                    fw.dma("sp", lft[j].t[:], lfh[d].ap[t0:t0 + 512, :].rearrange("(s p) f -> p s f", p=128),
                           reads=[lfh[d].b(i)], writes=[lft[j].b])
                    fw.dma("sp", llt[j].t[:], lfl[d].ap[t0:t0 + 512, :].rearrange("(s p) f -> p s f", p=128),
                           reads=[lfl[d].b(i)], writes=[llt[j].b])
                    fw.dma("sp", kt_[j].t[:], ktok[d].ap[t0:t0 + 512, :].rearrange("(s p) f -> p s f", p=128),
                           reads=[ktok[d].b(i)], writes=[kt_[j].b])
                    vsrc = vtok.ap[t0:t0 + 512, :].rearrange("(s p) f -> p s f", p=128)
                    fw.dma("sp", vt[j].t[:], vsrc, reads=[vtok.b(i)], writes=[vt[j].b])
                    for c in range(2):
                        fw.dma("sp", vm[j][c].t[c * 64:(c + 1) * 64, :, :], vsrc[c * 64:(c + 1) * 64],
                               reads=[vtok.b(i)], writes=[vm[j][c].b])
                    fw.dma("sp", kTt[j].t[:], kT[d].ap[:, t0:t0 + 512].rearrange("(h p) t -> p h t", p=128),
                           reads=[kT[d].b(i)], writes=[kTt[j].b])
                    fw.dma("sp", qTt[j].t[:], qT.ap[:, t0:t0 + 512].rearrange("(h p) t -> p h t", p=128),
                           reads=[qT.b(i)], writes=[qTt[j].b])

                def ephase(n):
                    j = n % nb
                    L, L2, K_, KT_, QT_ = lft[j], llt[j], kt_[j], kTt[j], qTt[j]
                    kd, qe, ke, ext = kd2[n % 2], qe2[n % 2], ke2[n % 2], ext2[n % 2]
                    for s in range(4):
                        p = palloc()
                        mm(p.t[:, :], C16(Bn), L.t[:, s, :], [cstb.b, L.b], [p.b], start=True, stop=False)
                        mm(p.t[:, :], C16(Bn), L2.t[:, s, :], [cstb.b, L2.b], [p.b], start=False, stop=True)
                        e3 = ex3[s % 2]
                        act(e3.t[:], p.t[:, :], AF.Exp, [p.b], [e3.b])
                        prel(p)
                        tt("pool", kd.t[:, s, :], K_.t[:, s, :], e3.t[:], ALU.mult, [K_.b, e3.b], [kd.b])
                    pe_ = palloc()
                    for h in range(4):
                        for s in range(4):
                            c0 = (h * 4 + s) * 4
                            mm(pe_.t[:, c0:c0 + 4], L.t[:, s, h * 128:(h + 1) * 128], C16(Mn, 4), [L.b, cstb.b], [pe_.b],
                               start=True, stop=False)
                            mm(pe_.t[:, c0:c0 + 4], L2.t[:, s, h * 128:(h + 1) * 128], C16(Mn, 4), [L2.b, cstb.b], [pe_.b],
                               start=False, stop=True)
                    act(ext.t[:], pe_.t[:, 0:64], AF.Exp, [pe_.b], [ext.b])
                    prel(pe_)
                    for h in range(4):
                        p = palloc()
                        for s in range(4):
                            mm(p.t[:, s * 128:(s + 1) * 128], L.t[:, s, h * 128:(h + 1) * 128], C16(An), [L.b, cstb.b], [p.b],
                               start=True, stop=False)
                            mm(p.t[:, s * 128:(s + 1) * 128], L2.t[:, s, h * 128:(h + 1) * 128], C16(An), [L2.b, cstb.b], [p.b],
                               start=False, stop=True)
                        a, b_ = qx[h % 2], kx[h % 2]
                        act(a.t[:], p.t[:, :], AF.Exp, [p.b], [a.b])
                        act(b_.t[:], p.t[:, :], AF.Exp, [p.b], [b_.b], scale=-1.0)
                        prel(p)
                        tt("dve", qe.t[:, h, :], QT_.t[:, h, :], a.t[:], ALU.mult, [QT_.b, a.b], [qe.b])
                        tt("pool", ke.t[:, h, :], KT_.t[:, h, :], b_.t[:], ALU.mult, [KT_.b, b_.b], [ke.b])

                load(0)
                if NT > 1:
                    load(1)
                ephase(0)
                smi = [0]
                pti = [0]
                for n in range(NT):
                    i = order[n]
                    j = n % nb
                    t0 = i * 512
                    V_, VM = vt[j], vm[j]
                    kd, qe, ke, ext = kd2[n % 2], qe2[n % 2], ke2[n % 2], ext2[n % 2]
                    o_ = oTs[n % 2]
                    subs = list(range(4)) if fwd else list(range(3, -1, -1))

                    def stageA(s):
                        ssl = slice(s * 128, (s + 1) * 128)
                        p = palloc()
                        for h in range(4):
                            mm(p.t[:, h * 128:(h + 1) * 128], ke.t[:, h, ssl], qe.t[:, h, ssl], [ke.b, qe.b], [p.b])
                        pt = PT[pti[0] % 3]
                        pti[0] += 1
                        tt("dve", pt.t[:], p.t[:, :].rearrange("p (h t) -> p h t", h=4), K4.t[:], ALU.mult, [p.b, K4.b], [pt.b])
                        prel(p)
                        pd = [palloc(), palloc()]
                        for c in range(2):
                            for h in range(4):
                                hs = slice(h * 128, (h + 1) * 128)
                                mm(pd[c].t[:, hs], kd.t[:, s, hs], VM[c].t[:, s, hs], [kd.b, VM[c].b], [pd[c].b])
                        return pt, pd

                    def stageC(s, pt, pd):
                        ssl = slice(s * 128, (s + 1) * 128)
                        tg = t0 + s * 128
                        po = palloc()
                        for h in range(4):
                            hs = slice(h * 128, (h + 1) * 128)
                            mm(po.t[:, hs], V_.t[:, s, hs], pt.t[:, h, :], [V_.b, pt.b], [po.b], start=(h == 0), stop=False)
                        chs = (0, 1) if fwd else (1, 0)
                        for ci, c in enumerate(chs):
                            tok = tg + c * 64
                            if (fwd and tok == HALF) or ((not fwd) and tok + 64 == HALF):
                                for h in range(4):
                                    ts("dve", S[h].t[:], S[h].t[:], flag.t[:, 0:1], ALU.mult, [S[h].b, flag.b], [S[h].b])
                            sms = []
                            for h in range(4):
                                ec = (h * 4 + s) * 4 + 2 * c
                                sm_ = Sm[smi[0] % 16]
                                smi[0] += 1
                                act(sm_.t[:], S[h].t[:], AF.Copy, [S[h].b, ext.b], [sm_.b], scale=ext.t[:, ec:ec + 1])
                                sms.append(sm_)
                            for h in range(4):
                                ec = (h * 4 + s) * 4 + 2 * c
                                stt(S[h].t[:], S[h].t[:], ext.t[:, ec + 1:ec + 2], pd[c].t[:, h * 128:(h + 1) * 128],
                                    ALU.mult, ALU.add, [S[h].b, ext.b, pd[c].b], [S[h].b])
                            for h in range(4):
                                mm(po.t[:, h * 128 + c * 64:h * 128 + (c + 1) * 64], sms[h].t[:],
                                   qe.t[:, h, s * 128 + c * 64:s * 128 + (c + 1) * 64],
                                   [sms[h].b, qe.b], [po.b], start=False, stop=(ci == 1))
                        prel(pd[0])
                        prel(pd[1])
                        cp("dve", o_.t[:, :, ssl], po.t[:, :].rearrange("p (h t) -> p h t", h=4), [po.b], [o_.b])
                        prel(po)

                    if n + 2 < NT:
                        load(n + 2)
                    prev = None
                    for bi, s in enumerate(subs):
                        cur = (s,) + stageA(s)
                        if bi == 1 and n + 1 < NT:
                            ephase(n + 1)
                        if prev is not None:
                            stageC(*prev)
                        prev = cur
                    stageC(*prev)
                    fw.dma("pool", oT[d].ap[:, t0:t0 + 512].rearrange("(h p) t -> p h t", p=128), o_.t[:], reads=[o_.b],
                           writes=[oT[d].b(i)])
                fw.barrier()

        def attn_core(sb, NTq, qsrc, kv_for_tile, masks, out_row0, tag):
            LA = 5
            qm = [[sb(f"{tag}_qm{j}_{par}", [128, 2, 512], BF16) for par in range(2)] for j in range(2)]
            for j in range(2):
                for par in range(2):
                    fw.op("pool", lambda e: e.memset(qm[j][par].t[:], 0.0), [], [qm[j][par].b])
            gt = [sb(f"{tag}_g{j}", [128, 2, 512], BF16) for j in range(2)]
            pT = [sb(f"{tag}_pT{j}", [128, 512], BF16) for j in range(8)]
            sc = [sb(f"{tag}_sc{j}", [128, 512], F32) for j in range(6)] if masks is not None else None
            otok = sb(f"{tag}_otok", [128, 4, 256], BF16)
            rden = sb(f"{tag}_rden", [128, 4], F32)
            mo = [sb(f"{tag}_mo{j}", [128, 2, 512], BF16) for j in range(2)]
            pi = [0]
            for i in range(NTq):
                t0 = i * 512
                QM, G = qm[i % 2], gt[i % 2]
                qv = qsrc.ap[:, t0:t0 + 512].rearrange("(h p) t -> p h t", p=128)
                for par in range(2):
                    fw.dma("sp", QM[par].t[par * 64:(par + 1) * 64, :, :], qv[par * 64:(par + 1) * 64], reads=[qsrc.b(i)],
                           writes=[QM[par].b])
                fw.dma("sp", G.t[:], gT.ap[512 + out_row0:512 + out_row0 + 256, t0:t0 + 512].rearrange("(h p) t -> p h t", p=128),
                       reads=[gT.b(i)], writes=[G.b])
                ktl = kv_for_tile(i)
                nk = len(ktl)
                po_h = {}

                started = {}

                def stage1(h, ki):
                    ch, pr = h // 2, (h % 2) * 64
                    kfn, vfn, mi, kbufs = ktl[ki]
                    s0, s1 = SUBR[mi] if masks is not None else (0, 4)
                    cs = slice(s0 * 128, s1 * 128)
                    p = psum()
                    mm(p.t[:, cs], kfn(ch, pr), QM[h % 2].t[:, ch, cs], kbufs + [QM[h % 2].b], [p.b])
                    e_ = pT[pi[0] % 8]
                    if masks is not None:
                        s_ = sc[pi[0] % 6]
                        tt("dve", s_.t[:, cs], p.t[:, cs], masks.t[:, mi, cs], ALU.add, [p.b, masks.b], [s_.b])
                        act(e_.t[:, cs], s_.t[:, cs], AF.Exp, [s_.b], [e_.b], scale=0.125)
                    else:
                        act(e_.t[:, cs], p.t[:, cs], AF.Exp, [p.b], [e_.b], scale=0.125)
                    pi[0] += 1
                    return e_

                def stage2(h, ki, e_):
                    kfn, vfn, mi, kbufs = ktl[ki]
                    if ki == 0:
                        po_h[h] = psum_acc()
                    po = po_h[h]
                    s0, s1 = SUBR[mi] if masks is not None else (0, 4)
                    for s in range(s0, s1):
                        mm(po.t[:, s * 128:s * 128 + 65], e_.t[:, s * 128:(s + 1) * 128], vfn(h), [e_.b] + kbufs, [po.b],
                           start=(h not in started), stop=(ki == nk - 1))
                        started[h] = True
                    if ki == nk - 1:
                        pov = po.t[:, :].rearrange("p (s e) -> p s e", e=128)
                        fw.op("dve", lambda e: e.reciprocal(out=rden.t[:, :], in_=pov[:, :, 64]), [po.b], [rden.b])
                        for s in range(4):
                            ts("dve", otok.t[:, s, h * 64:(h + 1) * 64], po.t[:, s * 128:s * 128 + 64], rden.t[:, s:s + 1],
                               ALU.mult, [po.b, rden.b], [otok.b])

                pend = []
                for pair in range(2):
                    for ki in range(nk):
                        for h in (2 * pair, 2 * pair + 1):
                            pend.append((h, ki, stage1(h, ki)))
                            if len(pend) > LA:
                                stage2(*pend.pop(0))
                while pend:
                    stage2(*pend.pop(0))
                M_ = mo[i % 2]
                for ch in range(2):
                    p = psum()
                    for s in range(4):
                        mm(p.t[:, s * 128:(s + 1) * 128], otok.t[:, s, ch * 128:(ch + 1) * 128], C16("ident"),
                           [otok.b, cstb.b], [p.b])
                    tt("dve", M_.t[:, ch, :], p.t[:, :], G.t[:, ch, :], ALU.mult, [p.b, G.b], [M_.b])
                fw.dma("pool", mixT.ap[out_row0:out_row0 + 256, t0:t0 + 512].rearrange("(h p) t -> p h t", p=128), M_.t[:],
                       reads=[M_.b], writes=[mixT.b((out_row0, i))])

        def pass_AT(l):
            with ExitStack() as es:
                def sb(name, shape, dt):
                    return Tl(es.enter_context(nc.sbuf_tensor(f"{name}_L{l}", list(shape), dt)))
                masks = sb("AT_masks", [128, 20, 512], F32)
                fw.dma("sp", masks.t[:], amask_in.rearrange("r p q -> p r q"), writes=[masks.b])
                NW = 20
                kw_ = [sb(f"AT_kw{j}", [128, 2, NW * 128], BF16) for j in range(2)]
                vw_ = [sb(f"AT_vw{j}", [128, NW, 260], BF16) for j in range(2)]

                def kv_for_tile(i):
                    KW, VW = kw_[i % 2], vw_[i % 2]
                    q0 = i * 512
                    half_lo = (q0 // HALF) * HALF
                    lo = max(q0 - 1024, 0)
                    hi = min(q0 + 512 + 1024, T)
                    n = (hi - lo) // 128
                    tiles = sorted(set(range(lo // 512, (hi + 511) // 512)))
                    fw.dma("sp", KW.t[:, :, 0:n * 128], akT.ap[:, lo:hi].rearrange("(h p) t -> p h t", p=128),
                           reads=[akT.b(x) for x in tiles], writes=[KW.b])
                    fw.dma("sp", VW.t[:, 0:n, :], va.ap[lo:hi, :].rearrange("(j p) f -> p j f", p=128),
                           reads=[va.b(x) for x in tiles], writes=[VW.b])
                    out = []
                    for jt in range(n):
                        k0 = lo + jt * 128
                        r = (k0 - q0) // 128
                        assert -8 <= r <= 11
                        if not (half_lo <= k0 < half_lo + HALF):
                            ts("dve", VW.t[:, jt, :], VW.t[:, jt, :], flag.t[:, 0:1], ALU.mult, [VW.b, flag.b], [VW.b])
                        out.append(((lambda ch, pr, jt=jt, KW=KW: KW.t[:, ch, jt * 128:(jt + 1) * 128]),
                                    (lambda h, jt=jt, VW=VW: VW.t[:, jt, h * 65:(h + 1) * 65]),
                                    r + 8, [KW.b, VW.b]))
                    return out
                attn_core(sb, NT, aqT, kv_for_tile, masks, 0, "AT")
                fw.barrier()

        def pass_ME(l):
            with ExitStack() as es:
                def sb(name, shape, dt):
                    return Tl(es.enter_context(nc.sbuf_tensor(f"{name}_L{l}", list(shape), dt)))
                wkv = sb("ME_w", [128, 8, 512], BF16)
                for kc in range(8):
                    fw.dma("pool", wkv.t[:, kc, :], mem_wkv[l, kc * 128:(kc + 1) * 128, :], writes=[wkv.b])
                mnw = sb("ME_mnw", [128, 8], F32)
                col_load(mnw, mem_norm_w[l], 8)
                gk = sb("ME_gk", [128, 1], F32)
                col_load64(gk, mk_w[l])
                mkT = [sb(f"ME_mkT{g}", [128, 2, 256], BF16) for g in range(2)]
                mv = [sb(f"ME_mv{g}", [128, 2, 260], BF16) for g in range(2)]
                xm = sb("ME_x", [128, 1024], F32)
                junk = sb("ME_junk", [128, 1024], BF16)
                ssm = sb("ME_ss", [128, 1], F32)
                hbm = sb("ME_hb", [128, 2, 1024], BF16)
                hTm = sb("ME_hT", [128, 8, 256], BF16)
                sqm = sb("ME_sq", [128, 256], BF16)
                rsm = sb("ME_rs", [128, 256], F32)
                for g in range(2):
                    fw.op("dve", lambda e: e.memset(mv[g].t[:], 1.0), [], [mv[g].b])
                    for s in range(2):
                        fw.dma("sp", xm.t[:], mem_in[g, s * 128:(s + 1) * 128, :], writes=[xm.b])
                        act(junk.t[:], xm.t[:], AF.Square, [xm.b], [junk.b, ssm.b], accum_out=ssm.t[:, 0:1])
                        rstd_inplace(ssm.t[:], 1024.0, ssm.b)
                        ts("dve", hbm.t[:, s, :], xm.t[:], ssm.t[:, 0:1], ALU.mult, [xm.b, ssm.b], [hbm.b])
                    for kc in range(8):
                        p = psum()
                        for s in range(2):
                            mm(p.t[:, s * 128:(s + 1) * 128], hbm.t[:, s, kc * 128:(kc + 1) * 128], C16("ident"),
                               [hbm.b, cstb.b], [p.b])
                        ts("dve", hTm.t[:, kc, :], p.t[:, 0:256], mnw.t[:, kc:kc + 1], ALU.mult, [p.b, mnw.b], [hTm.b])
                    for c in range(2):
                        p = psum()
                        for kc in range(8):
                            mm(p.t[:, 0:256], wkv.t[:, kc, c * 128:(c + 1) * 128], hTm.t[:, kc, :], [wkv.b, hTm.b], [p.b],
                               start=(kc == 0), stop=(kc == 7))
                        act(sqm.t[:], p.t[:, 0:256], AF.Square, [p.b], [sqm.b])
                        p2 = psum()
                        mm(p2.t[:, 0:256], C16("ones64"), sqm.t[:], [cstb.b, sqm.b], [p2.b])
                        act(rsm.t[:], p2.t[:, 0:256], AF.Ln, [p2.b], [rsm.b], scale=1.0 / 64, bias=EPS)
                        act(rsm.t[:], rsm.t[:], AF.Exp, [rsm.b], [rsm.b], scale=-0.5)
                        stt(mkT[g].t[:, c, :], p.t[:, 0:256], gk.t[:, 0:1], rsm.t[:], ALU.mult, ALU.mult,
                            [p.b, gk.b, rsm.b], [mkT[g].b])
                    for s in range(2):
                        p = psum()
                        for kc in range(8):
                            mm(p.t[:, 0:256], hTm.t[:, kc, s * 128:(s + 1) * 128], wkv.t[:, kc, 256:512], [hTm.b, wkv.b], [p.b],
                               start=(kc == 0), stop=(kc == 7))
                        cp("act", mv[g].t[:, s, :].rearrange("p (h e) -> p h e", e=65)[:, :, 0:64],
                           p.t[:, 0:256].rearrange("p (h e) -> p h e", e=64), [p.b], [mv[g].b])

                def kv_for_tile(i):
                    g = (i * 512) // HALF
                    out = []
                    for jt in range(2):
                        out.append(((lambda ch, pr, jt=jt, g=g: mkT[g].t[:, ch, jt * 128:(jt + 1) * 128]),
                                    (lambda h, jt=jt, g=g: mv[g].t[:, jt, h * 65:(h + 1) * 65]),
                                    0, [mkT[g].b, mv[g].b]))
                    return out
                attn_core(sb, NT, mqT, kv_for_tile, None, 256, "ME")
                fw.barrier()

        def pass_C(l):
            src = x_dt if l == 0 else y_out
            with ExitStack() as es:
                def sb(name, shape, dt):
                    return Tl(es.enter_context(nc.sbuf_tensor(f"{name}_L{l}", list(shape), dt)))
                wo = sb("C_w", [128, 8, 1024], BF16)
                for kc in range(8):
                    fw.dma("pool", wo.t[:, kc, :], w_out[l, kc * 128:(kc + 1) * 128, :], writes=[wo.b])
                gon = sb("C_gon", [128, 1], F32)
                fw.dma("sp", gon.t[:, 0:1], onorm_w[l].rearrange("(p o) -> p o", o=1), writes=[gon.b],
                       allow_slow_non_contiguous=True)
                of_ = [sb(f"C_of{j}", [128, 4, 512], F32) for j in range(2)]
                ob_ = [sb(f"C_ob{j}", [128, 4, 512], F32) for j in range(2)]
                gh = [sb(f"C_g{j}", [128, 4, 512], BF16) for j in range(2)]
                mx_ = [sb(f"C_mx{j}", [128, 8, 512], BF16) for j in range(2)]
                xt = [sb(f"C_x{j}", [128, 4, 1024], F32) for j in range(2)]
                osum = [sb(f"C_osum{h}", [128, 512], F32) for h in range(4)]
                sq = [sb(f"C_sq{h}", [128, 512], BF16) for h in range(4)]
                rsf = [sb(f"C_rsf{h}", [128, 512], F32) for h in range(4)]
                m1 = [sb(f"C_m1{h}", [128, 512], F32) for h in range(4)]
                mxb = [[Buf() for _ in range(8)] for _ in range(2)]
                yt = [sb(f"C_y{j}", [128, 1024], F32) for j in range(2)]

                def load(i):
                    j = i % 2
                    t0 = i * 512
                    fw.dma("sp", of_[j].t[:], oT["f"].ap[:, t0:t0 + 512].rearrange("(h p) t -> p h t", p=128),
                           reads=[oT["f"].b(i)], writes=[of_[j].b])
                    fw.dma("sp", ob_[j].t[:], oT["b"].ap[:, t0:t0 + 512].rearrange("(h p) t -> p h t", p=128),
                           reads=[oT["b"].b(i)], writes=[ob_[j].b])
                    fw.dma("sp", gh[j].t[:], gT.ap[0:512, t0:t0 + 512].rearrange("(h p) t -> p h t", p=128),
                           reads=[gT.b(i)], writes=[gh[j].b])
                    fw.dma("sp", mx_[j].t[:, 4:8, :], mixT.ap[:, t0:t0 + 512].rearrange("(h p) t -> p h t", p=128),
                           reads=[mixT.b((0, i)), mixT.b((256, i))], writes=mxb[j][4:8])
                    fw.dma("sp", xt[j].t[:], src.ap[t0:t0 + 512, :].rearrange("(s p) f -> p s f", p=128),
                           reads=[src.b(i)], writes=[xt[j].b])
                yi = [0]
                H4 = range(4)

                def normphase(i):
                    j = i % 2
                    for h in H4:
                        tt("pool", osum[h].t[:], of_[j].t[:, h, :], ob_[j].t[:, h, :], ALU.add, [of_[j].b, ob_[j].b], [osum[h].b])
                    for h in H4:
                        act(sq[h].t[:], osum[h].t[:], AF.Square, [osum[h].b], [sq[h].b])
                    pp = []
                    for h in H4:
                        p = psum()
                        mm(p.t[:, :], C16("ones128"), sq[h].t[:], [cstb.b, sq[h].b], [p.b])
                        pp.append(p)
                    for h in H4:
                        act(rsf[h].t[:], pp[h].t[:, :], AF.Ln, [pp[h].b], [rsf[h].b], scale=1.0 / 128, bias=EPS)
                    for h in H4:
                        act(rsf[h].t[:], rsf[h].t[:], AF.Exp, [rsf[h].b], [rsf[h].b], scale=-0.5)
                    for h in H4:
                        stt(m1[h].t[:], osum[h].t[:], gon.t[:, 0:1], rsf[h].t[:], ALU.mult, ALU.mult,
                            [osum[h].b, gon.b, rsf[h].b], [m1[h].b])
                    for h in H4:
                        tt("dve", mx_[j].t[:, h, :], m1[h].t[:], gh[j].t[:, h, :], ALU.mult, [m1[h].b, gh[j].b], [mxb[j][h]])

                def outproj(i):
                    j = i % 2
                    t0 = i * 512
                    for s in range(4):
                        Y = yt[yi[0] % 2]
                        yi[0] += 1
                        for nh in range(2):
                            p = psum()
                            for mc in range(8):
                                mm(p.t[:, :], mx_[j].t[:, mc, s * 128:(s + 1) * 128], wo.t[:, mc, nh * 512:(nh + 1) * 512],
                                   [mxb[j][mc], wo.b], [p.b], start=(mc == 0), stop=(mc == 7))
                            tt("dve", Y.t[:, nh * 512:(nh + 1) * 512], p.t[:, :], xt[j].t[:, s, nh * 512:(nh + 1) * 512], ALU.add,
                               [p.b, xt[j].b], [Y.b])
                        fw.dma("pool", y_out.ap[t0 + s * 128:t0 + (s + 1) * 128, :], Y.t[:], reads=[Y.b], writes=[y_out.b(i)])

                load(0)
                if NT > 1:
                    load(1)
                normphase(0)
                for i in range(NT):
                    if i + 1 < NT:
                        normphase(i + 1)
                    outproj(i)
                    if i + 2 < NT:
                        load(i + 2)
                fw.barrier()

        _P = _os.environ.get("KPASSES", "A,HF,HB,AT,ME,C").split(",")
        for l in range(depth):
            if "A" in _P:
                pass_A(l)
            if "HF" in _P:
                pass_H(l, "f")
            if "HB" in _P:
                pass_H(l, "b")
            if "AT" in _P:
                pass_AT(l)
            if "ME" in _P:
                pass_ME(l)
            if "C" in _P:
                pass_C(l)
        fw.barrier()
    return nc, fw


T_CORE = 16384
DEPTH = 4
_CACHE = {}


def kernel(x_prompt, x_sample, mem_prompt, mem_sample, norm_w, w_in, hgrn_lb_fwd, hgrn_lb_bwd, hgrn_onorm_w,
           attn_qnorm_w, attn_knorm_w, mem_norm_w, mem_wkv, mem_qnorm_w, mem_knorm_w, w_out):
    f = lambda a: np.ascontiguousarray(np.asarray(a, dtype=np.float32))
    x_prompt, x_sample, mem_prompt, mem_sample = f(x_prompt), f(x_sample), f(mem_prompt), f(mem_sample)
    T = T_CORE
    if "nc" not in _CACHE:
        _CACHE["nc"] = build(T, DEPTH)[0]
        _CACHE["hc"] = host_consts(T)
    nc = _CACHE["nc"]
    hc = _CACHE["hc"]
    pos_p = np.concatenate([np.arange(8192), np.arange(8192)])
    pos_s = np.arange(16384)
    Cp, Sp = rope_tables(pos_p)
    Cs, Ss = rope_tables(pos_s)
    shared = {"cst": hc["cst"], "amask": hc["amask"], "norm_w": f(norm_w), "w_in": f(w_in),
              "hgrn_lb_fwd": f(hgrn_lb_fwd), "hgrn_lb_bwd": f(hgrn_lb_bwd), "hgrn_onorm_w": f(hgrn_onorm_w),
              "attn_qnorm_w": f(attn_qnorm_w), "attn_knorm_w": f(attn_knorm_w), "mem_norm_w": f(mem_norm_w),
              "mem_wkv": f(mem_wkv), "mem_qnorm_w": f(mem_qnorm_w), "mem_knorm_w": f(mem_knorm_w), "w_out": f(w_out)}
    in_maps = []
    for c in range(8):
        m = dict(shared)
        if c < 4:
            m["x"] = x_prompt[2 * c:2 * c + 2].reshape(T, 1024)
            m["mem"] = mem_prompt[2 * c:2 * c + 2]
            m["flag"] = np.zeros((128, 1), np.float32)
            m["ropeC"], m["ropeS"] = Cp, Sp
        else:
            s = (c - 4) % 2
            m["x"] = x_sample[s]
            m["mem"] = np.stack([mem_sample[s], mem_sample[s]])
            m["flag"] = np.ones((128, 1), np.float32)
            m["ropeC"], m["ropeS"] = Cs, Ss
        in_maps.append(m)
    res = run_bass_kernel_spmd(nc, in_maps, core_ids=list(range(8)))
    ys = [np.asarray(r["y"], dtype=np.float32) for r in res.results]
    y_prompt = np.stack([ys[c].reshape(2, 8192, 1024) for c in range(4)]).reshape(8, 8192, 1024)
    y_sample = np.stack([ys[4], ys[5]])
    return (y_prompt, y_sample)
```

```python
import math
import os as _os
from contextlib import ExitStack
import numpy as np
import concourse.bass as bass
import concourse.mybir as mybir
from concourse.bass_utils import run_bass_kernel_spmd

F32 = mybir.dt.float32
BF16 = mybir.dt.bfloat16
AF = mybir.ActivationFunctionType
ALU = mybir.AluOpType
EPS = 1e-6
NSLOT = 8


class Buf:
    __slots__ = ("w", "r")

    def __init__(self):
        self.w = None
        self.r = {}


class Tl:
    def __init__(self, t):
        self.t = t
        self.b = Buf()


class FW:
    LIM = 30000

    def __init__(self, nc):
        self.nc = nc
        self.engs = {"pe": nc.tensor, "act": nc.scalar, "dve": nc.vector, "pool": nc.gpsimd, "sp": nc.sync}
        self.cur = {}
        self.seen = {k: {} for k in self.engs}
        self.last = {}
        self.nsem = 0
        self.dslots = {"sp": [], "pool": []}
        self.dnext = {"sp": 0, "pool": 0}
        self.nins = 0

    def newsem(self):
        self.nsem += 1
        return [self.nsem, self.nc.alloc_semaphore(name=f"fs{self.nsem}")]

    def _wait(self, e, ev):
        if self.seen[e].get(ev[0], 0) >= ev[2]:
            return
        self.engs[e].wait_ge(ev[1], ev[2])
        self.seen[e][ev[0]] = ev[2]

    def _deps(self, e, reads, writes):
        for b in reads:
            if b.w is not None and not (e == "pe" and b.w[3] == "pe"):
                self._wait(e, b.w)
        for b in writes:
            if b.w is not None and not (e == "pe" and b.w[3] == "pe"):
                self._wait(e, b.w)
            for ev in b.r.values():
                if not (e == "pe" and ev[3] == "pe"):
                    self._wait(e, ev)

    def _mark(self, ev, key, reads, writes):
        for b in reads:
            b.r[key] = ev
        for b in writes:
            b.w = ev
            b.r = {}

    def op(self, e, fn, reads=(), writes=()):
        self._deps(e, reads, writes)
        ins = fn(self.engs[e])
        c = self.cur.get(e)
        if c is None or c[2] >= self.LIM:
            c = self.newsem() + [0]
            self.cur[e] = c
        c[2] += 1
        ins.then_inc(c[1], 1)
        ev = (c[0], c[1], c[2], e)
        self._mark(ev, e, reads, writes)
        self.last[e] = ev
        self.nins += 1

    def dma(self, q, out, in_, reads=(), writes=(), **kw):
        self._deps(q, reads, writes)
        slots = self.dslots[q]
        if len(slots) < NSLOT:
            slots.append(self.newsem() + [0])
            s = slots[-1]
        else:
            s = slots[self.dnext[q] % NSLOT]
        self.dnext[q] += 1
        if s[2] > 0:
            self._wait(q, (s[0], s[1], s[2], "dma"))
        if s[2] + 16 > self.LIM:
            s[:] = self.newsem() + [0]
        ins = self.engs[q].dma_start(out=out, in_=in_, **kw)
        s[2] += 16
        ins.then_inc(s[1], 16)
        ev = (s[0], s[1], s[2], "dma")
        self._mark(ev, ("d", s[0], s[2]), reads, writes)
        self.nins += 1

    def barrier(self):
        evs = [self.last[e] for e in ("pe", "act", "dve", "pool") if e in self.last]
        for q in self.dslots:
            for s in self.dslots[q]:
                if s[2] > 0:
                    evs.append((s[0], s[1], s[2], "dma"))
        for e in self.engs:
            for ev in evs:
                if e == "pe" and ev[3] == "pe":
                    continue
                self._wait(e, ev)


class DT:
    def __init__(self, ap):
        self.ap = ap
        self.bufs = {}

    def b(self, i):
        if i not in self.bufs:
            self.bufs[i] = Buf()
        return self.bufs[i]


def host_consts(T):
    c = {}
    s = np.arange(128)[:, None]
    t = np.arange(128)[None, :]
    same = (s // 64) == (t // 64)
    c0 = (t // 64) * 64
    A_f = (same & (s <= t)).astype(np.float32) - (same & (s <= c0 + 31)).astype(np.float32)
    A_b = (same & (s >= t)).astype(np.float32) - (same & (s >= c0 + 32)).astype(np.float32)
    B_f = (same & (s > t)).astype(np.float32)
    B_b = (same & (s < t)).astype(np.float32)
    M_f = np.zeros((128, 4), np.float32)
    M_b = np.zeros((128, 4), np.float32)
    sv = np.arange(128)
    for ch in range(2):
        inc = (sv // 64) == ch
        M_f[:, 2 * ch] = inc & (sv <= ch * 64 + 31)
        M_f[:, 2 * ch + 1] = inc
        M_b[:, 2 * ch] = inc & (sv >= ch * 64 + 32)
        M_b[:, 2 * ch + 1] = inc
    K_f = (same & (s <= t)).astype(np.float32)
    K_b = (same & (s >= t)).astype(np.float32)
    ident = np.eye(128, dtype=np.float32)
    ones64 = ((s // 64) == (t // 64)).astype(np.float32)
    ones128 = np.ones((128, 128), np.float32)
    Rm = np.zeros((128, 128), np.float32)
    for m in range(128):
        j = m % 64
        if j < 8:
            Rm[m + 8, m] = -1.0
        elif j < 16:
            Rm[m - 8, m] = 1.0
    cst = np.concatenate([A_f, A_b, B_f, B_b, K_f, K_b, ident, ones64, ones128, Rm, M_f, M_b], axis=1)
    c["cst"] = np.ascontiguousarray(cst.astype(np.float32))
    am = np.zeros((20, 128, 512), np.float32)
    j = np.arange(128)[:, None]
    i = np.arange(512)[None, :]
    for ri, r in enumerate(range(-8, 12)):
        d = r * 128 + j - i
        am[ri] = ((np.abs(d) <= 64).astype(np.float32)
                  + ((d % 4 == 0) & (np.abs(d) <= 256)).astype(np.float32)
                  + ((d % 16 == 0) & (np.abs(d) <= 1024)).astype(np.float32))
    c["amask"] = np.where(am > 0, 8.0 * np.log(np.maximum(am, 1.0)), -80000.0).astype(np.float32)
    return c


def rope_tables(pos):
    half = 8
    inv = (500000.0 ** (-np.arange(half, dtype=np.float32) * 2.0 / 16.0)).astype(np.float32)
    ang = pos.astype(np.float32)[None, :] * inv[:, None]
    C = np.ones((128, pos.shape[0]), np.float32)
    S = np.zeros((128, pos.shape[0]), np.float32)
    for hb in (0, 64):
        C[hb:hb + 8] = np.cos(ang)
        C[hb + 8:hb + 16] = np.cos(ang)
        S[hb:hb + 8] = np.sin(ang)
        S[hb + 8:hb + 16] = np.sin(ang)
    return C, S


def _sub_ranges():
    out = []
    j = np.arange(128)[:, None]
    i = np.arange(512)[None, :]
    for r in range(-8, 12):
        d = r * 128 + j - i
        ok = (np.abs(d) <= 64) | ((d % 4 == 0) & (np.abs(d) <= 256)) | ((d % 16 == 0) & (np.abs(d) <= 1024))
        subs = [s for s in range(4) if ok[:, s * 128:(s + 1) * 128].any()]
        out.append((min(subs), max(subs) + 1))
    return out


SUBR = _sub_ranges()
CO = {"A_f": 0, "A_b": 128, "B_f": 256, "B_b": 384, "K_f": 512, "K_b": 640, "ident": 768, "ones64": 896,
      "ones128": 1024, "Rm": 1152, "M_f": 1280, "M_b": 1284}
NCST = 1288


def build(T, depth, debug=False):
    NT = T // 512
    HALF = T // 2
    nc = bass.Bass("TRN2", target_bir_lowering=False)
    fw = FW(nc)

    def din(name, shape, dt=F32):
        return nc.dram_tensor(name, list(shape), dt, kind="ExternalInput").ap()

    x_in = din("x", [T, 1024])
    mem_in = din("mem", [2, 256, 1024])
    flag_in = din("flag", [128, 1])
    ropeC_in = din("ropeC", [128, T])
    ropeS_in = din("ropeS", [128, T])
    cst_in = din("cst", [128, NCST])
    amask_in = din("amask", [20, 128, 512])
    norm_w = din("norm_w", [depth, 1024])
    w_in = din("w_in", [depth, 1024, 4096])
    lbp = {"f": din("hgrn_lb_fwd", [depth, 512]), "b": din("hgrn_lb_bwd", [depth, 512])}
    onorm_w = din("hgrn_onorm_w", [depth, 128])
    aq_w = din("attn_qnorm_w", [depth, 64])
    ak_w = din("attn_knorm_w", [depth, 64])
    mem_norm_w = din("mem_norm_w", [depth, 1024])
    mem_wkv = din("mem_wkv", [depth, 1024, 512])
    mq_w = din("mem_qnorm_w", [depth, 64])
    mk_w = din("mem_knorm_w", [depth, 64])
    w_out = din("w_out", [depth, 1024, 1024])
    y_out = DT(nc.dram_tensor("y", [T, 1024], F32, kind="ExternalOutput").ap())
    x_dt = DT(x_in)

    skind = "ExternalOutput" if debug else "Internal"

    def scr(name, shape, dt):
        return DT(nc.dram_tensor(name, list(shape), dt, kind=skind).ap())

    qT = scr("s_qT", [512, T], BF16)
    kT = {"f": scr("s_kTf", [512, T], BF16), "b": scr("s_kTb", [512, T], BF16)}
    ktok = {"f": scr("s_kf", [T, 512], BF16), "b": scr("s_kb", [T, 512], BF16)}
    lfh = {"f": scr("s_lfhf", [T, 512], BF16), "b": scr("s_lfhb", [T, 512], BF16)}
    lfl = {"f": scr("s_lflf", [T, 512], BF16), "b": scr("s_lflb", [T, 512], BF16)}
    vtok = scr("s_v", [T, 512], BF16)
    gT = scr("s_gT", [1024, T], BF16)
    aqT = scr("s_aqT", [256, T], BF16)
    akT = scr("s_akT", [256, T], BF16)
    va = scr("s_va", [T, 260], BF16)
    mqT = scr("s_mqT", [256, T], BF16)
    oT = {"f": scr("s_ofT", [512, T], F32), "b": scr("s_obT", [512, T], F32)}
    mixT = scr("s_mixT", [512, T], BF16)

    es0 = ExitStack()
    with es0:
        def sb0(name, shape, dt):
            return Tl(es0.enter_context(nc.sbuf_tensor(name, list(shape), dt)))

        ps_t = [Tl(es0.enter_context(nc.psum_tensor(f"ps{i}", [128, 512], F32))) for i in range(8)]
        ps_i = [0]

        def psum():
            p = ps_t[ps_i[0] % 6]
            ps_i[0] += 1
            return p
        pa_i = [0]

        def psum_acc():
            p = ps_t[6 + pa_i[0] % 2]
            pa_i[0] += 1
            return p

        def mm(out, lhsT, rhs, reads, writes, start=True, stop=True):
            fw.op("pe", lambda e: e.matmul(out, lhsT=lhsT, rhs=rhs, start=start, stop=stop), reads, writes)

        def act(out, in_, func, reads, writes, **kw):
            fw.op("act", lambda e: e.activation(out=out, in_=in_, func=func, **kw), reads, writes)

        def tt(eng, out, in0, in1, op, reads, writes):
            fw.op(eng, lambda e: e.tensor_tensor(out=out, in0=in0, in1=in1, op=op), reads, writes)

        def ts(eng, out, in0, s1, op0, reads, writes, s2=None, op1=None):
            if op1 is None:
                fw.op(eng, lambda e: e.tensor_scalar(out=out, in0=in0, scalar1=s1, scalar2=None, op0=op0), reads, writes)
            else:
                fw.op(eng, lambda e: e.tensor_scalar(out=out, in0=in0, scalar1=s1, scalar2=s2, op0=op0, op1=op1),
                      reads, writes)

        def stt(out, in0, scalar, in1, op0, op1, reads, writes):
            fw.op("dve", lambda e: e.scalar_tensor_tensor(out=out, in0=in0, scalar=scalar, in1=in1, op0=op0, op1=op1),
                  reads, writes)

        def cp(eng, out, in_, reads, writes):
            if eng == "act":
                fw.op("act", lambda e: e.copy(out=out, in_=in_), reads, writes)
            else:
                fw.op(eng, lambda e: e.tensor_copy(out=out, in_=in_), reads, writes)

        cst = sb0("cst_sb", [128, NCST], F32)
        fw.dma("sp", cst.t[:], cst_in[:, :], writes=[cst.b])
        cstb = sb0("cstb_sb", [128, NCST], BF16)
        cp("dve", cstb.t[:], cst.t[:], [cst.b], [cstb.b])
        flag = sb0("flag_sb", [128, 1], F32)
        fw.dma("sp", flag.t[:], flag_in[:, :], writes=[flag.b])

        def C32(name, w=128):
            return cst.t[:, CO[name]:CO[name] + w]

        def C16(name, w=128):
            return cstb.t[:, CO[name]:CO[name] + w]

        def col_load(dst, src1d, n):
            fw.dma("sp", dst.t[:, 0:n], src1d.rearrange("(c p) -> p c", p=128), writes=[dst.b],
                   allow_slow_non_contiguous=True)

        def col_load64(dst, src1d):
            for hb in (0, 64):
                fw.dma("sp", dst.t[hb:hb + 64, 0:1], src1d.rearrange("(p o) -> p o", o=1), writes=[dst.b],
                       allow_slow_non_contiguous=True)

        def rstd_inplace(tl_ap, n, reads_b):
            act(tl_ap, tl_ap, AF.Ln, [reads_b], [reads_b], scale=1.0 / n, bias=EPS)
            act(tl_ap, tl_ap, AF.Exp, [reads_b], [reads_b], scale=-0.5)

        def pass_A(l):
            src = x_dt if l == 0 else y_out
            with ExitStack() as es:
                def sb(name, shape, dt):
                    return Tl(es.enter_context(nc.sbuf_tensor(f"{name}_L{l}", list(shape), dt)))
                w = sb("A_w", [128, 8, 4096], BF16)
                for kc in range(8):
                    for q4 in range(4):
                        fw.dma("pool", w.t[:, kc, q4 * 1024:(q4 + 1) * 1024],
                               w_in[l, kc * 128:(kc + 1) * 128, q4 * 1024:(q4 + 1) * 1024], writes=[w.b])
                normw = sb("A_normw", [128, 8], F32)
                col_load(normw, norm_w[l], 8)
                gq = sb("A_gq", [128, 1], F32)
                gk = sb("A_gk", [128, 1], F32)
                gm = sb("A_gm", [128, 1], F32)
                col_load64(gq, aq_w[l])
                col_load64(gk, ak_w[l])
                col_load64(gm, mq_w[l])
                oml = {d: sb(f"A_oml{d}", [128, 512], F32) for d in "fb"}
                with ExitStack() as es2:
                    def sb2(name, shape, dt):
                        return Tl(es2.enter_context(nc.sbuf_tensor(f"{name}_L{l}", list(shape), dt)))
                    row = sb2("A_lbrow", [1, depth, 512], F32)
                    mx = sb2("A_lbmx", [1, 512], F32)
                    sm = sb2("A_lbsm", [1, 512], F32)
                    acc = sb2("A_lbacc", [1, 512], F32)
                    tmp = sb2("A_lbtmp", [1, 512], F32)
                    for d in ("f", "b"):
                        fw.dma("sp", row.t[0:1, :, :], lbp[d].rearrange("(o l) f -> o l f", o=1), writes=[row.b])
                        cp("dve", mx.t[:], row.t[0:1, 0, :], [row.b], [mx.b])
                        for j in range(1, depth):
                            tt("dve", mx.t[:], mx.t[:], row.t[0:1, j, :], ALU.max, [mx.b, row.b], [mx.b])
                        for j in range(depth):
                            tt("dve", row.t[0:1, j, :], row.t[0:1, j, :], mx.t[:], ALU.subtract, [row.b, mx.b], [row.b])
                        act(row.t[:], row.t[:], AF.Exp, [row.b], [row.b])
                        cp("dve", sm.t[:], row.t[0:1, 0, :], [row.b], [sm.b])
                        for j in range(1, depth):
                            tt("dve", sm.t[:], sm.t[:], row.t[0:1, j, :], ALU.add, [sm.b, row.b], [sm.b])
                        fw.op("dve", lambda e: e.reciprocal(out=sm.t[:], in_=sm.t[:]), [sm.b], [sm.b])
                        fw.op("dve", lambda e: e.memset(acc.t[:], 0.0), [], [acc.b])
                        for j in range(1, l + 1):
                            tt("dve", tmp.t[:], row.t[0:1, j, :], sm.t[:], ALU.mult, [row.b, sm.b], [tmp.b])
                            tt("dve", acc.t[:], acc.t[:], tmp.t[:], ALU.add, [acc.b, tmp.b], [acc.b])
                        ts("dve", acc.t[:], acc.t[:], -1.0, ALU.mult, [acc.b], [acc.b], s2=1.0, op1=ALU.add)
                        p = psum()
                        mm(p.t[:, :], C32("ones128")[0:1, :], acc.t[0:1, :], [cst.b, acc.b], [p.b])
                        cp("dve", oml[d].t[:], p.t[:, :], [p.b], [oml[d].b])
                    fw.barrier()

                xt = [sb(f"A_x{i}", [128, 1024], F32) for i in range(4)]
                ss = sb("A_ss", [128, 4], F32)
                hb = sb("A_hb", [128, 4, 1024], BF16)
                hTr = [sb(f"A_hT{j}", [128, 8, 512], BF16) for j in range(2)]
                qTs = sb("A_qTs", [128, 4, 512], BF16)
                gTs = sb("A_gTs", [128, 8, 512], BF16)
                aqTs = sb("A_aqTs", [128, 2, 512], BF16)
                akTs = sb("A_akTs", [128, 2, 512], BF16)
                mqTs = sb("A_mqTs", [128, 2, 512], BF16)
                lfhs = {d: sb(f"A_lfhs{d}", [128, 4, 512], BF16) for d in "fb"}
                lfls = {d: sb(f"A_lfls{d}", [128, 4, 512], BF16) for d in "fb"}
                ks = {d: sb(f"A_ks{d}", [128, 4, 512], BF16) for d in "fb"}
                kTs = {d: sb(f"A_kTs{d}", [128, 4, 512], BF16) for d in "fb"}
                vs = sb("A_vs", [128, 4, 512], BF16)
                vas = sb("A_vas", [128, 4, 260], BF16)
                fw.op("dve", lambda e: e.memset(vas.t[:], 1.0), [], [vas.b])
                NR = 2
                sq = [sb(f"A_sq{j}", [128, 512], BF16) for j in range(NR)]
                rsf = [sb(f"A_rsf{j}", [128, 512], F32) for j in range(NR)]
                qn = [sb(f"A_qn{j}", [128, 512], F32) for j in range(NR)]
                qnb = [sb(f"A_qnb{j}", [128, 512], BF16) for j in range(NR)]
                t1 = rsf
                t2 = [sb(f"A_t2{j}", [128, 512], F32) for j in range(NR)]
                rC = [sb(f"A_rC{j}", [128, 512], F32) for j in range(2)]
                rS = [sb(f"A_rS{j}", [128, 512], F32) for j in range(2)]
                NG = 2
                sg = [sb(f"A_sg{j}", [128, 512], F32) for j in range(NG)]
                k32 = [sb(f"A_k32{j}", [128, 512], F32) for j in range(NG)]
                sub_b = {}

                def SB(tl, idx):
                    key = (id(tl), idx)
                    if key not in sub_b:
                        sub_b[key] = Buf()
                    return sub_b[key]

                def SBall(tl, n):
                    return [SB(tl, j) for j in range(n)]
                pfree = list(ps_t)

                def palloc():
                    return pfree.pop(0)

                def prel(p):
                    pfree.append(p)

                def run_chains(gens, K):
                    active = []
                    it = iter(gens)
                    done = False
                    while True:
                        while len(active) < K and not done:
                            g = next(it, None)
                            if g is None:
                                done = True
                            else:
                                active.append(g)
                        if not active:
                            break
                        for g in list(active):
                            try:
                                next(g)
                            except StopIteration:
                                active.remove(g)

                def prologue(i):
                    t0 = i * 512
                    hT = hTr[i % 2]
                    fw.dma("sp", rC[i % 2].t[:], ropeC_in[:, t0:t0 + 512], writes=[rC[i % 2].b])
                    fw.dma("sp", rS[i % 2].t[:], ropeS_in[:, t0:t0 + 512], writes=[rS[i % 2].b])
                    for s in range(4):
                        fw.dma("sp", xt[s].t[:], src.ap[t0 + s * 128:t0 + (s + 1) * 128, :], reads=[src.b(i)], writes=[xt[s].b])
                        act(hb.t[:, s, :], xt[s].t[:], AF.Square, [xt[s].b], [SB(hb, s), ss.b], accum_out=ss.t[:, s:s + 1])
                        yield
                    rstd_inplace(ss.t[:], 1024.0, ss.b)
                    yield
                    for s in range(4):
                        ts("dve", hb.t[:, s, :], xt[s].t[:], ss.t[:, s:s + 1], ALU.mult, [xt[s].b, ss.b], [SB(hb, s)])
                        yield
                    for kc in range(8):
                        p = palloc()
                        for s in range(4):
                            mm(p.t[:, s * 128:(s + 1) * 128], hb.t[:, s, kc * 128:(kc + 1) * 128], C16("ident"),
                               [SB(hb, s), cstb.b], [p.b])
                        yield
                        ts("dve", hT.t[:, kc, :], p.t[:, :], normw.t[:, kc:kc + 1], ALU.mult, [p.b, normw.b], [SB(hT, kc)])
                        prel(p)
                        yield

                rfree = list(range(NR))
                gfree = list(range(NG))

                def tile_chains(i):
                    hT = hTr[i % 2]
                    hTb = SBall(hT, 8)
                    RC, RS = rC[i % 2], rS[i % 2]

                    def fm(p, col0):
                        for kc in range(8):
                            mm(p.t[:, :], w.t[:, kc, col0:col0 + 128], hT.t[:, kc, :], [w.b, hTb[kc]], [p.b],
                               start=(kc == 0), stop=(kc == 7))

                    def tm(p, s, col0, n):
                        for kc in range(8):
                            mm(p.t[:, 0:n], hT.t[:, kc, s * 128:(s + 1) * 128], w.t[:, kc, col0:col0 + n],
                               [hTb[kc], w.b], [p.b], start=(kc == 0), stop=(kc == 7))

                    def silu_chain(col0, dst, c):
                        p = palloc()
                        fm(p, col0)
                        yield
                        act(dst.t[:, c, :], p.t[:, :], AF.Silu, [p.b], [SB(dst, c)])
                        prel(p)

                    def v_chain(s):
                        p = palloc()
                        tm(p, s, 1536, 512)
                        yield
                        cp("act", vs.t[:, s, :], p.t[:, :], [p.b], [SB(vs, s)])
                        prel(p)

                    def av_chain(s):
                        p = palloc()
                        tm(p, s, 2560, 256)
                        yield
                        cp("act", vas.t[:, s, :].rearrange("p (h e) -> p h e", e=65)[:, :, 0:64],
                           p.t[:, 0:256].rearrange("p (h e) -> p h e", e=64), [p.b], [vas.b, SB(vas, s)])
                        prel(p)

                    def norm_chain(col0, gain, dst, c, rope):
                        while not rfree:
                            yield
                        r = rfree.pop(0)
                        p = palloc()
                        fm(p, col0 + c * 128)
                        yield
                        act(sq[r].t[:], p.t[:, :], AF.Square, [p.b], [sq[r].b])
                        yield
                        p2 = palloc()
                        mm(p2.t[:, :], C16("ones64"), sq[r].t[:], [cstb.b, sq[r].b], [p2.b])
                        yield
                        act(rsf[r].t[:], p2.t[:, :], AF.Ln, [p2.b], [rsf[r].b], scale=1.0 / 64, bias=EPS)
                        prel(p2)
                        act(rsf[r].t[:], rsf[r].t[:], AF.Exp, [rsf[r].b], [rsf[r].b], scale=-0.5)
                        yield
                        if not rope:
                            stt(dst.t[:, c, :], p.t[:, :], gain.t[:, 0:1], rsf[r].t[:], ALU.mult, ALU.mult,
                                [p.b, gain.b, rsf[r].b], [SB(dst, c)])
                            prel(p)
                            rfree.append(r)
                            return
                        stt(qn[r].t[:], p.t[:, :], gain.t[:, 0:1], rsf[r].t[:], ALU.mult, ALU.mult,
                            [p.b, gain.b, rsf[r].b], [qn[r].b])
                        prel(p)
                        yield
                        cp("act", qnb[r].t[:], qn[r].t[:], [qn[r].b], [qnb[r].b])
                        tt("pool", t2[r].t[:], qn[r].t[:], RC.t[:], ALU.mult, [qn[r].b, RC.b], [t2[r].b])
                        yield
                        p3 = palloc()
                        mm(p3.t[:, :], C16("Rm"), qnb[r].t[:], [cstb.b, qnb[r].b], [p3.b])
                        yield
                        tt("dve", t1[r].t[:], p3.t[:, :], RS.t[:], ALU.mult, [p3.b, RS.b], [t1[r].b])
                        prel(p3)
                        yield
                        tt("dve", dst.t[:, c, :], t1[r].t[:], t2[r].t[:], ALU.add, [t1[r].b, t2[r].b], [SB(dst, c)])
                        rfree.append(r)

                    def fgate_chain(s, d, col0):
                        while not gfree:
                            yield
                        g = gfree.pop(0)
                        p = palloc()
                        tm(p, s, col0, 512)
                        yield
                        act(sg[g].t[:], p.t[:, :], AF.Exp, [p.b], [sg[g].b])
                        prel(p)
                        yield
                        act(sg[g].t[:], sg[g].t[:], AF.Ln, [sg[g].b], [sg[g].b], bias=1.0)
                        yield
                        act(sg[g].t[:], sg[g].t[:], AF.Exp, [sg[g].b], [sg[g].b], scale=-1.0)
                        yield
                        tt("dve", k32[g].t[:], sg[g].t[:], oml[d].t[:], ALU.mult, [sg[g].b, oml[d].b], [k32[g].b])
                        yield
                        act(sg[g].t[:], k32[g].t[:], AF.Ln, [k32[g].b], [sg[g].b], scale=-1.0, bias=1.0)
                        cp("pool", ks[d].t[:, s, :], k32[g].t[:], [k32[g].b], [SB(ks[d], s)])
                        yield
                        cp("act", lfhs[d].t[:, s, :], sg[g].t[:], [sg[g].b], [SB(lfhs[d], s)])
                        yield
                        tt("pool", lfls[d].t[:, s, :], sg[g].t[:], lfhs[d].t[:, s, :], ALU.subtract,
                           [sg[g].b, SB(lfhs[d], s)], [SB(lfls[d], s)])
                        gfree.append(g)

                    def kT_chain(d, h):
                        p = palloc()
                        for s in range(4):
                            mm(p.t[:, s * 128:(s + 1) * 128], ks[d].t[:, s, h * 128:(h + 1) * 128], C16("ident"),
                               [SB(ks[d], s), cstb.b], [p.b])
                        yield
                        cp("dve", kTs[d].t[:, h, :], p.t[:, :], [p.b], [SB(kTs[d], h)])
                        prel(p)

                    ph1 = []
                    sil = [silu_chain(c * 128, qTs, c) for c in range(4)] + [silu_chain(3072 + c * 128, gTs, c) for c in range(8)]
                    cps = []
                    for s in range(4):
                        cps += [v_chain(s), av_chain(s)]
                    for j in range(12):
                        ph1.append(sil[j])
                        if j < 8:
                            ph1.append(cps[j])
                    nrm = []
                    for (col0, gain, dst, rope) in ((2048, gq, aqTs, True), (2304, gk, akTs, True), (2816, gm, mqTs, False)):
                        for c in range(2):
                            nrm.append(norm_chain(col0, gain, dst, c, rope))
                    fg = []
                    for s in range(4):
                        fg += [fgate_chain(s, "f", 512), fgate_chain(s, "b", 1024)]
                    ph2 = []
                    for j in range(8):
                        ph2.append(fg[j])
                        if j < 6:
                            ph2.append(nrm[j])
                    ph3 = [kT_chain(d, h) for d in "fb" for h in range(4)]
                    return ph1, ph2, ph3

                def stores(i):
                    t0 = i * 512
                    fmv = lambda dd: dd.ap[:, t0:t0 + 512].rearrange("(h p) t -> p h t", p=128)
                    tmv = lambda dd: dd.ap[t0:t0 + 512, :].rearrange("(s p) f -> p s f", p=128)
                    fw.dma("pool", fmv(qT), qTs.t[:], reads=SBall(qTs, 4), writes=[qT.b(i)])
                    fw.dma("pool", fmv(gT), gTs.t[:], reads=SBall(gTs, 8), writes=[gT.b(i)])
                    fw.dma("pool", tmv(vtok), vs.t[:], reads=SBall(vs, 4), writes=[vtok.b(i)])
                    fw.dma("pool", tmv(va), vas.t[:], reads=[vas.b] + SBall(vas, 4), writes=[va.b(i)])
                    for (dst, dd) in ((aqTs, aqT), (akTs, akT), (mqTs, mqT)):
                        fw.dma("pool", fmv(dd), dst.t[:], reads=SBall(dst, 2), writes=[dd.b(i)])
                    for d in "fb":
                        fw.dma("pool", tmv(lfh[d]), lfhs[d].t[:], reads=SBall(lfhs[d], 4), writes=[lfh[d].b(i)])
                        fw.dma("pool", tmv(lfl[d]), lfls[d].t[:], reads=SBall(lfls[d], 4), writes=[lfl[d].b(i)])
                        fw.dma("pool", tmv(ktok[d]), ks[d].t[:], reads=SBall(ks[d], 4), writes=[ktok[d].b(i)])
                        fw.dma("pool", fmv(kT[d]), kTs[d].t[:], reads=SBall(kTs[d], 4), writes=[kT[d].b(i)])

                def mark_store_reads(i):
                    pass

                run_chains([prologue(0)], 1)
                for i in range(NT):
                    ph1, ph2, ph3 = tile_chains(i)
                    run_chains(ph1, 3)
                    nxt = [prologue(i + 1)] if i + 1 < NT else []
                    run_chains(nxt + ph2, 4)
                    run_chains(ph3, 3)
                    stores(i)
                fw.barrier()

        def pass_H(l, d):
            fwd = d == "f"
            with ExitStack() as es:
                def sb(name, shape, dt):
                    return Tl(es.enter_context(nc.sbuf_tensor(f"{name}{d}_L{l}", list(shape), dt)))
                nb = 3
                lft = [sb(f"H_lfh{j}", [128, 4, 512], BF16) for j in range(nb)]
                llt = [sb(f"H_lfl{j}", [128, 4, 512], BF16) for j in range(nb)]
                kt_ = [sb(f"H_k{j}", [128, 4, 512], BF16) for j in range(nb)]
                kTt = [sb(f"H_kT{j}", [128, 4, 512], BF16) for j in range(nb)]
                qTt = [sb(f"H_qT{j}", [128, 4, 512], BF16) for j in range(nb)]
                vt = [sb(f"H_v{j}", [128, 4, 512], BF16) for j in range(nb)]
                ex3 = [sb(f"H_ex3{j}", [128, 512], F32) for j in range(2)]
                kd2 = [sb(f"H_kd{j}", [128, 4, 512], BF16) for j in range(2)]
                vm = [[sb(f"H_vm{j}_{c}", [128, 4, 512], BF16) for c in range(2)] for j in range(nb)]
                for j in range(nb):
                    for c in range(2):
                        fw.op("pool", lambda e: e.memset(vm[j][c].t[:], 0.0), [], [vm[j][c].b])
                qx = [sb(f"H_qx{j}", [128, 512], F32) for j in range(2)]
                kx = [sb(f"H_kx{j}", [128, 512], F32) for j in range(2)]
                qe2 = [sb(f"H_qe{j}", [128, 4, 512], BF16) for j in range(2)]
                ke2 = [sb(f"H_ke{j}", [128, 4, 512], BF16) for j in range(2)]
                ext2 = [sb(f"H_ext{j}", [128, 64], F32) for j in range(2)]
                PT = [sb(f"H_PT{j}", [128, 4, 128], BF16) for j in range(3)]
                S = [sb(f"H_S{h}", [128, 128], F32) for h in range(4)]
                Sm = [sb(f"H_Sm{j}", [128, 128], BF16) for j in range(16)]
                oTs = [sb(f"H_oT{j}", [128, 4, 512], F32) for j in range(2)]
                An, Bn, Kn, Mn = ("A_f", "B_f", "K_f", "M_f") if fwd else ("A_b", "B_b", "K_b", "M_b")
                K4 = sb("H_K4", [128, 4, 128], F32)
                for h in range(4):
                    cp("dve", K4.t[:, h, :], C32(Kn), [cst.b], [K4.b])
                    fw.op("dve", lambda e: e.memset(S[h].t[:], 0.0), [], [S[h].b])
                order = list(range(NT)) if fwd else list(range(NT - 1, -1, -1))
                pfree = list(ps_t)

                def palloc():
                    return pfree.pop(0)

                def prel(p):
                    pfree.append(p)

                def load(n):
                    i = order[n]
                    j = n % nb
                    t0 = i * 512
                    fw.dma("sp", lft[j].t[:], lfh[d].ap[t0:t0 + 512, :].rearrange("(s p) f -> p s f", p=128),
                           reads=[lfh[d].b(i)], writes=[lft[j].b])
                    fw.dma("sp", llt[j].t[:], lfl[d].ap[t0:t0 + 512, :].rearrange("(s p) f -> p s f", p=128),
                           reads=[lfl[d].b(i)], writes=[llt[j].b])
                    fw.dma("sp", kt_[j].t[:], ktok[d].ap[t0:t0 + 512, :].rearrange("(s p) f -> p s f", p=128),
                           reads=[ktok[d].b(i)], writes=[kt_[j].b])
                    vsrc = vtok.ap[t0:t0 + 512, :].rearrange("(s p) f -> p s f", p=128)
                    fw.dma("sp", vt[j].t[:], vsrc, reads=[vtok.b(i)], writes=[vt[j].b])
                    for c in range(2):
                        fw.dma("sp", vm[j][c].t[c * 64:(c + 1) * 64, :, :], vsrc[c * 64:(c + 1) * 64],
                               reads=[vtok.b(i)], writes=[vm[j][c].b])
                    fw.dma("sp", kTt[j].t[:], kT[d].ap[:, t0:t0 + 512].rearrange("(h p) t -> p h t", p=128),
                           reads=[kT[d].b(i)], writes=[kTt[j].b])
                    fw.dma("sp", qTt[j].t[:], qT.ap[:, t0:t0 + 512].rearrange("(h p) t -> p h t", p=128),
                           reads=[qT.b(i)], writes=[qTt[j].b])

                def ephase(n):
                    j = n % nb
                    L, L2, K_, KT_, QT_ = lft[j], llt[j], kt_[j], kTt[j], qTt[j]
                    kd, qe, ke, ext = kd2[n % 2], qe2[n % 2], ke2[n % 2], ext2[n % 2]
                    for s in range(4):
                        p = palloc()
                        mm(p.t[:, :], C16(Bn), L.t[:, s, :], [cstb.b, L.b], [p.b], start=True, stop=False)
                        mm(p.t[:, :], C16(Bn), L2.t[:, s, :], [cstb.b, L2.b], [p.b], start=False, stop=True)
                        e3 = ex3[s % 2]
                        act(e3.t[:], p.t[:, :], AF.Exp, [p.b], [e3.b])
                        prel(p)
                        tt("pool", kd.t[:, s, :], K_.t[:, s, :], e3.t[:], ALU.mult, [K_.b, e3.b], [kd.b])
                    pe_ = palloc()
                    for h in range(4):
                        for s in range(4):
                            c0 = (h * 4 + s) * 4
                            mm(pe_.t[:, c0:c0 + 4], L.t[:, s, h * 128:(h + 1) * 128], C16(Mn, 4), [L.b, cstb.b], [pe_.b],
                               start=True, stop=False)
                            mm(pe_.t[:, c0:c0 + 4], L2.t[:, s, h * 128:(h + 1) * 128], C16(Mn, 4), [L2.b, cstb.b], [pe_.b],
                               start=False, stop=True)
                    act(ext.t[:], pe_.t[:, 0:64], AF.Exp, [pe_.b], [ext.b])
                    prel(pe_)
                    for h in range(4):
                        p = palloc()
                        for s in range(4):
                            mm(p.t[:, s * 128:(s + 1) * 128], L.t[:, s, h * 128:(h + 1) * 128], C16(An), [L.b, cstb.b], [p.b],
                               start=True, stop=False)
                            mm(p.t[:, s * 128:(s + 1) * 128], L2.t[:, s, h * 128:(h + 1) * 128], C16(An), [L2.b, cstb.b], [p.b],
                               start=False, stop=True)
                        a, b_ = qx[h % 2], kx[h % 2]
                        act(a.t[:], p.t[:, :], AF.Exp, [p.b], [a.b])
                        act(b_.t[:], p.t[:, :], AF.Exp, [p.b], [b_.b], scale=-1.0)
                        prel(p)
                        tt("dve", qe.t[:, h, :], QT_.t[:, h, :], a.t[:], ALU.mult, [QT_.b, a.b], [qe.b])
                        tt("pool", ke.t[:, h, :], KT_.t[:, h, :], b_.t[:], ALU.mult, [KT_.b, b_.b], [ke.b])

                load(0)
                if NT > 1:
                    load(1)
                ephase(0)
                smi = [0]
                pti = [0]
                for n in range(NT):
                    i = order[n]
                    j = n % nb
                    t0 = i * 512
                    V_, VM = vt[j], vm[j]
                    kd, qe, ke, ext = kd2[n % 2], qe2[n % 2], ke2[n % 2], ext2[n % 2]
                    o_ = oTs[n % 2]
                    subs = list(range(4)) if fwd else list(range(3, -1, -1))

                    def stageA(s):
                        ssl = slice(s * 128, (s + 1) * 128)
                        p = palloc()
                        for h in range(4):
                            mm(p.t[:, h * 128:(h + 1) * 128], ke.t[:, h, ssl], qe.t[:, h, ssl], [ke.b, qe.b], [p.b])
                        pt = PT[pti[0] % 3]
                        pti[0] += 1
                        tt("dve", pt.t[:], p.t[:, :].rearrange("p (h t) -> p h t", h=4), K4.t[:], ALU.mult, [p.b, K4.b], [pt.b])
                        prel(p)
                        pd = [palloc(), palloc()]
                        for c in range(2):
                            for h in range(4):
                                hs = slice(h * 128, (h + 1) * 128)
                                mm(pd[c].t[:, hs], kd.t[:, s, hs], VM[c].t[:, s, hs], [kd.b, VM[c].b], [pd[c].b])
                        return pt, pd

                    def stageC(s, pt, pd):
                        ssl = slice(s * 128, (s + 1) * 128)
                        tg = t0 + s * 128
                        po = palloc()
                        for h in range(4):
                            hs = slice(h * 128, (h + 1) * 128)
                            mm(po.t[:, hs], V_.t[:, s, hs], pt.t[:, h, :], [V_.b, pt.b], [po.b], start=(h == 0), stop=False)
                        chs = (0, 1) if fwd else (1, 0)
                        for ci, c in enumerate(chs):
                            tok = tg + c * 64
                            if (fwd and tok == HALF) or ((not fwd) and tok + 64 == HALF):
                                for h in range(4):
                                    ts("dve", S[h].t[:], S[h].t[:], flag.t[:, 0:1], ALU.mult, [S[h].b, flag.b], [S[h].b])
                            sms = []
                            for h in range(4):
                                ec = (h * 4 + s) * 4 + 2 * c
                                sm_ = Sm[smi[0] % 16]
                                smi[0] += 1
                                act(sm_.t[:], S[h].t[:], AF.Copy, [S[h].b, ext.b], [sm_.b], scale=ext.t[:, ec:ec + 1])
                                sms.append(sm_)
                            for h in range(4):
                                ec = (h * 4 + s) * 4 + 2 * c
                                stt(S[h].t[:], S[h].t[:], ext.t[:, ec + 1:ec + 2], pd[c].t[:, h * 128:(h + 1) * 128],
                                    ALU.mult, ALU.add, [S[h].b, ext.b, pd[c].b], [S[h].b])
                            for h in range(4):
                                mm(po.t[:, h * 128 + c * 64:h * 128 + (c + 1) * 64], sms[h].t[:],
                                   qe.t[:, h, s * 128 + c * 64:s * 128 + (c + 1) * 64],
                                   [sms[h].b, qe.b], [po.b], start=False, stop=(ci == 1))
                        prel(pd[0])
                        prel(pd[1])
                        cp("dve", o_.t[:, :, ssl], po.t[:, :].rearrange("p (h t) -> p h t", h=4), [po.b], [o_.b])
                        prel(po)

                    if n + 2 < NT:
                        load(n + 2)
                    prev = None
                    for bi, s in enumerate(subs):
                        cur = (s,) + stageA(s)
                        if bi == 1 and n + 1 < NT:
                            ephase(n + 1)
                        if prev is not None:
                            stageC(*prev)
                        prev = cur
                    stageC(*prev)
                    fw.dma("pool", oT[d].ap[:, t0:t0 + 512].rearrange("(h p) t -> p h t", p=128), o_.t[:], reads=[o_.b],
                           writes=[oT[d].b(i)])
                fw.barrier()

        def attn_core(sb, NTq, qsrc, kv_for_tile, masks, out_row0, tag, mhl=None):
            LA = 5
            qm = [[sb(f"{tag}_qm{j}_{par}", [128, 2, 512], BF16) for par in range(2)] for j in range(2)]
            for j in range(2):
                for par in range(2):
                    fw.op("pool", lambda e: e.memset(qm[j][par].t[:], 0.0), [], [qm[j][par].b])
            gt = [sb(f"{tag}_g{j}", [128, 2, 512], BF16) for j in range(2)]
            pT = [sb(f"{tag}_pT{j}", [128, 512], BF16) for j in range(8)]
            sc = [sb(f"{tag}_sc{j}", [128, 512], F32) for j in range(6)] if masks is not None else None
            otok = sb(f"{tag}_otok", [128, 4, 256], BF16)
            rden = sb(f"{tag}_rden", [128, 4], F32)
            mo = [sb(f"{tag}_mo{j}", [128, 2, 512], BF16) for j in range(2)]
            pi = [0]
            for i in range(NTq):
                t0 = i * 512
                QM, G = qm[i % 2], gt[i % 2]
                qv = qsrc.ap[:, t0:t0 + 512].rearrange("(h p) t -> p h t", p=128)
                for par in range(2):
                    fw.dma("sp", QM[par].t[par * 64:(par + 1) * 64, :, :], qv[par * 64:(par + 1) * 64], reads=[qsrc.b(i)],
                           writes=[QM[par].b])
                fw.dma("sp", G.t[:], gT.ap[512 + out_row0:512 + out_row0 + 256, t0:t0 + 512].rearrange("(h p) t -> p h t", p=128),
                       reads=[gT.b(i)], writes=[G.b])
                ktl = kv_for_tile(i)
                nk = len(ktl)
                po_h = {}

                started = {}

                def stage1(h, ki):
                    ch, pr = h // 2, (h % 2) * 64
                    kfn, vfn, mi, kbufs = ktl[ki]
                    s0, s1 = SUBR[mi] if masks is not None else (0, 4)
                    cs = slice(s0 * 128, s1 * 128)
                    p = psum()
                    on_pe = masks is not None and mhl is not None and pi[0] % 3 == 2
                    mm(p.t[:, cs], kfn(ch, pr), QM[h % 2].t[:, ch, cs], kbufs + [QM[h % 2].b], [p.b], start=True, stop=not on_pe)
                    e_ = pT[pi[0] % 8]
                    if on_pe:
                        mm(p.t[:, cs], C16("ident"), mhl[0].t[:, mi, cs], [cstb.b, mhl[0].b], [p.b], start=False, stop=False)
                        mm(p.t[:, cs], C16("ident"), mhl[1].t[:, mi, cs], [cstb.b, mhl[1].b], [p.b], start=False, stop=True)
                        act(e_.t[:, cs], p.t[:, cs], AF.Exp, [p.b], [e_.b], scale=0.125)
                    elif masks is not None:
                        s_ = sc[pi[0] % 6]
                        tt("dve", s_.t[:, cs], p.t[:, cs], masks.t[:, mi, cs], ALU.add, [p.b, masks.b], [s_.b])
                        act(e_.t[:, cs], s_.t[:, cs], AF.Exp, [s_.b], [e_.b], scale=0.125)
                    else:
                        act(e_.t[:, cs], p.t[:, cs], AF.Exp, [p.b], [e_.b], scale=0.125)
                    pi[0] += 1
                    return e_

                def stage2(h, ki, e_):
                    kfn, vfn, mi, kbufs = ktl[ki]
                    if ki == 0:
                        po_h[h] = psum_acc()
                    po = po_h[h]
                    s0, s1 = SUBR[mi] if masks is not None else (0, 4)
                    for s in range(s0, s1):
                        mm(po.t[:, s * 128:s * 128 + 65], e_.t[:, s * 128:(s + 1) * 128], vfn(h), [e_.b] + kbufs, [po.b],
                           start=(h not in started), stop=(ki == nk - 1))
                        started[h] = True
                    if ki == nk - 1:
                        pov = po.t[:, :].rearrange("p (s e) -> p s e", e=128)
                        fw.op("dve", lambda e: e.reciprocal(out=rden.t[:, :], in_=pov[:, :, 64]), [po.b], [rden.b])
                        for s in range(4):
                            ts("dve", otok.t[:, s, h * 64:(h + 1) * 64], po.t[:, s * 128:s * 128 + 64], rden.t[:, s:s + 1],
                               ALU.mult, [po.b, rden.b], [otok.b])

                pend = []
                for pair in range(2):
                    for ki in range(nk):
                        for h in (2 * pair, 2 * pair + 1):
                            pend.append((h, ki, stage1(h, ki)))
                            if len(pend) > LA:
                                stage2(*pend.pop(0))
                while pend:
                    stage2(*pend.pop(0))
                M_ = mo[i % 2]
                for ch in range(2):
                    p = psum()
                    for s in range(4):
                        mm(p.t[:, s * 128:(s + 1) * 128], otok.t[:, s, ch * 128:(ch + 1) * 128], C16("ident"),
                           [otok.b, cstb.b], [p.b])
                    tt("dve", M_.t[:, ch, :], p.t[:, :], G.t[:, ch, :], ALU.mult, [p.b, G.b], [M_.b])
                fw.dma("pool", mixT.ap[out_row0:out_row0 + 256, t0:t0 + 512].rearrange("(h p) t -> p h t", p=128), M_.t[:],
                       reads=[M_.b], writes=[mixT.b((out_row0, i))])

        def pass_AT(l):
            with ExitStack() as es:
                def sb(name, shape, dt):
                    return Tl(es.enter_context(nc.sbuf_tensor(f"{name}_L{l}", list(shape), dt)))
                masks = sb("AT_masks", [128, 20, 512], F32)
                fw.dma("sp", masks.t[:], amask_in.rearrange("r p q -> p r q"), writes=[masks.b])
                NW = 20
                kw_ = [sb(f"AT_kw{j}", [128, 2, NW * 128], BF16) for j in range(2)]
                vw_ = [sb(f"AT_vw{j}", [128, NW, 260], BF16) for j in range(2)]

                def kv_for_tile(i):
                    KW, VW = kw_[i % 2], vw_[i % 2]
                    q0 = i * 512
                    half_lo = (q0 // HALF) * HALF
                    lo = max(q0 - 1024, 0)
                    hi = min(q0 + 512 + 1024, T)
                    n = (hi - lo) // 128
                    tiles = sorted(set(range(lo // 512, (hi + 511) // 512)))
                    fw.dma("sp", KW.t[:, :, 0:n * 128], akT.ap[:, lo:hi].rearrange("(h p) t -> p h t", p=128),
                           reads=[akT.b(x) for x in tiles], writes=[KW.b])
                    fw.dma("sp", VW.t[:, 0:n, :], va.ap[lo:hi, :].rearrange("(j p) f -> p j f", p=128),
                           reads=[va.b(x) for x in tiles], writes=[VW.b])
                    out = []
                    for jt in range(n):
                        k0 = lo + jt * 128
                        r = (k0 - q0) // 128
                        assert -8 <= r <= 11
                        if not (half_lo <= k0 < half_lo + HALF):
                            ts("dve", VW.t[:, jt, :], VW.t[:, jt, :], flag.t[:, 0:1], ALU.mult, [VW.b, flag.b], [VW.b])
                        out.append(((lambda ch, pr, jt=jt, KW=KW: KW.t[:, ch, jt * 128:(jt + 1) * 128]),
                                    (lambda h, jt=jt, VW=VW: VW.t[:, jt, h * 65:(h + 1) * 65]),
                                    r + 8, [KW.b, VW.b]))
                    return out
                mhl = [sb("AT_mhi", [128, 20, 512], BF16), sb("AT_mlo", [128, 20, 512], BF16)]
                for r in range(20):
                    cp("act", mhl[0].t[:, r, :], masks.t[:, r, :], [masks.b], [mhl[0].b])
                    tt("pool", mhl[1].t[:, r, :], masks.t[:, r, :], mhl[0].t[:, r, :], ALU.subtract, [masks.b, mhl[0].b], [mhl[1].b])
                attn_core(sb, NT, aqT, kv_for_tile, masks, 0, "AT", mhl=mhl)
                fw.barrier()

        def pass_ME(l):
            with ExitStack() as es:
                def sb(name, shape, dt):
                    return Tl(es.enter_context(nc.sbuf_tensor(f"{name}_L{l}", list(shape), dt)))
                wkv = sb("ME_w", [128, 8, 512], BF16)
                for kc in range(8):
                    fw.dma("pool", wkv.t[:, kc, :], mem_wkv[l, kc * 128:(kc + 1) * 128, :], writes=[wkv.b])
                mnw = sb("ME_mnw", [128, 8], F32)
                col_load(mnw, mem_norm_w[l], 8)
                gk = sb("ME_gk", [128, 1], F32)
                col_load64(gk, mk_w[l])
                mkT = [sb(f"ME_mkT{g}", [128, 2, 256], BF16) for g in range(2)]
                mv = [sb(f"ME_mv{g}", [128, 2, 260], BF16) for g in range(2)]
                xm = sb("ME_x", [128, 1024], F32)
                junk = sb("ME_junk", [128, 1024], BF16)
                ssm = sb("ME_ss", [128, 1], F32)
                hbm = sb("ME_hb", [128, 2, 1024], BF16)
                hTm = sb("ME_hT", [128, 8, 256], BF16)
                sqm = sb("ME_sq", [128, 256], BF16)
                rsm = sb("ME_rs", [128, 256], F32)
                for g in range(2):
                    fw.op("dve", lambda e: e.memset(mv[g].t[:], 1.0), [], [mv[g].b])
                    for s in range(2):
                        fw.dma("sp", xm.t[:], mem_in[g, s * 128:(s + 1) * 128, :], writes=[xm.b])
                        act(junk.t[:], xm.t[:], AF.Square, [xm.b], [junk.b, ssm.b], accum_out=ssm.t[:, 0:1])
                        rstd_inplace(ssm.t[:], 1024.0, ssm.b)
                        ts("dve", hbm.t[:, s, :], xm.t[:], ssm.t[:, 0:1], ALU.mult, [xm.b, ssm.b], [hbm.b])
                    for kc in range(8):
                        p = psum()
                        for s in range(2):
                            mm(p.t[:, s * 128:(s + 1) * 128], hbm.t[:, s, kc * 128:(kc + 1) * 128], C16("ident"),
                               [hbm.b, cstb.b], [p.b])
                        ts("dve", hTm.t[:, kc, :], p.t[:, 0:256], mnw.t[:, kc:kc + 1], ALU.mult, [p.b, mnw.b], [hTm.b])
                    for c in range(2):
                        p = psum()
                        for kc in range(8):
                            mm(p.t[:, 0:256], wkv.t[:, kc, c * 128:(c + 1) * 128], hTm.t[:, kc, :], [wkv.b, hTm.b], [p.b],
                               start=(kc == 0), stop=(kc == 7))
                        act(sqm.t[:], p.t[:, 0:256], AF.Square, [p.b], [sqm.b])
                        p2 = psum()
                        mm(p2.t[:, 0:256], C16("ones64"), sqm.t[:], [cstb.b, sqm.b], [p2.b])
                        act(rsm.t[:], p2.t[:, 0:256], AF.Ln, [p2.b], [rsm.b], scale=1.0 / 64, bias=EPS)
                        act(rsm.t[:], rsm.t[:], AF.Exp, [rsm.b], [rsm.b], scale=-0.5)
                        stt(mkT[g].t[:, c, :], p.t[:, 0:256], gk.t[:, 0:1], rsm.t[:], ALU.mult, ALU.mult,
                            [p.b, gk.b, rsm.b], [mkT[g].b])
                    for s in range(2):
                        p = psum()
                        for kc in range(8):
                            mm(p.t[:, 0:256], hTm.t[:, kc, s * 128:(s + 1) * 128], wkv.t[:, kc, 256:512], [hTm.b, wkv.b], [p.b],
                               start=(kc == 0), stop=(kc == 7))
                        cp("act", mv[g].t[:, s, :].rearrange("p (h e) -> p h e", e=65)[:, :, 0:64],
                           p.t[:, 0:256].rearrange("p (h e) -> p h e", e=64), [p.b], [mv[g].b])

                def kv_for_tile(i):
                    g = (i * 512) // HALF
                    out = []
                    for jt in range(2):
                        out.append(((lambda ch, pr, jt=jt, g=g: mkT[g].t[:, ch, jt * 128:(jt + 1) * 128]),
                                    (lambda h, jt=jt, g=g: mv[g].t[:, jt, h * 65:(h + 1) * 65]),
                                    0, [mkT[g].b, mv[g].b]))
                    return out
                attn_core(sb, NT, mqT, kv_for_tile, None, 256, "ME")
                fw.barrier()

        def pass_C(l):
            src = x_dt if l == 0 else y_out
            with ExitStack() as es:
                def sb(name, shape, dt):
                    return Tl(es.enter_context(nc.sbuf_tensor(f"{name}_L{l}", list(shape), dt)))
                wo = sb("C_w", [128, 8, 1024], BF16)
                for kc in range(8):
                    fw.dma("pool", wo.t[:, kc, :], w_out[l, kc * 128:(kc + 1) * 128, :], writes=[wo.b])
                gon = sb("C_gon", [128, 1], F32)
                fw.dma("sp", gon.t[:, 0:1], onorm_w[l].rearrange("(p o) -> p o", o=1), writes=[gon.b],
                       allow_slow_non_contiguous=True)
                of_ = [sb(f"C_of{j}", [128, 4, 512], F32) for j in range(2)]
                ob_ = [sb(f"C_ob{j}", [128, 4, 512], F32) for j in range(2)]
                gh = [sb(f"C_g{j}", [128, 4, 512], BF16) for j in range(2)]
                mx_ = [sb(f"C_mx{j}", [128, 8, 512], BF16) for j in range(2)]
                xt = [sb(f"C_x{j}", [128, 4, 1024], F32) for j in range(2)]
                osum = [sb(f"C_osum{h}", [128, 512], F32) for h in range(4)]
                sq = [sb(f"C_sq{h}", [128, 512], BF16) for h in range(4)]
                rsf = [sb(f"C_rsf{h}", [128, 512], F32) for h in range(4)]
                m1 = [sb(f"C_m1{h}", [128, 512], F32) for h in range(4)]
                mxb = [[Buf() for _ in range(8)] for _ in range(2)]
                yt = [sb(f"C_y{j}", [128, 1024], F32) for j in range(2)]

                def load(i):
                    j = i % 2
                    t0 = i * 512
                    fw.dma("sp", of_[j].t[:], oT["f"].ap[:, t0:t0 + 512].rearrange("(h p) t -> p h t", p=128),
                           reads=[oT["f"].b(i)], writes=[of_[j].b])
                    fw.dma("sp", ob_[j].t[:], oT["b"].ap[:, t0:t0 + 512].rearrange("(h p) t -> p h t", p=128),
                           reads=[oT["b"].b(i)], writes=[ob_[j].b])
                    fw.dma("sp", gh[j].t[:], gT.ap[0:512, t0:t0 + 512].rearrange("(h p) t -> p h t", p=128),
                           reads=[gT.b(i)], writes=[gh[j].b])
                    fw.dma("sp", mx_[j].t[:, 4:8, :], mixT.ap[:, t0:t0 + 512].rearrange("(h p) t -> p h t", p=128),
                           reads=[mixT.b((0, i)), mixT.b((256, i))], writes=mxb[j][4:8])
                    fw.dma("sp", xt[j].t[:], src.ap[t0:t0 + 512, :].rearrange("(s p) f -> p s f", p=128),
                           reads=[src.b(i)], writes=[xt[j].b])
                yi = [0]
                H4 = range(4)

                def normphase(i):
                    j = i % 2
                    for h in H4:
                        tt("pool", osum[h].t[:], of_[j].t[:, h, :], ob_[j].t[:, h, :], ALU.add, [of_[j].b, ob_[j].b], [osum[h].b])
                    for h in H4:
                        act(sq[h].t[:], osum[h].t[:], AF.Square, [osum[h].b], [sq[h].b])
                    pp = []
                    for h in H4:
                        p = psum()
                        mm(p.t[:, :], C16("ones128"), sq[h].t[:], [cstb.b, sq[h].b], [p.b])
                        pp.append(p)
                    for h in H4:
                        act(rsf[h].t[:], pp[h].t[:, :], AF.Ln, [pp[h].b], [rsf[h].b], scale=1.0 / 128, bias=EPS)
                    for h in H4:
                        act(rsf[h].t[:], rsf[h].t[:], AF.Exp, [rsf[h].b], [rsf[h].b], scale=-0.5)
                    for h in H4:
                        stt(m1[h].t[:], osum[h].t[:], gon.t[:, 0:1], rsf[h].t[:], ALU.mult, ALU.mult,
                            [osum[h].b, gon.b, rsf[h].b], [m1[h].b])
                    for h in H4:
                        tt("dve", mx_[j].t[:, h, :], m1[h].t[:], gh[j].t[:, h, :], ALU.mult, [m1[h].b, gh[j].b], [mxb[j][h]])

                def outproj(i):
                    j = i % 2
                    t0 = i * 512
                    for s in range(4):
                        Y = yt[yi[0] % 2]
                        yi[0] += 1
                        for nh in range(2):
                            p = psum()
                            for mc in range(8):
                                mm(p.t[:, :], mx_[j].t[:, mc, s * 128:(s + 1) * 128], wo.t[:, mc, nh * 512:(nh + 1) * 512],
                                   [mxb[j][mc], wo.b], [p.b], start=(mc == 0), stop=(mc == 7))
                            tt("dve", Y.t[:, nh * 512:(nh + 1) * 512], p.t[:, :], xt[j].t[:, s, nh * 512:(nh + 1) * 512], ALU.add,
                               [p.b, xt[j].b], [Y.b])
                        fw.dma("pool", y_out.ap[t0 + s * 128:t0 + (s + 1) * 128, :], Y.t[:], reads=[Y.b], writes=[y_out.b(i)])

                load(0)
                if NT > 1:
                    load(1)
                normphase(0)
                for i in range(NT):
                    if i + 1 < NT:
                        normphase(i + 1)
                    outproj(i)
                    if i + 2 < NT:
                        load(i + 2)
                fw.barrier()

        _P = _os.environ.get("KPASSES", "A,HF,HB,AT,ME,C").split(",")
        for l in range(depth):
            if "A" in _P:
                pass_A(l)
            if "HF" in _P:
                pass_H(l, "f")
            if "HB" in _P:
                pass_H(l, "b")
            if "AT" in _P:
                pass_AT(l)
            if "ME" in _P:
                pass_ME(l)
            if "C" in _P:
                pass_C(l)
        fw.barrier()
    return nc, fw


T_CORE = 16384
DEPTH = 4
_CACHE = {}


def kernel(x_prompt, x_sample, mem_prompt, mem_sample, norm_w, w_in, hgrn_lb_fwd, hgrn_lb_bwd, hgrn_onorm_w,
           attn_qnorm_w, attn_knorm_w, mem_norm_w, mem_wkv, mem_qnorm_w, mem_knorm_w, w_out):
    f = lambda a: np.ascontiguousarray(np.asarray(a, dtype=np.float32))
    x_prompt, x_sample, mem_prompt, mem_sample = f(x_prompt), f(x_sample), f(mem_prompt), f(mem_sample)
    T = T_CORE
    if "nc" not in _CACHE:
        _CACHE["nc"] = build(T, DEPTH)[0]
        _CACHE["hc"] = host_consts(T)
    nc = _CACHE["nc"]
    hc = _CACHE["hc"]
    pos_p = np.concatenate([np.arange(8192), np.arange(8192)])
    pos_s = np.arange(16384)
    Cp, Sp = rope_tables(pos_p)
    Cs, Ss = rope_tables(pos_s)
    shared = {"cst": hc["cst"], "amask": hc["amask"], "norm_w": f(norm_w), "w_in": f(w_in),
              "hgrn_lb_fwd": f(hgrn_lb_fwd), "hgrn_lb_bwd": f(hgrn_lb_bwd), "hgrn_onorm_w": f(hgrn_onorm_w),
              "attn_qnorm_w": f(attn_qnorm_w), "attn_knorm_w": f(attn_knorm_w), "mem_norm_w": f(mem_norm_w),
              "mem_wkv": f(mem_wkv), "mem_qnorm_w": f(mem_qnorm_w), "mem_knorm_w": f(mem_knorm_w), "w_out": f(w_out)}
    in_maps = []
    for c in range(8):
        m = dict(shared)
        if c < 4:
            m["x"] = x_prompt[2 * c:2 * c + 2].reshape(T, 1024)
            m["mem"] = mem_prompt[2 * c:2 * c + 2]
            m["flag"] = np.zeros((128, 1), np.float32)
            m["ropeC"], m["ropeS"] = Cp, Sp
        else:
            s = (c - 4) % 2
            m["x"] = x_sample[s]
            m["mem"] = np.stack([mem_sample[s], mem_sample[s]])
            m["flag"] = np.ones((128, 1), np.float32)
            m["ropeC"], m["ropeS"] = Cs, Ss
        in_maps.append(m)
    res = run_bass_kernel_spmd(nc, in_maps, core_ids=list(range(8)))
    ys = [np.asarray(r["y"], dtype=np.float32) for r in res.results]
    y_prompt = np.stack([ys[c].reshape(2, 8192, 1024) for c in range(4)]).reshape(8, 8192, 1024)
    y_sample = np.stack([ys[4], ys[5]])
    return (y_prompt, y_sample)
```

```python
import math
import os as _os
from contextlib import ExitStack
import numpy as np
import concourse.bass as bass
import concourse.mybir as mybir
from concourse.bass_utils import run_bass_kernel_spmd

F32 = mybir.dt.float32
BF16 = mybir.dt.bfloat16
AF = mybir.ActivationFunctionType
ALU = mybir.AluOpType
EPS = 1e-6
NSLOT = 8


class Buf:
    __slots__ = ("w", "r")

    def __init__(self):
        self.w = None
        self.r = {}


class Tl:
    def __init__(self, t):
        self.t = t
        self.b = Buf()


class FW:
    LIM = 30000

    def __init__(self, nc):
        self.nc = nc
        self.engs = {"pe": nc.tensor, "act": nc.scalar, "dve": nc.vector, "pool": nc.gpsimd, "sp": nc.sync}
        self.cur = {}
        self.seen = {k: {} for k in self.engs}
        self.last = {}
        self.nsem = 0
        self.dslots = {"sp": [], "pool": []}
        self.dnext = {"sp": 0, "pool": 0}
        self.nins = 0

    def newsem(self):
        self.nsem += 1
        return [self.nsem, self.nc.alloc_semaphore(name=f"fs{self.nsem}")]

    def _wait(self, e, ev):
        if self.seen[e].get(ev[0], 0) >= ev[2]:
            return
        self.engs[e].wait_ge(ev[1], ev[2])
        self.seen[e][ev[0]] = ev[2]

    def _deps(self, e, reads, writes):
        for b in reads:
            if b.w is not None and not (e == "pe" and b.w[3] == "pe"):
                self._wait(e, b.w)
        for b in writes:
            if b.w is not None and not (e == "pe" and b.w[3] == "pe"):
                self._wait(e, b.w)
            for ev in b.r.values():
                if not (e == "pe" and ev[3] == "pe"):
                    self._wait(e, ev)

    def _mark(self, ev, key, reads, writes):
        for b in reads:
            b.r[key] = ev
        for b in writes:
            b.w = ev
            b.r = {}

    def op(self, e, fn, reads=(), writes=()):
        self._deps(e, reads, writes)
        ins = fn(self.engs[e])
        c = self.cur.get(e)
        if c is None or c[2] >= self.LIM:
            c = self.newsem() + [0]
            self.cur[e] = c
        c[2] += 1
        ins.then_inc(c[1], 1)
        ev = (c[0], c[1], c[2], e)
        self._mark(ev, e, reads, writes)
        self.last[e] = ev
        self.nins += 1

    def dma(self, q, out, in_, reads=(), writes=(), **kw):
        self._deps(q, reads, writes)
        slots = self.dslots[q]
        if len(slots) < NSLOT:
            slots.append(self.newsem() + [0])
            s = slots[-1]
        else:
            s = slots[self.dnext[q] % NSLOT]
        self.dnext[q] += 1
        if s[2] > 0:
            self._wait(q, (s[0], s[1], s[2], "dma"))
        if s[2] + 16 > self.LIM:
            s[:] = self.newsem() + [0]
        ins = self.engs[q].dma_start(out=out, in_=in_, **kw)
        s[2] += 16
        ins.then_inc(s[1], 16)
        ev = (s[0], s[1], s[2], "dma")
        self._mark(ev, ("d", s[0], s[2]), reads, writes)
        self.nins += 1

    def barrier(self):
        evs = [self.last[e] for e in ("pe", "act", "dve", "pool") if e in self.last]
        for q in self.dslots:
            for s in self.dslots[q]:
                if s[2] > 0:
                    evs.append((s[0], s[1], s[2], "dma"))
        for e in self.engs:
            for ev in evs:
                if e == "pe" and ev[3] == "pe":
                    continue
                self._wait(e, ev)


class DT:
    def __init__(self, ap):
        self.ap = ap
        self.bufs = {}

    def b(self, i):
        if i not in self.bufs:
            self.bufs[i] = Buf()
        return self.bufs[i]


def host_consts(T):
    c = {}
    s = np.arange(128)[:, None]
    t = np.arange(128)[None, :]
    same = (s // 64) == (t // 64)
    c0 = (t // 64) * 64
    A_f = (same & (s <= t)).astype(np.float32) - (same & (s <= c0 + 31)).astype(np.float32)
    A_b = (same & (s >= t)).astype(np.float32) - (same & (s >= c0 + 32)).astype(np.float32)
    B_f = (same & (s > t)).astype(np.float32)
    B_b = (same & (s < t)).astype(np.float32)
    M_f = np.zeros((128, 4), np.float32)
    M_b = np.zeros((128, 4), np.float32)
    sv = np.arange(128)
    for ch in range(2):
        inc = (sv // 64) == ch
        M_f[:, 2 * ch] = inc & (sv <= ch * 64 + 31)
        M_f[:, 2 * ch + 1] = inc
        M_b[:, 2 * ch] = inc & (sv >= ch * 64 + 32)
        M_b[:, 2 * ch + 1] = inc
    K_f = (same & (s <= t)).astype(np.float32)
    K_b = (same & (s >= t)).astype(np.float32)
    ident = np.eye(128, dtype=np.float32)
    ones64 = ((s // 64) == (t // 64)).astype(np.float32)
    ones128 = np.ones((128, 128), np.float32)
    Rm = np.zeros((128, 128), np.float32)
    for m in range(128):
        j = m % 64
        if j < 8:
            Rm[m + 8, m] = -1.0
        elif j < 16:
            Rm[m - 8, m] = 1.0
    cst = np.concatenate([A_f, A_b, B_f, B_b, K_f, K_b, ident, ones64, ones128, Rm, M_f, M_b], axis=1)
    c["cst"] = np.ascontiguousarray(cst.astype(np.float32))
    am = np.zeros((20, 128, 512), np.float32)
    j = np.arange(128)[:, None]
    i = np.arange(512)[None, :]
    for ri, r in enumerate(range(-8, 12)):
        d = r * 128 + j - i
        am[ri] = ((np.abs(d) <= 64).astype(np.float32)
                  + ((d % 4 == 0) & (np.abs(d) <= 256)).astype(np.float32)
                  + ((d % 16 == 0) & (np.abs(d) <= 1024)).astype(np.float32))
    c["amask"] = np.where(am > 0, 8.0 * np.log(np.maximum(am, 1.0)), -80000.0).astype(np.float32)
    return c


def rope_tables(pos):
    half = 8
    inv = (500000.0 ** (-np.arange(half, dtype=np.float32) * 2.0 / 16.0)).astype(np.float32)
    ang = pos.astype(np.float32)[None, :] * inv[:, None]
    C = np.ones((128, pos.shape[0]), np.float32)
    S = np.zeros((128, pos.shape[0]), np.float32)
    for hb in (0, 64):
        C[hb:hb + 8] = np.cos(ang)
        C[hb + 8:hb + 16] = np.cos(ang)
        S[hb:hb + 8] = np.sin(ang)
        S[hb + 8:hb + 16] = np.sin(ang)
    return C, S


def _sub_ranges():
    out = []
    j = np.arange(128)[:, None]
    i = np.arange(512)[None, :]
    for r in range(-8, 12):
        d = r * 128 + j - i
        ok = (np.abs(d) <= 64) | ((d % 4 == 0) & (np.abs(d) <= 256)) | ((d % 16 == 0) & (np.abs(d) <= 1024))
        subs = [s for s in range(4) if ok[:, s * 128:(s + 1) * 128].any()]
        out.append((min(subs), max(subs) + 1))
    return out


SUBR = _sub_ranges()
CO = {"A_f": 0, "A_b": 128, "B_f": 256, "B_b": 384, "K_f": 512, "K_b": 640, "ident": 768, "ones64": 896,
      "ones128": 1024, "Rm": 1152, "M_f": 1280, "M_b": 1284}
NCST = 1288


def build(T, depth, debug=False):
    NT = T // 512
    HALF = T // 2
    nc = bass.Bass("TRN2", target_bir_lowering=False)
    fw = FW(nc)

    def din(name, shape, dt=F32):
        return nc.dram_tensor(name, list(shape), dt, kind="ExternalInput").ap()

    x_in = din("x", [T, 1024])
    mem_in = din("mem", [2, 256, 1024])
    flag_in = din("flag", [128, 1])
    ropeC_in = din("ropeC", [128, T])
    ropeS_in = din("ropeS", [128, T])
    cst_in = din("cst", [128, NCST])
    amask_in = din("amask", [20, 128, 512])
    norm_w = din("norm_w", [depth, 1024])
    w_in = din("w_in", [depth, 1024, 4096])
    lbp = {"f": din("hgrn_lb_fwd", [depth, 512]), "b": din("hgrn_lb_bwd", [depth, 512])}
    onorm_w = din("hgrn_onorm_w", [depth, 128])
    aq_w = din("attn_qnorm_w", [depth, 64])
    ak_w = din("attn_knorm_w", [depth, 64])
    mem_norm_w = din("mem_norm_w", [depth, 1024])
    mem_wkv = din("mem_wkv", [depth, 1024, 512])
    mq_w = din("mem_qnorm_w", [depth, 64])
    mk_w = din("mem_knorm_w", [depth, 64])
    w_out = din("w_out", [depth, 1024, 1024])
    y_out = DT(nc.dram_tensor("y", [T, 1024], F32, kind="ExternalOutput").ap())
    x_dt = DT(x_in)

    skind = "ExternalOutput" if debug else "Internal"

    def scr(name, shape, dt):
        return DT(nc.dram_tensor(name, list(shape), dt, kind=skind).ap())

    qT = scr("s_qT", [512, T], BF16)
    kT = {"f": scr("s_kTf", [512, T], BF16), "b": scr("s_kTb", [512, T], BF16)}
    ktok = {"f": scr("s_kf", [T, 512], BF16), "b": scr("s_kb", [T, 512], BF16)}
    lfh = {"f": scr("s_lfhf", [T, 512], BF16), "b": scr("s_lfhb", [T, 512], BF16)}
    lfl = {"f": scr("s_lflf", [T, 512], BF16), "b": scr("s_lflb", [T, 512], BF16)}
    vtok = scr("s_v", [T, 512], BF16)
    gT = scr("s_gT", [1024, T], BF16)
    aqT = scr("s_aqT", [256, T], BF16)
    akT = scr("s_akT", [256, T], BF16)
    va = scr("s_va", [T, 260], BF16)
    mqT = scr("s_mqT", [256, T], BF16)
    oT = {"f": scr("s_ofT", [512, T], F32), "b": scr("s_obT", [512, T], F32)}
    mixT = scr("s_mixT", [512, T], BF16)

    es0 = ExitStack()
    with es0:
        def sb0(name, shape, dt):
            return Tl(es0.enter_context(nc.sbuf_tensor(name, list(shape), dt)))

        ps_t = [Tl(es0.enter_context(nc.psum_tensor(f"ps{i}", [128, 512], F32))) for i in range(8)]
        ps_i = [0]

        def psum():
            p = ps_t[ps_i[0] % 6]
            ps_i[0] += 1
            return p
        pa_i = [0]

        def psum_acc():
            p = ps_t[6 + pa_i[0] % 2]
            pa_i[0] += 1
            return p

        def mm(out, lhsT, rhs, reads, writes, start=True, stop=True):
            fw.op("pe", lambda e: e.matmul(out, lhsT=lhsT, rhs=rhs, start=start, stop=stop), reads, writes)

        def act(out, in_, func, reads, writes, **kw):
            fw.op("act", lambda e: e.activation(out=out, in_=in_, func=func, **kw), reads, writes)

        def tt(eng, out, in0, in1, op, reads, writes):
            fw.op(eng, lambda e: e.tensor_tensor(out=out, in0=in0, in1=in1, op=op), reads, writes)

        def ts(eng, out, in0, s1, op0, reads, writes, s2=None, op1=None):
            if op1 is None:
                fw.op(eng, lambda e: e.tensor_scalar(out=out, in0=in0, scalar1=s1, scalar2=None, op0=op0), reads, writes)
            else:
                fw.op(eng, lambda e: e.tensor_scalar(out=out, in0=in0, scalar1=s1, scalar2=s2, op0=op0, op1=op1),
                      reads, writes)

        def stt(out, in0, scalar, in1, op0, op1, reads, writes):
            fw.op("dve", lambda e: e.scalar_tensor_tensor(out=out, in0=in0, scalar=scalar, in1=in1, op0=op0, op1=op1),
                  reads, writes)

        def cp(eng, out, in_, reads, writes):
            if eng == "act":
                fw.op("act", lambda e: e.copy(out=out, in_=in_), reads, writes)
            else:
                fw.op(eng, lambda e: e.tensor_copy(out=out, in_=in_), reads, writes)

        cst = sb0("cst_sb", [128, NCST], F32)
        fw.dma("sp", cst.t[:], cst_in[:, :], writes=[cst.b])
        cstb = sb0("cstb_sb", [128, NCST], BF16)
        cp("dve", cstb.t[:], cst.t[:], [cst.b], [cstb.b])
        flag = sb0("flag_sb", [128, 1], F32)
        fw.dma("sp", flag.t[:], flag_in[:, :], writes=[flag.b])

        def C32(name, w=128):
            return cst.t[:, CO[name]:CO[name] + w]

        def C16(name, w=128):
            return cstb.t[:, CO[name]:CO[name] + w]

        def col_load(dst, src1d, n):
            fw.dma("sp", dst.t[:, 0:n], src1d.rearrange("(c p) -> p c", p=128), writes=[dst.b],
                   allow_slow_non_contiguous=True)

        def col_load64(dst, src1d):
            for hb in (0, 64):
                fw.dma("sp", dst.t[hb:hb + 64, 0:1], src1d.rearrange("(p o) -> p o", o=1), writes=[dst.b],
                       allow_slow_non_contiguous=True)

        def rstd_inplace(tl_ap, n, reads_b):
            act(tl_ap, tl_ap, AF.Ln, [reads_b], [reads_b], scale=1.0 / n, bias=EPS)
            act(tl_ap, tl_ap, AF.Exp, [reads_b], [reads_b], scale=-0.5)

        def pass_A(l):
            src = x_dt if l == 0 else y_out
            with ExitStack() as es:
                def sb(name, shape, dt):
                    return Tl(es.enter_context(nc.sbuf_tensor(f"{name}_L{l}", list(shape), dt)))
                w = sb("A_w", [128, 8, 4096], BF16)
                for kc in range(8):
                    for q4 in range(4):
                        fw.dma("pool", w.t[:, kc, q4 * 1024:(q4 + 1) * 1024],
                               w_in[l, kc * 128:(kc + 1) * 128, q4 * 1024:(q4 + 1) * 1024], writes=[w.b])
                normw = sb("A_normw", [128, 8], F32)
                col_load(normw, norm_w[l], 8)
                gq = sb("A_gq", [128, 1], F32)
                gk = sb("A_gk", [128, 1], F32)
                gm = sb("A_gm", [128, 1], F32)
                col_load64(gq, aq_w[l])
                col_load64(gk, ak_w[l])
                col_load64(gm, mq_w[l])
                oml = {d: sb(f"A_oml{d}", [128, 512], F32) for d in "fb"}
                with ExitStack() as es2:
                    def sb2(name, shape, dt):
                        return Tl(es2.enter_context(nc.sbuf_tensor(f"{name}_L{l}", list(shape), dt)))
                    row = sb2("A_lbrow", [1, depth, 512], F32)
                    mx = sb2("A_lbmx", [1, 512], F32)
                    sm = sb2("A_lbsm", [1, 512], F32)
                    acc = sb2("A_lbacc", [1, 512], F32)
                    tmp = sb2("A_lbtmp", [1, 512], F32)
                    for d in ("f", "b"):
                        fw.dma("sp", row.t[0:1, :, :], lbp[d].rearrange("(o l) f -> o l f", o=1), writes=[row.b])
                        cp("dve", mx.t[:], row.t[0:1, 0, :], [row.b], [mx.b])
                        for j in range(1, depth):
                            tt("dve", mx.t[:], mx.t[:], row.t[0:1, j, :], ALU.max, [mx.b, row.b], [mx.b])
                        for j in range(depth):
                            tt("dve", row.t[0:1, j, :], row.t[0:1, j, :], mx.t[:], ALU.subtract, [row.b, mx.b], [row.b])
                        act(row.t[:], row.t[:], AF.Exp, [row.b], [row.b])
                        cp("dve", sm.t[:], row.t[0:1, 0, :], [row.b], [sm.b])
                        for j in range(1, depth):
                            tt("dve", sm.t[:], sm.t[:], row.t[0:1, j, :], ALU.add, [sm.b, row.b], [sm.b])
                        fw.op("dve", lambda e: e.reciprocal(out=sm.t[:], in_=sm.t[:]), [sm.b], [sm.b])
                        fw.op("dve", lambda e: e.memset(acc.t[:], 0.0), [], [acc.b])
                        for j in range(1, l + 1):
                            tt("dve", tmp.t[:], row.t[0:1, j, :], sm.t[:], ALU.mult, [row.b, sm.b], [tmp.b])
                            tt("dve", acc.t[:], acc.t[:], tmp.t[:], ALU.add, [acc.b, tmp.b], [acc.b])
                        ts("dve", acc.t[:], acc.t[:], -1.0, ALU.mult, [acc.b], [acc.b], s2=1.0, op1=ALU.add)
                        p = psum()
                        mm(p.t[:, :], C32("ones128")[0:1, :], acc.t[0:1, :], [cst.b, acc.b], [p.b])
                        cp("dve", oml[d].t[:], p.t[:, :], [p.b], [oml[d].b])
                    fw.barrier()

                xt = [sb(f"A_x{i}", [128, 1024], F32) for i in range(4)]
                ss = sb("A_ss", [128, 4], F32)
                hb = sb("A_hb", [128, 4, 1024], BF16)
                hTr = [sb(f"A_hT{j}", [128, 8, 512], BF16) for j in range(2)]
                qTs = sb("A_qTs", [128, 4, 512], BF16)
                gTs = sb("A_gTs", [128, 8, 512], BF16)
                aqTs = sb("A_aqTs", [128, 2, 512], BF16)
                akTs = sb("A_akTs", [128, 2, 512], BF16)
                mqTs = sb("A_mqTs", [128, 2, 512], BF16)
                lfhs = {d: sb(f"A_lfhs{d}", [128, 4, 512], BF16) for d in "fb"}
                lfls = {d: sb(f"A_lfls{d}", [128, 4, 512], BF16) for d in "fb"}
                ks = {d: sb(f"A_ks{d}", [128, 4, 512], BF16) for d in "fb"}
                kTs = {d: sb(f"A_kTs{d}", [128, 4, 512], BF16) for d in "fb"}
                vs = sb("A_vs", [128, 4, 512], BF16)
                vas = sb("A_vas", [128, 4, 260], BF16)
                fw.op("dve", lambda e: e.memset(vas.t[:], 1.0), [], [vas.b])
                NR = 2
                sq = [sb(f"A_sq{j}", [128, 512], BF16) for j in range(NR)]
                rsf = [sb(f"A_rsf{j}", [128, 512], F32) for j in range(NR)]
                qn = [sb(f"A_qn{j}", [128, 512], F32) for j in range(NR)]
                qnb = [sb(f"A_qnb{j}", [128, 512], BF16) for j in range(NR)]
                t1 = rsf
                t2 = [sb(f"A_t2{j}", [128, 512], F32) for j in range(NR)]
                rC = [sb(f"A_rC{j}", [128, 512], F32) for j in range(2)]
                rS = [sb(f"A_rS{j}", [128, 512], F32) for j in range(2)]
                NG = 2
                sg = [sb(f"A_sg{j}", [128, 512], F32) for j in range(NG)]
                k32 = [sb(f"A_k32{j}", [128, 512], F32) for j in range(NG)]
                sub_b = {}

                def SB(tl, idx):
                    key = (id(tl), idx)
                    if key not in sub_b:
                        sub_b[key] = Buf()
                    return sub_b[key]

                def SBall(tl, n):
                    return [SB(tl, j) for j in range(n)]
                pfree = list(ps_t)

                def palloc():
                    return pfree.pop(0)

                def prel(p):
                    pfree.append(p)

                def run_chains(gens, K):
                    active = []
                    it = iter(gens)
                    done = False
                    while True:
                        while len(active) < K and not done:
                            g = next(it, None)
                            if g is None:
                                done = True
                            else:
                                active.append(g)
                        if not active:
                            break
                        for g in list(active):
                            try:
                                next(g)
                            except StopIteration:
                                active.remove(g)

                def prologue(i):
                    t0 = i * 512
                    hT = hTr[i % 2]
                    fw.dma("sp", rC[i % 2].t[:], ropeC_in[:, t0:t0 + 512], writes=[rC[i % 2].b])
                    fw.dma("sp", rS[i % 2].t[:], ropeS_in[:, t0:t0 + 512], writes=[rS[i % 2].b])
                    for s in range(4):
                        fw.dma("sp", xt[s].t[:], src.ap[t0 + s * 128:t0 + (s + 1) * 128, :], reads=[src.b(i)], writes=[xt[s].b])
                        act(hb.t[:, s, :], xt[s].t[:], AF.Square, [xt[s].b], [SB(hb, s), ss.b], accum_out=ss.t[:, s:s + 1])
                        yield
                    rstd_inplace(ss.t[:], 1024.0, ss.b)
                    yield
                    for s in range(4):
                        ts("dve", hb.t[:, s, :], xt[s].t[:], ss.t[:, s:s + 1], ALU.mult, [xt[s].b, ss.b], [SB(hb, s)])
                        yield
                    for kc in range(8):
                        p = palloc()
                        for s in range(4):
                            mm(p.t[:, s * 128:(s + 1) * 128], hb.t[:, s, kc * 128:(kc + 1) * 128], C16("ident"),
                               [SB(hb, s), cstb.b], [p.b])
                        yield
                        ts("dve", hT.t[:, kc, :], p.t[:, :], normw.t[:, kc:kc + 1], ALU.mult, [p.b, normw.b], [SB(hT, kc)])
                        prel(p)
                        yield

                rfree = list(range(NR))
                gfree = list(range(NG))

                def tile_chains(i):
                    hT = hTr[i % 2]
                    hTb = SBall(hT, 8)
                    RC, RS = rC[i % 2], rS[i % 2]

                    def fm(p, col0):
                        for kc in range(8):
                            mm(p.t[:, :], w.t[:, kc, col0:col0 + 128], hT.t[:, kc, :], [w.b, hTb[kc]], [p.b],
                               start=(kc == 0), stop=(kc == 7))

                    def tm(p, s, col0, n):
                        for kc in range(8):
                            mm(p.t[:, 0:n], hT.t[:, kc, s * 128:(s + 1) * 128], w.t[:, kc, col0:col0 + n],
                               [hTb[kc], w.b], [p.b], start=(kc == 0), stop=(kc == 7))

                    def silu_chain(col0, dst, c):
                        p = palloc()
                        fm(p, col0)
                        yield
                        act(dst.t[:, c, :], p.t[:, :], AF.Silu, [p.b], [SB(dst, c)])
                        prel(p)

                    def v_chain(s):
                        p = palloc()
                        tm(p, s, 1536, 512)
                        yield
                        cp("act", vs.t[:, s, :], p.t[:, :], [p.b], [SB(vs, s)])
                        prel(p)

                    def av_chain(s):
                        p = palloc()
                        tm(p, s, 2560, 256)
                        yield
                        cp("act", vas.t[:, s, :].rearrange("p (h e) -> p h e", e=65)[:, :, 0:64],
                           p.t[:, 0:256].rearrange("p (h e) -> p h e", e=64), [p.b], [vas.b, SB(vas, s)])
                        prel(p)

                    def norm_chain(col0, gain, dst, c, rope):
                        while not rfree:
                            yield
                        r = rfree.pop(0)
                        p = palloc()
                        fm(p, col0 + c * 128)
                        yield
                        act(sq[r].t[:], p.t[:, :], AF.Square, [p.b], [sq[r].b])
                        yield
                        p2 = palloc()
                        mm(p2.t[:, :], C16("ones64"), sq[r].t[:], [cstb.b, sq[r].b], [p2.b])
                        yield
                        act(rsf[r].t[:], p2.t[:, :], AF.Ln, [p2.b], [rsf[r].b], scale=1.0 / 64, bias=EPS)
                        prel(p2)
                        act(rsf[r].t[:], rsf[r].t[:], AF.Exp, [rsf[r].b], [rsf[r].b], scale=-0.5)
                        yield
                        if not rope:
                            stt(dst.t[:, c, :], p.t[:, :], gain.t[:, 0:1], rsf[r].t[:], ALU.mult, ALU.mult,
                                [p.b, gain.b, rsf[r].b], [SB(dst, c)])
                            prel(p)
                            rfree.append(r)
                            return
                        stt(qn[r].t[:], p.t[:, :], gain.t[:, 0:1], rsf[r].t[:], ALU.mult, ALU.mult,
                            [p.b, gain.b, rsf[r].b], [qn[r].b])
                        prel(p)
                        yield
                        cp("act", qnb[r].t[:], qn[r].t[:], [qn[r].b], [qnb[r].b])
                        tt("pool", t2[r].t[:], qn[r].t[:], RC.t[:], ALU.mult, [qn[r].b, RC.b], [t2[r].b])
                        yield
                        p3 = palloc()
                        mm(p3.t[:, :], C16("Rm"), qnb[r].t[:], [cstb.b, qnb[r].b], [p3.b])
                        yield
                        tt("dve", t1[r].t[:], p3.t[:, :], RS.t[:], ALU.mult, [p3.b, RS.b], [t1[r].b])
                        prel(p3)
                        yield
                        tt("dve", dst.t[:, c, :], t1[r].t[:], t2[r].t[:], ALU.add, [t1[r].b, t2[r].b], [SB(dst, c)])
                        rfree.append(r)

                    def fgate_chain(s, d, col0):
                        while not gfree:
                            yield
                        g = gfree.pop(0)
                        p = palloc()
                        tm(p, s, col0, 512)
                        yield
                        act(sg[g].t[:], p.t[:, :], AF.Exp, [p.b], [sg[g].b])
                        prel(p)
                        yield
                        act(sg[g].t[:], sg[g].t[:], AF.Ln, [sg[g].b], [sg[g].b], bias=1.0)
                        yield
                        act(sg[g].t[:], sg[g].t[:], AF.Exp, [sg[g].b], [sg[g].b], scale=-1.0)
                        yield
                        tt("dve", k32[g].t[:], sg[g].t[:], oml[d].t[:], ALU.mult, [sg[g].b, oml[d].b], [k32[g].b])
                        yield
                        act(sg[g].t[:], k32[g].t[:], AF.Ln, [k32[g].b], [sg[g].b], scale=-1.0, bias=1.0)
                        cp("pool", ks[d].t[:, s, :], k32[g].t[:], [k32[g].b], [SB(ks[d], s)])
                        yield
                        cp("act", lfhs[d].t[:, s, :], sg[g].t[:], [sg[g].b], [SB(lfhs[d], s)])
                        yield
                        tt("pool", lfls[d].t[:, s, :], sg[g].t[:], lfhs[d].t[:, s, :], ALU.subtract,
                           [sg[g].b, SB(lfhs[d], s)], [SB(lfls[d], s)])
                        gfree.append(g)

                    def kT_chain(d, h):
                        p = palloc()
                        for s in range(4):
                            mm(p.t[:, s * 128:(s + 1) * 128], ks[d].t[:, s, h * 128:(h + 1) * 128], C16("ident"),
                               [SB(ks[d], s), cstb.b], [p.b])
                        yield
                        cp("dve", kTs[d].t[:, h, :], p.t[:, :], [p.b], [SB(kTs[d], h)])
                        prel(p)

                    ph1 = []
                    sil = [silu_chain(c * 128, qTs, c) for c in range(4)] + [silu_chain(3072 + c * 128, gTs, c) for c in range(8)]
                    cps = []
                    for s in range(4):
                        cps += [v_chain(s), av_chain(s)]
                    for j in range(12):
                        ph1.append(sil[j])
                        if j < 8:
                            ph1.append(cps[j])
                    nrm = []
                    for (col0, gain, dst, rope) in ((2048, gq, aqTs, True), (2304, gk, akTs, True), (2816, gm, mqTs, False)):
                        for c in range(2):
                            nrm.append(norm_chain(col0, gain, dst, c, rope))
                    fg = []
                    for s in range(4):
                        fg += [fgate_chain(s, "f", 512), fgate_chain(s, "b", 1024)]
                    ph2 = []
                    for j in range(8):
                        ph2.append(fg[j])
                        if j < 6:
                            ph2.append(nrm[j])
                    ph3 = [kT_chain(d, h) for d in "fb" for h in range(4)]
                    return ph1, ph2, ph3

                def stores(i):
                    t0 = i * 512
                    fmv = lambda dd: dd.ap[:, t0:t0 + 512].rearrange("(h p) t -> p h t", p=128)
                    tmv = lambda dd: dd.ap[t0:t0 + 512, :].rearrange("(s p) f -> p s f", p=128)
                    fw.dma("pool", fmv(qT), qTs.t[:], reads=SBall(qTs, 4), writes=[qT.b(i)])
                    fw.dma("pool", fmv(gT), gTs.t[:], reads=SBall(gTs, 8), writes=[gT.b(i)])
                    fw.dma("pool", tmv(vtok), vs.t[:], reads=SBall(vs, 4), writes=[vtok.b(i)])
                    fw.dma("pool", tmv(va), vas.t[:], reads=[vas.b] + SBall(vas, 4), writes=[va.b(i)])
                    for (dst, dd) in ((aqTs, aqT), (akTs, akT), (mqTs, mqT)):
                        fw.dma("pool", fmv(dd), dst.t[:], reads=SBall(dst, 2), writes=[dd.b(i)])
                    for d in "fb":
                        fw.dma("pool", tmv(lfh[d]), lfhs[d].t[:], reads=SBall(lfhs[d], 4), writes=[lfh[d].b(i)])
                        fw.dma("pool", tmv(lfl[d]), lfls[d].t[:], reads=SBall(lfls[d], 4), writes=[lfl[d].b(i)])
                        fw.dma("pool", tmv(ktok[d]), ks[d].t[:], reads=SBall(ks[d], 4), writes=[ktok[d].b(i)])
                        fw.dma("pool", fmv(kT[d]), kTs[d].t[:], reads=SBall(kTs[d], 4), writes=[kT[d].b(i)])

                def mark_store_reads(i):
                    pass

                run_chains([prologue(0)], 1)
                for i in range(NT):
                    ph1, ph2, ph3 = tile_chains(i)
                    run_chains(ph1, 3)
                    nxt = [prologue(i + 1)] if i + 1 < NT else []
                    run_chains(nxt + ph2, 4)
                    run_chains(ph3, 3)
                    stores(i)
                fw.barrier()

        def pass_H(l):
            with ExitStack() as es:
                def sb(name, shape, dt):
                    return Tl(es.enter_context(nc.sbuf_tensor(f"{name}_L{l}", list(shape), dt)))
                nb = 3
                lft = [sb(f"H_lfh{j}", [128, 4, 512], BF16) for j in range(nb)]
                llt = [sb(f"H_lfl{j}", [128, 4, 512], BF16) for j in range(nb)]
                kt_ = [sb(f"H_k{j}", [128, 4, 512], BF16) for j in range(nb)]
                kTt = [sb(f"H_kT{j}", [128, 4, 512], BF16) for j in range(nb)]
                qTt = [sb(f"H_qT{j}", [128, 4, 512], BF16) for j in range(nb)]
                vt = [sb(f"H_v{j}", [128, 4, 512], BF16) for j in range(nb)]
                ex3 = [sb(f"H_ex3{j}", [128, 512], F32) for j in range(2)]
                kd2 = [sb(f"H_kd{j}", [128, 4, 512], BF16) for j in range(2)]
                vm = [[sb(f"H_vm{j}_{c}", [128, 4, 512], BF16) for c in range(2)] for j in range(nb)]
                for j in range(nb):
                    for c in range(2):
                        fw.op("pool", lambda e: e.memset(vm[j][c].t[:], 0.0), [], [vm[j][c].b])
                qx = [sb(f"H_qx{j}", [128, 512], F32) for j in range(2)]
                kx = [sb(f"H_kx{j}", [128, 512], F32) for j in range(2)]
                qe2 = [sb(f"H_qe{j}", [128, 4, 512], BF16) for j in range(2)]
                ke2 = [sb(f"H_ke{j}", [128, 4, 512], BF16) for j in range(2)]
                ext2 = [sb(f"H_ext{j}", [128, 64], F32) for j in range(2)]
                PT = [sb(f"H_PT{j}", [128, 4, 128], BF16) for j in range(3)]
                S = [sb(f"H_S{h}", [128, 128], F32) for h in range(4)]
                Sm = [sb(f"H_Sm{j}", [128, 128], BF16) for j in range(16)]
                oTs = [sb(f"H_oT{j}", [128, 4, 512], F32) for j in range(2)]
                K4 = sb("H_K4", [128, 4, 128], F32)
                for d in ("f", "b"):
                    fwd = d == "f"
                    An, Bn, Kn, Mn = ("A_f", "B_f", "K_f", "M_f") if fwd else ("A_b", "B_b", "K_b", "M_b")
                    for h in range(4):
                        cp("dve", K4.t[:, h, :], C32(Kn), [cst.b], [K4.b])
                        fw.op("dve", lambda e: e.memset(S[h].t[:], 0.0), [], [S[h].b])
                    order = list(range(NT)) if fwd else list(range(NT - 1, -1, -1))
                    pfree = list(ps_t)

                    def palloc():
                        return pfree.pop(0)

                    def prel(p):
                        pfree.append(p)

                    def load(n):
                        i = order[n]
                        j = n % nb
                        t0 = i * 512
                        fw.dma("sp", lft[j].t[:], lfh[d].ap[t0:t0 + 512, :].rearrange("(s p) f -> p s f", p=128),
                               reads=[lfh[d].b(i)], writes=[lft[j].b])
                        fw.dma("sp", llt[j].t[:], lfl[d].ap[t0:t0 + 512, :].rearrange("(s p) f -> p s f", p=128),
                               reads=[lfl[d].b(i)], writes=[llt[j].b])
                        fw.dma("sp", kt_[j].t[:], ktok[d].ap[t0:t0 + 512, :].rearrange("(s p) f -> p s f", p=128),
                               reads=[ktok[d].b(i)], writes=[kt_[j].b])
                        vsrc = vtok.ap[t0:t0 + 512, :].rearrange("(s p) f -> p s f", p=128)
                        fw.dma("sp", vt[j].t[:], vsrc, reads=[vtok.b(i)], writes=[vt[j].b])
                        for c in range(2):
                            fw.dma("sp", vm[j][c].t[c * 64:(c + 1) * 64, :, :], vsrc[c * 64:(c + 1) * 64],
                                   reads=[vtok.b(i)], writes=[vm[j][c].b])
                        fw.dma("sp", kTt[j].t[:], kT[d].ap[:, t0:t0 + 512].rearrange("(h p) t -> p h t", p=128),
                               reads=[kT[d].b(i)], writes=[kTt[j].b])
                        fw.dma("sp", qTt[j].t[:], qT.ap[:, t0:t0 + 512].rearrange("(h p) t -> p h t", p=128),
                               reads=[qT.b(i)], writes=[qTt[j].b])

                    def ephase(n):
                        j = n % nb
                        L, L2, K_, KT_, QT_ = lft[j], llt[j], kt_[j], kTt[j], qTt[j]
                        kd, qe, ke, ext = kd2[n % 2], qe2[n % 2], ke2[n % 2], ext2[n % 2]
                        for s in range(4):
                            p = palloc()
                            mm(p.t[:, :], C16(Bn), L.t[:, s, :], [cstb.b, L.b], [p.b], start=True, stop=False)
                            mm(p.t[:, :], C16(Bn), L2.t[:, s, :], [cstb.b, L2.b], [p.b], start=False, stop=True)
                            e3 = ex3[s % 2]
                            act(e3.t[:], p.t[:, :], AF.Exp, [p.b], [e3.b])
                            prel(p)
                            tt("pool", kd.t[:, s, :], K_.t[:, s, :], e3.t[:], ALU.mult, [K_.b, e3.b], [kd.b])
                        pe_ = palloc()
                        for h in range(4):
                            for s in range(4):
                                c0 = (h * 4 + s) * 4
                                mm(pe_.t[:, c0:c0 + 4], L.t[:, s, h * 128:(h + 1) * 128], C16(Mn, 4), [L.b, cstb.b], [pe_.b],
                                   start=True, stop=False)
                                mm(pe_.t[:, c0:c0 + 4], L2.t[:, s, h * 128:(h + 1) * 128], C16(Mn, 4), [L2.b, cstb.b], [pe_.b],
                                   start=False, stop=True)
                        act(ext.t[:], pe_.t[:, 0:64], AF.Exp, [pe_.b], [ext.b])
                        prel(pe_)
                        for h in range(4):
                            p = palloc()
                            for s in range(4):
                                mm(p.t[:, s * 128:(s + 1) * 128], L.t[:, s, h * 128:(h + 1) * 128], C16(An), [L.b, cstb.b], [p.b],
                                   start=True, stop=False)
                                mm(p.t[:, s * 128:(s + 1) * 128], L2.t[:, s, h * 128:(h + 1) * 128], C16(An), [L2.b, cstb.b], [p.b],
                                   start=False, stop=True)
                            a, b_ = qx[h % 2], kx[h % 2]
                            act(a.t[:], p.t[:, :], AF.Exp, [p.b], [a.b])
                            act(b_.t[:], p.t[:, :], AF.Exp, [p.b], [b_.b], scale=-1.0)
                            prel(p)
                            tt("dve", qe.t[:, h, :], QT_.t[:, h, :], a.t[:], ALU.mult, [QT_.b, a.b], [qe.b])
                            tt("pool", ke.t[:, h, :], KT_.t[:, h, :], b_.t[:], ALU.mult, [KT_.b, b_.b], [ke.b])

                    load(0)
                    if NT > 1:
                        load(1)
                    ephase(0)
                    smi = [0]
                    pti = [0]
                    for n in range(NT):
                        i = order[n]
                        j = n % nb
                        t0 = i * 512
                        V_, VM = vt[j], vm[j]
                        kd, qe, ke, ext = kd2[n % 2], qe2[n % 2], ke2[n % 2], ext2[n % 2]
                        o_ = oTs[n % 2]
                        subs = list(range(4)) if fwd else list(range(3, -1, -1))

                        def stageA(s):
                            ssl = slice(s * 128, (s + 1) * 128)
                            p = palloc()
                            for h in range(4):
                                mm(p.t[:, h * 128:(h + 1) * 128], ke.t[:, h, ssl], qe.t[:, h, ssl], [ke.b, qe.b], [p.b])
                            pt = PT[pti[0] % 3]
                            pti[0] += 1
                            tt("dve", pt.t[:], p.t[:, :].rearrange("p (h t) -> p h t", h=4), K4.t[:], ALU.mult, [p.b, K4.b], [pt.b])
                            prel(p)
                            pd = [palloc(), palloc()]
                            for c in range(2):
                                for h in range(4):
                                    hs = slice(h * 128, (h + 1) * 128)
                                    mm(pd[c].t[:, hs], kd.t[:, s, hs], VM[c].t[:, s, hs], [kd.b, VM[c].b], [pd[c].b])
                            return pt, pd

                        def stageC(s, pt, pd):
                            ssl = slice(s * 128, (s + 1) * 128)
                            tg = t0 + s * 128
                            po = palloc()
                            for h in range(4):
                                hs = slice(h * 128, (h + 1) * 128)
                                mm(po.t[:, hs], V_.t[:, s, hs], pt.t[:, h, :], [V_.b, pt.b], [po.b], start=(h == 0), stop=False)
                            chs = (0, 1) if fwd else (1, 0)
                            for ci, c in enumerate(chs):
                                tok = tg + c * 64
                                if (fwd and tok == HALF) or ((not fwd) and tok + 64 == HALF):
                                    for h in range(4):
                                        ts("dve", S[h].t[:], S[h].t[:], flag.t[:, 0:1], ALU.mult, [S[h].b, flag.b], [S[h].b])
                                sms = []
                                for h in range(4):
                                    ec = (h * 4 + s) * 4 + 2 * c
                                    sm_ = Sm[smi[0] % 16]
                                    smi[0] += 1
                                    act(sm_.t[:], S[h].t[:], AF.Copy, [S[h].b, ext.b], [sm_.b], scale=ext.t[:, ec:ec + 1])
                                    sms.append(sm_)
                                for h in range(4):
                                    ec = (h * 4 + s) * 4 + 2 * c
                                    stt(S[h].t[:], S[h].t[:], ext.t[:, ec + 1:ec + 2], pd[c].t[:, h * 128:(h + 1) * 128],
                                        ALU.mult, ALU.add, [S[h].b, ext.b, pd[c].b], [S[h].b])
                                for h in range(4):
                                    mm(po.t[:, h * 128 + c * 64:h * 128 + (c + 1) * 64], sms[h].t[:],
                                       qe.t[:, h, s * 128 + c * 64:s * 128 + (c + 1) * 64],
                                       [sms[h].b, qe.b], [po.b], start=False, stop=(ci == 1))
                            prel(pd[0])
                            prel(pd[1])
                            cp("dve", o_.t[:, :, ssl], po.t[:, :].rearrange("p (h t) -> p h t", h=4), [po.b], [o_.b])
                            prel(po)

                        if n + 2 < NT:
                            load(n + 2)
                        prev = None
                        for bi, s in enumerate(subs):
                            cur = (s,) + stageA(s)
                            if bi == 1 and n + 1 < NT:
                                ephase(n + 1)
                            if prev is not None:
                                stageC(*prev)
                            prev = cur
                        stageC(*prev)
                        fw.dma("pool", oT[d].ap[:, t0:t0 + 512].rearrange("(h p) t -> p h t", p=128), o_.t[:], reads=[o_.b],
                               writes=[oT[d].b(i)])
                fw.barrier()

        def attn_core(sb, NTq, qsrc, kv_for_tile, masks, out_row0, tag, mhl=None):
            LA = 5
            qm = [[sb(f"{tag}_qm{j}_{par}", [128, 2, 512], BF16) for par in range(2)] for j in range(2)]
            for j in range(2):
                for par in range(2):
                    fw.op("pool", lambda e: e.memset(qm[j][par].t[:], 0.0), [], [qm[j][par].b])
            gt = [sb(f"{tag}_g{j}", [128, 2, 512], BF16) for j in range(2)]
            pT = [sb(f"{tag}_pT{j}", [128, 512], BF16) for j in range(8)]
            sc = [sb(f"{tag}_sc{j}", [128, 512], F32) for j in range(6)] if masks is not None else None
            otok = sb(f"{tag}_otok", [128, 4, 256], BF16)
            rden = sb(f"{tag}_rden", [128, 4], F32)
            mo = [sb(f"{tag}_mo{j}", [128, 2, 512], BF16) for j in range(2)]
            pi = [0]
            for i in range(NTq):
                t0 = i * 512
                QM, G = qm[i % 2], gt[i % 2]
                qv = qsrc.ap[:, t0:t0 + 512].rearrange("(h p) t -> p h t", p=128)
                for par in range(2):
                    fw.dma("sp", QM[par].t[par * 64:(par + 1) * 64, :, :], qv[par * 64:(par + 1) * 64], reads=[qsrc.b(i)],
                           writes=[QM[par].b])
                fw.dma("sp", G.t[:], gT.ap[512 + out_row0:512 + out_row0 + 256, t0:t0 + 512].rearrange("(h p) t -> p h t", p=128),
                       reads=[gT.b(i)], writes=[G.b])
                ktl = kv_for_tile(i)
                nk = len(ktl)
                po_h = {}

                started = {}

                def stage1(h, ki):
                    ch, pr = h // 2, (h % 2) * 64
                    kfn, vfn, mi, kbufs = ktl[ki]
                    s0, s1 = SUBR[mi] if masks is not None else (0, 4)
                    cs = slice(s0 * 128, s1 * 128)
                    p = psum()
                    on_pe = masks is not None and mhl is not None and pi[0] % 3 == 2
                    mm(p.t[:, cs], kfn(ch, pr), QM[h % 2].t[:, ch, cs], kbufs + [QM[h % 2].b], [p.b], start=True, stop=not on_pe)
                    e_ = pT[pi[0] % 8]
                    if on_pe:
                        mm(p.t[:, cs], C16("ident"), mhl[0].t[:, mi, cs], [cstb.b, mhl[0].b], [p.b], start=False, stop=False)
                        mm(p.t[:, cs], C16("ident"), mhl[1].t[:, mi, cs], [cstb.b, mhl[1].b], [p.b], start=False, stop=True)
                        act(e_.t[:, cs], p.t[:, cs], AF.Exp, [p.b], [e_.b], scale=0.125)
                    elif masks is not None:
                        s_ = sc[pi[0] % 6]
                        tt("dve", s_.t[:, cs], p.t[:, cs], masks.t[:, mi, cs], ALU.add, [p.b, masks.b], [s_.b])
                        act(e_.t[:, cs], s_.t[:, cs], AF.Exp, [s_.b], [e_.b], scale=0.125)
                    else:
                        act(e_.t[:, cs], p.t[:, cs], AF.Exp, [p.b], [e_.b], scale=0.125)
                    pi[0] += 1
                    return e_

                def stage2(h, ki, e_):
                    kfn, vfn, mi, kbufs = ktl[ki]
                    if ki == 0:
                        po_h[h] = psum_acc()
                    po = po_h[h]
                    s0, s1 = SUBR[mi] if masks is not None else (0, 4)
                    for s in range(s0, s1):
                        mm(po.t[:, s * 128:s * 128 + 65], e_.t[:, s * 128:(s + 1) * 128], vfn(h), [e_.b] + kbufs, [po.b],
                           start=(h not in started), stop=(ki == nk - 1))
                        started[h] = True
                    if ki == nk - 1:
                        pov = po.t[:, :].rearrange("p (s e) -> p s e", e=128)
                        fw.op("dve", lambda e: e.reciprocal(out=rden.t[:, :], in_=pov[:, :, 64]), [po.b], [rden.b])
                        for s in range(4):
                            ts("dve", otok.t[:, s, h * 64:(h + 1) * 64], po.t[:, s * 128:s * 128 + 64], rden.t[:, s:s + 1],
                               ALU.mult, [po.b, rden.b], [otok.b])

                pend = []
                for pair in range(2):
                    for ki in range(nk):
                        for h in (2 * pair, 2 * pair + 1):
                            pend.append((h, ki, stage1(h, ki)))
                            if len(pend) > LA:
                                stage2(*pend.pop(0))
                while pend:
                    stage2(*pend.pop(0))
                M_ = mo[i % 2]
                for ch in range(2):
                    p = psum()
                    for s in range(4):
                        mm(p.t[:, s * 128:(s + 1) * 128], otok.t[:, s, ch * 128:(ch + 1) * 128], C16("ident"),
                           [otok.b, cstb.b], [p.b])
                    tt("dve", M_.t[:, ch, :], p.t[:, :], G.t[:, ch, :], ALU.mult, [p.b, G.b], [M_.b])
                fw.dma("pool", mixT.ap[out_row0:out_row0 + 256, t0:t0 + 512].rearrange("(h p) t -> p h t", p=128), M_.t[:],
                       reads=[M_.b], writes=[mixT.b((out_row0, i))])

        def pass_AT(l):
            with ExitStack() as es:
                def sb(name, shape, dt):
                    return Tl(es.enter_context(nc.sbuf_tensor(f"{name}_L{l}", list(shape), dt)))
                masks = sb("AT_masks", [128, 20, 512], F32)
                fw.dma("sp", masks.t[:], amask_in.rearrange("r p q -> p r q"), writes=[masks.b])
                NW = 20
                kw_ = [sb(f"AT_kw{j}", [128, 2, NW * 128], BF16) for j in range(2)]
                vw_ = [sb(f"AT_vw{j}", [128, NW, 260], BF16) for j in range(2)]

                def kv_for_tile(i):
                    KW, VW = kw_[i % 2], vw_[i % 2]
                    q0 = i * 512
                    half_lo = (q0 // HALF) * HALF
                    lo = max(q0 - 1024, 0)
                    hi = min(q0 + 512 + 1024, T)
                    n = (hi - lo) // 128
                    tiles = sorted(set(range(lo // 512, (hi + 511) // 512)))
                    fw.dma("sp", KW.t[:, :, 0:n * 128], akT.ap[:, lo:hi].rearrange("(h p) t -> p h t", p=128),
                           reads=[akT.b(x) for x in tiles], writes=[KW.b])
                    fw.dma("sp", VW.t[:, 0:n, :], va.ap[lo:hi, :].rearrange("(j p) f -> p j f", p=128),
                           reads=[va.b(x) for x in tiles], writes=[VW.b])
                    out = []
                    for jt in range(n):
                        k0 = lo + jt * 128
                        r = (k0 - q0) // 128
                        assert -8 <= r <= 11
                        if not (half_lo <= k0 < half_lo + HALF):
                            ts("dve", VW.t[:, jt, :], VW.t[:, jt, :], flag.t[:, 0:1], ALU.mult, [VW.b, flag.b], [VW.b])
                        out.append(((lambda ch, pr, jt=jt, KW=KW: KW.t[:, ch, jt * 128:(jt + 1) * 128]),
                                    (lambda h, jt=jt, VW=VW: VW.t[:, jt, h * 65:(h + 1) * 65]),
                                    r + 8, [KW.b, VW.b]))
                    return out
                mhl = [sb("AT_mhi", [128, 20, 512], BF16), sb("AT_mlo", [128, 20, 512], BF16)]
                for r in range(20):
                    cp("act", mhl[0].t[:, r, :], masks.t[:, r, :], [masks.b], [mhl[0].b])
                    tt("pool", mhl[1].t[:, r, :], masks.t[:, r, :], mhl[0].t[:, r, :], ALU.subtract, [masks.b, mhl[0].b], [mhl[1].b])
                attn_core(sb, NT, aqT, kv_for_tile, masks, 0, "AT", mhl=mhl)
                fw.barrier()

        def pass_ME(l):
            with ExitStack() as es:
                def sb(name, shape, dt):
                    return Tl(es.enter_context(nc.sbuf_tensor(f"{name}_L{l}", list(shape), dt)))
                wkv = sb("ME_w", [128, 8, 512], BF16)
                for kc in range(8):
                    fw.dma("pool", wkv.t[:, kc, :], mem_wkv[l, kc * 128:(kc + 1) * 128, :], writes=[wkv.b])
                mnw = sb("ME_mnw", [128, 8], F32)
                col_load(mnw, mem_norm_w[l], 8)
                gk = sb("ME_gk", [128, 1], F32)
                col_load64(gk, mk_w[l])
                mkT = [sb(f"ME_mkT{g}", [128, 2, 256], BF16) for g in range(2)]
                mv = [sb(f"ME_mv{g}", [128, 2, 260], BF16) for g in range(2)]
                xm = sb("ME_x", [128, 1024], F32)
                junk = sb("ME_junk", [128, 1024], BF16)
                ssm = sb("ME_ss", [128, 1], F32)
                hbm = sb("ME_hb", [128, 2, 1024], BF16)
                hTm = sb("ME_hT", [128, 8, 256], BF16)
                sqm = sb("ME_sq", [128, 256], BF16)
                rsm = sb("ME_rs", [128, 256], F32)
                for g in range(2):
                    fw.op("dve", lambda e: e.memset(mv[g].t[:], 1.0), [], [mv[g].b])
                    for s in range(2):
                        fw.dma("sp", xm.t[:], mem_in[g, s * 128:(s + 1) * 128, :], writes=[xm.b])
                        act(junk.t[:], xm.t[:], AF.Square, [xm.b], [junk.b, ssm.b], accum_out=ssm.t[:, 0:1])
                        rstd_inplace(ssm.t[:], 1024.0, ssm.b)
                        ts("dve", hbm.t[:, s, :], xm.t[:], ssm.t[:, 0:1], ALU.mult, [xm.b, ssm.b], [hbm.b])
                    for kc in range(8):
                        p = psum()
                        for s in range(2):
                            mm(p.t[:, s * 128:(s + 1) * 128], hbm.t[:, s, kc * 128:(kc + 1) * 128], C16("ident"),
                               [hbm.b, cstb.b], [p.b])
                        ts("dve", hTm.t[:, kc, :], p.t[:, 0:256], mnw.t[:, kc:kc + 1], ALU.mult, [p.b, mnw.b], [hTm.b])
                    for c in range(2):
                        p = psum()
                        for kc in range(8):
                            mm(p.t[:, 0:256], wkv.t[:, kc, c * 128:(c + 1) * 128], hTm.t[:, kc, :], [wkv.b, hTm.b], [p.b],
                               start=(kc == 0), stop=(kc == 7))
                        act(sqm.t[:], p.t[:, 0:256], AF.Square, [p.b], [sqm.b])
                        p2 = psum()
                        mm(p2.t[:, 0:256], C16("ones64"), sqm.t[:], [cstb.b, sqm.b], [p2.b])
                        act(rsm.t[:], p2.t[:, 0:256], AF.Ln, [p2.b], [rsm.b], scale=1.0 / 64, bias=EPS)
                        act(rsm.t[:], rsm.t[:], AF.Exp, [rsm.b], [rsm.b], scale=-0.5)
                        stt(mkT[g].t[:, c, :], p.t[:, 0:256], gk.t[:, 0:1], rsm.t[:], ALU.mult, ALU.mult,
                            [p.b, gk.b, rsm.b], [mkT[g].b])
                    for s in range(2):
                        p = psum()
                        for kc in range(8):
                            mm(p.t[:, 0:256], hTm.t[:, kc, s * 128:(s + 1) * 128], wkv.t[:, kc, 256:512], [hTm.b, wkv.b], [p.b],
                               start=(kc == 0), stop=(kc == 7))
                        cp("act", mv[g].t[:, s, :].rearrange("p (h e) -> p h e", e=65)[:, :, 0:64],
                           p.t[:, 0:256].rearrange("p (h e) -> p h e", e=64), [p.b], [mv[g].b])

                def kv_for_tile(i):
                    g = (i * 512) // HALF
                    out = []
                    for jt in range(2):
                        out.append(((lambda ch, pr, jt=jt, g=g: mkT[g].t[:, ch, jt * 128:(jt + 1) * 128]),
                                    (lambda h, jt=jt, g=g: mv[g].t[:, jt, h * 65:(h + 1) * 65]),
                                    0, [mkT[g].b, mv[g].b]))
                    return out
                attn_core(sb, NT, mqT, kv_for_tile, None, 256, "ME")
                fw.barrier()

        def pass_C(l):
            src = x_dt if l == 0 else y_out
            with ExitStack() as es:
                def sb(name, shape, dt):
                    return Tl(es.enter_context(nc.sbuf_tensor(f"{name}_L{l}", list(shape), dt)))
                wo = sb("C_w", [128, 8, 1024], BF16)
                for kc in range(8):
                    fw.dma("pool", wo.t[:, kc, :], w_out[l, kc * 128:(kc + 1) * 128, :], writes=[wo.b])
                gon = sb("C_gon", [128, 1], F32)
                fw.dma("sp", gon.t[:, 0:1], onorm_w[l].rearrange("(p o) -> p o", o=1), writes=[gon.b],
                       allow_slow_non_contiguous=True)
                of_ = [sb(f"C_of{j}", [128, 4, 512], F32) for j in range(2)]
                ob_ = [sb(f"C_ob{j}", [128, 4, 512], F32) for j in range(2)]
                gh = [sb(f"C_g{j}", [128, 4, 512], BF16) for j in range(2)]
                mx_ = [sb(f"C_mx{j}", [128, 8, 512], BF16) for j in range(2)]
                xt = [sb(f"C_x{j}", [128, 4, 1024], F32) for j in range(2)]
                osum = [sb(f"C_osum{h}", [128, 512], F32) for h in range(4)]
                sq = [sb(f"C_sq{h}", [128, 512], BF16) for h in range(4)]
                rsf = [sb(f"C_rsf{h}", [128, 512], F32) for h in range(4)]
                m1 = [sb(f"C_m1{h}", [128, 512], F32) for h in range(4)]
                mxb = [[Buf() for _ in range(8)] for _ in range(2)]
                yt = [sb(f"C_y{j}", [128, 1024], F32) for j in range(2)]

                def load(i):
                    j = i % 2
                    t0 = i * 512
                    fw.dma("sp", of_[j].t[:], oT["f"].ap[:, t0:t0 + 512].rearrange("(h p) t -> p h t", p=128),
                           reads=[oT["f"].b(i)], writes=[of_[j].b])
                    fw.dma("sp", ob_[j].t[:], oT["b"].ap[:, t0:t0 + 512].rearrange("(h p) t -> p h t", p=128),
                           reads=[oT["b"].b(i)], writes=[ob_[j].b])
                    fw.dma("sp", gh[j].t[:], gT.ap[0:512, t0:t0 + 512].rearrange("(h p) t -> p h t", p=128),
                           reads=[gT.b(i)], writes=[gh[j].b])
                    fw.dma("sp", mx_[j].t[:, 4:8, :], mixT.ap[:, t0:t0 + 512].rearrange("(h p) t -> p h t", p=128),
                           reads=[mixT.b((0, i)), mixT.b((256, i))], writes=mxb[j][4:8])
                    fw.dma("sp", xt[j].t[:], src.ap[t0:t0 + 512, :].rearrange("(s p) f -> p s f", p=128),
                           reads=[src.b(i)], writes=[xt[j].b])
                yi = [0]
                H4 = range(4)

                def normphase(i):
                    j = i % 2
                    for h in H4:
                        tt("pool", osum[h].t[:], of_[j].t[:, h, :], ob_[j].t[:, h, :], ALU.add, [of_[j].b, ob_[j].b], [osum[h].b])
                    for h in H4:
                        act(sq[h].t[:], osum[h].t[:], AF.Square, [osum[h].b], [sq[h].b])
                    pp = []
                    for h in H4:
                        p = psum()
                        mm(p.t[:, :], C16("ones128"), sq[h].t[:], [cstb.b, sq[h].b], [p.b])
                        pp.append(p)
                    for h in H4:
                        act(rsf[h].t[:], pp[h].t[:, :], AF.Ln, [pp[h].b], [rsf[h].b], scale=1.0 / 128, bias=EPS)
                    for h in H4:
                        act(rsf[h].t[:], rsf[h].t[:], AF.Exp, [rsf[h].b], [rsf[h].b], scale=-0.5)
                    for h in H4:
                        stt(m1[h].t[:], osum[h].t[:], gon.t[:, 0:1], rsf[h].t[:], ALU.mult, ALU.mult,
                            [osum[h].b, gon.b, rsf[h].b], [m1[h].b])
                    for h in H4:
                        tt("dve", mx_[j].t[:, h, :], m1[h].t[:], gh[j].t[:, h, :], ALU.mult, [m1[h].b, gh[j].b], [mxb[j][h]])

                def outproj(i):
                    j = i % 2
                    t0 = i * 512
                    for s in range(4):
                        Y = yt[yi[0] % 2]
                        yi[0] += 1
                        for nh in range(2):
                            p = psum()
                            for mc in range(8):
                                mm(p.t[:, :], mx_[j].t[:, mc, s * 128:(s + 1) * 128], wo.t[:, mc, nh * 512:(nh + 1) * 512],
                                   [mxb[j][mc], wo.b], [p.b], start=(mc == 0), stop=(mc == 7))
                            tt("dve", Y.t[:, nh * 512:(nh + 1) * 512], p.t[:, :], xt[j].t[:, s, nh * 512:(nh + 1) * 512], ALU.add,
                               [p.b, xt[j].b], [Y.b])
                        fw.dma("pool", y_out.ap[t0 + s * 128:t0 + (s + 1) * 128, :], Y.t[:], reads=[Y.b], writes=[y_out.b(i)])

                load(0)
                if NT > 1:
                    load(1)
                normphase(0)
                for i in range(NT):
                    if i + 1 < NT:
                        normphase(i + 1)
                    outproj(i)
                    if i + 2 < NT:
                        load(i + 2)
                fw.barrier()

        _P = _os.environ.get("KPASSES", "A,HF,HB,AT,ME,C").split(",")
        for l in range(depth):
            if "A" in _P:
                pass_A(l)
            if "HF" in _P:
                pass_H(l)
            if "AT" in _P:
                pass_AT(l)
            if "ME" in _P:
                pass_ME(l)
            if "C" in _P:
                pass_C(l)
        fw.barrier()
    return nc, fw


T_CORE = 16384
DEPTH = 4
_CACHE = {}


def kernel(x_prompt, x_sample, mem_prompt, mem_sample, norm_w, w_in, hgrn_lb_fwd, hgrn_lb_bwd, hgrn_onorm_w,
           attn_qnorm_w, attn_knorm_w, mem_norm_w, mem_wkv, mem_qnorm_w, mem_knorm_w, w_out):
    f = lambda a: np.ascontiguousarray(np.asarray(a, dtype=np.float32))
    x_prompt, x_sample, mem_prompt, mem_sample = f(x_prompt), f(x_sample), f(mem_prompt), f(mem_sample)
    T = T_CORE
    if "nc" not in _CACHE:
        _CACHE["nc"] = build(T, DEPTH)[0]
        _CACHE["hc"] = host_consts(T)
    nc = _CACHE["nc"]
    hc = _CACHE["hc"]
    pos_p = np.concatenate([np.arange(8192), np.arange(8192)])
    pos_s = np.arange(16384)
    Cp, Sp = rope_tables(pos_p)
    Cs, Ss = rope_tables(pos_s)
    shared = {"cst": hc["cst"], "amask": hc["amask"], "norm_w": f(norm_w), "w_in": f(w_in),
              "hgrn_lb_fwd": f(hgrn_lb_fwd), "hgrn_lb_bwd": f(hgrn_lb_bwd), "hgrn_onorm_w": f(hgrn_onorm_w),
              "attn_qnorm_w": f(attn_qnorm_w), "attn_knorm_w": f(attn_knorm_w), "mem_norm_w": f(mem_norm_w),
              "mem_wkv": f(mem_wkv), "mem_qnorm_w": f(mem_qnorm_w), "mem_knorm_w": f(mem_knorm_w), "w_out": f(w_out)}
    in_maps = []
    for c in range(8):
        m = dict(shared)
        if c < 4:
            m["x"] = x_prompt[2 * c:2 * c + 2].reshape(T, 1024)
            m["mem"] = mem_prompt[2 * c:2 * c + 2]
            m["flag"] = np.zeros((128, 1), np.float32)
            m["ropeC"], m["ropeS"] = Cp, Sp
        else:
            s = (c - 4) % 2
            m["x"] = x_sample[s]
            m["mem"] = np.stack([mem_sample[s], mem_sample[s]])
            m["flag"] = np.ones((128, 1), np.float32)
            m["ropeC"], m["ropeS"] = Cs, Ss
        in_maps.append(m)
    res = run_bass_kernel_spmd(nc, in_maps, core_ids=list(range(8)))
    ys = [np.asarray(r["y"], dtype=np.float32) for r in res.results]
    y_prompt = np.stack([ys[c].reshape(2, 8192, 1024) for c in range(4)]).reshape(8, 8192, 1024)
    y_sample = np.stack([ys[4], ys[5]])
    return (y_prompt, y_sample)
```

```python
import math
import os as _os
from contextlib import ExitStack
import numpy as np
import concourse.bass as bass
import concourse.mybir as mybir
from concourse.bass_utils import run_bass_kernel_spmd

F32 = mybir.dt.float32
BF16 = mybir.dt.bfloat16
AF = mybir.ActivationFunctionType
ALU = mybir.AluOpType
EPS = 1e-6
NSLOT = 8


class Buf:
    __slots__ = ("w", "r")

    def __init__(self):
        self.w = None
        self.r = {}


class Tl:
    def __init__(self, t):
        self.t = t
        self.b = Buf()


class FW:
    LIM = 30000

    def __init__(self, nc):
        self.nc = nc
        self.engs = {"pe": nc.tensor, "act": nc.scalar, "dve": nc.vector, "pool": nc.gpsimd, "sp": nc.sync}
        self.cur = {}
        self.seen = {k: {} for k in self.engs}
        self.last = {}
        self.nsem = 0
        self.dslots = {"sp": [], "pool": []}
        self.dnext = {"sp": 0, "pool": 0}
        self.nins = 0

    def newsem(self):
        self.nsem += 1
        return [self.nsem, self.nc.alloc_semaphore(name=f"fs{self.nsem}")]

    def _wait(self, e, ev):
        if self.seen[e].get(ev[0], 0) >= ev[2]:
            return
        self.engs[e].wait_ge(ev[1], ev[2])
        self.seen[e][ev[0]] = ev[2]

    def _deps(self, e, reads, writes):
        for b in reads:
            if b.w is not None and not (e == "pe" and b.w[3] == "pe"):
                self._wait(e, b.w)
        for b in writes:
            if b.w is not None and not (e == "pe" and b.w[3] == "pe"):
                self._wait(e, b.w)
            for ev in b.r.values():
                if not (e == "pe" and ev[3] == "pe"):
                    self._wait(e, ev)

    def _mark(self, ev, key, reads, writes):
        for b in reads:
            b.r[key] = ev
        for b in writes:
            b.w = ev
            b.r = {}

    def op(self, e, fn, reads=(), writes=()):
        self._deps(e, reads, writes)
        ins = fn(self.engs[e])
        c = self.cur.get(e)
        if c is None or c[2] >= self.LIM:
            c = self.newsem() + [0]
            self.cur[e] = c
        c[2] += 1
        ins.then_inc(c[1], 1)
        ev = (c[0], c[1], c[2], e)
        self._mark(ev, e, reads, writes)
        self.last[e] = ev
        self.nins += 1

    def dma(self, q, out, in_, reads=(), writes=(), **kw):
        self._deps(q, reads, writes)
        slots = self.dslots[q]
        if len(slots) < NSLOT:
            slots.append(self.newsem() + [0])
            s = slots[-1]
        else:
            s = slots[self.dnext[q] % NSLOT]
        self.dnext[q] += 1
        if s[2] > 0:
            self._wait(q, (s[0], s[1], s[2], "dma"))
        if s[2] + 16 > self.LIM:
            s[:] = self.newsem() + [0]
        ins = self.engs[q].dma_start(out=out, in_=in_, **kw)
        s[2] += 16
        ins.then_inc(s[1], 16)
        ev = (s[0], s[1], s[2], "dma")
        self._mark(ev, ("d", s[0], s[2]), reads, writes)
        self.nins += 1

    def barrier(self):
        evs = [self.last[e] for e in ("pe", "act", "dve", "pool") if e in self.last]
        for q in self.dslots:
            for s in self.dslots[q]:
                if s[2] > 0:
                    evs.append((s[0], s[1], s[2], "dma"))
        for e in self.engs:
            for ev in evs:
                if e == "pe" and ev[3] == "pe":
                    continue
                self._wait(e, ev)


class DT:
    def __init__(self, ap):
        self.ap = ap
        self.bufs = {}

    def b(self, i):
        if i not in self.bufs:
            self.bufs[i] = Buf()
        return self.bufs[i]


def host_consts(T):
    c = {}
    s = np.arange(128)[:, None]
    t = np.arange(128)[None, :]
    same = (s // 64) == (t // 64)
    c0 = (t // 64) * 64
    A_f = (same & (s <= t)).astype(np.float32) - (same & (s <= c0 + 31)).astype(np.float32)
    A_b = (same & (s >= t)).astype(np.float32) - (same & (s >= c0 + 32)).astype(np.float32)
    B_f = (same & (s > t)).astype(np.float32)
    B_b = (same & (s < t)).astype(np.float32)
    M_f = np.zeros((128, 4), np.float32)
    M_b = np.zeros((128, 4), np.float32)
    sv = np.arange(128)
    for ch in range(2):
        inc = (sv // 64) == ch
        M_f[:, 2 * ch] = inc & (sv <= ch * 64 + 31)
        M_f[:, 2 * ch + 1] = inc
        M_b[:, 2 * ch] = inc & (sv >= ch * 64 + 32)
        M_b[:, 2 * ch + 1] = inc
    K_f = (same & (s <= t)).astype(np.float32)
    K_b = (same & (s >= t)).astype(np.float32)
    ident = np.eye(128, dtype=np.float32)
    ones64 = ((s // 64) == (t // 64)).astype(np.float32)
    ones128 = np.ones((128, 128), np.float32)
    Rm = np.zeros((128, 128), np.float32)
    for m in range(128):
        j = m % 64
        if j < 8:
            Rm[m + 8, m] = -1.0
        elif j < 16:
            Rm[m - 8, m] = 1.0
    cst = np.concatenate([A_f, A_b, B_f, B_b, K_f, K_b, ident, ones64, ones128, Rm, M_f, M_b], axis=1)
    c["cst"] = np.ascontiguousarray(cst.astype(np.float32))
    am = np.zeros((20, 128, 512), np.float32)
    j = np.arange(128)[:, None]
    i = np.arange(512)[None, :]
    for ri, r in enumerate(range(-8, 12)):
        d = r * 128 + j - i
        am[ri] = ((np.abs(d) <= 64).astype(np.float32)
                  + ((d % 4 == 0) & (np.abs(d) <= 256)).astype(np.float32)
                  + ((d % 16 == 0) & (np.abs(d) <= 1024)).astype(np.float32))
    c["amask"] = np.where(am > 0, 8.0 * np.log(np.maximum(am, 1.0)), -80000.0).astype(np.float32)
    return c


def rope_tables(pos):
    half = 8
    inv = (500000.0 ** (-np.arange(half, dtype=np.float32) * 2.0 / 16.0)).astype(np.float32)
    ang = pos.astype(np.float32)[None, :] * inv[:, None]
    C = np.ones((128, pos.shape[0]), np.float32)
    S = np.zeros((128, pos.shape[0]), np.float32)
    for hb in (0, 64):
        C[hb:hb + 8] = np.cos(ang)
        C[hb + 8:hb + 16] = np.cos(ang)
        S[hb:hb + 8] = np.sin(ang)
        S[hb + 8:hb + 16] = np.sin(ang)
    return C, S


def _sub_ranges():
    out = []
    j = np.arange(128)[:, None]
    i = np.arange(512)[None, :]
    for r in range(-8, 12):
        d = r * 128 + j - i
        ok = (np.abs(d) <= 64) | ((d % 4 == 0) & (np.abs(d) <= 256)) | ((d % 16 == 0) & (np.abs(d) <= 1024))
        subs = [s for s in range(4) if ok[:, s * 128:(s + 1) * 128].any()]
        out.append((min(subs), max(subs) + 1))
    return out


SUBR = _sub_ranges()
CO = {"A_f": 0, "A_b": 128, "B_f": 256, "B_b": 384, "K_f": 512, "K_b": 640, "ident": 768, "ones64": 896,
      "ones128": 1024, "Rm": 1152, "M_f": 1280, "M_b": 1284}
NCST = 1288


def build(T, depth, debug=False):
    NT = T // 512
    HALF = T // 2
    nc = bass.Bass("TRN2", target_bir_lowering=False)
    fw = FW(nc)

    def din(name, shape, dt=F32):
        return nc.dram_tensor(name, list(shape), dt, kind="ExternalInput").ap()

    x_in = din("x", [T, 1024])
    mem_in = din("mem", [2, 256, 1024])
    flag_in = din("flag", [128, 1])
    ropeC_in = din("ropeC", [128, T])
    ropeS_in = din("ropeS", [128, T])
    cst_in = din("cst", [128, NCST])
    amask_in = din("amask", [20, 128, 512])
    norm_w = din("norm_w", [depth, 1024])
    w_in = din("w_in", [depth, 1024, 4096])
    lbp = {"f": din("hgrn_lb_fwd", [depth, 512]), "b": din("hgrn_lb_bwd", [depth, 512])}
    onorm_w = din("hgrn_onorm_w", [depth, 128])
    aq_w = din("attn_qnorm_w", [depth, 64])
    ak_w = din("attn_knorm_w", [depth, 64])
    mem_norm_w = din("mem_norm_w", [depth, 1024])
    mem_wkv = din("mem_wkv", [depth, 1024, 512])
    mq_w = din("mem_qnorm_w", [depth, 64])
    mk_w = din("mem_knorm_w", [depth, 64])
    w_out = din("w_out", [depth, 1024, 1024])
    y_out = DT(nc.dram_tensor("y", [T, 1024], F32, kind="ExternalOutput").ap())
    x_dt = DT(x_in)

    skind = "ExternalOutput" if debug else "Internal"

    def scr(name, shape, dt):
        return DT(nc.dram_tensor(name, list(shape), dt, kind=skind).ap())

    qT = scr("s_qT", [512, T], BF16)
    kT = {"f": scr("s_kTf", [512, T], BF16), "b": scr("s_kTb", [512, T], BF16)}
    ktok = {"f": scr("s_kf", [T, 512], BF16), "b": scr("s_kb", [T, 512], BF16)}
    lfh = {"f": scr("s_lfhf", [T, 512], BF16), "b": scr("s_lfhb", [T, 512], BF16)}
    lfl = {"f": scr("s_lflf", [T, 512], BF16), "b": scr("s_lflb", [T, 512], BF16)}
    vtok = scr("s_v", [T, 512], BF16)
    gT = scr("s_gT", [1024, T], BF16)
    aqT = scr("s_aqT", [256, T], BF16)
    akT = scr("s_akT", [256, T], BF16)
    va = scr("s_va", [T, 260], BF16)
    mqT = scr("s_mqT", [256, T], BF16)
    oT = {"f": scr("s_ofT", [512, T], F32), "b": scr("s_obT", [512, T], F32)}
    mixT = scr("s_mixT", [512, T], BF16)

    es0 = ExitStack()
    with es0:
        def sb0(name, shape, dt):
            return Tl(es0.enter_context(nc.sbuf_tensor(name, list(shape), dt)))

        ps_t = [Tl(es0.enter_context(nc.psum_tensor(f"ps{i}", [128, 512], F32))) for i in range(8)]
        ps_i = [0]

        def psum():
            p = ps_t[ps_i[0] % 6]
            ps_i[0] += 1
            return p
        pa_i = [0]

        def psum_acc():
            p = ps_t[6 + pa_i[0] % 2]
            pa_i[0] += 1
            return p

        def mm(out, lhsT, rhs, reads, writes, start=True, stop=True):
            fw.op("pe", lambda e: e.matmul(out, lhsT=lhsT, rhs=rhs, start=start, stop=stop), reads, writes)

        def act(out, in_, func, reads, writes, **kw):
            fw.op("act", lambda e: e.activation(out=out, in_=in_, func=func, **kw), reads, writes)

        def tt(eng, out, in0, in1, op, reads, writes):
            fw.op(eng, lambda e: e.tensor_tensor(out=out, in0=in0, in1=in1, op=op), reads, writes)

        def ts(eng, out, in0, s1, op0, reads, writes, s2=None, op1=None):
            if op1 is None:
                fw.op(eng, lambda e: e.tensor_scalar(out=out, in0=in0, scalar1=s1, scalar2=None, op0=op0), reads, writes)
            else:
                fw.op(eng, lambda e: e.tensor_scalar(out=out, in0=in0, scalar1=s1, scalar2=s2, op0=op0, op1=op1),
                      reads, writes)

        def stt(out, in0, scalar, in1, op0, op1, reads, writes):
            fw.op("dve", lambda e: e.scalar_tensor_tensor(out=out, in0=in0, scalar=scalar, in1=in1, op0=op0, op1=op1),
                  reads, writes)

        def cp(eng, out, in_, reads, writes):
            if eng == "act":
                fw.op("act", lambda e: e.copy(out=out, in_=in_), reads, writes)
            else:
                fw.op(eng, lambda e: e.tensor_copy(out=out, in_=in_), reads, writes)

        cst = sb0("cst_sb", [128, NCST], F32)
        fw.dma("sp", cst.t[:], cst_in[:, :], writes=[cst.b])
        cstb = sb0("cstb_sb", [128, NCST], BF16)
        cp("dve", cstb.t[:], cst.t[:], [cst.b], [cstb.b])
        flag = sb0("flag_sb", [128, 1], F32)
        fw.dma("sp", flag.t[:], flag_in[:, :], writes=[flag.b])

        def C32(name, w=128):
            return cst.t[:, CO[name]:CO[name] + w]

        def C16(name, w=128):
            return cstb.t[:, CO[name]:CO[name] + w]

        def col_load(dst, src1d, n):
            fw.dma("sp", dst.t[:, 0:n], src1d.rearrange("(c p) -> p c", p=128), writes=[dst.b],
                   allow_slow_non_contiguous=True)

        def col_load64(dst, src1d):
            for hb in (0, 64):
                fw.dma("sp", dst.t[hb:hb + 64, 0:1], src1d.rearrange("(p o) -> p o", o=1), writes=[dst.b],
                       allow_slow_non_contiguous=True)

        def rstd_inplace(tl_ap, n, reads_b):
            act(tl_ap, tl_ap, AF.Ln, [reads_b], [reads_b], scale=1.0 / n, bias=EPS)
            act(tl_ap, tl_ap, AF.Exp, [reads_b], [reads_b], scale=-0.5)

        def pass_A(l):
            src = x_dt if l == 0 else y_out
            with ExitStack() as es:
                def sb(name, shape, dt):
                    return Tl(es.enter_context(nc.sbuf_tensor(f"{name}_L{l}", list(shape), dt)))
                w = sb("A_w", [128, 8, 4096], BF16)
                for kc in range(8):
                    for q4 in range(4):
                        fw.dma("pool", w.t[:, kc, q4 * 1024:(q4 + 1) * 1024],
                               w_in[l, kc * 128:(kc + 1) * 128, q4 * 1024:(q4 + 1) * 1024], writes=[w.b])
                normw = sb("A_normw", [128, 8], F32)
                col_load(normw, norm_w[l], 8)
                gq = sb("A_gq", [128, 1], F32)
                gk = sb("A_gk", [128, 1], F32)
                gm = sb("A_gm", [128, 1], F32)
                col_load64(gq, aq_w[l])
                col_load64(gk, ak_w[l])
                col_load64(gm, mq_w[l])
                oml = {d: sb(f"A_oml{d}", [128, 512], F32) for d in "fb"}
                with ExitStack() as es2:
                    def sb2(name, shape, dt):
                        return Tl(es2.enter_context(nc.sbuf_tensor(f"{name}_L{l}", list(shape), dt)))
                    row = sb2("A_lbrow", [1, depth, 512], F32)
                    mx = sb2("A_lbmx", [1, 512], F32)
                    sm = sb2("A_lbsm", [1, 512], F32)
                    acc = sb2("A_lbacc", [1, 512], F32)
                    tmp = sb2("A_lbtmp", [1, 512], F32)
                    for d in ("f", "b"):
                        fw.dma("sp", row.t[0:1, :, :], lbp[d].rearrange("(o l) f -> o l f", o=1), writes=[row.b])
                        cp("dve", mx.t[:], row.t[0:1, 0, :], [row.b], [mx.b])
                        for j in range(1, depth):
                            tt("dve", mx.t[:], mx.t[:], row.t[0:1, j, :], ALU.max, [mx.b, row.b], [mx.b])
                        for j in range(depth):
                            tt("dve", row.t[0:1, j, :], row.t[0:1, j, :], mx.t[:], ALU.subtract, [row.b, mx.b], [row.b])
                        act(row.t[:], row.t[:], AF.Exp, [row.b], [row.b])
                        cp("dve", sm.t[:], row.t[0:1, 0, :], [row.b], [sm.b])
                        for j in range(1, depth):
                            tt("dve", sm.t[:], sm.t[:], row.t[0:1, j, :], ALU.add, [sm.b, row.b], [sm.b])
                        fw.op("dve", lambda e: e.reciprocal(out=sm.t[:], in_=sm.t[:]), [sm.b], [sm.b])
                        fw.op("dve", lambda e: e.memset(acc.t[:], 0.0), [], [acc.b])
                        for j in range(1, l + 1):
                            tt("dve", tmp.t[:], row.t[0:1, j, :], sm.t[:], ALU.mult, [row.b, sm.b], [tmp.b])
                            tt("dve", acc.t[:], acc.t[:], tmp.t[:], ALU.add, [acc.b, tmp.b], [acc.b])
                        ts("dve", acc.t[:], acc.t[:], -1.0, ALU.mult, [acc.b], [acc.b], s2=1.0, op1=ALU.add)
                        p = psum()
                        mm(p.t[:, :], C32("ones128")[0:1, :], acc.t[0:1, :], [cst.b, acc.b], [p.b])
                        cp("dve", oml[d].t[:], p.t[:, :], [p.b], [oml[d].b])
                    fw.barrier()

                xt = [sb(f"A_x{i}", [128, 1024], F32) for i in range(4)]
                ss = sb("A_ss", [128, 4], F32)
                hb = sb("A_hb", [128, 4, 1024], BF16)
                hTr = [sb(f"A_hT{j}", [128, 8, 512], BF16) for j in range(2)]
                qTs = sb("A_qTs", [128, 4, 512], BF16)
                gTs = sb("A_gTs", [128, 8, 512], BF16)
                aqTs = sb("A_aqTs", [128, 2, 512], BF16)
                akTs = sb("A_akTs", [128, 2, 512], BF16)
                mqTs = sb("A_mqTs", [128, 2, 512], BF16)
                lfhs = {d: sb(f"A_lfhs{d}", [128, 4, 512], BF16) for d in "fb"}
                lfls = {d: sb(f"A_lfls{d}", [128, 4, 512], BF16) for d in "fb"}
                ks = {d: sb(f"A_ks{d}", [128, 4, 512], BF16) for d in "fb"}
                kTs = {d: sb(f"A_kTs{d}", [128, 4, 512], BF16) for d in "fb"}
                vs = sb("A_vs", [128, 4, 512], BF16)
                vas = sb("A_vas", [128, 4, 260], BF16)
                fw.op("dve", lambda e: e.memset(vas.t[:], 1.0), [], [vas.b])
                NR = 2
                sq = [sb(f"A_sq{j}", [128, 512], BF16) for j in range(NR)]
                rsf = [sb(f"A_rsf{j}", [128, 512], F32) for j in range(NR)]
                qn = [sb(f"A_qn{j}", [128, 512], F32) for j in range(NR)]
                qnb = [sb(f"A_qnb{j}", [128, 512], BF16) for j in range(NR)]
                t1 = rsf
                t2 = [sb(f"A_t2{j}", [128, 512], F32) for j in range(NR)]
                rC = [sb(f"A_rC{j}", [128, 512], F32) for j in range(2)]
                rS = [sb(f"A_rS{j}", [128, 512], F32) for j in range(2)]
                NG = 2
                sg = [sb(f"A_sg{j}", [128, 512], F32) for j in range(NG)]
                k32 = [sb(f"A_k32{j}", [128, 512], F32) for j in range(NG)]
                sub_b = {}

                def SB(tl, idx):
                    key = (id(tl), idx)
                    if key not in sub_b:
                        sub_b[key] = Buf()
                    return sub_b[key]

                def SBall(tl, n):
                    return [SB(tl, j) for j in range(n)]
                pfree = list(ps_t)

                def palloc():
                    return pfree.pop(0)

                def prel(p):
                    pfree.append(p)

                def run_chains(gens, K):
                    active = []
                    it = iter(gens)
                    done = False
                    while True:
                        while len(active) < K and not done:
                            g = next(it, None)
                            if g is None:
                                done = True
                            else:
                                active.append(g)
                        if not active:
                            break
                        for g in list(active):
                            try:
                                next(g)
                            except StopIteration:
                                active.remove(g)

                def prologue(i):
                    t0 = i * 512
                    hT = hTr[i % 2]
                    fw.dma("sp", rC[i % 2].t[:], ropeC_in[:, t0:t0 + 512], writes=[rC[i % 2].b])
                    fw.dma("sp", rS[i % 2].t[:], ropeS_in[:, t0:t0 + 512], writes=[rS[i % 2].b])
                    for s in range(4):
                        fw.dma("sp", xt[s].t[:], src.ap[t0 + s * 128:t0 + (s + 1) * 128, :], reads=[src.b(i)], writes=[xt[s].b])
                        act(hb.t[:, s, :], xt[s].t[:], AF.Square, [xt[s].b], [SB(hb, s), ss.b], accum_out=ss.t[:, s:s + 1])
                        yield
                    rstd_inplace(ss.t[:], 1024.0, ss.b)
                    yield
                    for s in range(4):
                        ts("dve", hb.t[:, s, :], xt[s].t[:], ss.t[:, s:s + 1], ALU.mult, [xt[s].b, ss.b], [SB(hb, s)])
                        yield
                    for kc in range(8):
                        p = palloc()
                        for s in range(4):
                            mm(p.t[:, s * 128:(s + 1) * 128], hb.t[:, s, kc * 128:(kc + 1) * 128], C16("ident"),
                               [SB(hb, s), cstb.b], [p.b])
                        yield
                        ts("dve", hT.t[:, kc, :], p.t[:, :], normw.t[:, kc:kc + 1], ALU.mult, [p.b, normw.b], [SB(hT, kc)])
                        prel(p)
                        yield

                rfree = list(range(NR))
                gfree = list(range(NG))

                def tile_chains(i):
                    hT = hTr[i % 2]
                    hTb = SBall(hT, 8)
                    RC, RS = rC[i % 2], rS[i % 2]

                    def fm(p, col0):
                        for kc in range(8):
                            mm(p.t[:, :], w.t[:, kc, col0:col0 + 128], hT.t[:, kc, :], [w.b, hTb[kc]], [p.b],
                               start=(kc == 0), stop=(kc == 7))

                    def tm(p, s, col0, n):
                        for kc in range(8):
                            mm(p.t[:, 0:n], hT.t[:, kc, s * 128:(s + 1) * 128], w.t[:, kc, col0:col0 + n],
                               [hTb[kc], w.b], [p.b], start=(kc == 0), stop=(kc == 7))

                    def silu_chain(col0, dst, c):
                        p = palloc()
                        fm(p, col0)
                        yield
                        act(dst.t[:, c, :], p.t[:, :], AF.Silu, [p.b], [SB(dst, c)])
                        prel(p)

                    def v_chain(s):
                        p = palloc()
                        tm(p, s, 1536, 512)
                        yield
                        cp("act", vs.t[:, s, :], p.t[:, :], [p.b], [SB(vs, s)])
                        prel(p)

                    def av_chain(s):
                        p = palloc()
                        tm(p, s, 2560, 256)
                        yield
                        cp("act", vas.t[:, s, :].rearrange("p (h e) -> p h e", e=65)[:, :, 0:64],
                           p.t[:, 0:256].rearrange("p (h e) -> p h e", e=64), [p.b], [vas.b, SB(vas, s)])
                        prel(p)

                    def norm_chain(col0, gain, dst, c, rope):
                        while not rfree:
                            yield
                        r = rfree.pop(0)
                        p = palloc()
                        fm(p, col0 + c * 128)
                        yield
                        act(sq[r].t[:], p.t[:, :], AF.Square, [p.b], [sq[r].b])
                        yield
                        p2 = palloc()
                        mm(p2.t[:, :], C16("ones64"), sq[r].t[:], [cstb.b, sq[r].b], [p2.b])
                        yield
                        act(rsf[r].t[:], p2.t[:, :], AF.Ln, [p2.b], [rsf[r].b], scale=1.0 / 64, bias=EPS)
                        prel(p2)
                        act(rsf[r].t[:], rsf[r].t[:], AF.Exp, [rsf[r].b], [rsf[r].b], scale=-0.5)
                        yield
                        if not rope:
                            stt(dst.t[:, c, :], p.t[:, :], gain.t[:, 0:1], rsf[r].t[:], ALU.mult, ALU.mult,
                                [p.b, gain.b, rsf[r].b], [SB(dst, c)])
                            prel(p)
                            rfree.append(r)
                            return
                        stt(qn[r].t[:], p.t[:, :], gain.t[:, 0:1], rsf[r].t[:], ALU.mult, ALU.mult,
                            [p.b, gain.b, rsf[r].b], [qn[r].b])
                        prel(p)
                        yield
                        cp("act", qnb[r].t[:], qn[r].t[:], [qn[r].b], [qnb[r].b])
                        tt("pool", t2[r].t[:], qn[r].t[:], RC.t[:], ALU.mult, [qn[r].b, RC.b], [t2[r].b])
                        yield
                        p3 = palloc()
                        mm(p3.t[:, :], C16("Rm"), qnb[r].t[:], [cstb.b, qnb[r].b], [p3.b])
                        yield
                        tt("dve", t1[r].t[:], p3.t[:, :], RS.t[:], ALU.mult, [p3.b, RS.b], [t1[r].b])
                        prel(p3)
                        yield
                        tt("dve", dst.t[:, c, :], t1[r].t[:], t2[r].t[:], ALU.add, [t1[r].b, t2[r].b], [SB(dst, c)])
                        rfree.append(r)

                    def fgate_chain(s, d, col0):
                        while not gfree:
                            yield
                        g = gfree.pop(0)
                        p = palloc()
                        tm(p, s, col0, 512)
                        yield
                        act(sg[g].t[:], p.t[:, :], AF.Exp, [p.b], [sg[g].b])
                        prel(p)
                        yield
                        act(sg[g].t[:], sg[g].t[:], AF.Ln, [sg[g].b], [sg[g].b], bias=1.0)
                        yield
                        act(sg[g].t[:], sg[g].t[:], AF.Exp, [sg[g].b], [sg[g].b], scale=-1.0)
                        yield
                        tt("dve", k32[g].t[:], sg[g].t[:], oml[d].t[:], ALU.mult, [sg[g].b, oml[d].b], [k32[g].b])
                        yield
                        act(sg[g].t[:], k32[g].t[:], AF.Ln, [k32[g].b], [sg[g].b], scale=-1.0, bias=1.0)
                        cp("pool", ks[d].t[:, s, :], k32[g].t[:], [k32[g].b], [SB(ks[d], s)])
                        yield
                        cp("act", lfhs[d].t[:, s, :], sg[g].t[:], [sg[g].b], [SB(lfhs[d], s)])
                        yield
                        tt("pool", lfls[d].t[:, s, :], sg[g].t[:], lfhs[d].t[:, s, :], ALU.subtract,
                           [sg[g].b, SB(lfhs[d], s)], [SB(lfls[d], s)])
                        gfree.append(g)

                    def kT_chain(d, h):
                        p = palloc()
                        for s in range(4):
                            mm(p.t[:, s * 128:(s + 1) * 128], ks[d].t[:, s, h * 128:(h + 1) * 128], C16("ident"),
                               [SB(ks[d], s), cstb.b], [p.b])
                        yield
                        cp("dve", kTs[d].t[:, h, :], p.t[:, :], [p.b], [SB(kTs[d], h)])
                        prel(p)

                    ph1 = []
                    sil = [silu_chain(c * 128, qTs, c) for c in range(4)] + [silu_chain(3072 + c * 128, gTs, c) for c in range(8)]
                    cps = []
                    for s in range(4):
                        cps += [v_chain(s), av_chain(s)]
                    for j in range(12):
                        ph1.append(sil[j])
                        if j < 8:
                            ph1.append(cps[j])
                    nrm = []
                    for (col0, gain, dst, rope) in ((2048, gq, aqTs, True), (2304, gk, akTs, True), (2816, gm, mqTs, False)):
                        for c in range(2):
                            nrm.append(norm_chain(col0, gain, dst, c, rope))
                    fg = []
                    for s in range(4):
                        fg += [fgate_chain(s, "f", 512), fgate_chain(s, "b", 1024)]
                    ph2 = []
                    for j in range(8):
                        ph2.append(fg[j])
                        if j < 6:
                            ph2.append(nrm[j])
                    ph3 = [kT_chain(d, h) for d in "fb" for h in range(4)]
                    return ph1, ph2, ph3

                def stores(i):
                    t0 = i * 512
                    fmv = lambda dd: dd.ap[:, t0:t0 + 512].rearrange("(h p) t -> p h t", p=128)
                    tmv = lambda dd: dd.ap[t0:t0 + 512, :].rearrange("(s p) f -> p s f", p=128)
                    fw.dma("pool", fmv(qT), qTs.t[:], reads=SBall(qTs, 4), writes=[qT.b(i)])
                    fw.dma("pool", fmv(gT), gTs.t[:], reads=SBall(gTs, 8), writes=[gT.b(i)])
                    fw.dma("pool", tmv(vtok), vs.t[:], reads=SBall(vs, 4), writes=[vtok.b(i)])
                    fw.dma("pool", tmv(va), vas.t[:], reads=[vas.b] + SBall(vas, 4), writes=[va.b(i)])
                    for (dst, dd) in ((aqTs, aqT), (akTs, akT), (mqTs, mqT)):
                        fw.dma("pool", fmv(dd), dst.t[:], reads=SBall(dst, 2), writes=[dd.b(i)])
                    for d in "fb":
                        fw.dma("pool", tmv(lfh[d]), lfhs[d].t[:], reads=SBall(lfhs[d], 4), writes=[lfh[d].b(i)])
                        fw.dma("pool", tmv(lfl[d]), lfls[d].t[:], reads=SBall(lfls[d], 4), writes=[lfl[d].b(i)])
                        fw.dma("pool", tmv(ktok[d]), ks[d].t[:], reads=SBall(ks[d], 4), writes=[ktok[d].b(i)])
                        fw.dma("pool", fmv(kT[d]), kTs[d].t[:], reads=SBall(kTs[d], 4), writes=[kT[d].b(i)])

                def mark_store_reads(i):
                    pass

                run_chains([prologue(0)], 1)
                for i in range(NT):
                    ph1, ph2, ph3 = tile_chains(i)
                    run_chains(ph1, 3)
                    nxt = [prologue(i + 1)] if i + 1 < NT else []
                    run_chains(nxt + ph2, 4)
                    run_chains(ph3, 3)
                    stores(i)
                fw.barrier()

        def pass_H(l, d):
            fwd = d == "f"
            with ExitStack() as es:
                def sb(name, shape, dt):
                    return Tl(es.enter_context(nc.sbuf_tensor(f"{name}{d}_L{l}", list(shape), dt)))
                nb = 3
                lft = [sb(f"H_lfh{j}", [128, 4, 512], BF16) for j in range(nb)]
                llt = [sb(f"H_lfl{j}", [128, 4, 512], BF16) for j in range(nb)]
                kt_ = [sb(f"H_k{j}", [128, 4, 512], BF16) for j in range(nb)]
                kTt = [sb(f"H_kT{j}", [128, 4, 512], BF16) for j in range(nb)]
                qTt = [sb(f"H_qT{j}", [128, 4, 512], BF16) for j in range(nb)]
                vt = [sb(f"H_v{j}", [128, 4, 512], BF16) for j in range(nb)]
                ex3 = [sb(f"H_ex3{j}", [128, 512], F32) for j in range(2)]
                kd2 = [sb(f"H_kd{j}", [128, 4, 512], BF16) for j in range(2)]
                vm = [[sb(f"H_vm{j}_{c}", [128, 4, 512], BF16) for c in range(2)] for j in range(nb)]
                for j in range(nb):
                    for c in range(2):
                        fw.op("pool", lambda e: e.memset(vm[j][c].t[:], 0.0), [], [vm[j][c].b])
                qx = [sb(f"H_qx{j}", [128, 512], F32) for j in range(2)]
                kx = [sb(f"H_kx{j}", [128, 512], F32) for j in range(2)]
                qe2 = [sb(f"H_qe{j}", [128, 4, 512], BF16) for j in range(2)]
                ke2 = [sb(f"H_ke{j}", [128, 4, 512], BF16) for j in range(2)]
                ext2 = [sb(f"H_ext{j}", [128, 64], F32) for j in range(2)]
                PT = [sb(f"H_PT{j}", [128, 4, 128], BF16) for j in range(3)]
                S = [sb(f"H_S{h}", [128, 128], F32) for h in range(4)]
                Sm = [sb(f"H_Sm{j}", [128, 128], BF16) for j in range(16)]
                oTs = [sb(f"H_oT{j}", [128, 4, 512], F32) for j in range(2)]
                ofl = [sb(f"H_ofl{j}", [128, 4, 512], F32) for j in range(nb)] if not fwd else None
                An, Bn, Kn, Mn = ("A_f", "B_f", "K_f", "M_f") if fwd else ("A_b", "B_b", "K_b", "M_b")
                K4 = sb("H_K4", [128, 4, 128], F32)
                for h in range(4):
                    cp("dve", K4.t[:, h, :], C32(Kn), [cst.b], [K4.b])
                    fw.op("dve", lambda e: e.memset(S[h].t[:], 0.0), [], [S[h].b])
                order = list(range(NT)) if fwd else list(range(NT - 1, -1, -1))
                pfree = list(ps_t)

                def palloc():
                    return pfree.pop(0)

                def prel(p):
                    pfree.append(p)

                def load(n):
                    i = order[n]
                    j = n % nb
                    t0 = i * 512
                    fw.dma("sp", lft[j].t[:], lfh[d].ap[t0:t0 + 512, :].rearrange("(s p) f -> p s f", p=128),
                           reads=[lfh[d].b(i)], writes=[lft[j].b])
                    fw.dma("sp", llt[j].t[:], lfl[d].ap[t0:t0 + 512, :].rearrange("(s p) f -> p s f", p=128),
                           reads=[lfl[d].b(i)], writes=[llt[j].b])
                    fw.dma("sp", kt_[j].t[:], ktok[d].ap[t0:t0 + 512, :].rearrange("(s p) f -> p s f", p=128),
                           reads=[ktok[d].b(i)], writes=[kt_[j].b])
                    vsrc = vtok.ap[t0:t0 + 512, :].rearrange("(s p) f -> p s f", p=128)
                    fw.dma("sp", vt[j].t[:], vsrc, reads=[vtok.b(i)], writes=[vt[j].b])
                    for c in range(2):
                        fw.dma("sp", vm[j][c].t[c * 64:(c + 1) * 64, :, :], vsrc[c * 64:(c + 1) * 64],
                               reads=[vtok.b(i)], writes=[vm[j][c].b])
                    fw.dma("sp", kTt[j].t[:], kT[d].ap[:, t0:t0 + 512].rearrange("(h p) t -> p h t", p=128),
                           reads=[kT[d].b(i)], writes=[kTt[j].b])
                    fw.dma("sp", qTt[j].t[:], qT.ap[:, t0:t0 + 512].rearrange("(h p) t -> p h t", p=128),
                           reads=[qT.b(i)], writes=[qTt[j].b])
                    if not fwd:
                        fw.dma("sp", ofl[j].t[:], oT["f"].ap[:, t0:t0 + 512].rearrange("(h p) t -> p h t", p=128),
                               reads=[oT["f"].b(i)], writes=[ofl[j].b])

                def ephase(n):
                    j = n % nb
                    L, L2, K_, KT_, QT_ = lft[j], llt[j], kt_[j], kTt[j], qTt[j]
                    kd, qe, ke, ext = kd2[n % 2], qe2[n % 2], ke2[n % 2], ext2[n % 2]
                    for s in range(4):
                        p = palloc()
                        mm(p.t[:, :], C16(Bn), L.t[:, s, :], [cstb.b, L.b], [p.b], start=True, stop=False)
                        mm(p.t[:, :], C16(Bn), L2.t[:, s, :], [cstb.b, L2.b], [p.b], start=False, stop=True)
                        e3 = ex3[s % 2]
                        act(e3.t[:], p.t[:, :], AF.Exp, [p.b], [e3.b])
                        prel(p)
                        tt("pool", kd.t[:, s, :], K_.t[:, s, :], e3.t[:], ALU.mult, [K_.b, e3.b], [kd.b])
                    pe_ = palloc()
                    for h in range(4):
                        for s in range(4):
                            c0 = (h * 4 + s) * 4
                            mm(pe_.t[:, c0:c0 + 4], L.t[:, s, h * 128:(h + 1) * 128], C16(Mn, 4), [L.b, cstb.b], [pe_.b],
                               start=True, stop=False)
                            mm(pe_.t[:, c0:c0 + 4], L2.t[:, s, h * 128:(h + 1) * 128], C16(Mn, 4), [L2.b, cstb.b], [pe_.b],
                               start=False, stop=True)
                    act(ext.t[:], pe_.t[:, 0:64], AF.Exp, [pe_.b], [ext.b])
                    prel(pe_)
                    for h in range(4):
                        p = palloc()
                        for s in range(4):
                            mm(p.t[:, s * 128:(s + 1) * 128], L.t[:, s, h * 128:(h + 1) * 128], C16(An), [L.b, cstb.b], [p.b],
                               start=True, stop=False)
                            mm(p.t[:, s * 128:(s + 1) * 128], L2.t[:, s, h * 128:(h + 1) * 128], C16(An), [L2.b, cstb.b], [p.b],
                               start=False, stop=True)
                        a, b_ = qx[h % 2], kx[h % 2]
                        act(a.t[:], p.t[:, :], AF.Exp, [p.b], [a.b])
                        act(b_.t[:], p.t[:, :], AF.Exp, [p.b], [b_.b], scale=-1.0)
                        prel(p)
                        tt("dve", qe.t[:, h, :], QT_.t[:, h, :], a.t[:], ALU.mult, [QT_.b, a.b], [qe.b])
                        tt("pool", ke.t[:, h, :], KT_.t[:, h, :], b_.t[:], ALU.mult, [KT_.b, b_.b], [ke.b])

                load(0)
                if NT > 1:
                    load(1)
                ephase(0)
                smi = [0]
                pti = [0]
                for n in range(NT):
                    i = order[n]
                    j = n % nb
                    t0 = i * 512
                    V_, VM = vt[j], vm[j]
                    kd, qe, ke, ext = kd2[n % 2], qe2[n % 2], ke2[n % 2], ext2[n % 2]
                    o_ = oTs[n % 2]
                    subs = list(range(4)) if fwd else list(range(3, -1, -1))

                    def stageA(s):
                        ssl = slice(s * 128, (s + 1) * 128)
                        p = palloc()
                        for h in range(4):
                            mm(p.t[:, h * 128:(h + 1) * 128], ke.t[:, h, ssl], qe.t[:, h, ssl], [ke.b, qe.b], [p.b])
                        pt = PT[pti[0] % 3]
                        pti[0] += 1
                        tt("dve", pt.t[:], p.t[:, :].rearrange("p (h t) -> p h t", h=4), K4.t[:], ALU.mult, [p.b, K4.b], [pt.b])
                        prel(p)
                        pd = [palloc(), palloc()]
                        for c in range(2):
                            for h in range(4):
                                hs = slice(h * 128, (h + 1) * 128)
                                mm(pd[c].t[:, hs], kd.t[:, s, hs], VM[c].t[:, s, hs], [kd.b, VM[c].b], [pd[c].b])
                        return pt, pd

                    def stageC(s, pt, pd):
                        ssl = slice(s * 128, (s + 1) * 128)
                        tg = t0 + s * 128
                        po = palloc()
                        for h in range(4):
                            hs = slice(h * 128, (h + 1) * 128)
                            mm(po.t[:, hs], V_.t[:, s, hs], pt.t[:, h, :], [V_.b, pt.b], [po.b], start=(h == 0), stop=False)
                        chs = (0, 1) if fwd else (1, 0)
                        for ci, c in enumerate(chs):
                            tok = tg + c * 64
                            if (fwd and tok == HALF) or ((not fwd) and tok + 64 == HALF):
                                for h in range(4):
                                    ts("dve", S[h].t[:], S[h].t[:], flag.t[:, 0:1], ALU.mult, [S[h].b, flag.b], [S[h].b])
                            sms = []
                            for h in range(4):
                                ec = (h * 4 + s) * 4 + 2 * c
                                sm_ = Sm[smi[0] % 16]
                                smi[0] += 1
                                act(sm_.t[:], S[h].t[:], AF.Copy, [S[h].b, ext.b], [sm_.b], scale=ext.t[:, ec:ec + 1])
                                sms.append(sm_)
                            for h in range(4):
                                ec = (h * 4 + s) * 4 + 2 * c
                                stt(S[h].t[:], S[h].t[:], ext.t[:, ec + 1:ec + 2], pd[c].t[:, h * 128:(h + 1) * 128],
                                    ALU.mult, ALU.add, [S[h].b, ext.b, pd[c].b], [S[h].b])
                            for h in range(4):
                                mm(po.t[:, h * 128 + c * 64:h * 128 + (c + 1) * 64], sms[h].t[:],
                                   qe.t[:, h, s * 128 + c * 64:s * 128 + (c + 1) * 64],
                                   [sms[h].b, qe.b], [po.b], start=False, stop=(ci == 1))
                        prel(pd[0])
                        prel(pd[1])
                        if fwd:
                            cp("dve", o_.t[:, :, ssl], po.t[:, :].rearrange("p (h t) -> p h t", h=4), [po.b], [o_.b])
                        else:
                            tt("dve", o_.t[:, :, ssl], po.t[:, :].rearrange("p (h t) -> p h t", h=4), ofl[j].t[:, :, ssl],
                               ALU.add, [po.b, ofl[j].b], [o_.b])
                        prel(po)

                    if n + 2 < NT:
                        load(n + 2)
                    prev = None
                    for bi, s in enumerate(subs):
                        cur = (s,) + stageA(s)
                        if bi == 1 and n + 1 < NT:
                            ephase(n + 1)
                        if prev is not None:
                            stageC(*prev)
                        prev = cur
                    stageC(*prev)
                    fw.dma("pool", oT[d].ap[:, t0:t0 + 512].rearrange("(h p) t -> p h t", p=128), o_.t[:], reads=[o_.b],
                           writes=[oT[d].b(i)])
                fw.barrier()

        def attn_core(sb, NTq, qsrc, kv_for_tile, masks, out_row0, tag, mhl=None):
            LA = 5
            qm = [[sb(f"{tag}_qm{j}_{par}", [128, 2, 512], BF16) for par in range(2)] for j in range(2)]
            for j in range(2):
                for par in range(2):
                    fw.op("pool", lambda e: e.memset(qm[j][par].t[:], 0.0), [], [qm[j][par].b])
            gt = [sb(f"{tag}_g{j}", [128, 2, 512], BF16) for j in range(2)]
            pT = [sb(f"{tag}_pT{j}", [128, 512], BF16) for j in range(8)]
            sc = [sb(f"{tag}_sc{j}", [128, 512], F32) for j in range(6)] if masks is not None else None
            otok = sb(f"{tag}_otok", [128, 4, 256], BF16)
            rden = sb(f"{tag}_rden", [128, 4], F32)
            mo = [sb(f"{tag}_mo{j}", [128, 2, 512], BF16) for j in range(2)]
            pi = [0]
            for i in range(NTq):
                t0 = i * 512
                QM, G = qm[i % 2], gt[i % 2]
                qv = qsrc.ap[:, t0:t0 + 512].rearrange("(h p) t -> p h t", p=128)
                for par in range(2):
                    fw.dma("sp", QM[par].t[par * 64:(par + 1) * 64, :, :], qv[par * 64:(par + 1) * 64], reads=[qsrc.b(i)],
                           writes=[QM[par].b])
                fw.dma("sp", G.t[:], gT.ap[512 + out_row0:512 + out_row0 + 256, t0:t0 + 512].rearrange("(h p) t -> p h t", p=128),
                       reads=[gT.b(i)], writes=[G.b])
                ktl = kv_for_tile(i)
                nk = len(ktl)
                po_h = {}

                started = {}

                def stage1(h, ki):
                    ch, pr = h // 2, (h % 2) * 64
                    kfn, vfn, mi, kbufs = ktl[ki]
                    s0, s1 = SUBR[mi] if masks is not None else (0, 4)
                    cs = slice(s0 * 128, s1 * 128)
                    p = psum()
                    on_pe = masks is not None and mhl is not None and pi[0] % 3 == 2
                    mm(p.t[:, cs], kfn(ch, pr), QM[h % 2].t[:, ch, cs], kbufs + [QM[h % 2].b], [p.b], start=True, stop=not on_pe)
                    e_ = pT[pi[0] % 8]
                    if on_pe:
                        mm(p.t[:, cs], C16("ident"), mhl[0].t[:, mi, cs], [cstb.b, mhl[0].b], [p.b], start=False, stop=False)
                        mm(p.t[:, cs], C16("ident"), mhl[1].t[:, mi, cs], [cstb.b, mhl[1].b], [p.b], start=False, stop=True)
                        act(e_.t[:, cs], p.t[:, cs], AF.Exp, [p.b], [e_.b], scale=0.125)
                    elif masks is not None:
                        s_ = sc[pi[0] % 6]
                        tt("dve", s_.t[:, cs], p.t[:, cs], masks.t[:, mi, cs], ALU.add, [p.b, masks.b], [s_.b])
                        act(e_.t[:, cs], s_.t[:, cs], AF.Exp, [s_.b], [e_.b], scale=0.125)
                    else:
                        act(e_.t[:, cs], p.t[:, cs], AF.Exp, [p.b], [e_.b], scale=0.125)
                    pi[0] += 1
                    return e_

                def stage2(h, ki, e_):
                    kfn, vfn, mi, kbufs = ktl[ki]
                    if ki == 0:
                        po_h[h] = psum_acc()
                    po = po_h[h]
                    s0, s1 = SUBR[mi] if masks is not None else (0, 4)
                    for s in range(s0, s1):
                        mm(po.t[:, s * 128:s * 128 + 65], e_.t[:, s * 128:(s + 1) * 128], vfn(h), [e_.b] + kbufs, [po.b],
                           start=(h not in started), stop=(ki == nk - 1))
                        started[h] = True
                    if ki == nk - 1:
                        pov = po.t[:, :].rearrange("p (s e) -> p s e", e=128)
                        fw.op("dve", lambda e: e.reciprocal(out=rden.t[:, :], in_=pov[:, :, 64]), [po.b], [rden.b])
                        for s in range(4):
                            ts("dve", otok.t[:, s, h * 64:(h + 1) * 64], po.t[:, s * 128:s * 128 + 64], rden.t[:, s:s + 1],
                               ALU.mult, [po.b, rden.b], [otok.b])

                pend = []
                for pair in range(2):
                    for ki in range(nk):
                        for h in (2 * pair, 2 * pair + 1):
                            pend.append((h, ki, stage1(h, ki)))
                            if len(pend) > LA:
                                stage2(*pend.pop(0))
                while pend:
                    stage2(*pend.pop(0))
                M_ = mo[i % 2]
                for ch in range(2):
                    p = psum()
                    for s in range(4):
                        mm(p.t[:, s * 128:(s + 1) * 128], otok.t[:, s, ch * 128:(ch + 1) * 128], C16("ident"),
                           [otok.b, cstb.b], [p.b])
                    tt("dve", M_.t[:, ch, :], p.t[:, :], G.t[:, ch, :], ALU.mult, [p.b, G.b], [M_.b])
                fw.dma("pool", mixT.ap[out_row0:out_row0 + 256, t0:t0 + 512].rearrange("(h p) t -> p h t", p=128), M_.t[:],
                       reads=[M_.b], writes=[mixT.b((out_row0, i))])

        def pass_AT(l):
            with ExitStack() as es:
                def sb(name, shape, dt):
                    return Tl(es.enter_context(nc.sbuf_tensor(f"{name}_L{l}", list(shape), dt)))
                masks = sb("AT_masks", [128, 20, 512], F32)
                fw.dma("sp", masks.t[:], amask_in.rearrange("r p q -> p r q"), writes=[masks.b])
                NW = 20
                kw_ = [sb(f"AT_kw{j}", [128, 2, NW * 128], BF16) for j in range(2)]
                vw_ = [sb(f"AT_vw{j}", [128, NW, 260], BF16) for j in range(2)]

                def kv_for_tile(i):
                    KW, VW = kw_[i % 2], vw_[i % 2]
                    q0 = i * 512
                    half_lo = (q0 // HALF) * HALF
                    lo = max(q0 - 1024, 0)
                    hi = min(q0 + 512 + 1024, T)
                    n = (hi - lo) // 128
                    tiles = sorted(set(range(lo // 512, (hi + 511) // 512)))
                    fw.dma("sp", KW.t[:, :, 0:n * 128], akT.ap[:, lo:hi].rearrange("(h p) t -> p h t", p=128),
                           reads=[akT.b(x) for x in tiles], writes=[KW.b])
                    fw.dma("sp", VW.t[:, 0:n, :], va.ap[lo:hi, :].rearrange("(j p) f -> p j f", p=128),
                           reads=[va.b(x) for x in tiles], writes=[VW.b])
                    out = []
                    for jt in range(n):
                        k0 = lo + jt * 128
                        r = (k0 - q0) // 128
                        assert -8 <= r <= 11
                        if not (half_lo <= k0 < half_lo + HALF):
                            ts("dve", VW.t[:, jt, :], VW.t[:, jt, :], flag.t[:, 0:1], ALU.mult, [VW.b, flag.b], [VW.b])
                        out.append(((lambda ch, pr, jt=jt, KW=KW: KW.t[:, ch, jt * 128:(jt + 1) * 128]),
                                    (lambda h, jt=jt, VW=VW: VW.t[:, jt, h * 65:(h + 1) * 65]),
                                    r + 8, [KW.b, VW.b]))
                    return out
                mhl = [sb("AT_mhi", [128, 20, 512], BF16), sb("AT_mlo", [128, 20, 512], BF16)]
                for r in range(20):
                    cp("act", mhl[0].t[:, r, :], masks.t[:, r, :], [masks.b], [mhl[0].b])
                    tt("pool", mhl[1].t[:, r, :], masks.t[:, r, :], mhl[0].t[:, r, :], ALU.subtract, [masks.b, mhl[0].b], [mhl[1].b])
                attn_core(sb, NT, aqT, kv_for_tile, masks, 0, "AT", mhl=mhl)
                fw.barrier()

        def pass_ME(l):
            with ExitStack() as es:
                def sb(name, shape, dt):
                    return Tl(es.enter_context(nc.sbuf_tensor(f"{name}_L{l}", list(shape), dt)))
                wkv = sb("ME_w", [128, 8, 512], BF16)
                for kc in range(8):
                    fw.dma("pool", wkv.t[:, kc, :], mem_wkv[l, kc * 128:(kc + 1) * 128, :], writes=[wkv.b])
                mnw = sb("ME_mnw", [128, 8], F32)
                col_load(mnw, mem_norm_w[l], 8)
                gk = sb("ME_gk", [128, 1], F32)
                col_load64(gk, mk_w[l])
                mkT = [sb(f"ME_mkT{g}", [128, 2, 256], BF16) for g in range(2)]
                mv = [sb(f"ME_mv{g}", [128, 2, 260], BF16) for g in range(2)]
                xm = sb("ME_x", [128, 1024], F32)
                junk = sb("ME_junk", [128, 1024], BF16)
                ssm = sb("ME_ss", [128, 1], F32)
                hbm = sb("ME_hb", [128, 2, 1024], BF16)
                hTm = sb("ME_hT", [128, 8, 256], BF16)
                sqm = sb("ME_sq", [128, 256], BF16)
                rsm = sb("ME_rs", [128, 256], F32)
                for g in range(2):
                    fw.op("dve", lambda e: e.memset(mv[g].t[:], 1.0), [], [mv[g].b])
                    for s in range(2):
                        fw.dma("sp", xm.t[:], mem_in[g, s * 128:(s + 1) * 128, :], writes=[xm.b])
                        act(junk.t[:], xm.t[:], AF.Square, [xm.b], [junk.b, ssm.b], accum_out=ssm.t[:, 0:1])
                        rstd_inplace(ssm.t[:], 1024.0, ssm.b)
                        ts("dve", hbm.t[:, s, :], xm.t[:], ssm.t[:, 0:1], ALU.mult, [xm.b, ssm.b], [hbm.b])
                    for kc in range(8):
                        p = psum()
                        for s in range(2):
                            mm(p.t[:, s * 128:(s + 1) * 128], hbm.t[:, s, kc * 128:(kc + 1) * 128], C16("ident"),
                               [hbm.b, cstb.b], [p.b])
                        ts("dve", hTm.t[:, kc, :], p.t[:, 0:256], mnw.t[:, kc:kc + 1], ALU.mult, [p.b, mnw.b], [hTm.b])
                    for c in range(2):
                        p = psum()
                        for kc in range(8):
                            mm(p.t[:, 0:256], wkv.t[:, kc, c * 128:(c + 1) * 128], hTm.t[:, kc, :], [wkv.b, hTm.b], [p.b],
                               start=(kc == 0), stop=(kc == 7))
                        act(sqm.t[:], p.t[:, 0:256], AF.Square, [p.b], [sqm.b])
                        p2 = psum()
                        mm(p2.t[:, 0:256], C16("ones64"), sqm.t[:], [cstb.b, sqm.b], [p2.b])
                        act(rsm.t[:], p2.t[:, 0:256], AF.Ln, [p2.b], [rsm.b], scale=1.0 / 64, bias=EPS)
                        act(rsm.t[:], rsm.t[:], AF.Exp, [rsm.b], [rsm.b], scale=-0.5)
                        stt(mkT[g].t[:, c, :], p.t[:, 0:256], gk.t[:, 0:1], rsm.t[:], ALU.mult, ALU.mult,
                            [p.b, gk.b, rsm.b], [mkT[g].b])
                    for s in range(2):
                        p = psum()
                        for kc in range(8):
                            mm(p.t[:, 0:256], hTm.t[:, kc, s * 128:(s + 1) * 128], wkv.t[:, kc, 256:512], [hTm.b, wkv.b], [p.b],
                               start=(kc == 0), stop=(kc == 7))
                        cp("act", mv[g].t[:, s, :].rearrange("p (h e) -> p h e", e=65)[:, :, 0:64],
                           p.t[:, 0:256].rearrange("p (h e) -> p h e", e=64), [p.b], [mv[g].b])

                def kv_for_tile(i):
                    g = (i * 512) // HALF
                    out = []
                    for jt in range(2):
                        out.append(((lambda ch, pr, jt=jt, g=g: mkT[g].t[:, ch, jt * 128:(jt + 1) * 128]),
                                    (lambda h, jt=jt, g=g: mv[g].t[:, jt, h * 65:(h + 1) * 65]),
                                    0, [mkT[g].b, mv[g].b]))
                    return out
                attn_core(sb, NT, mqT, kv_for_tile, None, 256, "ME")
                fw.barrier()

        def pass_C(l):
            src = x_dt if l == 0 else y_out
            with ExitStack() as es:
                def sb(name, shape, dt):
                    return Tl(es.enter_context(nc.sbuf_tensor(f"{name}_L{l}", list(shape), dt)))
                wo = sb("C_w", [128, 8, 1024], BF16)
                for kc in range(8):
                    fw.dma("pool", wo.t[:, kc, :], w_out[l, kc * 128:(kc + 1) * 128, :], writes=[wo.b])
                gon = sb("C_gon", [128, 1], F32)
                fw.dma("sp", gon.t[:, 0:1], onorm_w[l].rearrange("(p o) -> p o", o=1), writes=[gon.b],
                       allow_slow_non_contiguous=True)
                of_ = [sb(f"C_of{j}", [128, 4, 512], F32) for j in range(2)]
                ob_ = [sb(f"C_ob{j}", [128, 4, 512], F32) for j in range(2)]
                gh = [sb(f"C_g{j}", [128, 4, 512], BF16) for j in range(2)]
                mx_ = [sb(f"C_mx{j}", [128, 8, 512], BF16) for j in range(2)]
                xt = [sb(f"C_x{j}", [128, 4, 1024], F32) for j in range(2)]
                osum = [sb(f"C_osum{h}", [128, 512], F32) for h in range(4)]
                sq = [sb(f"C_sq{h}", [128, 512], BF16) for h in range(4)]
                rsf = [sb(f"C_rsf{h}", [128, 512], F32) for h in range(4)]
                m1 = [sb(f"C_m1{h}", [128, 512], F32) for h in range(4)]
                mxb = [[Buf() for _ in range(8)] for _ in range(2)]
                yt = [sb(f"C_y{j}", [128, 1024], F32) for j in range(2)]

                def load(i):
                    j = i % 2
                    t0 = i * 512
                    fw.dma("sp", ob_[j].t[:], oT["b"].ap[:, t0:t0 + 512].rearrange("(h p) t -> p h t", p=128),
                           reads=[oT["b"].b(i)], writes=[ob_[j].b])
                    fw.dma("sp", gh[j].t[:], gT.ap[0:512, t0:t0 + 512].rearrange("(h p) t -> p h t", p=128),
                           reads=[gT.b(i)], writes=[gh[j].b])
                    fw.dma("sp", mx_[j].t[:, 4:8, :], mixT.ap[:, t0:t0 + 512].rearrange("(h p) t -> p h t", p=128),
                           reads=[mixT.b((0, i)), mixT.b((256, i))], writes=mxb[j][4:8])
                    fw.dma("sp", xt[j].t[:], src.ap[t0:t0 + 512, :].rearrange("(s p) f -> p s f", p=128),
                           reads=[src.b(i)], writes=[xt[j].b])
                yi = [0]
                H4 = range(4)

                def normphase(i):
                    j = i % 2
                    for h in H4:
                        act(sq[h].t[:], ob_[j].t[:, h, :], AF.Square, [ob_[j].b], [sq[h].b])
                    pp = []
                    for h in H4:
                        p = psum()
                        mm(p.t[:, :], C16("ones128"), sq[h].t[:], [cstb.b, sq[h].b], [p.b])
                        pp.append(p)
                    for h in H4:
                        act(rsf[h].t[:], pp[h].t[:, :], AF.Ln, [pp[h].b], [rsf[h].b], scale=1.0 / 128, bias=EPS)
                    for h in H4:
                        act(rsf[h].t[:], rsf[h].t[:], AF.Exp, [rsf[h].b], [rsf[h].b], scale=-0.5)
                    for h in H4:
                        stt(m1[h].t[:], ob_[j].t[:, h, :], gon.t[:, 0:1], rsf[h].t[:], ALU.mult, ALU.mult,
                            [ob_[j].b, gon.b, rsf[h].b], [m1[h].b])
                    for h in H4:
                        tt("dve", mx_[j].t[:, h, :], m1[h].t[:], gh[j].t[:, h, :], ALU.mult, [m1[h].b, gh[j].b], [mxb[j][h]])

                def outproj(i):
                    j = i % 2
                    t0 = i * 512
                    for s in range(4):
                        Y = yt[yi[0] % 2]
                        yi[0] += 1
                        for nh in range(2):
                            p = psum()
                            for mc in range(8):
                                mm(p.t[:, :], mx_[j].t[:, mc, s * 128:(s + 1) * 128], wo.t[:, mc, nh * 512:(nh + 1) * 512],
                                   [mxb[j][mc], wo.b], [p.b], start=(mc == 0), stop=(mc == 7))
                            tt("dve", Y.t[:, nh * 512:(nh + 1) * 512], p.t[:, :], xt[j].t[:, s, nh * 512:(nh + 1) * 512], ALU.add,
                               [p.b, xt[j].b], [Y.b])
                        fw.dma("pool", y_out.ap[t0 + s * 128:t0 + (s + 1) * 128, :], Y.t[:], reads=[Y.b], writes=[y_out.b(i)])

                load(0)
                if NT > 1:
                    load(1)
                normphase(0)
                for i in range(NT):
                    if i + 1 < NT:
                        normphase(i + 1)
                    outproj(i)
                    if i + 2 < NT:
                        load(i + 2)
                fw.barrier()

        _P = _os.environ.get("KPASSES", "A,HF,HB,AT,ME,C").split(",")
        for l in range(depth):
            if "A" in _P:
                pass_A(l)
            if "HF" in _P:
                pass_H(l, "f")
            if "HB" in _P:
                pass_H(l, "b")
            if "AT" in _P:
                pass_AT(l)
            if "ME" in _P:
                pass_ME(l)
            if "C" in _P:
                pass_C(l)
        fw.barrier()
    return nc, fw


T_CORE = 16384
DEPTH = 4
_CACHE = {}


def kernel(x_prompt, x_sample, mem_prompt, mem_sample, norm_w, w_in, hgrn_lb_fwd, hgrn_lb_bwd, hgrn_onorm_w,
           attn_qnorm_w, attn_knorm_w, mem_norm_w, mem_wkv, mem_qnorm_w, mem_knorm_w, w_out):
    f = lambda a: np.ascontiguousarray(np.asarray(a, dtype=np.float32))
    x_prompt, x_sample, mem_prompt, mem_sample = f(x_prompt), f(x_sample), f(mem_prompt), f(mem_sample)
    T = T_CORE
    if "nc" not in _CACHE:
        _CACHE["nc"] = build(T, DEPTH)[0]
        _CACHE["hc"] = host_consts(T)
    nc = _CACHE["nc"]
    hc = _CACHE["hc"]
    pos_p = np.concatenate([np.arange(8192), np.arange(8192)])
    pos_s = np.arange(16384)
    Cp, Sp = rope_tables(pos_p)
    Cs, Ss = rope_tables(pos_s)
    shared = {"cst": hc["cst"], "amask": hc["amask"], "norm_w": f(norm_w), "w_in": f(w_in),
              "hgrn_lb_fwd": f(hgrn_lb_fwd), "hgrn_lb_bwd": f(hgrn_lb_bwd), "hgrn_onorm_w": f(hgrn_onorm_w),
              "attn_qnorm_w": f(attn_qnorm_w), "attn_knorm_w": f(attn_knorm_w), "mem_norm_w": f(mem_norm_w),
              "mem_wkv": f(mem_wkv), "mem_qnorm_w": f(mem_qnorm_w), "mem_knorm_w": f(mem_knorm_w), "w_out": f(w_out)}
    in_maps = []
    for c in range(8):
        m = dict(shared)
        if c < 4:
            m["x"] = x_prompt[2 * c:2 * c + 2].reshape(T, 1024)
            m["mem"] = mem_prompt[2 * c:2 * c + 2]
            m["flag"] = np.zeros((128, 1), np.float32)
            m["ropeC"], m["ropeS"] = Cp, Sp
        else:
            s = (c - 4) % 2
            m["x"] = x_sample[s]
            m["mem"] = np.stack([mem_sample[s], mem_sample[s]])
            m["flag"] = np.ones((128, 1), np.float32)
            m["ropeC"], m["ropeS"] = Cs, Ss
        in_maps.append(m)
    res = run_bass_kernel_spmd(nc, in_maps, core_ids=list(range(8)))
    ys = [np.asarray(r["y"], dtype=np.float32) for r in res.results]
    y_prompt = np.stack([ys[c].reshape(2, 8192, 1024) for c in range(4)]).reshape(8, 8192, 1024)
    y_sample = np.stack([ys[4], ys[5]])
    return (y_prompt, y_sample)
```
